# Optimizing a Trainium2 kernel written in Bass

```python
import jax, jax.numpy as jnp
from jax import lax
import numpy as np

D_MODEL = 2048
BATCH = 16
SEQ = 2048
DEPTH = 2
DEC_BATCH = 8
DEC_SEQ = 16
PAST_LEN = 1024

CHUNK = 64
Q_BLOCK = 64
HEAD_DIM = 128
ATT_WIDTH = D_MODEL // 2
CONV_WIDTH = D_MODEL // 4
POOL_WIDTH = D_MODEL // 4
N_HEADS = ATT_WIDTH // HEAD_DIM
N_KV_HEADS = 2
KV_WIDTH = N_KV_HEADS * HEAD_DIM
N_IDX_HEADS = 16
IDX_DIM = 64
TOPK_MAX = 256
ROPE_THETA = 10000.0
CONV_K = 31
POOL_WINDOWS = (2, 4, 8, 16)
N_POOL_GROUPS = 4
POOL_GROUP = POOL_WIDTH // N_POOL_GROUPS
POOL_HIST = 15
PLE_DIM = 256
EPS = 1e-6

_SPLITS = (("q", ATT_WIDTH), ("k", KV_WIDTH), ("v", KV_WIDTH), ("qi", N_IDX_HEADS * IDX_DIM),
           ("ki", IDX_DIM), ("wi", N_IDX_HEADS), ("g_att", ATT_WIDTH), ("glu", 2 * CONV_WIDTH),
           ("g_conv", CONV_WIDTH), ("pool_in", POOL_WIDTH), ("g_pool", POOL_WIDTH))
N_IN = 2 * ATT_WIDTH + 2 * KV_WIDTH + N_IDX_HEADS * IDX_DIM + IDX_DIM + N_IDX_HEADS + 3 * CONV_WIDTH + 2 * POOL_WIDTH

kernel_name = "hybrid_stream_dsa_conv_pool"


def _split_cols(z):
    out, off = {}, 0
    for name, w in _SPLITS:
        out[name] = z[..., off:off + w]
        off += w
    return out


def _rmsnorm(x, g):
    xf = x.astype(jnp.float32)
    y = xf * lax.rsqrt(jnp.mean(xf * xf, axis=-1, keepdims=True) + EPS) * g.astype(jnp.float32)
    return y.astype(x.dtype)


def _layernorm(x, g, b):
    xf = x.astype(jnp.float32)
    mu = jnp.mean(xf, axis=-1, keepdims=True)
    var = jnp.mean(jnp.square(xf - mu), axis=-1, keepdims=True)
    y = (xf - mu) * lax.rsqrt(var + EPS) * g.astype(jnp.float32) + b.astype(jnp.float32)
    return y.astype(x.dtype)


def _rope(x, pos):
    half = x.shape[-1] // 2
    freq = ROPE_THETA ** (-jnp.arange(half, dtype=jnp.float32) / half)
    ang = pos.astype(jnp.float32)[:, None] * freq[None, :]
    cos, sin = jnp.cos(ang)[:, None, :], jnp.sin(ang)[:, None, :]
    x1, x2 = x[..., :half].astype(jnp.float32), x[..., half:].astype(jnp.float32)
    return jnp.concatenate([x1 * cos - x2 * sin, x1 * sin + x2 * cos], axis=-1).astype(x.dtype)


def _dsa_block(q, qi, wi, k, v, ki, q_pos, k_pos, topk):
    B, Q = q.shape[:2]
    visible = (k_pos[None, :] // CHUNK) <= (q_pos[:, None] // CHUNK)
    logits = jnp.einsum('bqhd,bsd->bqhs', qi, ki, preferred_element_type=jnp.float32) * IDX_DIM ** -0.5
    score = jnp.einsum('bqhs,bqh->bqs', jax.nn.relu(logits), wi.astype(jnp.float32)) * N_IDX_HEADS ** -0.5
    score = jnp.where(visible[None], score, -jnp.inf)
    _, idx = lax.top_k(score, topk)
    sel_ok = (k_pos[idx] // CHUNK) <= (q_pos[None, :, None] // CHUNK)
    gather = jax.vmap(lambda a, i: a[i])
    kg, vg = gather(k, idx), gather(v, idx)
    qg = q.reshape(B, Q, N_KV_HEADS, N_HEADS // N_KV_HEADS, HEAD_DIM)
    s = jnp.einsum('bqgrd,bqkgd->bqgrk', qg, kg, preferred_element_type=jnp.float32) * HEAD_DIM ** -0.5
    s = jnp.where(sel_ok[:, :, None, None, :], s, -jnp.inf)
    pr = jax.nn.softmax(s, axis=-1).astype(v.dtype)
    o = jnp.einsum('bqgrk,bqkgd->bqgrd', pr, vg)
    return o.reshape(B, Q, ATT_WIDTH)


def _dsa_attention(q, qi, wi, k, v, ki, q_pos, k_pos, topk):
    B, T = q.shape[:2]
    if T <= Q_BLOCK:
        return _dsa_block(q, qi, wi, k, v, ki, q_pos, k_pos, topk)
    nb = T // Q_BLOCK
    blk = lambda a: jnp.moveaxis(a.reshape((B, nb, Q_BLOCK) + a.shape[2:]), 1, 0)
    out = lax.map(lambda a: _dsa_block(a[0], a[1], a[2], k, v, ki, a[3], k_pos, topk),
                  (blk(q), blk(qi), blk(wi), q_pos.reshape(nb, Q_BLOCK)))
    return jnp.moveaxis(out, 0, 1).reshape(B, T, ATT_WIDTH)


def _conformer_conv(u, prev, conv_w, conv_b, ln_g, ln_b, pw):
    xp = jnp.concatenate([prev.astype(u.dtype), u], axis=1)
    y = lax.conv_general_dilated(xp, conv_w.astype(u.dtype)[:, None, :], (1,), 'VALID',
                                 dimension_numbers=('NWC', 'WIO', 'NWC'), feature_group_count=u.shape[-1])
    y = jax.nn.silu(_layernorm(y + conv_b, ln_g, ln_b))
    return y @ pw, xp[:, -(CONV_K - 1):]


def _multiscale_pool(u, prev, pos, pool_w, pool_scale):
    B, T, C = u.shape
    xp = jnp.concatenate([prev.astype(u.dtype), u], axis=1)
    cs = jnp.cumsum(xp.astype(jnp.float32), axis=1)
    cs = jnp.concatenate([jnp.zeros((B, 1, C), jnp.float32), cs], axis=1)
    end = cs[:, POOL_HIST + 1:POOL_HIST + 1 + T]
    means = []
    for gi, w in enumerate(POOL_WINDOWS):
        sl = slice(gi * POOL_GROUP, (gi + 1) * POOL_GROUP)
        win = end[..., sl] - cs[:, POOL_HIST + 1 - w:POOL_HIST + 1 - w + T, sl]
        cnt = jnp.minimum(pos + 1, w).astype(jnp.float32)[:, None]
        means.append(win / cnt)
    r = (jnp.concatenate(means, axis=-1) - u.astype(jnp.float32)).astype(u.dtype)
    r = r.reshape(B, T, N_POOL_GROUPS, POOL_GROUP)
    y = jnp.einsum('btgc,gcd->btgd', r, pool_w).reshape(B, T, C) * pool_scale
    return y, xp[:, -POOL_HIST:]


def _layer(x, p, q_pos, k_past, v_past, ki_past, conv_prev, pool_prev, topk,
           norm_mix, w_in, conv_w, conv_b, conv_ln_g, conv_ln_b, conv_pw, pool_w, pool_scale,
           w_out, norm_ple, w_ple_gate, w_ple_proj):
    B, T, _ = x.shape
    z = _split_cols(_rmsnorm(x, norm_mix) @ w_in)
    q = _rope(z['q'].reshape(B, T, N_HEADS, HEAD_DIM), q_pos)
    k = _rope(z['k'].reshape(B, T, N_KV_HEADS, HEAD_DIM), q_pos)
    v = z['v'].reshape(B, T, N_KV_HEADS, HEAD_DIM)
    qi = _rope(z['qi'].reshape(B, T, N_IDX_HEADS, IDX_DIM), q_pos)
    ki = _rope(z['ki'].reshape(B, T, 1, IDX_DIM), q_pos)[:, :, 0]
    if k_past is None:
        k_all, v_all, ki_all, k_pos = k, v, ki, q_pos
    else:
        k_all = jnp.concatenate([k_past.astype(k.dtype), k], axis=1)
        v_all = jnp.concatenate([v_past.astype(v.dtype), v], axis=1)
        ki_all = jnp.concatenate([ki_past.astype(ki.dtype), ki], axis=1)
        k_pos = jnp.arange(k_all.shape[1])
    o_att = _dsa_attention(q, qi, z['wi'], k_all, v_all, ki_all, q_pos, k_pos, topk) * jax.nn.silu(z['g_att'])
    u = z['glu'][..., :CONV_WIDTH] * jax.nn.sigmoid(z['glu'][..., CONV_WIDTH:])
    o_conv, conv_state = _conformer_conv(u, conv_prev, conv_w, conv_b, conv_ln_g, conv_ln_b, conv_pw)
    o_conv = o_conv * jax.nn.silu(z['g_conv'])
    o_pool, pool_state = _multiscale_pool(z['pool_in'], pool_prev, q_pos, pool_w, pool_scale)
    o_pool = o_pool * jax.nn.silu(z['g_pool'])
    x = x + jnp.concatenate([o_att, o_conv, o_pool], axis=-1) @ w_out
    x = x + jax.nn.sigmoid(_rmsnorm(x, norm_ple) @ w_ple_gate) * (p @ w_ple_proj)
    return x, k, v, ki, conv_state, pool_state


def setup_inputs(seed: int = 0) -> dict:
    key = jax.random.key(seed)
    ks = jax.random.split(key, 24)
    nrm = lambda k, shape, s: jax.random.normal(k, shape, jnp.float32) * s
    return {
        "x_prompt": nrm(ks[0], (BATCH, SEQ, D_MODEL), 1.0),
        "x_sample": nrm(ks[1], (DEC_BATCH, DEC_SEQ, D_MODEL), 1.0),
        "p_prompt": nrm(ks[2], (DEPTH, BATCH, SEQ, PLE_DIM), 1.0),
        "p_sample": nrm(ks[3], (DEPTH, DEC_BATCH, DEC_SEQ, PLE_DIM), 1.0),
        "cache_k": nrm(ks[4], (DEPTH, DEC_BATCH, PAST_LEN, N_KV_HEADS, HEAD_DIM), 1.0),
        "cache_v": nrm(ks[5], (DEPTH, DEC_BATCH, PAST_LEN, N_KV_HEADS, HEAD_DIM), 1.0),
        "cache_kidx": nrm(ks[6], (DEPTH, DEC_BATCH, PAST_LEN, IDX_DIM), 1.0),
        "state_conv": nrm(ks[7], (DEPTH, DEC_BATCH, CONV_K - 1, CONV_WIDTH), 0.5),
        "state_pool": nrm(ks[8], (DEPTH, DEC_BATCH, POOL_HIST, POOL_WIDTH), 1.0),
        "norm_mix": 1.0 + nrm(ks[9], (DEPTH, D_MODEL), 0.05),
        "w_in": nrm(ks[10], (DEPTH, D_MODEL, N_IN), D_MODEL ** -0.5),
        "conv_w": nrm(ks[11], (DEPTH, CONV_K, CONV_WIDTH), CONV_K ** -0.5),
        "conv_b": nrm(ks[12], (DEPTH, CONV_WIDTH), 0.02),
        "conv_ln_g": 1.0 + nrm(ks[13], (DEPTH, CONV_WIDTH), 0.05),
        "conv_ln_b": nrm(ks[14], (DEPTH, CONV_WIDTH), 0.02),
        "conv_pw": nrm(ks[15], (DEPTH, CONV_WIDTH, CONV_WIDTH), CONV_WIDTH ** -0.5),
        "pool_w": nrm(ks[16], (DEPTH, N_POOL_GROUPS, POOL_GROUP, POOL_GROUP), POOL_GROUP ** -0.5),
        "pool_scale": 1.0 + nrm(ks[17], (DEPTH, POOL_WIDTH), 0.1),
        "w_out": nrm(ks[18], (DEPTH, D_MODEL, D_MODEL), D_MODEL ** -0.5),
        "norm_ple": 1.0 + nrm(ks[19], (DEPTH, D_MODEL), 0.05),
        "w_ple_gate": nrm(ks[20], (DEPTH, D_MODEL, D_MODEL), D_MODEL ** -0.5),
        "w_ple_proj": nrm(ks[21], (DEPTH, PLE_DIM, D_MODEL), PLE_DIM ** -0.5),
        "norm_final": 1.0 + nrm(ks[22], (D_MODEL,), 0.05),
    }


def reference(x_prompt, x_sample, p_prompt, p_sample, cache_k, cache_v, cache_kidx, state_conv, state_pool,
              norm_mix, w_in, conv_w, conv_b, conv_ln_g, conv_ln_b, conv_pw, pool_w, pool_scale,
              w_out, norm_ple, w_ple_gate, w_ple_proj, norm_final):
    Bp, Tp, _ = x_prompt.shape
    Bs, Ts, _ = x_sample.shape
    past = cache_k.shape[2]
    topk_p = min(TOPK_MAX, Tp // 4)
    topk_s = min(TOPK_MAX, (past + Ts) // 4)
    pos_p = jnp.arange(Tp)
    pos_s = past + jnp.arange(Ts)
    conv0 = jnp.zeros((Bp, CONV_K - 1, CONV_WIDTH), x_prompt.dtype)
    pool0 = jnp.zeros((Bp, POOL_HIST, POOL_WIDTH), x_prompt.dtype)
    hp, hs = x_prompt, x_sample
    kp, vp, kip, cp, pp = [], [], [], [], []
    kss, vss, kis, cs_, ps = [], [], [], [], []
    for i in range(DEPTH):
        lw = (norm_mix[i], w_in[i], conv_w[i], conv_b[i], conv_ln_g[i], conv_ln_b[i], conv_pw[i],
              pool_w[i], pool_scale[i], w_out[i], norm_ple[i], w_ple_gate[i], w_ple_proj[i])
        hp, k1, v1, ki1, c1, q1 = _layer(hp, p_prompt[i], pos_p, None, None, None, conv0, pool0, topk_p, *lw)
        hs, k2, v2, ki2, c2, q2 = _layer(hs, p_sample[i], pos_s, cache_k[i], cache_v[i], cache_kidx[i],
                                         state_conv[i], state_pool[i], topk_s, *lw)
        kp.append(k1); vp.append(v1); kip.append(ki1); cp.append(c1); pp.append(q1)
        kss.append(k2); vss.append(v2); kis.append(ki2); cs_.append(c2); ps.append(q2)
    y_prompt = _rmsnorm(hp, norm_final)
    y_sample = _rmsnorm(hs, norm_final)
    return (y_prompt, y_sample,
            jnp.stack(kp), jnp.stack(vp), jnp.stack(kip), jnp.stack(cp), jnp.stack(pp),
            jnp.stack(kss), jnp.stack(vss), jnp.stack(kis), jnp.stack(cs_), jnp.stack(ps))
```

```python
from contextlib import ExitStack
import numpy as np
import ml_dtypes
import concourse.bass as bass
import concourse.mybir as mybir
from concourse.bass_utils import run_bass_kernel_spmd

F32 = mybir.dt.float32
BF16 = mybir.dt.bfloat16
AF = mybir.ActivationFunctionType
ALU = mybir.AluOpType

NPOOL = 16
EPOCH = 30000

D = 2048
NIN = 6224
PLE = 256
PAST = 1024
TS = 16
EPS = 1e-6
NIT = 22
W0 = 8.0


class Buf:
    __slots__ = ("name", "w", "r")

    def __init__(self, name=""):
        self.name = name
        self.w = None
        self.r = {}


class Op:
    __slots__ = ("id", "eng", "fn", "deps", "dma", "sig")


class Prog:
    ENGS = ("pe", "act", "dve", "pool", "sp")

    def __init__(self, nc):
        self.nc = nc
        self.ops = []
        self.by_eng = {e: [] for e in self.ENGS}
        self.out_dmas = []
        self.last = {e: None for e in self.ENGS}
        self.dmas_since_bar = []

    def emit(self, eng, fn, reads=(), writes=(), dma=False, out=False):
        op = Op()
        op.id = len(self.ops)
        op.eng = eng
        op.fn = fn
        op.dma = dma
        op.sig = None
        deps = {}
        for b in reads:
            if b.w is not None:
                deps[b.w] = "RAW"
        for b in writes:
            if b.w is not None:
                deps.setdefault(b.w, "WAW")
            for r in b.r.values():
                deps.setdefault(r, "WAR")
        key = ("dma", op.id) if dma else eng
        for b in reads:
            b.r[key] = op.id
        for b in writes:
            b.w = op.id
            b.r = {}
        op.deps = deps
        self.ops.append(op)
        self.by_eng[eng].append(op)
        if dma:
            self.dmas_since_bar.append(op.id)
        else:
            self.last[eng] = op.id
        if out:
            self.out_dmas.append(op.id)
        return op

    def dma(self, eng, out, in_, reads, writes, is_out=False):
        return self.emit(eng, lambda e: e.dma_start(out=out, in_=in_), reads, writes, dma=True, out=is_out)

    def barrier(self):
        deps = {}
        for e in self.ENGS:
            if self.last[e] is not None:
                deps[self.last[e]] = "RAW"
        for d in self.dmas_since_bar:
            deps[d] = "RAW"
        self.dmas_since_bar = []
        for e in self.ENGS:
            op = Op()
            op.id = len(self.ops)
            op.eng = e
            op.fn = None
            op.dma = False
            op.sig = None
            op.deps = {k: ("BAR" if self.ops[k].eng != e or self.ops[k].dma else "WAW") for k in deps}
            self.ops.append(op)
            self.by_eng[e].append(op)

    def _skip(self, o, d, kind):
        if d.dma or o.dma:
            return False
        if o.eng == d.eng:
            if o.eng == "pe":
                return True
            return kind != "RAW"
        return False

    def finish(self):
        op = Op()
        op.id = len(self.ops)
        op.eng = "sp"
        op.fn = None
        op.dma = False
        op.sig = None
        op.deps = {d: "RAW" for d in self.out_dmas}
        self.ops.append(op)
        self.by_eng["sp"].append(op)

    def build(self, stack):
        nc = self.nc
        ops = self.ops
        for q in ("sp", "pool", "act"):
            dma_ops = [o for o in ops if o.dma and o.eng == q]
            for j, o in enumerate(dma_ops):
                o.sig = (("dma", q + str(j % NPOOL)), 16 * (j // NPOOL + 1))
                if j >= NPOOL:
                    o.deps.setdefault(dma_ops[j - NPOOL].id, "GUARD")
        needed = set()
        for o in ops:
            for d, kind in o.deps.items():
                if not self._skip(o, ops[d], kind):
                    needed.add(d)
        cnt = {e: 0 for e in self.ENGS}
        semkeys = set()
        for o in ops:
            if o.dma:
                semkeys.add(o.sig[0])
            elif o.id in needed:
                c = cnt[o.eng]
                o.sig = ((o.eng, c // EPOCH), c % EPOCH + 1)
                cnt[o.eng] = c + 1
                semkeys.add(o.sig[0])
        sems = {}
        for k in sorted(semkeys, key=str):
            sems[k] = stack.enter_context(nc.semaphore("s_%s_%s" % (k[0], k[1])))
        self.n_sems = len(sems)
        self.n_waits = 0
        block = stack.enter_context(nc.Block())

        def run(engname, e):
            waited = {}
            for o in self.by_eng[engname]:
                req = {}
                for d in o.deps:
                    dop = ops[d]
                    if self._skip(o, dop, o.deps[d]):
                        continue
                    sk, val = dop.sig
                    if sk[0] == "dma":
                        if req.get(sk, 0) < val:
                            req[sk] = val
                    else:
                        cur = req.get(sk[0], (-1, 0))
                        if cur < (sk[1], val):
                            req[sk[0]] = (sk[1], val)
                for k in sorted(req, key=str):
                    v = req[k]
                    if isinstance(k, tuple):
                        if waited.get(k, 0) >= v:
                            continue
                        waited[k] = v
                        e.wait_ge(sems[k], v)
                    else:
                        if waited.get(k, (-1, 0)) >= v:
                            continue
                        waited[k] = v
                        e.wait_ge(sems[(k, v[0])], v[1])
                    self.n_waits += 1
                if o.fn is None:
                    continue
                ins = o.fn(e)
                if o.sig is not None:
                    ins.then_inc(sems[o.sig[0]], 16 if o.dma else 1)

        @block.tensor
        def _(e):
            run("pe", e)

        @block.scalar
        def _(e):
            run("act", e)

        @block.vector
        def _(e):
            run("dve", e)

        @block.gpsimd
        def _(e):
            run("pool", e)

        @block.sync
        def _(e):
            run("sp", e)


COLBLOCKS = [
    ("q", [(0, 512)], 0), ("q", [(512, 512)], 1),
    ("kv", [(1024, 512)], 0),
    ("qi", [(1536, 512)], 0), ("qi", [(2048, 512)], 1),
    ("kiwi", [(2560, 80)], 0),
    ("gate", [(2640, 512)], 0), ("gate", [(3152, 512)], 512),
    ("glu", [(3664, 256), (4176, 256)], 0), ("glu", [(3920, 256), (4432, 256)], 1),
    ("gate", [(4688, 512)], 1024),
    ("pin", [(5200, 512)], 0),
    ("gate", [(5712, 512)], 1536),
]


class Arena:
    def __init__(self, ap, nwords):
        self.ap = ap
        self.n = nwords
        self.off = 0

    def mark(self):
        return self.off

    def release(self, m):
        self.off = m

    def alloc(self, free_shape, dt=F32):
        n = int(np.prod(free_shape))
        w = n if dt == F32 else (n + 1) // 2
        a = self.ap[:, self.off:self.off + w]
        self.off += (w + 7) // 8 * 8
        assert self.off <= self.n, "SBUF arena overflow %d > %d" % (self.off, self.n)
        if dt != F32:
            a = a.bitcast(dt)
        if len(free_shape) == 2:
            a = a.rearrange("p (a b) -> p a b", b=free_shape[1])
        elif len(free_shape) == 3:
            a = a.rearrange("p (a b c) -> p a b c", b=free_shape[1], c=free_shape[2])
        return a


class _Stop(Exception):
    pass


def build_program(NP, T, L, has_sample=True, arena_words=53200, dbg_stop=None):
    nc = bass.Bass("TRN2", target_bir_lowering=False)
    NTP = T // 128
    TOPK_P = min(256, T // 4)
    TOPK_S = min(256, (PAST + TS) // 4)

    def din(name, shape, dt=F32):
        return nc.dram_tensor(name, list(shape), dt, kind="ExternalInput").ap()

    def dout(name, shape, dt=F32):
        return nc.dram_tensor(name, list(shape), dt, kind="ExternalOutput").ap()

    def dscr(name, shape, dt=F32):
        return nc.dram_tensor(name, list(shape), dt).ap()

    I = {}
    I["xp"] = din("xp", [NP, T, D])
    I["pp"] = din("pp", [L, NP, T, PLE])
    I["xs"] = din("xs", [1, TS, D])
    I["ps"] = din("ps", [L, 1, TS, PLE])
    I["ck"] = din("ck", [L, 1, PAST, 256])
    I["cv"] = din("cv", [L, 1, PAST, 256])
    I["cki"] = din("cki", [L, 1, PAST, 64])
    I["sconv"] = din("sconv", [L, 1, 30, 512])
    I["spool"] = din("spool", [L, 1, 15, 512])
    I["norm_mix"] = din("norm_mix", [L, D])
    I["w_in"] = din("w_in", [L, D, NIN])
    I["conv_w"] = din("conv_w", [L, 31, 512])
    I["conv_b"] = din("conv_b", [L, 512])
    I["conv_ln_g"] = din("conv_ln_g", [L, 512])
    I["conv_ln_b"] = din("conv_ln_b", [L, 512])
    I["conv_pw"] = din("conv_pw", [L, 512, 512])
    I["pool_w"] = din("pool_w", [L, 4, 128, 128])
    I["pool_scale"] = din("pool_scale", [L, 512])
    I["w_out"] = din("w_out", [L, D, D])
    I["norm_ple"] = din("norm_ple", [L, D])
    I["w_ple_gate"] = din("w_ple_gate", [L, D, D])
    I["w_ple_proj"] = din("w_ple_proj", [L, PLE, D])
    I["norm_final"] = din("norm_final", [1, D])
    I["c_ident_bf"] = din("c_ident_bf", [128, 128], BF16)
    I["c_ident_f"] = din("c_ident_f", [128, 128])
    I["c_rope_p"] = din("c_rope_p", [T, 192])
    I["c_rope_s"] = din("c_rope_s", [TS, 192])
    I["c_negmask"] = din("c_negmask", [128, 128])
    I["c_bands"] = din("c_bands", [128, 12, 128], BF16)
    I["c_bands_s"] = din("c_bands_s", [16, 8, 16], BF16)

    O = {}
    O["y_p"] = dout("y_p", [NP, T, D])
    O["y_s"] = dout("y_s", [1, TS, D])
    O["k_p"] = dout("k_p", [L, NP, T, 256])
    O["v_p"] = dout("v_p", [L, NP, T, 256])
    O["ki_p"] = dout("ki_p", [L, NP, T, 64])
    O["conv_p"] = dout("conv_p", [L, NP, 30, 512])
    O["pool_p"] = dout("pool_p", [L, NP, 15, 512])
    O["k_s"] = dout("k_s", [L, 1, TS, 256])
    O["v_s"] = dout("v_s", [L, 1, TS, 256])
    O["ki_s"] = dout("ki_s", [L, 1, TS, 64])
    O["conv_s"] = dout("conv_s", [L, 1, 30, 512])
    O["pool_s"] = dout("pool_s", [L, 1, 15, 512])

    S_qT = dscr("s_qT", [NTP, 128, 1024], BF16)
    S_qiT = dscr("s_qiT", [NTP, 128, 1024], BF16)
    S_gates = dscr("s_gates", [T, D], BF16)
    S_uT = dscr("s_uT", [NTP, 128, 512], BF16)
    S_pin = dscr("s_pin", [T, 512], BF16)
    S_dg = dscr("s_dg", [L, 128, 31 * 4 * 128], BF16)
    S_pw = dscr("s_pw", [L, 128, 4 * 512], BF16)
    S_plw = dscr("s_plw", [L, 128, 4 * 128], BF16)
    p2_cached = set()
    S_x1 = dscr("s_x1", [T, D])
    S_xres = dscr("s_xres", [T, D])

    st = ExitStack()
    P = Prog(nc)
    arena_t = st.enter_context(nc.sbuf_tensor("arena", [128, arena_words], F32))
    A = Arena(arena_t[:, :], arena_words)
    banks = [st.enter_context(nc.psum_tensor("pb%d" % i, [128, 512], F32)) for i in range(8)]
    PB = [b[:, :] for b in banks]
    PBb = [Buf("pb%d" % i) for i in range(8)]
    TRb = PB[7].bitcast(BF16)
    TRb6 = PB[6].bitcast(BF16)
    TR = [TRb[:, 0:512], TRb6[:, 0:512]]
    TRB = [PBb[7], PBb[6]]

    def tr_mode(two):
        if two:
            TR[0], TR[1] = TRb[:, 0:512], TRb6[:, 0:512]
            TRB[0], TRB[1] = PBb[7], PBb[6]
        else:
            TR[0], TR[1] = TRb[:, 0:512], TRb[:, 512:1024]
            TRB[0], TRB[1] = PBb[7], PBb[7]
    trc = [0]
    phase_ctr = [0]

    def phase_end():
        P.barrier()
        phase_ctr[0] += 1
        if dbg_stop is not None and phase_ctr[0] >= dbg_stop:
            raise _Stop()

    ident = A.alloc([128], BF16); B_const = Buf("const")
    identf = A.alloc([128])
    negmask = A.alloc([128])
    epsb = A.alloc([8])
    actT = A.alloc([16, T], BF16); B_actT = Buf("actT")
    wbuf = [None, None]
    B_wbuf = [Buf("w0"), Buf("w1")]

    def alloc_wbuf(need_stage):
        wbuf[0] = A.alloc([16, 512], BF16)
        wbuf[1] = A.alloc([16, 512], BF16)
        if need_stage:
            stg[0] = A.alloc([8, 512])
    NKMAX = max(T, PAST + TS)
    NBLK = (NKMAX + 127) // 128
    kT = A.alloc([2, NKMAX], BF16); B_kT = Buf("kT")
    V1 = A.alloc([NBLK, 2, 129], BF16); B_V1 = Buf("V1")
    kiT2 = A.alloc([NKMAX], BF16); B_kiT = Buf("kiT")
    wia = A.alloc([NTP, 32]); B_wi = Buf("wi")
    gbc_h = [None]; B_gbc = Buf("gbc")
    wcnt = [0]

    P.dma("sp", ident, I["c_ident_bf"], [], [B_const])
    P.dma("sp", identf, I["c_ident_f"], [], [B_const])
    P.dma("sp", negmask, I["c_negmask"], [], [B_const])
    P.emit("dve", lambda e: e.memset(epsb, EPS), [], [B_const])
    P.emit("dve", lambda e: e.memset(V1[:, :, :, 128:129], 1.0), [], [B_V1])

    def transpose_to(dst_fn, src, R, ncol, reads, writes, eng="act"):
        raise NotImplementedError

    def tr_group(srcs, R, dst, reads, writes, eng="act"):
        i = trc[0] % 2
        trc[0] += 1
        n = len(srcs)
        c = srcs[0].shape[1]
        tv = TR[i].rearrange("p (a b) -> p a b", b=128)
        for j, s in enumerate(srcs):
            P.emit("pe", lambda e, s=s, j=j: e.transpose(tv[:c, j, :R], s, ident[:R, :R]),
                   reads + [B_const], [TRB[i]])
        if eng == "act":
            P.emit("act", lambda e: e.copy(dst, tv[:c, 0:n, :R]), [TRB[i]], writes)
        else:
            P.emit("dve", lambda e: e.tensor_copy(dst, tv[:c, 0:n, :R]), [TRB[i]], writes)

    WB = {}
    converted = set()
    stg = [None]
    B_stg = Buf("stg")

    NBLK_W = {"w_in": len(COLBLOCKS), "w_out": 4, "w_ple_gate": 4}

    def load_wblock(name, layer, ranges, kch, bid):
        w_ap = I[name][layer]
        key = (name, layer)
        if key not in WB:
            WB[key] = dscr("wb_%s_%d" % (name, layer), [NBLK_W[name], 128, 16 * 512], BF16)
        wsc = WB[key]
        i = wcnt[0] % 2
        wcnt[0] += 1
        o = sum(n for (_, n) in ranges)
        ck = (name, layer, bid)
        if ck in converted:
            P.dma("sp", wbuf[i].rearrange("p a b -> p (a b)"), wsc[bid], [], [B_wbuf[i]])
        else:
            converted.add(ck)
            sg_ = stg[0]
            o2 = 0
            for (c0, n) in ranges:
                for k8 in range(0, kch, 8):
                    for k4 in range(k8, min(kch, k8 + 8), 4):
                        kk = min(4, kch - k4)
                        P.dma("pool", sg_[:, k4 - k8:k4 - k8 + kk, 0:n],
                              w_ap[k4 * 128:(k4 + kk) * 128, c0:c0 + n].rearrange("(kc p) n -> p kc n", p=128), [], [B_stg])
                    for k4 in range(k8, min(kch, k8 + 8), 4):
                        kk = min(4, kch - k4)
                        P.emit("pool", lambda e, k4=k4, kk=kk, k8=k8, o2=o2, n=n, i=i, sg_=sg_: e.tensor_copy(
                            wbuf[i][:, k4:k4 + kk, o2:o2 + n], sg_[:, k4 - k8:k4 - k8 + kk, 0:n]), [B_stg], [B_wbuf[i]])
                o2 += n
            P.dma("pool", wsc[bid], wbuf[i].rearrange("p a b -> p (a b)"), [B_wbuf[i]], [])
        return wbuf[i], B_wbuf[i], o

    def make_sc():
        sc = {}
        sc["xt"] = [A.alloc([D]), A.alloc([D])]; sc["B_xt"] = [Buf(), Buf()]
        sc["junk"] = A.alloc([D], BF16); sc["B_junk"] = Buf()
        sc["xn"] = [A.alloc([D], BF16), A.alloc([D], BF16)]; sc["B_xn"] = [Buf(), Buf()]
        sc["ss"] = A.alloc([8]); sc["B_ss"] = Buf()
        return sc

    def rmsnorm_T(j, R, tok0, sc):
        xt, B_xt = sc["xt"][j], sc["B_xt"][j]
        xn, B_xn = sc["xn"][j], sc["B_xn"][j]
        gbc = gbc_h[0]
        P.emit("act", lambda e: e.activation(sc["junk"][:R, :], xt[:R, :], AF.Square, accum_out=sc["ss"][:R, 0:1]),
               [B_xt], [sc["B_junk"], sc["B_ss"]])
        P.emit("act", lambda e: e.activation(sc["ss"][:R, 1:2], sc["ss"][:R, 0:1], AF.Sqrt, bias=epsb[:R, 0:1], scale=1.0 / D),
               [sc["B_ss"], B_const], [sc["B_ss"]])
        P.emit("dve", lambda e: e.reciprocal(sc["ss"][:R, 2:3], sc["ss"][:R, 1:2]), [sc["B_ss"]], [sc["B_ss"]])
        P.emit("dve", lambda e: e.scalar_tensor_tensor(out=xn[:R, :], in0=xt[:R, :], scalar=sc["ss"][:R, 2:3],
                                                       in1=gbc[:R, :], op0=ALU.mult, op1=ALU.mult),
               [B_xt, sc["B_ss"], B_gbc], [B_xn])

        def part_b():
            for c4 in range(4):
                srcs = [xn[:R, (c4 * 4 + jj) * 128:(c4 * 4 + jj + 1) * 128] for jj in range(4)]
                tr_group(srcs, R, actT[:, c4 * 4:c4 * 4 + 4, tok0:tok0 + R], [B_xn], [B_actT], eng=("act" if c4 % 2 == 0 else "dve"))
        return part_b

    def rope(e_eng, src, dst1, dst2, cos, sin, tmp, R, nh, half, reads, writes_list, B_tmp):
        s4 = src.rearrange("p (h t d) -> p h t d", h=nh, t=2)
        x1 = s4[:, :, 0, :]
        x2 = s4[:, :, 1, :]
        cb = cos.unsqueeze(1).to_broadcast([R, nh, half])
        sb_ = sin.unsqueeze(1).to_broadcast([R, nh, half])
        t4 = tmp[:R, 0:nh * half * 4].rearrange("p (k h d) -> p k h d", k=4, h=nh)
        P.emit("dve", lambda e: e.tensor_tensor(t4[:, 0], x1, cb, ALU.mult), reads, [B_tmp])
        P.emit("dve", lambda e: e.tensor_tensor(t4[:, 1], x2, sb_, ALU.mult), reads, [B_tmp])
        P.emit("dve", lambda e: e.tensor_tensor(t4[:, 2], x1, sb_, ALU.mult), reads, [B_tmp])
        P.emit("dve", lambda e: e.tensor_tensor(t4[:, 3], x2, cb, ALU.mult), reads, [B_tmp])
        P.emit("dve", lambda e: e.tensor_tensor(dst1, t4[:, 0], t4[:, 1], ALU.subtract), [B_tmp], writes_list)
        P.emit("dve", lambda e: e.tensor_tensor(dst2, t4[:, 2], t4[:, 3], ALU.add), [B_tmp], writes_list)

    def run_seq_layer(seq, layer, is_sample, x_src, last_layer):
        Tn = TS if is_sample else T
        NT = 1 if is_sample else NTP
        R_of = (lambda t: TS) if is_sample else (lambda t: 128)
        b = 0 if is_sample else seq
        rope_src = I["c_rope_s"] if is_sample else I["c_rope_p"]
        kbase = PAST if is_sample else 0
        topk = TOPK_S if is_sample else TOPK_P
        kname = "s" if is_sample else "p"
        p_src = I["ps"][layer, 0] if is_sample else I["pp"][layer, seq]

        if is_sample:
            m0 = A.mark()
            tr_mode(True)
            ckb = A.alloc([256], BF16); B_ckb = Buf()
            ckib = A.alloc([128], BF16); B_ckib = Buf()
            for blk in range(PAST // 128):
                r0 = blk * 128
                P.dma("pool", ckb, I["ck"][layer, 0, r0:r0 + 128, :], [], [B_ckb])
                tr_group([ckb[:, 0:128], ckb[:, 128:256]], 128, kT[:, :, r0:r0 + 128], [B_ckb], [B_kT])
                P.dma("pool", V1[:, blk, :, 0:128], I["cv"][layer, 0, r0:r0 + 128, :].rearrange("p (g d) -> p g d", g=2),
                      [], [B_V1])
                P.dma("pool", ckib[:, 0:64], I["cki"][layer, 0, r0:r0 + 128, :], [], [B_ckib])
                P.dma("pool", ckib[:, 64:128], I["cki"][layer, 0, r0:r0 + 128, :], [], [B_ckib])
                tr_group([ckib[:, 0:128]], 128, kiT2[:, r0:r0 + 128].unsqueeze(1), [B_ckib], [B_kiT])
            A.release(m0)
            phase_end()

        m1 = A.mark()
        tr_mode(True)
        first_use = ("w_in", layer, 0) not in converted
        alloc_wbuf(first_use)
        sc = make_sc()
        ropet = A.alloc([NT, 192]); B_rope = Buf()
        zs = [A.alloc([512]), A.alloc([512])]; B_zs = [Buf(), Buf()]
        tmp = A.alloc([1024]); B_tmp = Buf()
        ob = [A.alloc([512], BF16), A.alloc([512], BF16)]; B_ob = [Buf(), Buf()]
        of = [A.alloc([512]), A.alloc([512])]; B_of = [Buf(), Buf()]
        tT = [A.alloc([4, 128], BF16), A.alloc([4, 128], BF16)]; B_tT = [Buf(), Buf()]

        gbc_h[0] = A.alloc([D])
        P.dma("sp", gbc_h[0], I["norm_mix"][layer, :].partition_broadcast(128), [], [B_gbc])
        if is_sample:
            P.dma("sp", ropet[:TS, 0, :], rope_src, [], [B_rope])
        else:
            P.dma("sp", ropet, rope_src.rearrange("(n p) c -> p n c", p=128), [], [B_rope])
        nxt_w = load_wblock("w_in", layer, COLBLOCKS[0][1], 16, 0)
        pend = []
        for t in range(NT):
            R = R_of(t)
            P.dma("sp", sc["xt"][t % 2][:R, :], x_src[t * 128:t * 128 + R, :], [], [sc["B_xt"][t % 2]])
            pb_fn = rmsnorm_T(t % 2, R, t * 128, sc)
            while pend:
                pend.pop(0)()
            pend.append(pb_fn)
        while pend:
            pend.pop(0)()

        ec = [0]
        for bi_, (kind, ranges, arg) in enumerate(COLBLOCKS):
            wb, B_wb, ncols = nxt_w
            if bi_ + 1 < len(COLBLOCKS):
                nxt_w = load_wblock("w_in", layer, COLBLOCKS[bi_ + 1][1], 16, bi_ + 1)
            for t in range(NT):
                R = R_of(t)
                tok0 = t * 128
                i = ec[0] % 2
                ec[0] += 1
                ps, B_ps = PB[(ec[0] - 1) % 4], PBb[(ec[0] - 1) % 4]
                for kc in range(16):
                    P.emit("pe", lambda e, kc=kc, ps=ps, R=R, tok0=tok0, wb=wb, ncols=ncols:
                           e.matmul(ps[:R, :ncols], actT[:, kc, tok0:tok0 + R], wb[:, kc, :ncols],
                                    start=(kc == 0), stop=(kc == 15)),
                           [B_actT, B_wb], [B_ps])
                while pend:
                    pend.pop(0)()
                z, B_z = zs[i], B_zs[i]
                o_b, B_o = ob[i], B_ob[i]
                o_f, B_f = of[i], B_of[i]
                tt_, B_tt = tT[i], B_tT[i]
                cos128 = ropet[:R, t, 0:64]
                sin128 = ropet[:R, t, 64:128]
                cos64 = ropet[:R, t, 128:160]
                sin64 = ropet[:R, t, 160:192]
                if kind == "q":
                    P.emit("act", lambda e, z=z, ps=ps, R=R: e.copy(z[:R, :], ps[:R, :]), [B_ps], [B_z])
                    o4 = o_b[:R, :].rearrange("p (h t d) -> p h t d", h=4, t=2)
                    rope("dve", z[:R, :], o4[:, :, 0, :], o4[:, :, 1, :], cos128, sin128, tmp, R, 4, 64,
                         [B_z, B_rope], [B_o], B_tmp)
                    def pb_(o_b=o_b, R=R, tt_=tt_, B_o=B_o, B_tt=B_tt, t=t, arg=arg):
                        tr_group([o_b[:R, j * 128:(j + 1) * 128] for j in range(4)], R, tt_[:, :, :R], [B_o], [B_tt])
                        P.dma("sp", S_qT[t, :, arg * 512:(arg + 1) * 512].rearrange("p (a b) -> p a b", b=128)[:, :, :R],
                              tt_[:, :, :R], [B_tt], [])
                    pend.append(pb_)
                elif kind == "qi":
                    P.emit("act", lambda e, z=z, ps=ps, R=R: e.copy(z[:R, :], ps[:R, :]), [B_ps], [B_z])
                    o4 = o_b[:R, :].rearrange("p (h t d) -> p h t d", h=8, t=2)
                    rope("dve", z[:R, :], o4[:, :, 0, :], o4[:, :, 1, :], cos64, sin64, tmp, R, 8, 32,
                         [B_z, B_rope], [B_o], B_tmp)
                    def pb_(o_b=o_b, R=R, tt_=tt_, B_o=B_o, B_tt=B_tt, t=t, arg=arg):
                        tr_group([o_b[:R, j * 128:(j + 1) * 128] for j in range(4)], R, tt_[:, :, :R], [B_o], [B_tt])
                        P.dma("sp", S_qiT[t, :, arg * 512:(arg + 1) * 512].rearrange("p (a b) -> p a b", b=128)[:, :, :R],
                              tt_[:, :, :R], [B_tt], [])
                    pend.append(pb_)
                elif kind == "kv":
                    P.emit("act", lambda e, z=z, ps=ps, R=R: e.copy(z[:R, :], ps[:R, :]), [B_ps], [B_z])
                    f4 = o_f[:R, 0:256].rearrange("p (h t d) -> p h t d", h=2, t=2)
                    rope("dve", z[:R, 0:256], f4[:, :, 0, :], f4[:, :, 1, :], cos128, sin128, tmp, R, 2, 64,
                         [B_z, B_rope], [B_f], B_tmp)
                    P.dma("sp", O["k_" + kname][layer, b, tok0:tok0 + R, :], o_f[:R, 0:256], [B_f], [], is_out=True)
                    P.dma("sp", O["v_" + kname][layer, b, tok0:tok0 + R, :], z[:R, 256:512], [B_z], [], is_out=True)
                    P.emit("act", lambda e, o_b=o_b, o_f=o_f, R=R: e.copy(o_b[:R, 0:256], o_f[:R, 0:256]), [B_f], [B_o])
                    k0 = kbase + tok0
                    pend.append(lambda o_b=o_b, R=R, k0=k0, B_o=B_o: tr_group(
                        [o_b[:R, 0:128], o_b[:R, 128:256]], R, kT[:, :, k0:k0 + R], [B_o], [B_kT]))
                    blk = k0 // 128
                    P.emit("dve", lambda e, z=z, R=R, blk=blk: e.tensor_copy(
                        V1[:R, blk, :, 0:128], z[:R, 256:512].rearrange("p (g d) -> p g d", g=2)), [B_z], [B_V1])
                elif kind == "kiwi":
                    P.emit("act", lambda e, z=z, ps=ps, R=R: e.copy(z[:R, 0:80], ps[:R, 0:80]), [B_ps], [B_z])
                    f4 = o_f[:R, 0:64].rearrange("p (h t d) -> p h t d", h=1, t=2)
                    rope("dve", z[:R, 0:64], f4[:, :, 0, :], f4[:, :, 1, :], cos64, sin64, tmp, R, 1, 32,
                         [B_z, B_rope], [B_f], B_tmp)
                    P.dma("sp", O["ki_" + kname][layer, b, tok0:tok0 + R, :], o_f[:R, 0:64], [B_f], [], is_out=True)
                    P.emit("act", lambda e, o_b=o_b, o_f=o_f, R=R: e.copy(o_b[:R, 0:64], o_f[:R, 0:64]), [B_f], [B_o])
                    P.emit("act", lambda e, o_b=o_b, o_f=o_f, R=R: e.copy(o_b[:R, 64:128], o_f[:R, 0:64]), [B_f], [B_o])
                    k0 = kbase + tok0
                    pend.append(lambda o_b=o_b, R=R, k0=k0, B_o=B_o: tr_group(
                        [o_b[:R, 0:128]], R, kiT2[:, k0:k0 + R].unsqueeze(1), [B_o], [B_kiT]))
                    P.emit("act", lambda e, z=z, R=R, t=t: e.activation(wia[:R, t, 0:16], z[:R, 64:80], AF.Abs, scale=1.0 / 32.0),
                           [B_z], [B_wi])
                    P.emit("act", lambda e, z=z, R=R, t=t: e.activation(wia[:R, t, 16:32], z[:R, 64:80], AF.Sign),
                           [B_z], [B_wi])
                elif kind == "gate":
                    P.emit("act", lambda e, o_b=o_b, ps=ps, R=R: e.activation(o_b[:R, :], ps[:R, :], AF.Silu), [B_ps], [B_o])
                    P.dma("sp", S_gates[tok0:tok0 + R, arg:arg + 512], o_b[:R, :], [B_o], [])
                elif kind == "glu":
                    P.emit("act", lambda e, z=z, ps=ps, R=R: e.activation(z[:R, 0:256], ps[:R, 256:512], AF.Sigmoid), [B_ps], [B_z])
                    P.emit("dve", lambda e, z=z, ps=ps, R=R, o_f=o_f: e.tensor_tensor(o_f[:R, 0:256], ps[:R, 0:256], z[:R, 0:256], ALU.mult),
                           [B_ps, B_z], [B_f])
                    c0 = arg * 256
                    if is_sample:
                        P.dma("sp", O["conv_s"][layer, 0, 14:30, c0:c0 + 256], o_f[:TS, 0:256], [B_f], [], is_out=True)
                    elif t == NT - 1:
                        P.dma("sp", O["conv_p"][layer, b, 0:30, c0:c0 + 256], o_f[98:128, 0:256], [B_f], [], is_out=True)
                    P.emit("act", lambda e, o_b=o_b, o_f=o_f, R=R: e.copy(o_b[:R, 0:256], o_f[:R, 0:256]), [B_f], [B_o])
                    def pb_(o_b=o_b, R=R, tt_=tt_, B_o=B_o, B_tt=B_tt, t=t, c0=c0):
                        tr_group([o_b[:R, 0:128], o_b[:R, 128:256]], R, tt_[:, 0:2, :R], [B_o], [B_tt])
                        P.dma("sp", S_uT[t, :, c0:c0 + 256].rearrange("p (a b) -> p a b", b=128)[:, :, :R],
                              tt_[:, 0:2, :R], [B_tt], [])
                    pend.append(pb_)
                elif kind == "pin":
                    P.emit("act", lambda e, z=z, ps=ps, R=R: e.copy(z[:R, :], ps[:R, :]), [B_ps], [B_z])
                    if is_sample:
                        P.dma("sp", O["pool_s"][layer, 0, 0:15, :], z[1:16, :], [B_z], [], is_out=True)
                    elif t == NT - 1:
                        P.dma("sp", O["pool_p"][layer, b, 0:15, :], z[113:128, :], [B_z], [], is_out=True)
                    P.emit("dve", lambda e, z=z, o_b=o_b, R=R: e.tensor_copy(o_b[:R, :], z[:R, :]), [B_z], [B_o])
                    P.dma("sp", S_pin[tok0:tok0 + R, :], o_b[:R, :], [B_o], [])
        while pend:
            pend.pop(0)()
        if is_sample:
            P.dma("sp", O["conv_s"][layer, 0, 0:14, :], I["sconv"][layer, 0, 16:30, :], [], [], is_out=True)
        A.release(m1)
        phase_end()

        m2 = A.mark()
        tr_mode(False)
        dg = A.alloc([31, 4, 128], BF16); B_dg = Buf()
        pw = A.alloc([4, 512], BF16); B_pw = Buf()
        plw = A.alloc([4, 128], BF16); B_plw = Buf()
        bands = A.alloc([12, 128], BF16)
        bands_s = A.alloc([8, 16], BF16)
        convb = A.alloc([512]); pscale = A.alloc([512]); lng = A.alloc([8]); B_c2 = Buf()
        cwb = A.alloc([512]); B_cwb = Buf()
        qTt = A.alloc([8, 128], BF16); B_qT = Buf()
        qiTt = A.alloc([8, 128], BF16); B_qiT = Buf()
        gat = A.alloc([D], BF16); B_gat = Buf()
        uTb = [A.alloc([4, 160], BF16), A.alloc([4, 160], BF16)]; B_uT = [Buf(), Buf()]
        pinb = [A.alloc([512], BF16), A.alloc([512], BF16)]; B_pin = [Buf(), Buf()]
        NK = NKMAX
        saccs = [A.alloc([NK]), A.alloc([NK])]; B_saccs = [Buf(), Buf()]
        rb = [A.alloc([512], BF16) for _ in range(4)]; B_rb = [Buf() for _ in range(4)]
        Dg = A.alloc([16, 128], BF16); B_Dg = Buf()
        mask = A.alloc([NK], BF16); B_mask = Buf()
        junk, B_junk = mask, B_mask
        maskTs = [A.alloc([NBLK, 128], BF16), A.alloc([NBLK, 128], BF16)]; B_maskTs = [Buf(), Buf()]
        bis = A.alloc([16]); B_bis = Buf()
        Eb = [A.alloc([512], BF16) for _ in range(4)]; B_E = [Buf() for _ in range(4)]
        PTb = [A.alloc([512], BF16) for _ in range(4)]; B_PT = [Buf() for _ in range(4)]
        catb = A.alloc([D], BF16); B_catb = Buf()
        rinv = A.alloc([8]); B_rinv = Buf()
        yb = A.alloc([512]); B_yb = Buf()
        yhb = A.alloc([512], BF16); B_yhb = Buf()
        lnT = A.alloc([4, 128], BF16); B_lnT = Buf()
        rTb = A.alloc([4, 128], BF16); B_rTb = Buf()
        bnb = A.alloc([16]); B_bn = Buf()
        spb = A.alloc([512], BF16); B_spb = Buf()

        P.dma("sp", bands, I["c_bands"], [], [B_c2])
        P.dma("sp", bands_s[:16], I["c_bands_s"], [], [B_c2])
        P.dma("sp", convb, I["conv_b"][layer, :].partition_broadcast(128), [], [B_c2])
        P.dma("sp", pscale, I["pool_scale"][layer, :].partition_broadcast(128), [], [B_c2])
        for g in range(4):
            P.dma("sp", lng[:, g:g + 1], I["conv_ln_g"][layer, g * 128:(g + 1) * 128].unsqueeze(1), [], [B_c2])
            P.dma("sp", lng[:, 4 + g:5 + g], I["conv_ln_b"][layer, g * 128:(g + 1) * 128].unsqueeze(1), [], [B_c2])
        if layer not in p2_cached:
            p2_cached.add(layer)
            for g in range(4):
                P.dma("pool", pw[:, g, :], I["conv_pw"][layer][g * 128:(g + 1) * 128, :], [], [B_pw])
                P.dma("pool", plw[:, g, :], I["pool_w"][layer, g], [], [B_plw])
            P.dma("sp", S_pw[layer], pw.rearrange("p a b -> p (a b)"), [B_pw], [])
            P.dma("sp", S_plw[layer], plw.rearrange("p a b -> p (a b)"), [B_plw], [])
            cwbs = [cwb, yb]
            B_cwbs = [B_cwb, B_yb]
            for k in range(31):
                P.dma("sp", cwbs[k % 2], I["conv_w"][layer, k, :].partition_broadcast(128), [], [B_cwbs[k % 2]])
                P.emit("dve", lambda e, k=k, cw_=cwbs[k % 2]: e.tensor_tensor(dg[:, k, :, :], cw_.rearrange("p (g d) -> p g d", g=4),
                                                                          identf.unsqueeze(1).to_broadcast([128, 4, 128]), ALU.mult),
                       [B_cwbs[k % 2], B_const], [B_dg])
            P.dma("sp", S_dg[layer], dg.rearrange("p a b c -> p (a b c)"), [B_dg], [])
        else:
            P.dma("sp", pw.rearrange("p a b -> p (a b)"), S_pw[layer], [], [B_pw])
            P.dma("sp", plw.rearrange("p a b -> p (a b)"), S_plw[layer], [], [B_plw])
            P.dma("sp", dg.rearrange("p a b c -> p (a b c)"), S_dg[layer], [], [B_dg])
        if is_sample:
            P.dma("pool", spb[:30, :], I["sconv"][layer, 0, :, :], [], [B_spb])
            tr_group([spb[:30, j * 128:(j + 1) * 128] for j in range(4)], 30, uTb[0][:, :, 0:30], [B_spb], [B_uT[0]])
            P.dma("pool", pinb[1][:15, :], I["spool"][layer, 0, :, :], [], [B_pin[1]])
        else:
            P.emit("dve", lambda e: e.memset(uTb[0][:, :, 0:30], 0.0), [], [B_uT[0]])

        def gen_IDX(t):
            R = R_of(t)
            tok0 = t * 128
            cur, prv = t % 2, (t + 1) % 2
            maskT, B_maskT = maskTs[t % 2], B_maskTs[t % 2]
            nk = kbase + tok0 + R
            kblocks = []
            k0 = 0
            while k0 < nk:
                n = min(128, nk - k0)
                kblocks.append((k0 // 128, k0, n))
                k0 += n
            P.dma("sp", qiTt[:, :, :R], S_qiT[t].rearrange("p (a b) -> p a b", b=128)[:, :, :R], [], [B_qiT])
            for h in range(16):
                P.emit("dve", lambda e, h=h, R=R, t=t: e.tensor_scalar(
                    Dg[:R, h, :R], identf[:R, :R], wia[:R, t, 16 + h:17 + h], None, op0=ALU.mult), [B_const, B_wi], [B_Dg])
            yield
            li = [0]
            for c0 in range(0, nk, 512):
                n = min(512, nk - c0)
                prev_ = None

                def diag_mm(i_, n_, h_, R=R):
                    P.emit("pe", lambda e: e.matmul(PB[2][:R, :n_], Dg[:R, h_, :R], rb[i_][:R, :n_],
                                                    start=(h_ == 0), stop=(h_ == 15)), [B_Dg, B_rb[i_]], [PBb[2]])
                for h in range(16):
                    i = li[0] % 2
                    i4 = li[0] % 4
                    li[0] += 1
                    hp, j = h % 2, h // 2
                    P.emit("pe", lambda e, i=i, hp=hp, j=j, c0=c0, n=n, R=R: e.matmul(
                        PB[i][:R, :n], qiTt[hp * 64:(hp + 1) * 64, j, :R], kiT2[hp * 64:(hp + 1) * 64, c0:c0 + n],
                        start=True, stop=True), [B_qiT, B_kiT], [PBb[i]])
                    P.emit("act", lambda e, i=i, i4=i4, n=n, R=R, h=h, t=t: e.activation(
                        rb[i4][:R, :n], PB[i][:R, :n], AF.Relu, scale=wia[:R, t, h:h + 1]), [PBb[i], B_wi], [B_rb[i4]])
                    if prev_ is not None:
                        diag_mm(*prev_)
                    prev_ = (i4, n, h)
                    yield
                diag_mm(*prev_)
                P.emit("act", lambda e, n=n, R=R, c0=c0, t=t: e.copy(saccs[t % 2][:R, c0:c0 + n], PB[2][:R, :n]),
                       [PBb[2]], [B_saccs[t % 2]])
            if not is_sample:
                P.emit("dve", lambda e, tok0=tok0: e.tensor_tensor(saccs[t % 2][:, tok0:tok0 + 128], saccs[t % 2][:, tok0:tok0 + 128],
                                                                 negmask, ALU.add), [B_saccs[t % 2], B_const], [B_saccs[t % 2]])

        def gen_BIS(t):
            R = R_of(t)
            tok0 = t * 128
            cur, prv = t % 2, (t + 1) % 2
            maskT, B_maskT = maskTs[t % 2], B_maskTs[t % 2]
            nk = kbase + tok0 + R
            kblocks = []
            k0 = 0
            while k0 < nk:
                n = min(128, nk - k0)
                kblocks.append((k0 // 128, k0, n))
                k0 += n
            svis_min = nk if is_sample else (tok0 + 64)
            if svis_min > topk:
                P.emit("dve", lambda e, R=R: e.memset(bis[:R, 0:1], 0.0), [], [B_bis])
                w = W0
                for it in range(NIT):
                    a, bnew = it % 2, (it + 1) % 2
                    w = w / 2.0
                    P.emit("dve", lambda e, R=R, nk=nk, a=a: e.tensor_scalar(
                        bis[:R, 8:9].to_broadcast([R, nk]), saccs[t % 2][:R, :nk], bis[:R, a:a + 1], 0.0, op0=ALU.is_ge, op1=ALU.add,
                        accum_out=bis[:R, 2:3]), [B_saccs[t % 2], B_bis], [B_bis])
                    P.emit("dve", lambda e, R=R, w=w: e.tensor_scalar(
                        bis[:R, 3:4], bis[:R, 2:3], float(topk), 2.0 * w, op0=ALU.is_ge, op1=ALU.mult),
                        [B_bis], [B_bis])
                    P.emit("dve", lambda e, R=R, w=w, a=a, bnew=bnew: e.tensor_scalar(
                        bis[:R, bnew:bnew + 1], bis[:R, 3:4], -w, bis[:R, a:a + 1], op0=ALU.add, op1=ALU.add),
                        [B_bis], [B_bis])
                    yield
                fin = NIT % 2
                P.emit("dve", lambda e, R=R, w=w, fin=fin: e.tensor_scalar(
                    bis[:R, 4:5], bis[:R, fin:fin + 1], -w, None, op0=ALU.add), [B_bis], [B_bis])
            else:
                P.emit("dve", lambda e, R=R: e.memset(bis[:R, 4:5], -1e29), [], [B_bis])
            P.emit("dve", lambda e, R=R, nk=nk: e.tensor_scalar(
                mask[:R, :nk], saccs[t % 2][:R, :nk], bis[:R, 4:5], None, op0=ALU.is_ge), [B_saccs[t % 2], B_bis], [B_mask])
            for (blk, k0, n) in kblocks:
                tr_group([mask[:R, k0:k0 + n]], R, maskT[:n, blk:blk + 1, :R], [B_mask], [B_maskT], eng="dve")
                yield


        def gen_CD(t):
            R = R_of(t)
            tok0 = t * 128
            cur, prv = t % 2, (t + 1) % 2
            maskT, B_maskT = maskTs[t % 2], B_maskTs[t % 2]
            nk = kbase + tok0 + R
            kblocks = []
            k0 = 0
            while k0 < nk:
                n = min(128, nk - k0)
                kblocks.append((k0 // 128, k0, n))
                k0 += n
            P.dma("sp", qTt[:, :, :R], S_qT[t].rearrange("p (a b) -> p a b", b=128)[:, :, :R], [], [B_qT])
            P.dma("sp", gat[:R, :], S_gates[tok0:tok0 + R, :], [], [B_gat])
            P.dma("sp", uTb[cur][:, :, 30:30 + R], S_uT[t].rearrange("p (a b) -> p a b", b=128)[:, :, :R], [], [B_uT[cur]])
            P.dma("sp", pinb[cur][:R, :], S_pin[tok0:tok0 + R, :], [], [B_pin[cur]])
            yield
            ai = [0]
            nkb = len(kblocks)
            for g in range(2):
                pipe_m = []
                pipe_v = []

                def emit_mult(i, n, blk, R=R):
                    P.emit("dve", lambda e: e.tensor_tensor(
                        PTb[i][:n, 0:4 * R].rearrange("p (a b) -> p a b", b=R),
                        Eb[i][:n, 0:4 * R].rearrange("p (a b) -> p a b", b=R),
                        maskT[:n, blk:blk + 1, :R].to_broadcast([n, 4, R]), ALU.mult), [B_E[i], B_maskT], [B_PT[i]])

                def emit_pv(i, n, blk, bi, g=g, R=R):
                    for hh in range(4):
                        ob_, B_obk = (PB[5], PBb[5]) if hh < 3 else (PB[6], PBb[6])
                        oc = (hh % 3) * 129
                        P.emit("pe", lambda e, hh=hh, ob_=ob_, oc=oc,
                               st_=(bi == 0 and hh in (0, 3)), sp2=(bi == nkb - 1 and hh in (2, 3)): e.matmul(
                            ob_[:R, oc:oc + 129], PTb[i][:n, hh * R:(hh + 1) * R], V1[:n, blk, g, :],
                            start=st_, stop=sp2), [B_PT[i], B_V1], [B_obk])

                for bi, (blk, k0, n) in enumerate(kblocks):
                    i = ai[0] % 4
                    sp_, B_sp = PB[3 + ai[0] % 2], PBb[3 + ai[0] % 2]
                    ai[0] += 1
                    P.emit("pe", lambda e, sp_=sp_, g=g, k0=k0, n=n, R=R: e.matmul(
                        sp_[:n, 0:4 * R].rearrange("p (a b) -> p a b", b=R), kT[:, g, k0:k0 + n],
                        qTt[:, g * 4:(g + 1) * 4, :R], start=True, stop=True), [B_kT, B_qT], [B_sp])
                    P.emit("act", lambda e, sp_=sp_, i=i, n=n, R=R: e.activation(
                        Eb[i][:n, 0:4 * R], sp_[:n, 0:4 * R], AF.Exp, scale=128.0 ** -0.5), [B_sp], [B_E[i]])
                    if pipe_v:
                        emit_pv(*pipe_v.pop(0))
                    if pipe_m:
                        a_ = pipe_m.pop(0)
                        emit_mult(a_[0], a_[1], a_[2])
                        pipe_v.append(a_)
                    pipe_m.append((i, n, blk, bi))
                    yield
                while pipe_m or pipe_v:
                    if pipe_v:
                        emit_pv(*pipe_v.pop(0))
                    if pipe_m:
                        a_ = pipe_m.pop(0)
                        emit_mult(a_[0], a_[1], a_[2])
                        pipe_v.append(a_)
                    yield
                for hh in range(4):
                    h = g * 4 + hh
                    ob_, B_obk = (PB[5], PBb[5]) if hh < 3 else (PB[6], PBb[6])
                    oc = (hh % 3) * 129
                    P.emit("dve", lambda e, ob_=ob_, oc=oc, h=h, R=R: e.reciprocal(rinv[:R, h:h + 1], ob_[:R, oc + 128:oc + 129]),
                           [B_obk], [B_rinv])
                    P.emit("dve", lambda e, ob_=ob_, oc=oc, h=h, R=R: e.scalar_tensor_tensor(
                        out=catb[:R, h * 128:(h + 1) * 128], in0=ob_[:R, oc:oc + 128], scalar=rinv[:R, h:h + 1],
                        in1=gat[:R, h * 128:(h + 1) * 128], op0=ALU.mult, op1=ALU.mult),
                        [B_obk, B_rinv, B_gat], [B_catb])
                yield
            ub = uTb[cur]
            for g in range(4):
                for k in range(31):
                    P.emit("pe", lambda e, g=g, k=k, R=R, ub=ub: e.matmul(
                        PB[3][:R, g * 128:(g + 1) * 128], ub[:, g, k:k + R], dg[:, k, g, :],
                        start=(k == 0), stop=(k == 30)), [B_uT[cur], B_dg], [PBb[3]])
                yield
            if t + 1 < NT:
                P.emit("act", lambda e, ub=ub, nb=uTb[prv]: e.copy(nb[:, :, 0:30], ub[:, :, 128:158]), [B_uT[cur]], [B_uT[prv]])
            P.emit("dve", lambda e, R=R: e.tensor_tensor(yb[:R, :], PB[3][:R, :], convb[:R, :], ALU.add), [PBb[3], B_c2], [B_yb])
            P.emit("dve", lambda e, R=R: e.bn_stats(bnb[:R, 0:6], yb[:R, :]), [B_yb], [B_bn])
            P.emit("dve", lambda e, R=R: e.bn_aggr(bnb[:R, 6:8], bnb[:R, 0:6]), [B_bn], [B_bn])
            P.emit("act", lambda e, R=R: e.activation(bnb[:R, 8:9], bnb[:R, 7:8], AF.Sqrt, bias=epsb[:R, 0:1], scale=1.0),
                   [B_bn, B_const], [B_bn])
            P.emit("dve", lambda e, R=R: e.reciprocal(bnb[:R, 9:10], bnb[:R, 8:9]), [B_bn], [B_bn])
            P.emit("dve", lambda e, R=R: e.tensor_scalar(yhb[:R, :], yb[:R, :], bnb[:R, 6:7], bnb[:R, 9:10],
                                                        op0=ALU.subtract, op1=ALU.mult), [B_yb, B_bn], [B_yhb])
            i = trc[0] % 2
            trc[0] += 1
            tv = TR[i].rearrange("p (a b) -> p a b", b=128)
            for g in range(4):
                P.emit("pe", lambda e, g=g, R=R, tv=tv: e.transpose(tv[:, g, :R], yhb[:R, g * 128:(g + 1) * 128], ident[:R, :R]),
                       [B_yhb, B_const], [TRB[i]])
            for g in range(4):
                P.emit("act", lambda e, g=g, R=R, tv=tv: e.activation(lnT[:, g, :R], tv[:, g, :R], AF.Silu,
                                                                    bias=lng[:, 4 + g:5 + g], scale=lng[:, g:g + 1]),
                       [TRB[i], B_c2], [B_lnT])
            for g in range(4):
                P.emit("pe", lambda e, g=g, R=R: e.matmul(PB[4][:R, :], lnT[:, g, :R], pw[:, g, :], start=(g == 0), stop=(g == 3)),
                       [B_lnT, B_pw], [PBb[4]])
            P.emit("dve", lambda e, R=R: e.tensor_tensor(catb[:R, 1024:1536], PB[4][:R, :], gat[:R, 1024:1536], ALU.mult),
                   [PBb[4], B_gat], [B_catb])
            yield
            pc, pp_ = pinb[cur], pinb[prv]
            for g in range(4):
                outp = PB[3][:, g * 128:g * 128 + R]
                if is_sample:
                    P.emit("pe", lambda e, g=g, outp=outp, pc=pc: e.matmul(outp, pc[:TS, g * 128:(g + 1) * 128], bands_s[:TS, g, :],
                                                                          start=True, stop=False), [B_pin[cur], B_c2], [PBb[3]])
                    P.emit("pe", lambda e, g=g, outp=outp, pp_=pp_: e.matmul(outp, pp_[:15, g * 128:(g + 1) * 128], bands_s[:15, 4 + g, :],
                                                                            start=False, stop=True), [B_pin[prv], B_c2], [PBb[3]])
                elif t == 0:
                    P.emit("pe", lambda e, g=g, outp=outp, pc=pc: e.matmul(outp, pc[:, g * 128:(g + 1) * 128], bands[:, 8 + g, :],
                                                                          start=True, stop=True), [B_pin[cur], B_c2], [PBb[3]])
                else:
                    P.emit("pe", lambda e, g=g, outp=outp, pc=pc: e.matmul(outp, pc[:, g * 128:(g + 1) * 128], bands[:, g, :],
                                                                          start=True, stop=False), [B_pin[cur], B_c2], [PBb[3]])
                    P.emit("pe", lambda e, g=g, outp=outp, pp_=pp_: e.matmul(outp, pp_[64:128, g * 128:(g + 1) * 128], bands[64:128, 4 + g, :],
                                                                            start=False, stop=True), [B_pin[prv], B_c2], [PBb[3]])
            P.emit("act", lambda e, R=R: e.copy(rTb[:, :, :R], PB[3][:, :].rearrange("p (a b) -> p a b", b=128)[:, :, :R]),
                   [PBb[3]], [B_rTb])
            for g in range(4):
                P.emit("pe", lambda e, g=g, R=R: e.matmul(PB[4][:R, g * 128:(g + 1) * 128], rTb[:, g, :R], plw[:, g, :],
                                                         start=True, stop=True), [B_rTb, B_plw], [PBb[4]])
            P.emit("dve", lambda e, R=R: e.tensor_tensor(yb[:R, :], PB[4][:R, :], pscale[:R, :], ALU.mult),
                   [PBb[4], B_c2], [B_yb])
            P.emit("dve", lambda e, R=R: e.tensor_tensor(catb[:R, 1536:2048], yb[:R, :], gat[:R, 1536:2048], ALU.mult),
                   [B_yb, B_gat], [B_catb])
            yield
            for c4 in range(4):
                srcs = [catb[:R, (c4 * 4 + j) * 128:(c4 * 4 + j + 1) * 128] for j in range(4)]
                tr_group(srcs, R, actT[:, c4 * 4:c4 * 4 + 4, tok0:tok0 + R], [B_catb], [B_actT])

        def interleave(gens):
            gens = list(gens)
            while gens:
                for g_ in list(gens):
                    try:
                        next(g_)
                    except StopIteration:
                        gens.remove(g_)

        def chain(*gs):
            for g_ in gs:
                for _ in g_:
                    yield

        def interleave_n(items):
            st_ = [[g_, max(1, l_), 0] for (g_, l_) in items]
            while st_:
                st_.sort(key=lambda x: x[2] / x[1])
                x = st_[0]
                try:
                    next(x[0])
                    x[2] += 1
                except StopIteration:
                    st_.remove(x)

        def nblk_of(t):
            return (kbase + t * 128 + R_of(t) + 127) // 128

        def len_idx(t):
            return 16 * ((kbase + t * 128 + R_of(t) + 511) // 512) + 1

        def len_bis(t):
            return NIT + nblk_of(t) + 2

        def len_cd(t):
            return 2 * nblk_of(t) + 12

        for _ in gen_IDX(0):
            pass
        items = [(gen_BIS(0), len_bis(0))]
        if NT > 1:
            items.append((gen_IDX(1), len_idx(1)))
        interleave_n(items)
        for t in range(NT):
            items = [(gen_CD(t), len_cd(t))]
            if t + 1 < NT:
                items.append((gen_BIS(t + 1), len_bis(t + 1)))
            if t + 2 < NT:
                items.append((gen_IDX(t + 2), len_idx(t + 2)))
            interleave_n(items)
        A.release(m2)
        phase_end()

        m3 = A.mark()
        tr_mode(True)
        alloc_wbuf(("w_out", layer, 0) not in converted)
        xr = [A.alloc([512]), A.alloc([512])]; B_xr = [Buf(), Buf()]
        x1o = [A.alloc([512]), A.alloc([512])]; B_x1o = [Buf(), Buf()]
        ec = [0]
        nxt_w = load_wblock("w_out", layer, [(0, 512)], 16, 0)
        for nb in range(4):
            wb, B_wb, ncols = nxt_w
            if nb + 1 < 4:
                nxt_w = load_wblock("w_out", layer, [((nb + 1) * 512, 512)], 16, nb + 1)
            for t in range(NT):
                R = R_of(t)
                tok0 = t * 128
                i = ec[0] % 2
                ib = ec[0] % 4
                ec[0] += 1
                P.dma("sp", xr[i][:R, :], x_src[tok0:tok0 + R, nb * 512:(nb + 1) * 512], [], [B_xr[i]])
                for kc in range(16):
                    P.emit("pe", lambda e, kc=kc, ib=ib, R=R, tok0=tok0, wb=wb: e.matmul(
                        PB[ib][:R, :], actT[:, kc, tok0:tok0 + R], wb[:, kc, :], start=(kc == 0), stop=(kc == 15)),
                        [B_actT, B_wb], [PBb[ib]])
                P.emit("dve", lambda e, i=i, ib=ib, R=R: e.tensor_tensor(x1o[i][:R, :], PB[ib][:R, :], xr[i][:R, :], ALU.add),
                       [PBb[ib], B_xr[i]], [B_x1o[i]])
                P.dma("sp", S_x1[tok0:tok0 + R, nb * 512:(nb + 1) * 512], x1o[i][:R, :], [B_x1o[i]], [])
        A.release(m3)
        phase_end()

        m4 = A.mark()
        tr_mode(True)
        alloc_wbuf(("w_ple_gate", layer, 0) not in converted)
        sc = make_sc()
        pT = A.alloc([2, Tn if not is_sample else 128], BF16); B_pT = Buf()
        pf = A.alloc([256]); B_pf = Buf()
        pb_ = A.alloc([256], BF16); B_pb = Buf()
        wp = [A.alloc([2, 512], BF16), A.alloc([2, 512], BF16)]; B_wp = [Buf(), Buf()]
        xr2 = [A.alloc([512]), A.alloc([512])]; B_xr2 = [Buf(), Buf()]
        sg = [A.alloc([512]), A.alloc([512])]; B_sg = [Buf(), Buf()]
        x2o = [A.alloc([512]), A.alloc([512])]; B_x2o = [Buf(), Buf()]
        gbc_h[0] = A.alloc([D])
        P.dma("sp", gbc_h[0], I["norm_ple"][layer, :].partition_broadcast(128), [], [B_gbc])
        nxt_w = load_wblock("w_ple_gate", layer, [(0, 512)], 16, 0)
        pend = []
        for t in range(NT):
            R = R_of(t)
            tok0 = t * 128
            P.dma("sp", sc["xt"][t % 2][:R, :], S_x1[tok0:tok0 + R, :], [], [sc["B_xt"][t % 2]])
            pb_fn = rmsnorm_T(t % 2, R, tok0, sc)
            while pend:
                pend.pop(0)()
            pend.append(pb_fn)
            P.dma("sp", pf[:R, :], p_src[tok0:tok0 + R, :], [], [B_pf])
            P.emit("dve", lambda e, R=R: e.tensor_copy(pb_[:R, :], pf[:R, :]), [B_pf], [B_pb])
            tr_group([pb_[:R, 0:128], pb_[:R, 128:256]], R, pT[:, :, tok0:tok0 + R], [B_pb], [B_pT])
        while pend:
            pend.pop(0)()
        dst = S_xres
        ec = [0]
        for nb in range(4):
            wb, B_wb, ncols = nxt_w
            if nb + 1 < 4:
                nxt_w = load_wblock("w_ple_gate", layer, [((nb + 1) * 512, 512)], 16, nb + 1)
            wi_ = nb % 2
            for kc in range(2):
                P.dma("pool", wp[wi_][:, kc, :], I["w_ple_proj"][layer][kc * 128:(kc + 1) * 128, nb * 512:(nb + 1) * 512],
                      [], [B_wp[wi_]])
            for t in range(NT):
                R = R_of(t)
                tok0 = t * 128
                i = ec[0] % 2
                ig = ec[0] % 4
                ip = 4 + ec[0] % 3
                ec[0] += 1
                P.dma("sp", xr2[i][:R, :], S_x1[tok0:tok0 + R, nb * 512:(nb + 1) * 512], [], [B_xr2[i]])
                for kc in range(16):
                    P.emit("pe", lambda e, kc=kc, ig=ig, R=R, tok0=tok0, wb=wb: e.matmul(
                        PB[ig][:R, :], actT[:, kc, tok0:tok0 + R], wb[:, kc, :], start=(kc == 0), stop=(kc == 15)),
                        [B_actT, B_wb], [PBb[ig]])
                for kc in range(2):
                    P.emit("pe", lambda e, kc=kc, ip=ip, R=R, tok0=tok0, wi_=wi_: e.matmul(
                        PB[ip][:R, :], pT[:, kc, tok0:tok0 + R], wp[wi_][:, kc, :], start=(kc == 0), stop=(kc == 1)),
                        [B_pT, B_wp[wi_]], [PBb[ip]])
                P.emit("act", lambda e, i=i, ig=ig, R=R: e.activation(sg[i][:R, :], PB[ig][:R, :], AF.Sigmoid), [PBb[ig]], [B_sg[i]])
                P.emit("dve", lambda e, i=i, ip=ip, R=R: e.tensor_tensor(sg[i][:R, :], sg[i][:R, :], PB[ip][:R, :], ALU.mult),
                       [B_sg[i], PBb[ip]], [B_sg[i]])
                P.emit("dve", lambda e, i=i, R=R: e.tensor_tensor(x2o[i][:R, :], sg[i][:R, :], xr2[i][:R, :], ALU.add),
                       [B_sg[i], B_xr2[i]], [B_x2o[i]])
                P.dma("sp", dst[tok0:tok0 + R, nb * 512:(nb + 1) * 512], x2o[i][:R, :], [B_x2o[i]], [])
        A.release(m4)
        phase_end()

        if last_layer:
            m5 = A.mark()
            xt = [A.alloc([D]), A.alloc([D])]; B_xt = [Buf(), Buf()]
            jk = A.alloc([D], BF16); B_jk = Buf()
            ss = A.alloc([8]); B_ss = Buf()
            yo = [A.alloc([D]), A.alloc([D])]; B_yo = [Buf(), Buf()]
            gbc = A.alloc([D])
            P.dma("sp", gbc, I["norm_final"][0, :].partition_broadcast(128), [], [B_gbc])
            ydst = O["y_s"][0] if is_sample else O["y_p"][seq]
            for t in range(NT):
                R = R_of(t)
                tok0 = t * 128
                i = t % 2
                P.dma("sp", xt[i][:R, :], S_xres[tok0:tok0 + R, :], [], [B_xt[i]])
                P.emit("act", lambda e, i=i, R=R: e.activation(jk[:R, :], xt[i][:R, :], AF.Square, accum_out=ss[:R, 0:1]),
                       [B_xt[i]], [B_jk, B_ss])
                P.emit("act", lambda e, R=R: e.activation(ss[:R, 1:2], ss[:R, 0:1], AF.Sqrt, bias=epsb[:R, 0:1], scale=1.0 / D),
                       [B_ss, B_const], [B_ss])
                P.emit("dve", lambda e, R=R: e.reciprocal(ss[:R, 2:3], ss[:R, 1:2]), [B_ss], [B_ss])
                P.emit("dve", lambda e, i=i, R=R: e.scalar_tensor_tensor(out=yo[i][:R, :], in0=xt[i][:R, :], scalar=ss[:R, 2:3],
                                                                        in1=gbc[:R, :], op0=ALU.mult, op1=ALU.mult),
                       [B_xt[i], B_ss, B_gbc], [B_yo[i]])
                P.dma("sp", ydst[tok0:tok0 + R, :], yo[i][:R, :], [B_yo[i]], [], is_out=True)
            A.release(m5)
            phase_end()

    seqs = [(s, False) for s in range(NP)] + ([(0, True)] if has_sample else [])
    try:
        for (s, is_s) in seqs:
            for layer in range(L):
                if layer == 0:
                    x_src = I["xs"][0] if is_s else I["xp"][s]
                else:
                    x_src = S_xres
                run_seq_layer(s, layer, is_s, x_src, layer == L - 1)
    except _Stop:
        pass
    P.finish()
    P.build(st)
    st.close()
    return nc, P


def make_consts(T):
    c = {}
    c["c_ident_bf"] = np.eye(128, dtype=np.float32).astype(ml_dtypes.bfloat16)
    c["c_ident_f"] = np.eye(128, dtype=np.float32)

    def rope_tab(pos):
        out = np.zeros((len(pos), 192), np.float32)
        for half, o in ((64, 0), (32, 128)):
            freq = (np.float32(10000.0) ** (-np.arange(half, dtype=np.float32) / np.float32(half))).astype(np.float32)
            ang = pos.astype(np.float32)[:, None] * freq[None, :]
            out[:, o:o + half] = np.cos(ang)
            out[:, o + half:o + 2 * half] = np.sin(ang)
        return out
    c["c_rope_p"] = rope_tab(np.arange(T))
    c["c_rope_s"] = rope_tab(PAST + np.arange(TS))
    nm = np.zeros((128, 128), np.float32)
    nm[:64, 64:] = -1e30
    c["c_negmask"] = nm
    bands = np.zeros((128, 12, 128), np.float32)
    bs = np.zeros((16, 8, 16), np.float32)
    for gi, w in enumerate((2, 4, 8, 16)):
        for tok in range(128):
            for j in range(tok - w + 1, tok + 1):
                if j >= 0:
                    bands[j, gi, tok] += 1.0 / w
                    bands[j, 8 + gi, tok] += 1.0 / min(tok + 1, w)
                else:
                    bands[128 + j, 4 + gi, tok] += 1.0 / w
            bands[tok, gi, tok] -= 1.0
            bands[tok, 8 + gi, tok] -= 1.0
        for tok in range(16):
            for j in range(tok - w + 1, tok + 1):
                if j >= 0:
                    bs[j, gi, tok] += 1.0 / w
                else:
                    bs[15 + j, 4 + gi, tok] += 1.0 / w
            bs[tok, gi, tok] -= 1.0
    c["c_bands"] = bands.astype(ml_dtypes.bfloat16)
    c["c_bands_s"] = bs.astype(ml_dtypes.bfloat16)
    return c


_CACHE = {}


def run(inputs, n_cores, NP, T, L, has_sample=True, dbg_stop=None):
    key = (NP, T, L, has_sample, dbg_stop)
    if key not in _CACHE:
        _CACHE[key] = build_program(NP, T, L, has_sample, dbg_stop=dbg_stop)
    nc, P = _CACHE[key]
    f = lambda a: np.ascontiguousarray(np.asarray(a, dtype=np.float32))
    consts = make_consts(T)
    wnames = ["norm_mix", "w_in", "conv_w", "conv_b", "conv_ln_g", "conv_ln_b", "conv_pw", "pool_w", "pool_scale",
              "w_out", "norm_ple", "w_ple_gate", "w_ple_proj"]
    shared = {n: f(inputs[n]) for n in wnames}
    shared["norm_final"] = f(inputs["norm_final"]).reshape(1, D)
    shared.update(consts)
    in_maps = []
    for c in range(n_cores):
        m = dict(shared)
        m["xp"] = f(inputs["x_prompt"][c * NP:(c + 1) * NP])
        m["pp"] = f(inputs["p_prompt"][:, c * NP:(c + 1) * NP])
        m["xs"] = f(inputs["x_sample"][c:c + 1])
        m["ps"] = f(inputs["p_sample"][:, c:c + 1])
        m["ck"] = f(inputs["cache_k"][:, c:c + 1]).reshape(L, 1, PAST, 256)
        m["cv"] = f(inputs["cache_v"][:, c:c + 1]).reshape(L, 1, PAST, 256)
        m["cki"] = f(inputs["cache_kidx"][:, c:c + 1])
        m["sconv"] = f(inputs["state_conv"][:, c:c + 1])
        m["spool"] = f(inputs["state_pool"][:, c:c + 1])
        in_maps.append(m)
    res = run_bass_kernel_spmd(nc, in_maps, core_ids=list(range(n_cores)))
    rs = res.results
    cat = lambda name, ax: np.concatenate([r[name] for r in rs], axis=ax)
    y_p = cat("y_p", 0)
    y_s = cat("y_s", 0)
    k_p = cat("k_p", 1).reshape(L, n_cores * NP, T, 2, 128)
    v_p = cat("v_p", 1).reshape(L, n_cores * NP, T, 2, 128)
    ki_p = cat("ki_p", 1)
    conv_p = cat("conv_p", 1)
    pool_p = cat("pool_p", 1)
    k_s = cat("k_s", 1).reshape(L, n_cores, TS, 2, 128)
    v_s = cat("v_s", 1).reshape(L, n_cores, TS, 2, 128)
    ki_s = cat("ki_s", 1)
    conv_s = cat("conv_s", 1)
    pool_s = cat("pool_s", 1)
    return (y_p, y_s, k_p, v_p, ki_p, conv_p, pool_p, k_s, v_s, ki_s, conv_s, pool_s)


def kernel(**inputs):
    outs = run(inputs, 8, 2, 2048, 2, True)
    return tuple(np.asarray(o, dtype=np.float32) for o in outs)
```

```python
from contextlib import ExitStack
import numpy as np
import ml_dtypes
import concourse.bass as bass
import concourse.mybir as mybir
from concourse.bass_utils import run_bass_kernel_spmd

F32 = mybir.dt.float32
BF16 = mybir.dt.bfloat16
AF = mybir.ActivationFunctionType
ALU = mybir.AluOpType

NPOOL = 16
EPOCH = 30000

D = 2048
NIN = 6224
PLE = 256
PAST = 1024
TS = 16
EPS = 1e-6
NIT = 22
W0 = 8.0


class Buf:
    __slots__ = ("name", "w", "r")

    def __init__(self, name=""):
        self.name = name
        self.w = None
        self.r = {}


class Op:
    __slots__ = ("id", "eng", "fn", "deps", "dma", "sig")


class Prog:
    ENGS = ("pe", "act", "dve", "pool", "sp")

    def __init__(self, nc):
        self.nc = nc
        self.ops = []
        self.by_eng = {e: [] for e in self.ENGS}
        self.out_dmas = []
        self.last = {e: None for e in self.ENGS}
        self.dmas_since_bar = []

    def emit(self, eng, fn, reads=(), writes=(), dma=False, out=False):
        op = Op()
        op.id = len(self.ops)
        op.eng = eng
        op.fn = fn
        op.dma = dma
        op.sig = None
        deps = {}
        for b in reads:
            if b.w is not None:
                deps[b.w] = "RAW"
        for b in writes:
            if b.w is not None:
                deps.setdefault(b.w, "WAW")
            for r in b.r.values():
                deps.setdefault(r, "WAR")
        key = ("dma", op.id) if dma else eng
        for b in reads:
            b.r[key] = op.id
        for b in writes:
            b.w = op.id
            b.r = {}
        op.deps = deps
        self.ops.append(op)
        self.by_eng[eng].append(op)
        if dma:
            self.dmas_since_bar.append(op.id)
        else:
            self.last[eng] = op.id
        if out:
            self.out_dmas.append(op.id)
        return op

    def dma(self, eng, out, in_, reads, writes, is_out=False):
        return self.emit(eng, lambda e: e.dma_start(out=out, in_=in_), reads, writes, dma=True, out=is_out)

    def barrier(self):
        deps = {}
        for e in self.ENGS:
            if self.last[e] is not None:
                deps[self.last[e]] = "RAW"
        for d in self.dmas_since_bar:
            deps[d] = "RAW"
        self.dmas_since_bar = []
        for e in self.ENGS:
            op = Op()
            op.id = len(self.ops)
            op.eng = e
            op.fn = None
            op.dma = False
            op.sig = None
            op.deps = {k: ("BAR" if self.ops[k].eng != e or self.ops[k].dma else "WAW") for k in deps}
            self.ops.append(op)
            self.by_eng[e].append(op)

    def _skip(self, o, d, kind):
        if d.dma or o.dma:
            return False
        if o.eng == d.eng:
            if o.eng == "pe":
                return True
            return kind != "RAW"
        return False

    def finish(self):
        op = Op()
        op.id = len(self.ops)
        op.eng = "sp"
        op.fn = None
        op.dma = False
        op.sig = None
        op.deps = {d: "RAW" for d in self.out_dmas}
        self.ops.append(op)
        self.by_eng["sp"].append(op)

    def build(self, stack):
        nc = self.nc
        ops = self.ops
        for q in ("sp", "pool", "act"):
            dma_ops = [o for o in ops if o.dma and o.eng == q]
            for j, o in enumerate(dma_ops):
                o.sig = (("dma", q + str(j % NPOOL)), 16 * (j // NPOOL + 1))
                if j >= NPOOL:
                    o.deps.setdefault(dma_ops[j - NPOOL].id, "GUARD")
        needed = set()
        for o in ops:
            for d, kind in o.deps.items():
                if not self._skip(o, ops[d], kind):
                    needed.add(d)
        cnt = {e: 0 for e in self.ENGS}
        semkeys = set()
        for o in ops:
            if o.dma:
                semkeys.add(o.sig[0])
            elif o.id in needed:
                c = cnt[o.eng]
                o.sig = ((o.eng, c // EPOCH), c % EPOCH + 1)
                cnt[o.eng] = c + 1
                semkeys.add(o.sig[0])
        sems = {}
        for k in sorted(semkeys, key=str):
            sems[k] = stack.enter_context(nc.semaphore("s_%s_%s" % (k[0], k[1])))
        self.n_sems = len(sems)
        self.n_waits = 0
        block = stack.enter_context(nc.Block())

        def run(engname, e):
            waited = {}
            for o in self.by_eng[engname]:
                req = {}
                for d in o.deps:
                    dop = ops[d]
                    if self._skip(o, dop, o.deps[d]):
                        continue
                    sk, val = dop.sig
                    if sk[0] == "dma":
                        if req.get(sk, 0) < val:
                            req[sk] = val
                    else:
                        cur = req.get(sk[0], (-1, 0))
                        if cur < (sk[1], val):
                            req[sk[0]] = (sk[1], val)
                for k in sorted(req, key=str):
                    v = req[k]
                    if isinstance(k, tuple):
                        if waited.get(k, 0) >= v:
                            continue
                        waited[k] = v
                        e.wait_ge(sems[k], v)
                    else:
                        if waited.get(k, (-1, 0)) >= v:
                            continue
                        waited[k] = v
                        e.wait_ge(sems[(k, v[0])], v[1])
                    self.n_waits += 1
                if o.fn is None:
                    continue
                ins = o.fn(e)
                if o.sig is not None:
                    ins.then_inc(sems[o.sig[0]], 16 if o.dma else 1)

        @block.tensor
        def _(e):
            run("pe", e)

        @block.scalar
        def _(e):
            run("act", e)

        @block.vector
        def _(e):
            run("dve", e)

        @block.gpsimd
        def _(e):
            run("pool", e)

        @block.sync
        def _(e):
            run("sp", e)


COLBLOCKS = [
    ("q", [(0, 512)], 0), ("q", [(512, 512)], 1),
    ("kv", [(1024, 512)], 0),
    ("qi", [(1536, 512)], 0), ("qi", [(2048, 512)], 1),
    ("kiwi", [(2560, 80)], 0),
    ("gate", [(2640, 512)], 0), ("gate", [(3152, 512)], 512),
    ("glu", [(3664, 256), (4176, 256)], 0), ("glu", [(3920, 256), (4432, 256)], 1),
    ("gate", [(4688, 512)], 1024),
    ("pin", [(5200, 512)], 0),
    ("gate", [(5712, 512)], 1536),
]


class Arena:
    def __init__(self, ap, nwords):
        self.ap = ap
        self.n = nwords
        self.off = 0

    def mark(self):
        return self.off

    def release(self, m):
        self.off = m

    def alloc(self, free_shape, dt=F32):
        n = int(np.prod(free_shape))
        w = n if dt == F32 else (n + 1) // 2
        a = self.ap[:, self.off:self.off + w]
        self.off += (w + 7) // 8 * 8
        assert self.off <= self.n, "SBUF arena overflow %d > %d" % (self.off, self.n)
        if dt != F32:
            a = a.bitcast(dt)
        if len(free_shape) == 2:
            a = a.rearrange("p (a b) -> p a b", b=free_shape[1])
        elif len(free_shape) == 3:
            a = a.rearrange("p (a b c) -> p a b c", b=free_shape[1], c=free_shape[2])
        return a


class _Stop(Exception):
    pass


def build_program(NP, T, L, has_sample=True, arena_words=53200, dbg_stop=None):
    nc = bass.Bass("TRN2", target_bir_lowering=False)
    NTP = T // 128
    TOPK_P = min(256, T // 4)
    TOPK_S = min(256, (PAST + TS) // 4)

    def din(name, shape, dt=F32):
        return nc.dram_tensor(name, list(shape), dt, kind="ExternalInput").ap()

    def dout(name, shape, dt=F32):
        return nc.dram_tensor(name, list(shape), dt, kind="ExternalOutput").ap()

    def dscr(name, shape, dt=F32):
        return nc.dram_tensor(name, list(shape), dt).ap()

    I = {}
    I["xp"] = din("xp", [NP, T, D])
    I["pp"] = din("pp", [L, NP, T, PLE])
    I["xs"] = din("xs", [1, TS, D])
    I["ps"] = din("ps", [L, 1, TS, PLE])
    I["ck"] = din("ck", [L, 1, PAST, 256])
    I["cv"] = din("cv", [L, 1, PAST, 256])
    I["cki"] = din("cki", [L, 1, PAST, 64])
    I["sconv"] = din("sconv", [L, 1, 30, 512])
    I["spool"] = din("spool", [L, 1, 15, 512])
    I["norm_mix"] = din("norm_mix", [L, D])
    I["w_in"] = din("w_in", [L, D, NIN])
    I["conv_w"] = din("conv_w", [L, 31, 512])
    I["conv_b"] = din("conv_b", [L, 512])
    I["conv_ln_g"] = din("conv_ln_g", [L, 512])
    I["conv_ln_b"] = din("conv_ln_b", [L, 512])
    I["conv_pw"] = din("conv_pw", [L, 512, 512])
    I["pool_w"] = din("pool_w", [L, 4, 128, 128])
    I["pool_scale"] = din("pool_scale", [L, 512])
    I["w_out"] = din("w_out", [L, D, D])
    I["norm_ple"] = din("norm_ple", [L, D])
    I["w_ple_gate"] = din("w_ple_gate", [L, D, D])
    I["w_ple_proj"] = din("w_ple_proj", [L, PLE, D])
    I["norm_final"] = din("norm_final", [1, D])
    I["c_ident_bf"] = din("c_ident_bf", [128, 128], BF16)
    I["c_ident_f"] = din("c_ident_f", [128, 128])
    I["c_rope_p"] = din("c_rope_p", [T, 192])
    I["c_rope_s"] = din("c_rope_s", [TS, 192])
    I["c_negmask"] = din("c_negmask", [128, 128])
    I["c_bands"] = din("c_bands", [128, 12, 128], BF16)
    I["c_bands_s"] = din("c_bands_s", [16, 8, 16], BF16)

    O = {}
    O["y_p"] = dout("y_p", [NP, T, D])
    O["y_s"] = dout("y_s", [1, TS, D])
    O["k_p"] = dout("k_p", [L, NP, T, 256])
    O["v_p"] = dout("v_p", [L, NP, T, 256])
    O["ki_p"] = dout("ki_p", [L, NP, T, 64])
    O["conv_p"] = dout("conv_p", [L, NP, 30, 512])
    O["pool_p"] = dout("pool_p", [L, NP, 15, 512])
    O["k_s"] = dout("k_s", [L, 1, TS, 256])
    O["v_s"] = dout("v_s", [L, 1, TS, 256])
    O["ki_s"] = dout("ki_s", [L, 1, TS, 64])
    O["conv_s"] = dout("conv_s", [L, 1, 30, 512])
    O["pool_s"] = dout("pool_s", [L, 1, 15, 512])

    S_qT = dscr("s_qT", [NTP, 128, 1024], BF16)
    S_qiT = dscr("s_qiT", [NTP, 128, 1024], BF16)
    S_gates = dscr("s_gates", [T, D], BF16)
    S_uT = dscr("s_uT", [NTP, 128, 512], BF16)
    S_pin = dscr("s_pin", [T, 512], BF16)
    S_dg = dscr("s_dg", [L, 128, 31 * 4 * 128], BF16)
    S_pw = dscr("s_pw", [L, 128, 4 * 512], BF16)
    S_plw = dscr("s_plw", [L, 128, 4 * 128], BF16)
    p2_cached = set()
    S_x1 = dscr("s_x1", [T, D])
    S_xres = dscr("s_xres", [T, D])

    st = ExitStack()
    P = Prog(nc)
    arena_t = st.enter_context(nc.sbuf_tensor("arena", [128, arena_words], F32))
    A = Arena(arena_t[:, :], arena_words)
    banks = [st.enter_context(nc.psum_tensor("pb%d" % i, [128, 512], F32)) for i in range(8)]
    PB = [b[:, :] for b in banks]
    PBb = [Buf("pb%d" % i) for i in range(8)]
    TRb = PB[7].bitcast(BF16)
    TRb6 = PB[6].bitcast(BF16)
    TR = [TRb[:, 0:512], TRb6[:, 0:512]]
    TRB = [PBb[7], PBb[6]]

    def tr_mode(two):
        if two:
            TR[0], TR[1] = TRb[:, 0:512], TRb6[:, 0:512]
            TRB[0], TRB[1] = PBb[7], PBb[6]
        else:
            TR[0], TR[1] = TRb[:, 0:512], TRb[:, 512:1024]
            TRB[0], TRB[1] = PBb[7], PBb[7]
    trc = [0]
    phase_ctr = [0]

    def phase_end():
        P.barrier()
        phase_ctr[0] += 1
        if dbg_stop is not None and phase_ctr[0] >= dbg_stop:
            raise _Stop()

    ident = A.alloc([128], BF16); B_const = Buf("const")
    identf = A.alloc([128])
    negmask = A.alloc([128])
    epsb = A.alloc([8])
    actT = A.alloc([16, T], BF16); B_actT = Buf("actT")
    wbuf = [None, None]
    B_wbuf = [Buf("w0"), Buf("w1")]

    def alloc_wbuf(need_stage):
        wbuf[0] = A.alloc([16, 512], BF16)
        wbuf[1] = A.alloc([16, 512], BF16)
        if need_stage:
            stg[0] = A.alloc([8, 512])
    NKMAX = max(T, PAST + TS)
    NBLK = (NKMAX + 127) // 128
    kT = A.alloc([2, NKMAX], BF16); B_kT = Buf("kT")
    V1 = A.alloc([NBLK, 2, 129], BF16); B_V1 = Buf("V1")
    kiT2 = A.alloc([NKMAX], BF16); B_kiT = Buf("kiT")
    wia = A.alloc([NTP, 32]); B_wi = Buf("wi")
    gbc_h = [None]; B_gbc = Buf("gbc")
    wcnt = [0]

    P.dma("sp", ident, I["c_ident_bf"], [], [B_const])
    P.dma("sp", identf, I["c_ident_f"], [], [B_const])
    P.dma("sp", negmask, I["c_negmask"], [], [B_const])
    P.emit("dve", lambda e: e.memset(epsb, EPS), [], [B_const])
    P.emit("dve", lambda e: e.memset(V1[:, :, :, 128:129], 1.0), [], [B_V1])

    def transpose_to(dst_fn, src, R, ncol, reads, writes, eng="act"):
        raise NotImplementedError

    def tr_group(srcs, R, dst, reads, writes, eng="act"):
        i = trc[0] % 2
        trc[0] += 1
        n = len(srcs)
        c = srcs[0].shape[1]
        tv = TR[i].rearrange("p (a b) -> p a b", b=128)
        for j, s in enumerate(srcs):
            P.emit("pe", lambda e, s=s, j=j: e.transpose(tv[:c, j, :R], s, ident[:R, :R]),
                   reads + [B_const], [TRB[i]])
        if eng == "act":
            P.emit("act", lambda e: e.copy(dst, tv[:c, 0:n, :R]), [TRB[i]], writes)
        else:
            P.emit("dve", lambda e: e.tensor_copy(dst, tv[:c, 0:n, :R]), [TRB[i]], writes)

    WB = {}
    converted = set()
    stg = [None]
    B_stg = Buf("stg")

    NBLK_W = {"w_in": len(COLBLOCKS), "w_out": 4, "w_ple_gate": 4}

    def load_wblock(name, layer, ranges, kch, bid):
        w_ap = I[name][layer]
        key = (name, layer)
        if key not in WB:
            WB[key] = dscr("wb_%s_%d" % (name, layer), [NBLK_W[name], 128, 16 * 512], BF16)
        wsc = WB[key]
        i = wcnt[0] % 2
        wcnt[0] += 1
        o = sum(n for (_, n) in ranges)
        ck = (name, layer, bid)
        if ck in converted:
            P.dma("sp", wbuf[i].rearrange("p a b -> p (a b)"), wsc[bid], [], [B_wbuf[i]])
        else:
            converted.add(ck)
            sg_ = stg[0]
            o2 = 0
            for (c0, n) in ranges:
                for k8 in range(0, kch, 8):
                    for k4 in range(k8, min(kch, k8 + 8), 4):
                        kk = min(4, kch - k4)
                        P.dma("pool", sg_[:, k4 - k8:k4 - k8 + kk, 0:n],
                              w_ap[k4 * 128:(k4 + kk) * 128, c0:c0 + n].rearrange("(kc p) n -> p kc n", p=128), [], [B_stg])
                    for k4 in range(k8, min(kch, k8 + 8), 4):
                        kk = min(4, kch - k4)
                        P.emit("pool", lambda e, k4=k4, kk=kk, k8=k8, o2=o2, n=n, i=i, sg_=sg_: e.tensor_copy(
                            wbuf[i][:, k4:k4 + kk, o2:o2 + n], sg_[:, k4 - k8:k4 - k8 + kk, 0:n]), [B_stg], [B_wbuf[i]])
                o2 += n
            P.dma("pool", wsc[bid], wbuf[i].rearrange("p a b -> p (a b)"), [B_wbuf[i]], [])
        return wbuf[i], B_wbuf[i], o

    def make_sc():
        sc = {}
        sc["xt"] = [A.alloc([D]), A.alloc([D])]; sc["B_xt"] = [Buf(), Buf()]
        sc["junk"] = A.alloc([D], BF16); sc["B_junk"] = Buf()
        sc["xn"] = [A.alloc([D], BF16), A.alloc([D], BF16)]; sc["B_xn"] = [Buf(), Buf()]
        sc["ss"] = A.alloc([8]); sc["B_ss"] = Buf()
        return sc

    def rmsnorm_T(j, R, tok0, sc):
        xt, B_xt = sc["xt"][j], sc["B_xt"][j]
        xn, B_xn = sc["xn"][j], sc["B_xn"][j]
        gbc = gbc_h[0]
        P.emit("act", lambda e: e.activation(sc["junk"][:R, :], xt[:R, :], AF.Square, accum_out=sc["ss"][:R, 0:1]),
               [B_xt], [sc["B_junk"], sc["B_ss"]])
        P.emit("act", lambda e: e.activation(sc["ss"][:R, 1:2], sc["ss"][:R, 0:1], AF.Sqrt, bias=epsb[:R, 0:1], scale=1.0 / D),
               [sc["B_ss"], B_const], [sc["B_ss"]])
        P.emit("dve", lambda e: e.reciprocal(sc["ss"][:R, 2:3], sc["ss"][:R, 1:2]), [sc["B_ss"]], [sc["B_ss"]])
        P.emit("dve", lambda e: e.scalar_tensor_tensor(out=xn[:R, :], in0=xt[:R, :], scalar=sc["ss"][:R, 2:3],
                                                       in1=gbc[:R, :], op0=ALU.mult, op1=ALU.mult),
               [B_xt, sc["B_ss"], B_gbc], [B_xn])

        def part_b():
            for c4 in range(4):
                srcs = [xn[:R, (c4 * 4 + jj) * 128:(c4 * 4 + jj + 1) * 128] for jj in range(4)]
                tr_group(srcs, R, actT[:, c4 * 4:c4 * 4 + 4, tok0:tok0 + R], [B_xn], [B_actT], eng=("act" if c4 % 2 == 0 else "dve"))
        return part_b

    def rope(e_eng, src, dst1, dst2, cos, sin, tmp, R, nh, half, reads, writes_list, B_tmp):
        s4 = src.rearrange("p (h t d) -> p h t d", h=nh, t=2)
        x1 = s4[:, :, 0, :]
        x2 = s4[:, :, 1, :]
        cb = cos.unsqueeze(1).to_broadcast([R, nh, half])
        sb_ = sin.unsqueeze(1).to_broadcast([R, nh, half])
        t4 = tmp[:R, 0:nh * half * 4].rearrange("p (k h d) -> p k h d", k=4, h=nh)
        P.emit("dve", lambda e: e.tensor_tensor(t4[:, 0], x1, cb, ALU.mult), reads, [B_tmp])
        P.emit("dve", lambda e: e.tensor_tensor(t4[:, 1], x2, sb_, ALU.mult), reads, [B_tmp])
        P.emit("dve", lambda e: e.tensor_tensor(t4[:, 2], x1, sb_, ALU.mult), reads, [B_tmp])
        P.emit("dve", lambda e: e.tensor_tensor(t4[:, 3], x2, cb, ALU.mult), reads, [B_tmp])
        P.emit("dve", lambda e: e.tensor_tensor(dst1, t4[:, 0], t4[:, 1], ALU.subtract), [B_tmp], writes_list)
        P.emit("dve", lambda e: e.tensor_tensor(dst2, t4[:, 2], t4[:, 3], ALU.add), [B_tmp], writes_list)

    def run_seq_layer(seq, layer, is_sample, x_src, last_layer):
        Tn = TS if is_sample else T
        NT = 1 if is_sample else NTP
        R_of = (lambda t: TS) if is_sample else (lambda t: 128)
        b = 0 if is_sample else seq
        rope_src = I["c_rope_s"] if is_sample else I["c_rope_p"]
        kbase = PAST if is_sample else 0
        topk = TOPK_S if is_sample else TOPK_P
        kname = "s" if is_sample else "p"
        p_src = I["ps"][layer, 0] if is_sample else I["pp"][layer, seq]

        if is_sample:
            m0 = A.mark()
            tr_mode(True)
            ckb = A.alloc([256], BF16); B_ckb = Buf()
            ckib = A.alloc([128], BF16); B_ckib = Buf()
            for blk in range(PAST // 128):
                r0 = blk * 128
                P.dma("pool", ckb, I["ck"][layer, 0, r0:r0 + 128, :], [], [B_ckb])
                tr_group([ckb[:, 0:128], ckb[:, 128:256]], 128, kT[:, :, r0:r0 + 128], [B_ckb], [B_kT])
                P.dma("pool", V1[:, blk, :, 0:128], I["cv"][layer, 0, r0:r0 + 128, :].rearrange("p (g d) -> p g d", g=2),
                      [], [B_V1])
                P.dma("pool", ckib[:, 0:64], I["cki"][layer, 0, r0:r0 + 128, :], [], [B_ckib])
                P.dma("pool", ckib[:, 64:128], I["cki"][layer, 0, r0:r0 + 128, :], [], [B_ckib])
                tr_group([ckib[:, 0:128]], 128, kiT2[:, r0:r0 + 128].unsqueeze(1), [B_ckib], [B_kiT])
            A.release(m0)
            phase_end()

        m1 = A.mark()
        tr_mode(True)
        first_use = ("w_in", layer, 0) not in converted
        alloc_wbuf(first_use)
        sc = make_sc()
        ropet = A.alloc([NT, 192]); B_rope = Buf()
        zs = [A.alloc([512]), A.alloc([512])]; B_zs = [Buf(), Buf()]
        tmp = A.alloc([1024]); B_tmp = Buf()
        ob = [A.alloc([512], BF16), A.alloc([512], BF16)]; B_ob = [Buf(), Buf()]
        of = [A.alloc([512]), A.alloc([512])]; B_of = [Buf(), Buf()]
        tT = [A.alloc([4, 128], BF16), A.alloc([4, 128], BF16)]; B_tT = [Buf(), Buf()]

        gbc_h[0] = A.alloc([D])
        P.dma("sp", gbc_h[0], I["norm_mix"][layer, :].partition_broadcast(128), [], [B_gbc])
        if is_sample:
            P.dma("sp", ropet[:TS, 0, :], rope_src, [], [B_rope])
        else:
            P.dma("sp", ropet, rope_src.rearrange("(n p) c -> p n c", p=128), [], [B_rope])
        nxt_w = load_wblock("w_in", layer, COLBLOCKS[0][1], 16, 0)
        pend = []
        for t in range(NT):
            R = R_of(t)
            P.dma("sp", sc["xt"][t % 2][:R, :], x_src[t * 128:t * 128 + R, :], [], [sc["B_xt"][t % 2]])
            pb_fn = rmsnorm_T(t % 2, R, t * 128, sc)
            while pend:
                pend.pop(0)()
            pend.append(pb_fn)
        while pend:
            pend.pop(0)()

        ec = [0]
        for bi_, (kind, ranges, arg) in enumerate(COLBLOCKS):
            wb, B_wb, ncols = nxt_w
            if bi_ + 1 < len(COLBLOCKS):
                nxt_w = load_wblock("w_in", layer, COLBLOCKS[bi_ + 1][1], 16, bi_ + 1)
            for t in range(NT):
                R = R_of(t)
                tok0 = t * 128
                i = ec[0] % 2
                ec[0] += 1
                ps, B_ps = PB[(ec[0] - 1) % 4], PBb[(ec[0] - 1) % 4]
                for kc in range(16):
                    P.emit("pe", lambda e, kc=kc, ps=ps, R=R, tok0=tok0, wb=wb, ncols=ncols:
                           e.matmul(ps[:R, :ncols], actT[:, kc, tok0:tok0 + R], wb[:, kc, :ncols],
                                    start=(kc == 0), stop=(kc == 15)),
                           [B_actT, B_wb], [B_ps])
                while pend:
                    pend.pop(0)()
                z, B_z = zs[i], B_zs[i]
                o_b, B_o = ob[i], B_ob[i]
                o_f, B_f = of[i], B_of[i]
                tt_, B_tt = tT[i], B_tT[i]
                cos128 = ropet[:R, t, 0:64]
                sin128 = ropet[:R, t, 64:128]
                cos64 = ropet[:R, t, 128:160]
                sin64 = ropet[:R, t, 160:192]
                if kind == "q":
                    P.emit("act", lambda e, z=z, ps=ps, R=R: e.copy(z[:R, :], ps[:R, :]), [B_ps], [B_z])
                    o4 = o_b[:R, :].rearrange("p (h t d) -> p h t d", h=4, t=2)
                    rope("dve", z[:R, :], o4[:, :, 0, :], o4[:, :, 1, :], cos128, sin128, tmp, R, 4, 64,
                         [B_z, B_rope], [B_o], B_tmp)
                    def pb_(o_b=o_b, R=R, tt_=tt_, B_o=B_o, B_tt=B_tt, t=t, arg=arg):
                        tr_group([o_b[:R, j * 128:(j + 1) * 128] for j in range(4)], R, tt_[:, :, :R], [B_o], [B_tt])
                        P.dma("sp", S_qT[t, :, arg * 512:(arg + 1) * 512].rearrange("p (a b) -> p a b", b=128)[:, :, :R],
                              tt_[:, :, :R], [B_tt], [])
                    pend.append(pb_)
                elif kind == "qi":
                    P.emit("act", lambda e, z=z, ps=ps, R=R: e.copy(z[:R, :], ps[:R, :]), [B_ps], [B_z])
                    o4 = o_b[:R, :].rearrange("p (h t d) -> p h t d", h=8, t=2)
                    rope("dve", z[:R, :], o4[:, :, 0, :], o4[:, :, 1, :], cos64, sin64, tmp, R, 8, 32,
                         [B_z, B_rope], [B_o], B_tmp)
                    def pb_(o_b=o_b, R=R, tt_=tt_, B_o=B_o, B_tt=B_tt, t=t, arg=arg):
                        tr_group([o_b[:R, j * 128:(j + 1) * 128] for j in range(4)], R, tt_[:, :, :R], [B_o], [B_tt])
                        P.dma("sp", S_qiT[t, :, arg * 512:(arg + 1) * 512].rearrange("p (a b) -> p a b", b=128)[:, :, :R],
                              tt_[:, :, :R], [B_tt], [])
                    pend.append(pb_)
                elif kind == "kv":
                    P.emit("act", lambda e, z=z, ps=ps, R=R: e.copy(z[:R, :], ps[:R, :]), [B_ps], [B_z])
                    f4 = o_f[:R, 0:256].rearrange("p (h t d) -> p h t d", h=2, t=2)
                    rope("dve", z[:R, 0:256], f4[:, :, 0, :], f4[:, :, 1, :], cos128, sin128, tmp, R, 2, 64,
                         [B_z, B_rope], [B_f], B_tmp)
                    P.dma("sp", O["k_" + kname][layer, b, tok0:tok0 + R, :], o_f[:R, 0:256], [B_f], [], is_out=True)
                    P.dma("sp", O["v_" + kname][layer, b, tok0:tok0 + R, :], z[:R, 256:512], [B_z], [], is_out=True)
                    P.emit("act", lambda e, o_b=o_b, o_f=o_f, R=R: e.copy(o_b[:R, 0:256], o_f[:R, 0:256]), [B_f], [B_o])
                    k0 = kbase + tok0
                    pend.append(lambda o_b=o_b, R=R, k0=k0, B_o=B_o: tr_group(
                        [o_b[:R, 0:128], o_b[:R, 128:256]], R, kT[:, :, k0:k0 + R], [B_o], [B_kT]))
                    blk = k0 // 128
                    P.emit("dve", lambda e, z=z, R=R, blk=blk: e.tensor_copy(
                        V1[:R, blk, :, 0:128], z[:R, 256:512].rearrange("p (g d) -> p g d", g=2)), [B_z], [B_V1])
                elif kind == "kiwi":
                    P.emit("act", lambda e, z=z, ps=ps, R=R: e.copy(z[:R, 0:80], ps[:R, 0:80]), [B_ps], [B_z])
                    f4 = o_f[:R, 0:64].rearrange("p (h t d) -> p h t d", h=1, t=2)
                    rope("dve", z[:R, 0:64], f4[:, :, 0, :], f4[:, :, 1, :], cos64, sin64, tmp, R, 1, 32,
                         [B_z, B_rope], [B_f], B_tmp)
                    P.dma("sp", O["ki_" + kname][layer, b, tok0:tok0 + R, :], o_f[:R, 0:64], [B_f], [], is_out=True)
                    P.emit("act", lambda e, o_b=o_b, o_f=o_f, R=R: e.copy(o_b[:R, 0:64], o_f[:R, 0:64]), [B_f], [B_o])
                    P.emit("act", lambda e, o_b=o_b, o_f=o_f, R=R: e.copy(o_b[:R, 64:128], o_f[:R, 0:64]), [B_f], [B_o])
                    k0 = kbase + tok0
                    pend.append(lambda o_b=o_b, R=R, k0=k0, B_o=B_o: tr_group(
                        [o_b[:R, 0:128]], R, kiT2[:, k0:k0 + R].unsqueeze(1), [B_o], [B_kiT]))
                    P.emit("act", lambda e, z=z, R=R, t=t: e.activation(wia[:R, t, 0:16], z[:R, 64:80], AF.Abs, scale=1.0 / 32.0),
                           [B_z], [B_wi])
                    P.emit("act", lambda e, z=z, R=R, t=t: e.activation(wia[:R, t, 16:32], z[:R, 64:80], AF.Sign),
                           [B_z], [B_wi])
                elif kind == "gate":
                    P.emit("act", lambda e, o_b=o_b, ps=ps, R=R: e.activation(o_b[:R, :], ps[:R, :], AF.Silu), [B_ps], [B_o])
                    P.dma("sp", S_gates[tok0:tok0 + R, arg:arg + 512], o_b[:R, :], [B_o], [])
                elif kind == "glu":
                    P.emit("act", lambda e, z=z, ps=ps, R=R: e.activation(z[:R, 0:256], ps[:R, 256:512], AF.Sigmoid), [B_ps], [B_z])
                    P.emit("dve", lambda e, z=z, ps=ps, R=R, o_f=o_f: e.tensor_tensor(o_f[:R, 0:256], ps[:R, 0:256], z[:R, 0:256], ALU.mult),
                           [B_ps, B_z], [B_f])
                    c0 = arg * 256
                    if is_sample:
                        P.dma("sp", O["conv_s"][layer, 0, 14:30, c0:c0 + 256], o_f[:TS, 0:256], [B_f], [], is_out=True)
                    elif t == NT - 1:
                        P.dma("sp", O["conv_p"][layer, b, 0:30, c0:c0 + 256], o_f[98:128, 0:256], [B_f], [], is_out=True)
                    P.emit("act", lambda e, o_b=o_b, o_f=o_f, R=R: e.copy(o_b[:R, 0:256], o_f[:R, 0:256]), [B_f], [B_o])
                    def pb_(o_b=o_b, R=R, tt_=tt_, B_o=B_o, B_tt=B_tt, t=t, c0=c0):
                        tr_group([o_b[:R, 0:128], o_b[:R, 128:256]], R, tt_[:, 0:2, :R], [B_o], [B_tt])
                        P.dma("sp", S_uT[t, :, c0:c0 + 256].rearrange("p (a b) -> p a b", b=128)[:, :, :R],
                              tt_[:, 0:2, :R], [B_tt], [])
                    pend.append(pb_)
                elif kind == "pin":
                    P.emit("act", lambda e, z=z, ps=ps, R=R: e.copy(z[:R, :], ps[:R, :]), [B_ps], [B_z])
                    if is_sample:
                        P.dma("sp", O["pool_s"][layer, 0, 0:15, :], z[1:16, :], [B_z], [], is_out=True)
                    elif t == NT - 1:
                        P.dma("sp", O["pool_p"][layer, b, 0:15, :], z[113:128, :], [B_z], [], is_out=True)
                    P.emit("dve", lambda e, z=z, o_b=o_b, R=R: e.tensor_copy(o_b[:R, :], z[:R, :]), [B_z], [B_o])
                    P.dma("sp", S_pin[tok0:tok0 + R, :], o_b[:R, :], [B_o], [])
        while pend:
            pend.pop(0)()
        if is_sample:
            P.dma("sp", O["conv_s"][layer, 0, 0:14, :], I["sconv"][layer, 0, 16:30, :], [], [], is_out=True)
        A.release(m1)
        phase_end()

        m2 = A.mark()
        tr_mode(False)
        dg = A.alloc([31, 4, 128], BF16); B_dg = Buf()
        pw = A.alloc([4, 512], BF16); B_pw = Buf()
        plw = A.alloc([4, 128], BF16); B_plw = Buf()
        bands = A.alloc([12, 128], BF16)
        bands_s = A.alloc([8, 16], BF16)
        convb = A.alloc([512]); pscale = A.alloc([512]); lng = A.alloc([8]); B_c2 = Buf()
        cwb = A.alloc([512]); B_cwb = Buf()
        qTt = A.alloc([8, 128], BF16); B_qT = Buf()
        qiTt = A.alloc([8, 128], BF16); B_qiT = Buf()
        gat = A.alloc([D], BF16); B_gat = Buf()
        uTb = [A.alloc([4, 160], BF16), A.alloc([4, 160], BF16)]; B_uT = [Buf(), Buf()]
        pinb = [A.alloc([512], BF16), A.alloc([512], BF16)]; B_pin = [Buf(), Buf()]
        NK = NKMAX
        saccs = [A.alloc([NK]), A.alloc([NK])]; B_saccs = [Buf(), Buf()]
        rb = [A.alloc([512], BF16) for _ in range(4)]; B_rb = [Buf() for _ in range(4)]
        Dg = A.alloc([16, 128], BF16); B_Dg = Buf()
        mask = A.alloc([NK], BF16); B_mask = Buf()
        junk, B_junk = mask, B_mask
        maskTs = [A.alloc([NBLK, 128], BF16), A.alloc([NBLK, 128], BF16)]; B_maskTs = [Buf(), Buf()]
        bis = A.alloc([16]); B_bis = Buf()
        Eb = [A.alloc([512], BF16) for _ in range(4)]; B_E = [Buf() for _ in range(4)]
        PTb = [A.alloc([512], BF16) for _ in range(4)]; B_PT = [Buf() for _ in range(4)]
        catb = A.alloc([D], BF16); B_catb = Buf()
        rinv = A.alloc([8]); B_rinv = Buf()
        yb = A.alloc([512]); B_yb = Buf()
        yhb = A.alloc([512], BF16); B_yhb = Buf()
        lnT = A.alloc([4, 128], BF16); B_lnT = Buf()
        rTb = A.alloc([4, 128], BF16); B_rTb = Buf()
        bnb = A.alloc([16]); B_bn = Buf()
        spb = A.alloc([512], BF16); B_spb = Buf()

        P.dma("sp", bands, I["c_bands"], [], [B_c2])
        P.dma("sp", bands_s[:16], I["c_bands_s"], [], [B_c2])
        P.dma("sp", convb, I["conv_b"][layer, :].partition_broadcast(128), [], [B_c2])
        P.dma("sp", pscale, I["pool_scale"][layer, :].partition_broadcast(128), [], [B_c2])
        for g in range(4):
            P.dma("sp", lng[:, g:g + 1], I["conv_ln_g"][layer, g * 128:(g + 1) * 128].unsqueeze(1), [], [B_c2])
            P.dma("sp", lng[:, 4 + g:5 + g], I["conv_ln_b"][layer, g * 128:(g + 1) * 128].unsqueeze(1), [], [B_c2])
        if layer not in p2_cached:
            p2_cached.add(layer)
            for g in range(4):
                P.dma("pool", pw[:, g, :], I["conv_pw"][layer][g * 128:(g + 1) * 128, :], [], [B_pw])
                P.dma("pool", plw[:, g, :], I["pool_w"][layer, g], [], [B_plw])
            P.dma("sp", S_pw[layer], pw.rearrange("p a b -> p (a b)"), [B_pw], [])
            P.dma("sp", S_plw[layer], plw.rearrange("p a b -> p (a b)"), [B_plw], [])
            cwbs = [cwb, yb]
            B_cwbs = [B_cwb, B_yb]
            for k in range(31):
                P.dma("sp", cwbs[k % 2], I["conv_w"][layer, k, :].partition_broadcast(128), [], [B_cwbs[k % 2]])
                P.emit("dve", lambda e, k=k, cw_=cwbs[k % 2]: e.tensor_tensor(dg[:, k, :, :], cw_.rearrange("p (g d) -> p g d", g=4),
                                                                          identf.unsqueeze(1).to_broadcast([128, 4, 128]), ALU.mult),
                       [B_cwbs[k % 2], B_const], [B_dg])
            P.dma("sp", S_dg[layer], dg.rearrange("p a b c -> p (a b c)"), [B_dg], [])
        else:
            P.dma("sp", pw.rearrange("p a b -> p (a b)"), S_pw[layer], [], [B_pw])
            P.dma("sp", plw.rearrange("p a b -> p (a b)"), S_plw[layer], [], [B_plw])
            P.dma("sp", dg.rearrange("p a b c -> p (a b c)"), S_dg[layer], [], [B_dg])
        if is_sample:
            P.dma("pool", spb[:30, :], I["sconv"][layer, 0, :, :], [], [B_spb])
            tr_group([spb[:30, j * 128:(j + 1) * 128] for j in range(4)], 30, uTb[0][:, :, 0:30], [B_spb], [B_uT[0]])
            P.dma("pool", pinb[1][:15, :], I["spool"][layer, 0, :, :], [], [B_pin[1]])
        else:
            P.emit("dve", lambda e: e.memset(uTb[0][:, :, 0:30], 0.0), [], [B_uT[0]])

        def gen_IDX(t):
            R = R_of(t)
            tok0 = t * 128
            cur, prv = t % 2, (t + 1) % 2
            maskT, B_maskT = maskTs[t % 2], B_maskTs[t % 2]
            nk = kbase + tok0 + R
            kblocks = []
            k0 = 0
            while k0 < nk:
                n = min(128, nk - k0)
                kblocks.append((k0 // 128, k0, n))
                k0 += n
            P.dma("sp", qiTt[:, :, :R], S_qiT[t].rearrange("p (a b) -> p a b", b=128)[:, :, :R], [], [B_qiT])
            for h in range(16):
                P.emit("dve", lambda e, h=h, R=R, t=t: e.tensor_scalar(
                    Dg[:R, h, :R], identf[:R, :R], wia[:R, t, 16 + h:17 + h], None, op0=ALU.mult), [B_const, B_wi], [B_Dg])
            yield
            li = [0]
            for c0 in range(0, nk, 512):
                n = min(512, nk - c0)
                prev_ = None

                def diag_mm(i_, n_, h_, R=R):
                    P.emit("pe", lambda e: e.matmul(PB[2][:R, :n_], Dg[:R, h_, :R], rb[i_][:R, :n_],
                                                    start=(h_ == 0), stop=(h_ == 15)), [B_Dg, B_rb[i_]], [PBb[2]])
                for h in range(16):
                    i = li[0] % 2
                    i4 = li[0] % 4
                    li[0] += 1
                    hp, j = h % 2, h // 2
                    P.emit("pe", lambda e, i=i, hp=hp, j=j, c0=c0, n=n, R=R: e.matmul(
                        PB[i][:R, :n], qiTt[hp * 64:(hp + 1) * 64, j, :R], kiT2[hp * 64:(hp + 1) * 64, c0:c0 + n],
                        start=True, stop=True), [B_qiT, B_kiT], [PBb[i]])
                    P.emit("act", lambda e, i=i, i4=i4, n=n, R=R, h=h, t=t: e.activation(
                        rb[i4][:R, :n], PB[i][:R, :n], AF.Relu, scale=wia[:R, t, h:h + 1]), [PBb[i], B_wi], [B_rb[i4]])
                    if prev_ is not None:
                        diag_mm(*prev_)
                    prev_ = (i4, n, h)
                    yield
                diag_mm(*prev_)
                yield
                P.emit("act", lambda e, n=n, R=R, c0=c0, t=t: e.copy(saccs[t % 2][:R, c0:c0 + n], PB[2][:R, :n]),
                       [PBb[2]], [B_saccs[t % 2]])
            yield
            yield
            if not is_sample:
                P.emit("dve", lambda e, tok0=tok0: e.tensor_tensor(saccs[t % 2][:, tok0:tok0 + 128], saccs[t % 2][:, tok0:tok0 + 128],
                                                                 negmask, ALU.add), [B_saccs[t % 2], B_const], [B_saccs[t % 2]])

        def gen_BIS(t):
            R = R_of(t)
            tok0 = t * 128
            cur, prv = t % 2, (t + 1) % 2
            maskT, B_maskT = maskTs[t % 2], B_maskTs[t % 2]
            nk = kbase + tok0 + R
            kblocks = []
            k0 = 0
            while k0 < nk:
                n = min(128, nk - k0)
                kblocks.append((k0 // 128, k0, n))
                k0 += n
            svis_min = nk if is_sample else (tok0 + 64)
            if svis_min > topk:
                P.emit("dve", lambda e, R=R: e.memset(bis[:R, 0:1], 0.0), [], [B_bis])
                w = W0
                for it in range(NIT):
                    a, bnew = it % 2, (it + 1) % 2
                    w = w / 2.0
                    P.emit("dve", lambda e, R=R, nk=nk, a=a: e.tensor_scalar(
                        junk[:R, :nk], saccs[t % 2][:R, :nk], bis[:R, a:a + 1], 0.0, op0=ALU.is_ge, op1=ALU.add,
                        accum_out=bis[:R, 2:3]), [B_saccs[t % 2], B_bis], [B_junk, B_bis])
                    P.emit("dve", lambda e, R=R, w=w: e.tensor_scalar(
                        bis[:R, 3:4], bis[:R, 2:3], float(topk), 2.0 * w, op0=ALU.is_ge, op1=ALU.mult),
                        [B_bis], [B_bis])
                    P.emit("dve", lambda e, R=R, w=w, a=a, bnew=bnew: e.tensor_scalar(
                        bis[:R, bnew:bnew + 1], bis[:R, 3:4], -w, bis[:R, a:a + 1], op0=ALU.add, op1=ALU.add),
                        [B_bis], [B_bis])
                    yield
                fin = NIT % 2
                P.emit("dve", lambda e, R=R, w=w, fin=fin: e.tensor_scalar(
                    bis[:R, 4:5], bis[:R, fin:fin + 1], -w, None, op0=ALU.add), [B_bis], [B_bis])
            else:
                P.emit("dve", lambda e, R=R: e.memset(bis[:R, 4:5], -1e29), [], [B_bis])
            P.emit("dve", lambda e, R=R, nk=nk: e.tensor_scalar(
                mask[:R, :nk], saccs[t % 2][:R, :nk], bis[:R, 4:5], None, op0=ALU.is_ge), [B_saccs[t % 2], B_bis], [B_mask])
            for (blk, k0, n) in kblocks:
                tr_group([mask[:R, k0:k0 + n]], R, maskT[:n, blk:blk + 1, :R], [B_mask], [B_maskT], eng="dve")
                yield


        def gen_CD(t):
            R = R_of(t)
            tok0 = t * 128
            cur, prv = t % 2, (t + 1) % 2
            maskT, B_maskT = maskTs[t % 2], B_maskTs[t % 2]
            nk = kbase + tok0 + R
            kblocks = []
            k0 = 0
            while k0 < nk:
                n = min(128, nk - k0)
                kblocks.append((k0 // 128, k0, n))
                k0 += n
            P.dma("sp", qTt[:, :, :R], S_qT[t].rearrange("p (a b) -> p a b", b=128)[:, :, :R], [], [B_qT])
            P.dma("sp", gat[:R, :], S_gates[tok0:tok0 + R, :], [], [B_gat])
            P.dma("sp", uTb[cur][:, :, 30:30 + R], S_uT[t].rearrange("p (a b) -> p a b", b=128)[:, :, :R], [], [B_uT[cur]])
            P.dma("sp", pinb[cur][:R, :], S_pin[tok0:tok0 + R, :], [], [B_pin[cur]])
            yield
            ai = [0]
            nkb = len(kblocks)
            for g in range(2):
                pipe_m = []
                pipe_v = []

                def emit_mult(i, n, blk, R=R):
                    P.emit("dve", lambda e: e.tensor_tensor(
                        PTb[i][:n, 0:4 * R].rearrange("p (a b) -> p a b", b=R),
                        Eb[i][:n, 0:4 * R].rearrange("p (a b) -> p a b", b=R),
                        maskT[:n, blk:blk + 1, :R].to_broadcast([n, 4, R]), ALU.mult), [B_E[i], B_maskT], [B_PT[i]])

                def emit_pv(i, n, blk, bi, g=g, R=R):
                    for hh in range(4):
                        ob_, B_obk = (PB[5], PBb[5]) if hh < 3 else (PB[6], PBb[6])
                        oc = (hh % 3) * 129
                        P.emit("pe", lambda e, hh=hh, ob_=ob_, oc=oc,
                               st_=(bi == 0 and hh in (0, 3)), sp2=(bi == nkb - 1 and hh in (2, 3)): e.matmul(
                            ob_[:R, oc:oc + 129], PTb[i][:n, hh * R:(hh + 1) * R], V1[:n, blk, g, :],
                            start=st_, stop=sp2), [B_PT[i], B_V1], [B_obk])

                for bi, (blk, k0, n) in enumerate(kblocks):
                    i = ai[0] % 4
                    sp_, B_sp = PB[3 + ai[0] % 2], PBb[3 + ai[0] % 2]
                    ai[0] += 1
                    P.emit("pe", lambda e, sp_=sp_, g=g, k0=k0, n=n, R=R: e.matmul(
                        sp_[:n, 0:4 * R].rearrange("p (a b) -> p a b", b=R), kT[:, g, k0:k0 + n],
                        qTt[:, g * 4:(g + 1) * 4, :R], start=True, stop=True), [B_kT, B_qT], [B_sp])
                    P.emit("act", lambda e, sp_=sp_, i=i, n=n, R=R: e.activation(
                        Eb[i][:n, 0:4 * R], sp_[:n, 0:4 * R], AF.Exp, scale=128.0 ** -0.5), [B_sp], [B_E[i]])
                    if pipe_v:
                        emit_pv(*pipe_v.pop(0))
                    if pipe_m:
                        a_ = pipe_m.pop(0)
                        emit_mult(a_[0], a_[1], a_[2])
                        pipe_v.append(a_)
                    pipe_m.append((i, n, blk, bi))
                    yield
                while pipe_m or pipe_v:
                    if pipe_v:
                        emit_pv(*pipe_v.pop(0))
                    if pipe_m:
                        a_ = pipe_m.pop(0)
                        emit_mult(a_[0], a_[1], a_[2])
                        pipe_v.append(a_)
                    yield
                yield
                yield
                for hh in range(4):
                    h = g * 4 + hh
                    ob_, B_obk = (PB[5], PBb[5]) if hh < 3 else (PB[6], PBb[6])
                    oc = (hh % 3) * 129
                    P.emit("dve", lambda e, ob_=ob_, oc=oc, h=h, R=R: e.reciprocal(rinv[:R, h:h + 1], ob_[:R, oc + 128:oc + 129]),
                           [B_obk], [B_rinv])
                    P.emit("dve", lambda e, ob_=ob_, oc=oc, h=h, R=R: e.scalar_tensor_tensor(
                        out=catb[:R, h * 128:(h + 1) * 128], in0=ob_[:R, oc:oc + 128], scalar=rinv[:R, h:h + 1],
                        in1=gat[:R, h * 128:(h + 1) * 128], op0=ALU.mult, op1=ALU.mult),
                        [B_obk, B_rinv, B_gat], [B_catb])
                yield
            ub = uTb[cur]
            for g in range(4):
                for k in range(31):
                    P.emit("pe", lambda e, g=g, k=k, R=R, ub=ub: e.matmul(
                        PB[3][:R, g * 128:(g + 1) * 128], ub[:, g, k:k + R], dg[:, k, g, :],
                        start=(k == 0), stop=(k == 30)), [B_uT[cur], B_dg], [PBb[3]])
                yield
            if t + 1 < NT:
                P.emit("act", lambda e, ub=ub, nb=uTb[prv]: e.copy(nb[:, :, 0:30], ub[:, :, 128:158]), [B_uT[cur]], [B_uT[prv]])
            yield
            yield
            yield
            yield
            P.emit("dve", lambda e, R=R: e.tensor_tensor(yb[:R, :], PB[3][:R, :], convb[:R, :], ALU.add), [PBb[3], B_c2], [B_yb])
            P.emit("dve", lambda e, R=R: e.bn_stats(bnb[:R, 0:6], yb[:R, :]), [B_yb], [B_bn])
            P.emit("dve", lambda e, R=R: e.bn_aggr(bnb[:R, 6:8], bnb[:R, 0:6]), [B_bn], [B_bn])
            yield
            P.emit("act", lambda e, R=R: e.activation(bnb[:R, 8:9], bnb[:R, 7:8], AF.Sqrt, bias=epsb[:R, 0:1], scale=1.0),
                   [B_bn, B_const], [B_bn])
            yield
            P.emit("dve", lambda e, R=R: e.reciprocal(bnb[:R, 9:10], bnb[:R, 8:9]), [B_bn], [B_bn])
            P.emit("dve", lambda e, R=R: e.tensor_scalar(yhb[:R, :], yb[:R, :], bnb[:R, 6:7], bnb[:R, 9:10],
                                                        op0=ALU.subtract, op1=ALU.mult), [B_yb, B_bn], [B_yhb])
            yield
            i = trc[0] % 2
            trc[0] += 1
            tv = TR[i].rearrange("p (a b) -> p a b", b=128)
            for g in range(4):
                P.emit("pe", lambda e, g=g, R=R, tv=tv: e.transpose(tv[:, g, :R], yhb[:R, g * 128:(g + 1) * 128], ident[:R, :R]),
                       [B_yhb, B_const], [TRB[i]])
            yield
            for g in range(4):
                P.emit("act", lambda e, g=g, R=R, tv=tv: e.activation(lnT[:, g, :R], tv[:, g, :R], AF.Silu,
                                                                    bias=lng[:, 4 + g:5 + g], scale=lng[:, g:g + 1]),
                       [TRB[i], B_c2], [B_lnT])
            yield
            for g in range(4):
                P.emit("pe", lambda e, g=g, R=R: e.matmul(PB[4][:R, :], lnT[:, g, :R], pw[:, g, :], start=(g == 0), stop=(g == 3)),
                       [B_lnT, B_pw], [PBb[4]])
            yield
            yield
            P.emit("dve", lambda e, R=R: e.tensor_tensor(catb[:R, 1024:1536], PB[4][:R, :], gat[:R, 1024:1536], ALU.mult),
                   [PBb[4], B_gat], [B_catb])
            yield
            pc, pp_ = pinb[cur], pinb[prv]
            for g in range(4):
                outp = PB[3][:, g * 128:g * 128 + R]
                if is_sample:
                    P.emit("pe", lambda e, g=g, outp=outp, pc=pc: e.matmul(outp, pc[:TS, g * 128:(g + 1) * 128], bands_s[:TS, g, :],
                                                                          start=True, stop=False), [B_pin[cur], B_c2], [PBb[3]])
                    P.emit("pe", lambda e, g=g, outp=outp, pp_=pp_: e.matmul(outp, pp_[:15, g * 128:(g + 1) * 128], bands_s[:15, 4 + g, :],
                                                                            start=False, stop=True), [B_pin[prv], B_c2], [PBb[3]])
                elif t == 0:
                    P.emit("pe", lambda e, g=g, outp=outp, pc=pc: e.matmul(outp, pc[:, g * 128:(g + 1) * 128], bands[:, 8 + g, :],
                                                                          start=True, stop=True), [B_pin[cur], B_c2], [PBb[3]])
                else:
                    P.emit("pe", lambda e, g=g, outp=outp, pc=pc: e.matmul(outp, pc[:, g * 128:(g + 1) * 128], bands[:, g, :],
                                                                          start=True, stop=False), [B_pin[cur], B_c2], [PBb[3]])
                    P.emit("pe", lambda e, g=g, outp=outp, pp_=pp_: e.matmul(outp, pp_[64:128, g * 128:(g + 1) * 128], bands[64:128, 4 + g, :],
                                                                            start=False, stop=True), [B_pin[prv], B_c2], [PBb[3]])
            yield
            P.emit("act", lambda e, R=R: e.copy(rTb[:, :, :R], PB[3][:, :].rearrange("p (a b) -> p a b", b=128)[:, :, :R]),
                   [PBb[3]], [B_rTb])
            yield
            for g in range(4):
                P.emit("pe", lambda e, g=g, R=R: e.matmul(PB[4][:R, g * 128:(g + 1) * 128], rTb[:, g, :R], plw[:, g, :],
                                                         start=True, stop=True), [B_rTb, B_plw], [PBb[4]])
            yield
            yield
            P.emit("dve", lambda e, R=R: e.tensor_tensor(yb[:R, :], PB[4][:R, :], pscale[:R, :], ALU.mult),
                   [PBb[4], B_c2], [B_yb])
            P.emit("dve", lambda e, R=R: e.tensor_tensor(catb[:R, 1536:2048], yb[:R, :], gat[:R, 1536:2048], ALU.mult),
                   [B_yb, B_gat], [B_catb])
            yield
            for c4 in range(4):
                srcs = [catb[:R, (c4 * 4 + j) * 128:(c4 * 4 + j + 1) * 128] for j in range(4)]
                tr_group(srcs, R, actT[:, c4 * 4:c4 * 4 + 4, tok0:tok0 + R], [B_catb], [B_actT])

        def interleave(gens):
            gens = list(gens)
            while gens:
                for g_ in list(gens):
                    try:
                        next(g_)
                    except StopIteration:
                        gens.remove(g_)

        def chain(*gs):
            for g_ in gs:
                for _ in g_:
                    yield

        def interleave_n(items):
            st_ = [[g_, max(1, l_), 0] for (g_, l_) in items]
            while st_:
                st_.sort(key=lambda x: x[2] / x[1])
                x = st_[0]
                try:
                    next(x[0])
                    x[2] += 1
                except StopIteration:
                    st_.remove(x)

        def nblk_of(t):
            return (kbase + t * 128 + R_of(t) + 127) // 128

        def len_idx(t):
            return 17 * ((kbase + t * 128 + R_of(t) + 511) // 512) + 3

        def len_bis(t):
            return NIT + nblk_of(t) + 2

        def len_cd(t):
            return 2 * nblk_of(t) + 32

        for _ in gen_IDX(0):
            pass
        items = [(gen_BIS(0), len_bis(0))]
        if NT > 1:
            items.append((gen_IDX(1), len_idx(1)))
        interleave_n(items)
        for t in range(NT):
            items = [(gen_CD(t), len_cd(t))]
            if t + 1 < NT:
                items.append((gen_BIS(t + 1), len_bis(t + 1)))
            if t + 2 < NT:
                items.append((gen_IDX(t + 2), len_idx(t + 2)))
            interleave_n(items)
        A.release(m2)
        phase_end()

        m3 = A.mark()
        tr_mode(True)
        alloc_wbuf(("w_out", layer, 0) not in converted)
        xr = [A.alloc([512]), A.alloc([512])]; B_xr = [Buf(), Buf()]
        x1o = [A.alloc([512]), A.alloc([512])]; B_x1o = [Buf(), Buf()]
        ec = [0]
        nxt_w = load_wblock("w_out", layer, [(0, 512)], 16, 0)
        for nb in range(4):
            wb, B_wb, ncols = nxt_w
            if nb + 1 < 4:
                nxt_w = load_wblock("w_out", layer, [((nb + 1) * 512, 512)], 16, nb + 1)
            for t in range(NT):
                R = R_of(t)
                tok0 = t * 128
                i = ec[0] % 2
                ib = ec[0] % 4
                ec[0] += 1
                P.dma("sp", xr[i][:R, :], x_src[tok0:tok0 + R, nb * 512:(nb + 1) * 512], [], [B_xr[i]])
                for kc in range(16):
                    P.emit("pe", lambda e, kc=kc, ib=ib, R=R, tok0=tok0, wb=wb: e.matmul(
                        PB[ib][:R, :], actT[:, kc, tok0:tok0 + R], wb[:, kc, :], start=(kc == 0), stop=(kc == 15)),
                        [B_actT, B_wb], [PBb[ib]])
                P.emit("dve", lambda e, i=i, ib=ib, R=R: e.tensor_tensor(x1o[i][:R, :], PB[ib][:R, :], xr[i][:R, :], ALU.add),
                       [PBb[ib], B_xr[i]], [B_x1o[i]])
                P.dma("sp", S_x1[tok0:tok0 + R, nb * 512:(nb + 1) * 512], x1o[i][:R, :], [B_x1o[i]], [])
        A.release(m3)
        phase_end()

        m4 = A.mark()
        tr_mode(True)
        alloc_wbuf(("w_ple_gate", layer, 0) not in converted)
        sc = make_sc()
        pT = A.alloc([2, Tn if not is_sample else 128], BF16); B_pT = Buf()
        pf = A.alloc([256]); B_pf = Buf()
        pb_ = A.alloc([256], BF16); B_pb = Buf()
        wp = [A.alloc([2, 512], BF16), A.alloc([2, 512], BF16)]; B_wp = [Buf(), Buf()]
        xr2 = [A.alloc([512]), A.alloc([512])]; B_xr2 = [Buf(), Buf()]
        sg = [A.alloc([512]), A.alloc([512])]; B_sg = [Buf(), Buf()]
        x2o = [A.alloc([512]), A.alloc([512])]; B_x2o = [Buf(), Buf()]
        gbc_h[0] = A.alloc([D])
        P.dma("sp", gbc_h[0], I["norm_ple"][layer, :].partition_broadcast(128), [], [B_gbc])
        nxt_w = load_wblock("w_ple_gate", layer, [(0, 512)], 16, 0)
        pend = []
        for t in range(NT):
            R = R_of(t)
            tok0 = t * 128
            P.dma("sp", sc["xt"][t % 2][:R, :], S_x1[tok0:tok0 + R, :], [], [sc["B_xt"][t % 2]])
            pb_fn = rmsnorm_T(t % 2, R, tok0, sc)
            while pend:
                pend.pop(0)()
            pend.append(pb_fn)
            P.dma("sp", pf[:R, :], p_src[tok0:tok0 + R, :], [], [B_pf])
            P.emit("dve", lambda e, R=R: e.tensor_copy(pb_[:R, :], pf[:R, :]), [B_pf], [B_pb])
            tr_group([pb_[:R, 0:128], pb_[:R, 128:256]], R, pT[:, :, tok0:tok0 + R], [B_pb], [B_pT])
        while pend:
            pend.pop(0)()
        dst = S_xres
        ec = [0]
        for nb in range(4):
            wb, B_wb, ncols = nxt_w
            if nb + 1 < 4:
                nxt_w = load_wblock("w_ple_gate", layer, [((nb + 1) * 512, 512)], 16, nb + 1)
            wi_ = nb % 2
            for kc in range(2):
                P.dma("pool", wp[wi_][:, kc, :], I["w_ple_proj"][layer][kc * 128:(kc + 1) * 128, nb * 512:(nb + 1) * 512],
                      [], [B_wp[wi_]])
            for t in range(NT):
                R = R_of(t)
                tok0 = t * 128
                i = ec[0] % 2
                ig = ec[0] % 4
                ip = 4 + ec[0] % 3
                ec[0] += 1
                P.dma("sp", xr2[i][:R, :], S_x1[tok0:tok0 + R, nb * 512:(nb + 1) * 512], [], [B_xr2[i]])
                for kc in range(16):
                    P.emit("pe", lambda e, kc=kc, ig=ig, R=R, tok0=tok0, wb=wb: e.matmul(
                        PB[ig][:R, :], actT[:, kc, tok0:tok0 + R], wb[:, kc, :], start=(kc == 0), stop=(kc == 15)),
                        [B_actT, B_wb], [PBb[ig]])
                for kc in range(2):
                    P.emit("pe", lambda e, kc=kc, ip=ip, R=R, tok0=tok0, wi_=wi_: e.matmul(
                        PB[ip][:R, :], pT[:, kc, tok0:tok0 + R], wp[wi_][:, kc, :], start=(kc == 0), stop=(kc == 1)),
                        [B_pT, B_wp[wi_]], [PBb[ip]])
                P.emit("act", lambda e, i=i, ig=ig, R=R: e.activation(sg[i][:R, :], PB[ig][:R, :], AF.Sigmoid), [PBb[ig]], [B_sg[i]])
                P.emit("dve", lambda e, i=i, ip=ip, R=R: e.tensor_tensor(sg[i][:R, :], sg[i][:R, :], PB[ip][:R, :], ALU.mult),
                       [B_sg[i], PBb[ip]], [B_sg[i]])
                P.emit("dve", lambda e, i=i, R=R: e.tensor_tensor(x2o[i][:R, :], sg[i][:R, :], xr2[i][:R, :], ALU.add),
                       [B_sg[i], B_xr2[i]], [B_x2o[i]])
                P.dma("sp", dst[tok0:tok0 + R, nb * 512:(nb + 1) * 512], x2o[i][:R, :], [B_x2o[i]], [])
        A.release(m4)
        phase_end()

        if last_layer:
            m5 = A.mark()
            xt = [A.alloc([D]), A.alloc([D])]; B_xt = [Buf(), Buf()]
            jk = A.alloc([D], BF16); B_jk = Buf()
            ss = A.alloc([8]); B_ss = Buf()
            yo = [A.alloc([D]), A.alloc([D])]; B_yo = [Buf(), Buf()]
            gbc = A.alloc([D])
            P.dma("sp", gbc, I["norm_final"][0, :].partition_broadcast(128), [], [B_gbc])
            ydst = O["y_s"][0] if is_sample else O["y_p"][seq]
            for t in range(NT):
                R = R_of(t)
                tok0 = t * 128
                i = t % 2
                P.dma("sp", xt[i][:R, :], S_xres[tok0:tok0 + R, :], [], [B_xt[i]])
                P.emit("act", lambda e, i=i, R=R: e.activation(jk[:R, :], xt[i][:R, :], AF.Square, accum_out=ss[:R, 0:1]),
                       [B_xt[i]], [B_jk, B_ss])
                P.emit("act", lambda e, R=R: e.activation(ss[:R, 1:2], ss[:R, 0:1], AF.Sqrt, bias=epsb[:R, 0:1], scale=1.0 / D),
                       [B_ss, B_const], [B_ss])
                P.emit("dve", lambda e, R=R: e.reciprocal(ss[:R, 2:3], ss[:R, 1:2]), [B_ss], [B_ss])
                P.emit("dve", lambda e, i=i, R=R: e.scalar_tensor_tensor(out=yo[i][:R, :], in0=xt[i][:R, :], scalar=ss[:R, 2:3],
                                                                        in1=gbc[:R, :], op0=ALU.mult, op1=ALU.mult),
                       [B_xt[i], B_ss, B_gbc], [B_yo[i]])
                P.dma("sp", ydst[tok0:tok0 + R, :], yo[i][:R, :], [B_yo[i]], [], is_out=True)
            A.release(m5)
            phase_end()

    seqs = [(s, False) for s in range(NP)] + ([(0, True)] if has_sample else [])
    try:
        for (s, is_s) in seqs:
            for layer in range(L):
                if layer == 0:
                    x_src = I["xs"][0] if is_s else I["xp"][s]
                else:
                    x_src = S_xres
                run_seq_layer(s, layer, is_s, x_src, layer == L - 1)
    except _Stop:
        pass
    P.finish()
    P.build(st)
    st.close()
    return nc, P


def make_consts(T):
    c = {}
    c["c_ident_bf"] = np.eye(128, dtype=np.float32).astype(ml_dtypes.bfloat16)
    c["c_ident_f"] = np.eye(128, dtype=np.float32)

    def rope_tab(pos):
        out = np.zeros((len(pos), 192), np.float32)
        for half, o in ((64, 0), (32, 128)):
            freq = (np.float32(10000.0) ** (-np.arange(half, dtype=np.float32) / np.float32(half))).astype(np.float32)
            ang = pos.astype(np.float32)[:, None] * freq[None, :]
            out[:, o:o + half] = np.cos(ang)
            out[:, o + half:o + 2 * half] = np.sin(ang)
        return out
    c["c_rope_p"] = rope_tab(np.arange(T))
    c["c_rope_s"] = rope_tab(PAST + np.arange(TS))
    nm = np.zeros((128, 128), np.float32)
    nm[:64, 64:] = -1e30
    c["c_negmask"] = nm
    bands = np.zeros((128, 12, 128), np.float32)
    bs = np.zeros((16, 8, 16), np.float32)
    for gi, w in enumerate((2, 4, 8, 16)):
        for tok in range(128):
            for j in range(tok - w + 1, tok + 1):
                if j >= 0:
                    bands[j, gi, tok] += 1.0 / w
                    bands[j, 8 + gi, tok] += 1.0 / min(tok + 1, w)
                else:
                    bands[128 + j, 4 + gi, tok] += 1.0 / w
            bands[tok, gi, tok] -= 1.0
            bands[tok, 8 + gi, tok] -= 1.0
        for tok in range(16):
            for j in range(tok - w + 1, tok + 1):
                if j >= 0:
                    bs[j, gi, tok] += 1.0 / w
                else:
                    bs[15 + j, 4 + gi, tok] += 1.0 / w
            bs[tok, gi, tok] -= 1.0
    c["c_bands"] = bands.astype(ml_dtypes.bfloat16)
    c["c_bands_s"] = bs.astype(ml_dtypes.bfloat16)
    return c


_CACHE = {}


def run(inputs, n_cores, NP, T, L, has_sample=True, dbg_stop=None):
    key = (NP, T, L, has_sample, dbg_stop)
    if key not in _CACHE:
        _CACHE[key] = build_program(NP, T, L, has_sample, dbg_stop=dbg_stop)
    nc, P = _CACHE[key]
    f = lambda a: np.ascontiguousarray(np.asarray(a, dtype=np.float32))
    consts = make_consts(T)
    wnames = ["norm_mix", "w_in", "conv_w", "conv_b", "conv_ln_g", "conv_ln_b", "conv_pw", "pool_w", "pool_scale",
              "w_out", "norm_ple", "w_ple_gate", "w_ple_proj"]
    shared = {n: f(inputs[n]) for n in wnames}
    shared["norm_final"] = f(inputs["norm_final"]).reshape(1, D)
    shared.update(consts)
    in_maps = []
    for c in range(n_cores):
        m = dict(shared)
        m["xp"] = f(inputs["x_prompt"][c * NP:(c + 1) * NP])
        m["pp"] = f(inputs["p_prompt"][:, c * NP:(c + 1) * NP])
        m["xs"] = f(inputs["x_sample"][c:c + 1])
        m["ps"] = f(inputs["p_sample"][:, c:c + 1])
        m["ck"] = f(inputs["cache_k"][:, c:c + 1]).reshape(L, 1, PAST, 256)
        m["cv"] = f(inputs["cache_v"][:, c:c + 1]).reshape(L, 1, PAST, 256)
        m["cki"] = f(inputs["cache_kidx"][:, c:c + 1])
        m["sconv"] = f(inputs["state_conv"][:, c:c + 1])
        m["spool"] = f(inputs["state_pool"][:, c:c + 1])
        in_maps.append(m)
    res = run_bass_kernel_spmd(nc, in_maps, core_ids=list(range(n_cores)))
    rs = res.results
    cat = lambda name, ax: np.concatenate([r[name] for r in rs], axis=ax)
    y_p = cat("y_p", 0)
    y_s = cat("y_s", 0)
    k_p = cat("k_p", 1).reshape(L, n_cores * NP, T, 2, 128)
    v_p = cat("v_p", 1).reshape(L, n_cores * NP, T, 2, 128)
    ki_p = cat("ki_p", 1)
    conv_p = cat("conv_p", 1)
    pool_p = cat("pool_p", 1)
    k_s = cat("k_s", 1).reshape(L, n_cores, TS, 2, 128)
    v_s = cat("v_s", 1).reshape(L, n_cores, TS, 2, 128)
    ki_s = cat("ki_s", 1)
    conv_s = cat("conv_s", 1)
    pool_s = cat("pool_s", 1)
    return (y_p, y_s, k_p, v_p, ki_p, conv_p, pool_p, k_s, v_s, ki_s, conv_s, pool_s)


def kernel(**inputs):
    outs = run(inputs, 8, 2, 2048, 2, True)
    return tuple(np.asarray(o, dtype=np.float32) for o in outs)
```

```python
from contextlib import ExitStack
import numpy as np
import ml_dtypes
import concourse.bass as bass
import concourse.mybir as mybir
from concourse.bass_utils import run_bass_kernel_spmd

F32 = mybir.dt.float32
BF16 = mybir.dt.bfloat16
AF = mybir.ActivationFunctionType
ALU = mybir.AluOpType

NPOOL = 16
EPOCH = 30000

D = 2048
NIN = 6224
PLE = 256
PAST = 1024
TS = 16
EPS = 1e-6
NIT = 22
W0 = 8.0


class Buf:
    __slots__ = ("name", "w", "r")

    def __init__(self, name=""):
        self.name = name
        self.w = None
        self.r = {}


class Op:
    __slots__ = ("id", "eng", "fn", "deps", "dma", "sig")


class Prog:
    ENGS = ("pe", "act", "dve", "pool", "sp")

    def __init__(self, nc):
        self.nc = nc
        self.ops = []
        self.by_eng = {e: [] for e in self.ENGS}
        self.out_dmas = []
        self.last = {e: None for e in self.ENGS}
        self.dmas_since_bar = []

    def emit(self, eng, fn, reads=(), writes=(), dma=False, out=False):
        op = Op()
        op.id = len(self.ops)
        op.eng = eng
        op.fn = fn
        op.dma = dma
        op.sig = None
        deps = {}
        for b in reads:
            if b.w is not None:
                deps[b.w] = "RAW"
        for b in writes:
            if b.w is not None:
                deps.setdefault(b.w, "WAW")
            for r in b.r.values():
                deps.setdefault(r, "WAR")
        key = ("dma", op.id) if dma else eng
        for b in reads:
            b.r[key] = op.id
        for b in writes:
            b.w = op.id
            b.r = {}
        op.deps = deps
        self.ops.append(op)
        self.by_eng[eng].append(op)
        if dma:
            self.dmas_since_bar.append(op.id)
        else:
            self.last[eng] = op.id
        if out:
            self.out_dmas.append(op.id)
        return op

    def dma(self, eng, out, in_, reads, writes, is_out=False):
        return self.emit(eng, lambda e: e.dma_start(out=out, in_=in_), reads, writes, dma=True, out=is_out)

    def barrier(self):
        deps = {}
        for e in self.ENGS:
            if self.last[e] is not None:
                deps[self.last[e]] = "RAW"
        for d in self.dmas_since_bar:
            deps[d] = "RAW"
        self.dmas_since_bar = []
        for e in self.ENGS:
            op = Op()
            op.id = len(self.ops)
            op.eng = e
            op.fn = None
            op.dma = False
            op.sig = None
            op.deps = {k: ("BAR" if self.ops[k].eng != e or self.ops[k].dma else "WAW") for k in deps}
            self.ops.append(op)
            self.by_eng[e].append(op)

    def _skip(self, o, d, kind):
        if d.dma or o.dma:
            return False
        if o.eng == d.eng:
            if o.eng == "pe":
                return True
            return kind != "RAW"
        return False

    def finish(self):
        op = Op()
        op.id = len(self.ops)
        op.eng = "sp"
        op.fn = None
        op.dma = False
        op.sig = None
        op.deps = {d: "RAW" for d in self.out_dmas}
        self.ops.append(op)
        self.by_eng["sp"].append(op)

    def build(self, stack):
        nc = self.nc
        ops = self.ops
        for q in ("sp", "pool", "act"):
            dma_ops = [o for o in ops if o.dma and o.eng == q]
            for j, o in enumerate(dma_ops):
                o.sig = (("dma", q + str(j % NPOOL)), 16 * (j // NPOOL + 1))
                if j >= NPOOL:
                    o.deps.setdefault(dma_ops[j - NPOOL].id, "GUARD")
        needed = set()
        for o in ops:
            for d, kind in o.deps.items():
                if not self._skip(o, ops[d], kind):
                    needed.add(d)
        cnt = {e: 0 for e in self.ENGS}
        semkeys = set()
        for o in ops:
            if o.dma:
                semkeys.add(o.sig[0])
            elif o.id in needed:
                c = cnt[o.eng]
                o.sig = ((o.eng, c // EPOCH), c % EPOCH + 1)
                cnt[o.eng] = c + 1
                semkeys.add(o.sig[0])
        sems = {}
        for k in sorted(semkeys, key=str):
            sems[k] = stack.enter_context(nc.semaphore("s_%s_%s" % (k[0], k[1])))
        self.n_sems = len(sems)
        self.n_waits = 0
        block = stack.enter_context(nc.Block())

        def run(engname, e):
            waited = {}
            for o in self.by_eng[engname]:
                req = {}
                for d in o.deps:
                    dop = ops[d]
                    if self._skip(o, dop, o.deps[d]):
                        continue
                    sk, val = dop.sig
                    if sk[0] == "dma":
                        if req.get(sk, 0) < val:
                            req[sk] = val
                    else:
                        cur = req.get(sk[0], (-1, 0))
                        if cur < (sk[1], val):
                            req[sk[0]] = (sk[1], val)
                for k in sorted(req, key=str):
                    v = req[k]
                    if isinstance(k, tuple):
                        if waited.get(k, 0) >= v:
                            continue
                        waited[k] = v
                        e.wait_ge(sems[k], v)
                    else:
                        if waited.get(k, (-1, 0)) >= v:
                            continue
                        waited[k] = v
                        e.wait_ge(sems[(k, v[0])], v[1])
                    self.n_waits += 1
                if o.fn is None:
                    continue
                ins = o.fn(e)
                if o.sig is not None:
                    ins.then_inc(sems[o.sig[0]], 16 if o.dma else 1)

        @block.tensor
        def _(e):
            run("pe", e)

        @block.scalar
        def _(e):
            run("act", e)

        @block.vector
        def _(e):
            run("dve", e)

        @block.gpsimd
        def _(e):
            run("pool", e)

        @block.sync
        def _(e):
            run("sp", e)


COLBLOCKS = [
    ("q", [(0, 512)], 0), ("q", [(512, 512)], 1),
    ("kv", [(1024, 512)], 0),
    ("qi", [(1536, 512)], 0), ("qi", [(2048, 512)], 1),
    ("kiwi", [(2560, 80)], 0),
    ("gate", [(2640, 512)], 0), ("gate", [(3152, 512)], 512),
    ("glu", [(3664, 256), (4176, 256)], 0), ("glu", [(3920, 256), (4432, 256)], 1),
    ("gate", [(4688, 512)], 1024),
    ("pin", [(5200, 512)], 0),
    ("gate", [(5712, 512)], 1536),
]


class Arena:
    def __init__(self, ap, nwords):
        self.ap = ap
        self.n = nwords
        self.off = 0

    def mark(self):
        return self.off

    def release(self, m):
        self.off = m

    def alloc(self, free_shape, dt=F32):
        n = int(np.prod(free_shape))
        w = n if dt == F32 else (n + 1) // 2
        a = self.ap[:, self.off:self.off + w]
        self.off += (w + 7) // 8 * 8
        assert self.off <= self.n, "SBUF arena overflow %d > %d" % (self.off, self.n)
        if dt != F32:
            a = a.bitcast(dt)
        if len(free_shape) == 2:
            a = a.rearrange("p (a b) -> p a b", b=free_shape[1])
        elif len(free_shape) == 3:
            a = a.rearrange("p (a b c) -> p a b c", b=free_shape[1], c=free_shape[2])
        return a


class _Stop(Exception):
    pass


def build_program(NP, T, L, has_sample=True, arena_words=53200, dbg_stop=None):
    nc = bass.Bass("TRN2", target_bir_lowering=False)
    NTP = T // 128
    TOPK_P = min(256, T // 4)
    TOPK_S = min(256, (PAST + TS) // 4)

    def din(name, shape, dt=F32):
        return nc.dram_tensor(name, list(shape), dt, kind="ExternalInput").ap()

    def dout(name, shape, dt=F32):
        return nc.dram_tensor(name, list(shape), dt, kind="ExternalOutput").ap()

    def dscr(name, shape, dt=F32):
        return nc.dram_tensor(name, list(shape), dt).ap()

    I = {}
    I["xp"] = din("xp", [NP, T, D])
    I["pp"] = din("pp", [L, NP, T, PLE])
    I["xs"] = din("xs", [1, TS, D])
    I["ps"] = din("ps", [L, 1, TS, PLE])
    I["ck"] = din("ck", [L, 1, PAST, 256])
    I["cv"] = din("cv", [L, 1, PAST, 256])
    I["cki"] = din("cki", [L, 1, PAST, 64])
    I["sconv"] = din("sconv", [L, 1, 30, 512])
    I["spool"] = din("spool", [L, 1, 15, 512])
    I["norm_mix"] = din("norm_mix", [L, D])
    I["w_in"] = din("w_in", [L, D, NIN])
    I["conv_w"] = din("conv_w", [L, 31, 512])
    I["conv_b"] = din("conv_b", [L, 512])
    I["conv_ln_g"] = din("conv_ln_g", [L, 512])
    I["conv_ln_b"] = din("conv_ln_b", [L, 512])
    I["conv_pw"] = din("conv_pw", [L, 512, 512])
    I["pool_w"] = din("pool_w", [L, 4, 128, 128])
    I["pool_scale"] = din("pool_scale", [L, 512])
    I["w_out"] = din("w_out", [L, D, D])
    I["norm_ple"] = din("norm_ple", [L, D])
    I["w_ple_gate"] = din("w_ple_gate", [L, D, D])
    I["w_ple_proj"] = din("w_ple_proj", [L, PLE, D])
    I["norm_final"] = din("norm_final", [1, D])
    I["c_ident_bf"] = din("c_ident_bf", [128, 128], BF16)
    I["c_ident_f"] = din("c_ident_f", [128, 128])
    I["c_rope_p"] = din("c_rope_p", [T, 192])
    I["c_rope_s"] = din("c_rope_s", [TS, 192])
    I["c_negmask"] = din("c_negmask", [128, 128])
    I["c_bands"] = din("c_bands", [128, 12, 128], BF16)
    I["c_bands_s"] = din("c_bands_s", [16, 8, 16], BF16)

    O = {}
    O["y_p"] = dout("y_p", [NP, T, D])
    O["y_s"] = dout("y_s", [1, TS, D])
    O["k_p"] = dout("k_p", [L, NP, T, 256])
    O["v_p"] = dout("v_p", [L, NP, T, 256])
    O["ki_p"] = dout("ki_p", [L, NP, T, 64])
    O["conv_p"] = dout("conv_p", [L, NP, 30, 512])
    O["pool_p"] = dout("pool_p", [L, NP, 15, 512])
    O["k_s"] = dout("k_s", [L, 1, TS, 256])
    O["v_s"] = dout("v_s", [L, 1, TS, 256])
    O["ki_s"] = dout("ki_s", [L, 1, TS, 64])
    O["conv_s"] = dout("conv_s", [L, 1, 30, 512])
    O["pool_s"] = dout("pool_s", [L, 1, 15, 512])

    S_qT = dscr("s_qT", [NTP, 128, 1024], BF16)
    S_qiT = dscr("s_qiT", [NTP, 128, 1024], BF16)
    S_gates = dscr("s_gates", [T, D], BF16)
    S_uT = dscr("s_uT", [NTP, 128, 512], BF16)
    S_pin = dscr("s_pin", [T, 512], BF16)
    S_dg = dscr("s_dg", [L, 128, 31 * 4 * 128], BF16)
    S_pw = dscr("s_pw", [L, 128, 4 * 512], BF16)
    S_plw = dscr("s_plw", [L, 128, 4 * 128], BF16)
    p2_cached = set()
    S_x1 = dscr("s_x1", [T, D])
    S_xres = dscr("s_xres", [T, D])

    st = ExitStack()
    P = Prog(nc)
    arena_t = st.enter_context(nc.sbuf_tensor("arena", [128, arena_words], F32))
    A = Arena(arena_t[:, :], arena_words)
    banks = [st.enter_context(nc.psum_tensor("pb%d" % i, [128, 512], F32)) for i in range(8)]
    PB = [b[:, :] for b in banks]
    PBb = [Buf("pb%d" % i) for i in range(8)]
    TRb = PB[7].bitcast(BF16)
    TRb6 = PB[6].bitcast(BF16)
    TR = [TRb[:, 0:512], TRb6[:, 0:512]]
    TRB = [PBb[7], PBb[6]]

    def tr_mode(two):
        if two:
            TR[0], TR[1] = TRb[:, 0:512], TRb6[:, 0:512]
            TRB[0], TRB[1] = PBb[7], PBb[6]
        else:
            TR[0], TR[1] = TRb[:, 0:512], TRb[:, 512:1024]
            TRB[0], TRB[1] = PBb[7], PBb[7]
    trc = [0]
    phase_ctr = [0]

    def phase_end():
        P.barrier()
        phase_ctr[0] += 1
        if dbg_stop is not None and phase_ctr[0] >= dbg_stop:
            raise _Stop()

    ident = A.alloc([128], BF16); B_const = Buf("const")
    identf = A.alloc([128])
    negmask = A.alloc([128])
    epsb = A.alloc([8])
    actT = A.alloc([16, T], BF16); B_actT = Buf("actT")
    wbuf = [None, None]
    B_wbuf = [Buf("w0"), Buf("w1")]

    def alloc_wbuf(need_stage):
        wbuf[0] = A.alloc([16, 512], BF16)
        wbuf[1] = A.alloc([16, 512], BF16)
        if need_stage:
            stg[0] = A.alloc([8, 512])
    NKMAX = max(T, PAST + TS)
    NBLK = (NKMAX + 127) // 128
    kT = A.alloc([2, NKMAX], BF16); B_kT = Buf("kT")
    V1 = A.alloc([NBLK, 2, 129], BF16); B_V1 = Buf("V1")
    kiT2 = A.alloc([NKMAX], BF16); B_kiT = Buf("kiT")
    wia = A.alloc([NTP, 32]); B_wi = Buf("wi")
    gbc_h = [None]; B_gbc = Buf("gbc")
    wcnt = [0]

    P.dma("sp", ident, I["c_ident_bf"], [], [B_const])
    P.dma("sp", identf, I["c_ident_f"], [], [B_const])
    P.dma("sp", negmask, I["c_negmask"], [], [B_const])
    P.emit("dve", lambda e: e.memset(epsb, EPS), [], [B_const])
    P.emit("dve", lambda e: e.memset(V1[:, :, :, 128:129], 1.0), [], [B_V1])

    def transpose_to(dst_fn, src, R, ncol, reads, writes, eng="act"):
        raise NotImplementedError

    def tr_group(srcs, R, dst, reads, writes, eng="act"):
        i = trc[0] % 2
        trc[0] += 1
        n = len(srcs)
        c = srcs[0].shape[1]
        tv = TR[i].rearrange("p (a b) -> p a b", b=128)
        for j, s in enumerate(srcs):
            P.emit("pe", lambda e, s=s, j=j: e.transpose(tv[:c, j, :R], s, ident[:R, :R]),
                   reads + [B_const], [TRB[i]])
        if eng == "act":
            P.emit("act", lambda e: e.copy(dst, tv[:c, 0:n, :R]), [TRB[i]], writes)
        else:
            P.emit("dve", lambda e: e.tensor_copy(dst, tv[:c, 0:n, :R]), [TRB[i]], writes)

    WB = {}
    converted = set()
    stg = [None]
    B_stg = Buf("stg")

    NBLK_W = {"w_in": len(COLBLOCKS), "w_out": 4, "w_ple_gate": 4}

    def load_wblock(name, layer, ranges, kch, bid):
        w_ap = I[name][layer]
        key = (name, layer)
        if key not in WB:
            WB[key] = dscr("wb_%s_%d" % (name, layer), [NBLK_W[name], 128, 16 * 512], BF16)
        wsc = WB[key]
        i = wcnt[0] % 2
        wcnt[0] += 1
        o = sum(n for (_, n) in ranges)
        ck = (name, layer, bid)
        if ck in converted:
            P.dma("sp", wbuf[i].rearrange("p a b -> p (a b)"), wsc[bid], [], [B_wbuf[i]])
        else:
            converted.add(ck)
            sg_ = stg[0]
            o2 = 0
            for (c0, n) in ranges:
                for k8 in range(0, kch, 8):
                    for k4 in range(k8, min(kch, k8 + 8), 4):
                        kk = min(4, kch - k4)
                        P.dma("pool", sg_[:, k4 - k8:k4 - k8 + kk, 0:n],
                              w_ap[k4 * 128:(k4 + kk) * 128, c0:c0 + n].rearrange("(kc p) n -> p kc n", p=128), [], [B_stg])
                    for k4 in range(k8, min(kch, k8 + 8), 4):
                        kk = min(4, kch - k4)
                        P.emit("pool", lambda e, k4=k4, kk=kk, k8=k8, o2=o2, n=n, i=i, sg_=sg_: e.tensor_copy(
                            wbuf[i][:, k4:k4 + kk, o2:o2 + n], sg_[:, k4 - k8:k4 - k8 + kk, 0:n]), [B_stg], [B_wbuf[i]])
                o2 += n
            P.dma("pool", wsc[bid], wbuf[i].rearrange("p a b -> p (a b)"), [B_wbuf[i]], [])
        return wbuf[i], B_wbuf[i], o

    def make_sc():
        sc = {}
        sc["xt"] = [A.alloc([D]), A.alloc([D])]; sc["B_xt"] = [Buf(), Buf()]
        sc["junk"] = A.alloc([D], BF16); sc["B_junk"] = Buf()
        sc["xn"] = [A.alloc([D], BF16), A.alloc([D], BF16)]; sc["B_xn"] = [Buf(), Buf()]
        sc["ss"] = A.alloc([8]); sc["B_ss"] = Buf()
        return sc

    def rmsnorm_T(j, R, tok0, sc):
        xt, B_xt = sc["xt"][j], sc["B_xt"][j]
        xn, B_xn = sc["xn"][j], sc["B_xn"][j]
        gbc = gbc_h[0]
        P.emit("act", lambda e: e.activation(sc["junk"][:R, :], xt[:R, :], AF.Square, accum_out=sc["ss"][:R, 0:1]),
               [B_xt], [sc["B_junk"], sc["B_ss"]])
        P.emit("act", lambda e: e.activation(sc["ss"][:R, 1:2], sc["ss"][:R, 0:1], AF.Sqrt, bias=epsb[:R, 0:1], scale=1.0 / D),
               [sc["B_ss"], B_const], [sc["B_ss"]])
        P.emit("dve", lambda e: e.reciprocal(sc["ss"][:R, 2:3], sc["ss"][:R, 1:2]), [sc["B_ss"]], [sc["B_ss"]])
        P.emit("dve", lambda e: e.scalar_tensor_tensor(out=xn[:R, :], in0=xt[:R, :], scalar=sc["ss"][:R, 2:3],
                                                       in1=gbc[:R, :], op0=ALU.mult, op1=ALU.mult),
               [B_xt, sc["B_ss"], B_gbc], [B_xn])

        def part_b():
            for c4 in range(4):
                srcs = [xn[:R, (c4 * 4 + jj) * 128:(c4 * 4 + jj + 1) * 128] for jj in range(4)]
                tr_group(srcs, R, actT[:, c4 * 4:c4 * 4 + 4, tok0:tok0 + R], [B_xn], [B_actT], eng=("act" if c4 % 2 == 0 else "dve"))
        return part_b

    def rope(e_eng, src, dst1, dst2, cos, sin, tmp, R, nh, half, reads, writes_list, B_tmp):
        s4 = src.rearrange("p (h t d) -> p h t d", h=nh, t=2)
        x1 = s4[:, :, 0, :]
        x2 = s4[:, :, 1, :]
        cb = cos.unsqueeze(1).to_broadcast([R, nh, half])
        sb_ = sin.unsqueeze(1).to_broadcast([R, nh, half])
        t4 = tmp[:R, 0:nh * half * 4].rearrange("p (k h d) -> p k h d", k=4, h=nh)
        P.emit("dve", lambda e: e.tensor_tensor(t4[:, 0], x1, cb, ALU.mult), reads, [B_tmp])
        P.emit("dve", lambda e: e.tensor_tensor(t4[:, 1], x2, sb_, ALU.mult), reads, [B_tmp])
        P.emit("dve", lambda e: e.tensor_tensor(t4[:, 2], x1, sb_, ALU.mult), reads, [B_tmp])
        P.emit("dve", lambda e: e.tensor_tensor(t4[:, 3], x2, cb, ALU.mult), reads, [B_tmp])
        P.emit("dve", lambda e: e.tensor_tensor(dst1, t4[:, 0], t4[:, 1], ALU.subtract), [B_tmp], writes_list)
        P.emit("dve", lambda e: e.tensor_tensor(dst2, t4[:, 2], t4[:, 3], ALU.add), [B_tmp], writes_list)

    def run_seq_layer(seq, layer, is_sample, x_src, last_layer):
        Tn = TS if is_sample else T
        NT = 1 if is_sample else NTP
        R_of = (lambda t: TS) if is_sample else (lambda t: 128)
        b = 0 if is_sample else seq
        rope_src = I["c_rope_s"] if is_sample else I["c_rope_p"]
        kbase = PAST if is_sample else 0
        topk = TOPK_S if is_sample else TOPK_P
        kname = "s" if is_sample else "p"
        p_src = I["ps"][layer, 0] if is_sample else I["pp"][layer, seq]

        if is_sample:
            m0 = A.mark()
            tr_mode(True)
            NB0 = PAST // 128
            ckf = [A.alloc([256]), A.alloc([256])]; B_ckf = [Buf(), Buf()]
            cvf = [A.alloc([256]), A.alloc([256])]; B_cvf = [Buf(), Buf()]
            ckif = [A.alloc([64]), A.alloc([64])]; B_ckif = [Buf(), Buf()]
            ckb = [A.alloc([256], BF16), A.alloc([256], BF16)]; B_ckb = [Buf(), Buf()]
            ckib = [A.alloc([128], BF16), A.alloc([128], BF16)]; B_ckib = [Buf(), Buf()]

            def p0_load(blk):
                j = blk % 2
                r0 = blk * 128
                P.dma("sp", ckf[j], I["ck"][layer, 0, r0:r0 + 128, :], [], [B_ckf[j]])
                P.dma("sp", cvf[j], I["cv"][layer, 0, r0:r0 + 128, :], [], [B_cvf[j]])
                P.dma("sp", ckif[j], I["cki"][layer, 0, r0:r0 + 128, :], [], [B_ckif[j]])

            p0_load(0)
            for blk in range(NB0):
                j = blk % 2
                r0 = blk * 128
                if blk + 1 < NB0:
                    p0_load(blk + 1)
                P.emit("act", lambda e, j=j: e.copy(ckb[j], ckf[j]), [B_ckf[j]], [B_ckb[j]])
                P.emit("dve", lambda e, j=j, blk=blk: e.tensor_copy(
                    V1[:, blk, :, 0:128], cvf[j].rearrange("p (g d) -> p g d", g=2)), [B_cvf[j]], [B_V1])
                P.emit("act", lambda e, j=j: e.copy(ckib[j][:, 0:64], ckif[j]), [B_ckif[j]], [B_ckib[j]])
                P.emit("dve", lambda e, j=j: e.tensor_copy(ckib[j][:, 64:128], ckif[j]), [B_ckif[j]], [B_ckib[j]])
                tr_group([ckb[j][:, 0:128], ckb[j][:, 128:256]], 128, kT[:, :, r0:r0 + 128], [B_ckb[j]], [B_kT])
                tr_group([ckib[j][:, 0:128]], 128, kiT2[:, r0:r0 + 128].unsqueeze(1), [B_ckib[j]], [B_kiT], eng="dve")
            A.release(m0)
            phase_end()

        m1 = A.mark()
        tr_mode(True)
        first_use = ("w_in", layer, 0) not in converted
        alloc_wbuf(first_use)
        sc = make_sc()
        ropet = A.alloc([NT, 192]); B_rope = Buf()
        zs = [A.alloc([512]), A.alloc([512])]; B_zs = [Buf(), Buf()]
        tmp = A.alloc([1024]); B_tmp = Buf()
        ob = [A.alloc([512], BF16), A.alloc([512], BF16)]; B_ob = [Buf(), Buf()]
        of = [A.alloc([512]), A.alloc([512])]; B_of = [Buf(), Buf()]
        tT = [A.alloc([4, 128], BF16), A.alloc([4, 128], BF16)]; B_tT = [Buf(), Buf()]

        gbc_h[0] = A.alloc([D])
        P.dma("sp", gbc_h[0], I["norm_mix"][layer, :].partition_broadcast(128), [], [B_gbc])
        if is_sample:
            P.dma("sp", ropet[:TS, 0, :], rope_src, [], [B_rope])
        else:
            P.dma("sp", ropet, rope_src.rearrange("(n p) c -> p n c", p=128), [], [B_rope])
        nxt_w = load_wblock("w_in", layer, COLBLOCKS[0][1], 16, 0)
        pend = []
        for t in range(NT):
            R = R_of(t)
            P.dma("sp", sc["xt"][t % 2][:R, :], x_src[t * 128:t * 128 + R, :], [], [sc["B_xt"][t % 2]])
            pb_fn = rmsnorm_T(t % 2, R, t * 128, sc)
            while pend:
                pend.pop(0)()
            pend.append(pb_fn)
        while pend:
            pend.pop(0)()

        ec = [0]
        for bi_, (kind, ranges, arg) in enumerate(COLBLOCKS):
            wb, B_wb, ncols = nxt_w
            if bi_ + 1 < len(COLBLOCKS):
                nxt_w = load_wblock("w_in", layer, COLBLOCKS[bi_ + 1][1], 16, bi_ + 1)
            for t in range(NT):
                R = R_of(t)
                tok0 = t * 128
                i = ec[0] % 2
                ec[0] += 1
                ps, B_ps = PB[(ec[0] - 1) % 4], PBb[(ec[0] - 1) % 4]
                for kc in range(16):
                    P.emit("pe", lambda e, kc=kc, ps=ps, R=R, tok0=tok0, wb=wb, ncols=ncols:
                           e.matmul(ps[:R, :ncols], actT[:, kc, tok0:tok0 + R], wb[:, kc, :ncols],
                                    start=(kc == 0), stop=(kc == 15)),
                           [B_actT, B_wb], [B_ps])
                while pend:
                    pend.pop(0)()
                z, B_z = zs[i], B_zs[i]
                o_b, B_o = ob[i], B_ob[i]
                o_f, B_f = of[i], B_of[i]
                tt_, B_tt = tT[i], B_tT[i]
                cos128 = ropet[:R, t, 0:64]
                sin128 = ropet[:R, t, 64:128]
                cos64 = ropet[:R, t, 128:160]
                sin64 = ropet[:R, t, 160:192]
                if kind == "q":
                    P.emit("act", lambda e, z=z, ps=ps, R=R: e.copy(z[:R, :], ps[:R, :]), [B_ps], [B_z])
                    o4 = o_b[:R, :].rearrange("p (h t d) -> p h t d", h=4, t=2)
                    rope("dve", z[:R, :], o4[:, :, 0, :], o4[:, :, 1, :], cos128, sin128, tmp, R, 4, 64,
                         [B_z, B_rope], [B_o], B_tmp)
                    def pb_(o_b=o_b, R=R, tt_=tt_, B_o=B_o, B_tt=B_tt, t=t, arg=arg):
                        tr_group([o_b[:R, j * 128:(j + 1) * 128] for j in range(4)], R, tt_[:, :, :R], [B_o], [B_tt])
                        P.dma("sp", S_qT[t, :, arg * 512:(arg + 1) * 512].rearrange("p (a b) -> p a b", b=128)[:, :, :R],
                              tt_[:, :, :R], [B_tt], [])
                    pend.append(pb_)
                elif kind == "qi":
                    P.emit("act", lambda e, z=z, ps=ps, R=R: e.copy(z[:R, :], ps[:R, :]), [B_ps], [B_z])
                    o4 = o_b[:R, :].rearrange("p (h t d) -> p h t d", h=8, t=2)
                    rope("dve", z[:R, :], o4[:, :, 0, :], o4[:, :, 1, :], cos64, sin64, tmp, R, 8, 32,
                         [B_z, B_rope], [B_o], B_tmp)
                    def pb_(o_b=o_b, R=R, tt_=tt_, B_o=B_o, B_tt=B_tt, t=t, arg=arg):
                        tr_group([o_b[:R, j * 128:(j + 1) * 128] for j in range(4)], R, tt_[:, :, :R], [B_o], [B_tt])
                        P.dma("sp", S_qiT[t, :, arg * 512:(arg + 1) * 512].rearrange("p (a b) -> p a b", b=128)[:, :, :R],
                              tt_[:, :, :R], [B_tt], [])
                    pend.append(pb_)
                elif kind == "kv":
                    P.emit("act", lambda e, z=z, ps=ps, R=R: e.copy(z[:R, :], ps[:R, :]), [B_ps], [B_z])
                    f4 = o_f[:R, 0:256].rearrange("p (h t d) -> p h t d", h=2, t=2)
                    rope("dve", z[:R, 0:256], f4[:, :, 0, :], f4[:, :, 1, :], cos128, sin128, tmp, R, 2, 64,
                         [B_z, B_rope], [B_f], B_tmp)
                    P.dma("sp", O["k_" + kname][layer, b, tok0:tok0 + R, :], o_f[:R, 0:256], [B_f], [], is_out=True)
                    P.dma("sp", O["v_" + kname][layer, b, tok0:tok0 + R, :], z[:R, 256:512], [B_z], [], is_out=True)
                    P.emit("act", lambda e, o_b=o_b, o_f=o_f, R=R: e.copy(o_b[:R, 0:256], o_f[:R, 0:256]), [B_f], [B_o])
                    k0 = kbase + tok0
                    pend.append(lambda o_b=o_b, R=R, k0=k0, B_o=B_o: tr_group(
                        [o_b[:R, 0:128], o_b[:R, 128:256]], R, kT[:, :, k0:k0 + R], [B_o], [B_kT]))
                    blk = k0 // 128
                    P.emit("dve", lambda e, z=z, R=R, blk=blk: e.tensor_copy(
                        V1[:R, blk, :, 0:128], z[:R, 256:512].rearrange("p (g d) -> p g d", g=2)), [B_z], [B_V1])
                elif kind == "kiwi":
                    P.emit("act", lambda e, z=z, ps=ps, R=R: e.copy(z[:R, 0:80], ps[:R, 0:80]), [B_ps], [B_z])
                    f4 = o_f[:R, 0:64].rearrange("p (h t d) -> p h t d", h=1, t=2)
                    rope("dve", z[:R, 0:64], f4[:, :, 0, :], f4[:, :, 1, :], cos64, sin64, tmp, R, 1, 32,
                         [B_z, B_rope], [B_f], B_tmp)
                    P.dma("sp", O["ki_" + kname][layer, b, tok0:tok0 + R, :], o_f[:R, 0:64], [B_f], [], is_out=True)
                    P.emit("act", lambda e, o_b=o_b, o_f=o_f, R=R: e.copy(o_b[:R, 0:64], o_f[:R, 0:64]), [B_f], [B_o])
                    P.emit("act", lambda e, o_b=o_b, o_f=o_f, R=R: e.copy(o_b[:R, 64:128], o_f[:R, 0:64]), [B_f], [B_o])
                    k0 = kbase + tok0
                    pend.append(lambda o_b=o_b, R=R, k0=k0, B_o=B_o: tr_group(
                        [o_b[:R, 0:128]], R, kiT2[:, k0:k0 + R].unsqueeze(1), [B_o], [B_kiT]))
                    P.emit("act", lambda e, z=z, R=R, t=t: e.activation(wia[:R, t, 0:16], z[:R, 64:80], AF.Abs, scale=1.0 / 32.0),
                           [B_z], [B_wi])
                    P.emit("act", lambda e, z=z, R=R, t=t: e.activation(wia[:R, t, 16:32], z[:R, 64:80], AF.Sign),
                           [B_z], [B_wi])
                elif kind == "gate":
                    P.emit("act", lambda e, o_b=o_b, ps=ps, R=R: e.activation(o_b[:R, :], ps[:R, :], AF.Silu), [B_ps], [B_o])
                    P.dma("sp", S_gates[tok0:tok0 + R, arg:arg + 512], o_b[:R, :], [B_o], [])
                elif kind == "glu":
                    P.emit("act", lambda e, z=z, ps=ps, R=R: e.activation(z[:R, 0:256], ps[:R, 256:512], AF.Sigmoid), [B_ps], [B_z])
                    P.emit("dve", lambda e, z=z, ps=ps, R=R, o_f=o_f: e.tensor_tensor(o_f[:R, 0:256], ps[:R, 0:256], z[:R, 0:256], ALU.mult),
                           [B_ps, B_z], [B_f])
                    c0 = arg * 256
                    if is_sample:
                        P.dma("sp", O["conv_s"][layer, 0, 14:30, c0:c0 + 256], o_f[:TS, 0:256], [B_f], [], is_out=True)
                    elif t == NT - 1:
                        P.dma("sp", O["conv_p"][layer, b, 0:30, c0:c0 + 256], o_f[98:128, 0:256], [B_f], [], is_out=True)
                    P.emit("act", lambda e, o_b=o_b, o_f=o_f, R=R: e.copy(o_b[:R, 0:256], o_f[:R, 0:256]), [B_f], [B_o])
                    def pb_(o_b=o_b, R=R, tt_=tt_, B_o=B_o, B_tt=B_tt, t=t, c0=c0):
                        tr_group([o_b[:R, 0:128], o_b[:R, 128:256]], R, tt_[:, 0:2, :R], [B_o], [B_tt])
                        P.dma("sp", S_uT[t, :, c0:c0 + 256].rearrange("p (a b) -> p a b", b=128)[:, :, :R],
                              tt_[:, 0:2, :R], [B_tt], [])
                    pend.append(pb_)
                elif kind == "pin":
                    P.emit("act", lambda e, z=z, ps=ps, R=R: e.copy(z[:R, :], ps[:R, :]), [B_ps], [B_z])
                    if is_sample:
                        P.dma("sp", O["pool_s"][layer, 0, 0:15, :], z[1:16, :], [B_z], [], is_out=True)
                    elif t == NT - 1:
                        P.dma("sp", O["pool_p"][layer, b, 0:15, :], z[113:128, :], [B_z], [], is_out=True)
                    P.emit("dve", lambda e, z=z, o_b=o_b, R=R: e.tensor_copy(o_b[:R, :], z[:R, :]), [B_z], [B_o])
                    P.dma("sp", S_pin[tok0:tok0 + R, :], o_b[:R, :], [B_o], [])
        while pend:
            pend.pop(0)()
        if is_sample:
            P.dma("sp", O["conv_s"][layer, 0, 0:14, :], I["sconv"][layer, 0, 16:30, :], [], [], is_out=True)
        A.release(m1)
        phase_end()

        m2 = A.mark()
        tr_mode(False)
        dg = A.alloc([31, 4, 128], BF16); B_dg = Buf()
        pw = A.alloc([4, 512], BF16); B_pw = Buf()
        plw = A.alloc([4, 128], BF16); B_plw = Buf()
        bands = A.alloc([12, 128], BF16)
        bands_s = A.alloc([8, 16], BF16)
        convb = A.alloc([512]); pscale = A.alloc([512]); lng = A.alloc([8]); B_c2 = Buf()
        cwb = A.alloc([512]); B_cwb = Buf()
        qTt = A.alloc([8, 128], BF16); B_qT = Buf()
        qiTt = A.alloc([8, 128], BF16); B_qiT = Buf()
        gat = A.alloc([D], BF16); B_gat = Buf()
        uTb = [A.alloc([4, 160], BF16), A.alloc([4, 160], BF16)]; B_uT = [Buf(), Buf()]
        pinb = [A.alloc([512], BF16), A.alloc([512], BF16)]; B_pin = [Buf(), Buf()]
        NK = NKMAX
        saccs = [A.alloc([NK]), A.alloc([NK])]; B_saccs = [Buf(), Buf()]
        rb = [A.alloc([512], BF16) for _ in range(4)]; B_rb = [Buf() for _ in range(4)]
        Dg = A.alloc([16, 128], BF16); B_Dg = Buf()
        mask = A.alloc([NK], BF16); B_mask = Buf()
        junk, B_junk = mask, B_mask
        maskTs = [A.alloc([NBLK, 128], BF16), A.alloc([NBLK, 128], BF16)]; B_maskTs = [Buf(), Buf()]
        bis = A.alloc([16]); B_bis = Buf()
        Eb = [A.alloc([512], BF16) for _ in range(4)]; B_E = [Buf() for _ in range(4)]
        PTb = [A.alloc([512], BF16) for _ in range(4)]; B_PT = [Buf() for _ in range(4)]
        catb = A.alloc([D], BF16); B_catb = Buf()
        rinv = A.alloc([8]); B_rinv = Buf()
        yb = A.alloc([512]); B_yb = Buf()
        yhb = A.alloc([512], BF16); B_yhb = Buf()
        lnT = A.alloc([4, 128], BF16); B_lnT = Buf()
        rTb = A.alloc([4, 128], BF16); B_rTb = Buf()
        bnb = A.alloc([16]); B_bn = Buf()
        spb = A.alloc([512], BF16); B_spb = Buf()

        P.dma("sp", bands, I["c_bands"], [], [B_c2])
        P.dma("sp", bands_s[:16], I["c_bands_s"], [], [B_c2])
        P.dma("sp", convb, I["conv_b"][layer, :].partition_broadcast(128), [], [B_c2])
        P.dma("sp", pscale, I["pool_scale"][layer, :].partition_broadcast(128), [], [B_c2])
        for g in range(4):
            P.dma("sp", lng[:, g:g + 1], I["conv_ln_g"][layer, g * 128:(g + 1) * 128].unsqueeze(1), [], [B_c2])
            P.dma("sp", lng[:, 4 + g:5 + g], I["conv_ln_b"][layer, g * 128:(g + 1) * 128].unsqueeze(1), [], [B_c2])
        if layer not in p2_cached:
            p2_cached.add(layer)
            for g in range(4):
                P.dma("pool", pw[:, g, :], I["conv_pw"][layer][g * 128:(g + 1) * 128, :], [], [B_pw])
                P.dma("pool", plw[:, g, :], I["pool_w"][layer, g], [], [B_plw])
            P.dma("sp", S_pw[layer], pw.rearrange("p a b -> p (a b)"), [B_pw], [])
            P.dma("sp", S_plw[layer], plw.rearrange("p a b -> p (a b)"), [B_plw], [])
            cwbs = [cwb, yb]
            B_cwbs = [B_cwb, B_yb]
            for k in range(31):
                P.dma("sp", cwbs[k % 2], I["conv_w"][layer, k, :].partition_broadcast(128), [], [B_cwbs[k % 2]])
                P.emit("dve", lambda e, k=k, cw_=cwbs[k % 2]: e.tensor_tensor(dg[:, k, :, :], cw_.rearrange("p (g d) -> p g d", g=4),
                                                                          identf.unsqueeze(1).to_broadcast([128, 4, 128]), ALU.mult),
                       [B_cwbs[k % 2], B_const], [B_dg])
            P.dma("sp", S_dg[layer], dg.rearrange("p a b c -> p (a b c)"), [B_dg], [])
        else:
            P.dma("sp", pw.rearrange("p a b -> p (a b)"), S_pw[layer], [], [B_pw])
            P.dma("sp", plw.rearrange("p a b -> p (a b)"), S_plw[layer], [], [B_plw])
            P.dma("sp", dg.rearrange("p a b c -> p (a b c)"), S_dg[layer], [], [B_dg])
        if is_sample:
            P.dma("pool", spb[:30, :], I["sconv"][layer, 0, :, :], [], [B_spb])
            tr_group([spb[:30, j * 128:(j + 1) * 128] for j in range(4)], 30, uTb[0][:, :, 0:30], [B_spb], [B_uT[0]])
            P.dma("pool", pinb[1][:15, :], I["spool"][layer, 0, :, :], [], [B_pin[1]])
        else:
            P.emit("dve", lambda e: e.memset(uTb[0][:, :, 0:30], 0.0), [], [B_uT[0]])

        def gen_IDX(t):
            R = R_of(t)
            tok0 = t * 128
            cur, prv = t % 2, (t + 1) % 2
            maskT, B_maskT = maskTs[t % 2], B_maskTs[t % 2]
            nk = kbase + tok0 + R
            kblocks = []
            k0 = 0
            while k0 < nk:
                n = min(128, nk - k0)
                kblocks.append((k0 // 128, k0, n))
                k0 += n
            P.dma("sp", qiTt[:, :, :R], S_qiT[t].rearrange("p (a b) -> p a b", b=128)[:, :, :R], [], [B_qiT])
            for h in range(16):
                P.emit("dve", lambda e, h=h, R=R, t=t: e.tensor_scalar(
                    Dg[:R, h, :R], identf[:R, :R], wia[:R, t, 16 + h:17 + h], None, op0=ALU.mult), [B_const, B_wi], [B_Dg])
            yield
            li = [0]
            for c0 in range(0, nk, 512):
                n = min(512, nk - c0)
                prev_ = None

                def diag_mm(i_, n_, h_, R=R):
                    P.emit("pe", lambda e: e.matmul(PB[2][:R, :n_], Dg[:R, h_, :R], rb[i_][:R, :n_],
                                                    start=(h_ == 0), stop=(h_ == 15)), [B_Dg, B_rb[i_]], [PBb[2]])
                for h in range(16):
                    i = li[0] % 2
                    i4 = li[0] % 4
                    li[0] += 1
                    hp, j = h % 2, h // 2
                    P.emit("pe", lambda e, i=i, hp=hp, j=j, c0=c0, n=n, R=R: e.matmul(
                        PB[i][:R, :n], qiTt[hp * 64:(hp + 1) * 64, j, :R], kiT2[hp * 64:(hp + 1) * 64, c0:c0 + n],
                        start=True, stop=True), [B_qiT, B_kiT], [PBb[i]])
                    P.emit("act", lambda e, i=i, i4=i4, n=n, R=R, h=h, t=t: e.activation(
                        rb[i4][:R, :n], PB[i][:R, :n], AF.Relu, scale=wia[:R, t, h:h + 1]), [PBb[i], B_wi], [B_rb[i4]])
                    if prev_ is not None:
                        diag_mm(*prev_)
                    prev_ = (i4, n, h)
                    yield
                diag_mm(*prev_)
                P.emit("act", lambda e, n=n, R=R, c0=c0, t=t: e.copy(saccs[t % 2][:R, c0:c0 + n], PB[2][:R, :n]),
                       [PBb[2]], [B_saccs[t % 2]])
            if not is_sample:
                P.emit("dve", lambda e, tok0=tok0: e.tensor_tensor(saccs[t % 2][:, tok0:tok0 + 128], saccs[t % 2][:, tok0:tok0 + 128],
                                                                 negmask, ALU.add), [B_saccs[t % 2], B_const], [B_saccs[t % 2]])

        def gen_BIS(t):
            R = R_of(t)
            tok0 = t * 128
            cur, prv = t % 2, (t + 1) % 2
            maskT, B_maskT = maskTs[t % 2], B_maskTs[t % 2]
            nk = kbase + tok0 + R
            kblocks = []
            k0 = 0
            while k0 < nk:
                n = min(128, nk - k0)
                kblocks.append((k0 // 128, k0, n))
                k0 += n
            svis_min = nk if is_sample else (tok0 + 64)
            if svis_min > topk:
                P.emit("dve", lambda e, R=R: e.memset(bis[:R, 0:1], 0.0), [], [B_bis])
                w = W0
                for it in range(NIT):
                    a, bnew = it % 2, (it + 1) % 2
                    w = w / 2.0
                    P.emit("dve", lambda e, R=R, nk=nk, a=a: e.tensor_scalar(
                        junk[:R, :nk], saccs[t % 2][:R, :nk], bis[:R, a:a + 1], 0.0, op0=ALU.is_ge, op1=ALU.add,
                        accum_out=bis[:R, 2:3]), [B_saccs[t % 2], B_bis], [B_junk, B_bis])
                    P.emit("dve", lambda e, R=R, w=w: e.tensor_scalar(
                        bis[:R, 3:4], bis[:R, 2:3], float(topk), 2.0 * w, op0=ALU.is_ge, op1=ALU.mult),
                        [B_bis], [B_bis])
                    P.emit("dve", lambda e, R=R, w=w, a=a, bnew=bnew: e.tensor_scalar(
                        bis[:R, bnew:bnew + 1], bis[:R, 3:4], -w, bis[:R, a:a + 1], op0=ALU.add, op1=ALU.add),
                        [B_bis], [B_bis])
                    yield
                fin = NIT % 2
                P.emit("dve", lambda e, R=R, w=w, fin=fin: e.tensor_scalar(
                    bis[:R, 4:5], bis[:R, fin:fin + 1], -w, None, op0=ALU.add), [B_bis], [B_bis])
            else:
                P.emit("dve", lambda e, R=R: e.memset(bis[:R, 4:5], -1e29), [], [B_bis])
            P.emit("dve", lambda e, R=R, nk=nk: e.tensor_scalar(
                mask[:R, :nk], saccs[t % 2][:R, :nk], bis[:R, 4:5], None, op0=ALU.is_ge), [B_saccs[t % 2], B_bis], [B_mask])
            for (blk, k0, n) in kblocks:
                tr_group([mask[:R, k0:k0 + n]], R, maskT[:n, blk:blk + 1, :R], [B_mask], [B_maskT], eng="dve")
                yield


        def gen_CD(t):
            R = R_of(t)
            tok0 = t * 128
            cur, prv = t % 2, (t + 1) % 2
            maskT, B_maskT = maskTs[t % 2], B_maskTs[t % 2]
            nk = kbase + tok0 + R
            kblocks = []
            k0 = 0
            while k0 < nk:
                n = min(128, nk - k0)
                kblocks.append((k0 // 128, k0, n))
                k0 += n
            P.dma("sp", qTt[:, :, :R], S_qT[t].rearrange("p (a b) -> p a b", b=128)[:, :, :R], [], [B_qT])
            P.dma("sp", gat[:R, :], S_gates[tok0:tok0 + R, :], [], [B_gat])
            P.dma("sp", uTb[cur][:, :, 30:30 + R], S_uT[t].rearrange("p (a b) -> p a b", b=128)[:, :, :R], [], [B_uT[cur]])
            P.dma("sp", pinb[cur][:R, :], S_pin[tok0:tok0 + R, :], [], [B_pin[cur]])
            yield
            ai = [0]
            nkb = len(kblocks)
            for g in range(2):
                pipe_m = []
                pipe_v = []

                def emit_mult(i, n, blk, R=R):
                    P.emit("dve", lambda e: e.tensor_tensor(
                        PTb[i][:n, 0:4 * R].rearrange("p (a b) -> p a b", b=R),
                        Eb[i][:n, 0:4 * R].rearrange("p (a b) -> p a b", b=R),
                        maskT[:n, blk:blk + 1, :R].to_broadcast([n, 4, R]), ALU.mult), [B_E[i], B_maskT], [B_PT[i]])

                def emit_pv(i, n, blk, bi, g=g, R=R):
                    for hh in range(4):
                        ob_, B_obk = (PB[5], PBb[5]) if hh < 3 else (PB[6], PBb[6])
                        oc = (hh % 3) * 129
                        P.emit("pe", lambda e, hh=hh, ob_=ob_, oc=oc,
                               st_=(bi == 0 and hh in (0, 3)), sp2=(bi == nkb - 1 and hh in (2, 3)): e.matmul(
                            ob_[:R, oc:oc + 129], PTb[i][:n, hh * R:(hh + 1) * R], V1[:n, blk, g, :],
                            start=st_, stop=sp2), [B_PT[i], B_V1], [B_obk])

                for bi, (blk, k0, n) in enumerate(kblocks):
                    i = ai[0] % 4
                    sp_, B_sp = PB[3 + ai[0] % 2], PBb[3 + ai[0] % 2]
                    ai[0] += 1
                    P.emit("pe", lambda e, sp_=sp_, g=g, k0=k0, n=n, R=R: e.matmul(
                        sp_[:n, 0:4 * R].rearrange("p (a b) -> p a b", b=R), kT[:, g, k0:k0 + n],
                        qTt[:, g * 4:(g + 1) * 4, :R], start=True, stop=True), [B_kT, B_qT], [B_sp])
                    P.emit("act", lambda e, sp_=sp_, i=i, n=n, R=R: e.activation(
                        Eb[i][:n, 0:4 * R], sp_[:n, 0:4 * R], AF.Exp, scale=128.0 ** -0.5), [B_sp], [B_E[i]])
                    if pipe_v:
                        emit_pv(*pipe_v.pop(0))
                    if pipe_m:
                        a_ = pipe_m.pop(0)
                        emit_mult(a_[0], a_[1], a_[2])
                        pipe_v.append(a_)
                    pipe_m.append((i, n, blk, bi))
                    yield
                while pipe_m or pipe_v:
                    if pipe_v:
                        emit_pv(*pipe_v.pop(0))
                    if pipe_m:
                        a_ = pipe_m.pop(0)
                        emit_mult(a_[0], a_[1], a_[2])
                        pipe_v.append(a_)
                    yield
                for hh in range(4):
                    h = g * 4 + hh
                    ob_, B_obk = (PB[5], PBb[5]) if hh < 3 else (PB[6], PBb[6])
                    oc = (hh % 3) * 129
                    P.emit("dve", lambda e, ob_=ob_, oc=oc, h=h, R=R: e.reciprocal(rinv[:R, h:h + 1], ob_[:R, oc + 128:oc + 129]),
                           [B_obk], [B_rinv])
                    P.emit("dve", lambda e, ob_=ob_, oc=oc, h=h, R=R: e.scalar_tensor_tensor(
                        out=catb[:R, h * 128:(h + 1) * 128], in0=ob_[:R, oc:oc + 128], scalar=rinv[:R, h:h + 1],
                        in1=gat[:R, h * 128:(h + 1) * 128], op0=ALU.mult, op1=ALU.mult),
                        [B_obk, B_rinv, B_gat], [B_catb])
                yield
            ub = uTb[cur]
            for g in range(4):
                for k in range(31):
                    P.emit("pe", lambda e, g=g, k=k, R=R, ub=ub: e.matmul(
                        PB[3][:R, g * 128:(g + 1) * 128], ub[:, g, k:k + R], dg[:, k, g, :],
                        start=(k == 0), stop=(k == 30)), [B_uT[cur], B_dg], [PBb[3]])
                yield
            if t + 1 < NT:
                P.emit("act", lambda e, ub=ub, nb=uTb[prv]: e.copy(nb[:, :, 0:30], ub[:, :, 128:158]), [B_uT[cur]], [B_uT[prv]])
            yield
            yield
            yield
            yield
            P.emit("dve", lambda e, R=R: e.tensor_tensor(yb[:R, :], PB[3][:R, :], convb[:R, :], ALU.add), [PBb[3], B_c2], [B_yb])
            P.emit("dve", lambda e, R=R: e.bn_stats(bnb[:R, 0:6], yb[:R, :]), [B_yb], [B_bn])
            P.emit("dve", lambda e, R=R: e.bn_aggr(bnb[:R, 6:8], bnb[:R, 0:6]), [B_bn], [B_bn])
            yield
            P.emit("act", lambda e, R=R: e.activation(bnb[:R, 8:9], bnb[:R, 7:8], AF.Sqrt, bias=epsb[:R, 0:1], scale=1.0),
                   [B_bn, B_const], [B_bn])
            yield
            P.emit("dve", lambda e, R=R: e.reciprocal(bnb[:R, 9:10], bnb[:R, 8:9]), [B_bn], [B_bn])
            P.emit("dve", lambda e, R=R: e.tensor_scalar(yhb[:R, :], yb[:R, :], bnb[:R, 6:7], bnb[:R, 9:10],
                                                        op0=ALU.subtract, op1=ALU.mult), [B_yb, B_bn], [B_yhb])
            yield
            i = trc[0] % 2
            trc[0] += 1
            tv = TR[i].rearrange("p (a b) -> p a b", b=128)
            for g in range(4):
                P.emit("pe", lambda e, g=g, R=R, tv=tv: e.transpose(tv[:, g, :R], yhb[:R, g * 128:(g + 1) * 128], ident[:R, :R]),
                       [B_yhb, B_const], [TRB[i]])
            yield
            for g in range(4):
                P.emit("act", lambda e, g=g, R=R, tv=tv: e.activation(lnT[:, g, :R], tv[:, g, :R], AF.Silu,
                                                                    bias=lng[:, 4 + g:5 + g], scale=lng[:, g:g + 1]),
                       [TRB[i], B_c2], [B_lnT])
            yield
            for g in range(4):
                P.emit("pe", lambda e, g=g, R=R: e.matmul(PB[4][:R, :], lnT[:, g, :R], pw[:, g, :], start=(g == 0), stop=(g == 3)),
                       [B_lnT, B_pw], [PBb[4]])
            yield
            yield
            P.emit("dve", lambda e, R=R: e.tensor_tensor(catb[:R, 1024:1536], PB[4][:R, :], gat[:R, 1024:1536], ALU.mult),
                   [PBb[4], B_gat], [B_catb])
            yield
            pc, pp_ = pinb[cur], pinb[prv]
            for g in range(4):
                outp = PB[3][:, g * 128:g * 128 + R]
                if is_sample:
                    P.emit("pe", lambda e, g=g, outp=outp, pc=pc: e.matmul(outp, pc[:TS, g * 128:(g + 1) * 128], bands_s[:TS, g, :],
                                                                          start=True, stop=False), [B_pin[cur], B_c2], [PBb[3]])
                    P.emit("pe", lambda e, g=g, outp=outp, pp_=pp_: e.matmul(outp, pp_[:15, g * 128:(g + 1) * 128], bands_s[:15, 4 + g, :],
                                                                            start=False, stop=True), [B_pin[prv], B_c2], [PBb[3]])
                elif t == 0:
                    P.emit("pe", lambda e, g=g, outp=outp, pc=pc: e.matmul(outp, pc[:, g * 128:(g + 1) * 128], bands[:, 8 + g, :],
                                                                          start=True, stop=True), [B_pin[cur], B_c2], [PBb[3]])
                else:
                    P.emit("pe", lambda e, g=g, outp=outp, pc=pc: e.matmul(outp, pc[:, g * 128:(g + 1) * 128], bands[:, g, :],
                                                                          start=True, stop=False), [B_pin[cur], B_c2], [PBb[3]])
                    P.emit("pe", lambda e, g=g, outp=outp, pp_=pp_: e.matmul(outp, pp_[64:128, g * 128:(g + 1) * 128], bands[64:128, 4 + g, :],
                                                                            start=False, stop=True), [B_pin[prv], B_c2], [PBb[3]])
            yield
            P.emit("act", lambda e, R=R: e.copy(rTb[:, :, :R], PB[3][:, :].rearrange("p (a b) -> p a b", b=128)[:, :, :R]),
                   [PBb[3]], [B_rTb])
            yield
            for g in range(4):
                P.emit("pe", lambda e, g=g, R=R: e.matmul(PB[4][:R, g * 128:(g + 1) * 128], rTb[:, g, :R], plw[:, g, :],
                                                         start=True, stop=True), [B_rTb, B_plw], [PBb[4]])
            yield
            yield
            P.emit("dve", lambda e, R=R: e.tensor_tensor(yb[:R, :], PB[4][:R, :], pscale[:R, :], ALU.mult),
                   [PBb[4], B_c2], [B_yb])
            P.emit("dve", lambda e, R=R: e.tensor_tensor(catb[:R, 1536:2048], yb[:R, :], gat[:R, 1536:2048], ALU.mult),
                   [B_yb, B_gat], [B_catb])
            yield
            for c4 in range(4):
                srcs = [catb[:R, (c4 * 4 + j) * 128:(c4 * 4 + j + 1) * 128] for j in range(4)]
                tr_group(srcs, R, actT[:, c4 * 4:c4 * 4 + 4, tok0:tok0 + R], [B_catb], [B_actT])

        def interleave(gens):
            gens = list(gens)
            while gens:
                for g_ in list(gens):
                    try:
                        next(g_)
                    except StopIteration:
                        gens.remove(g_)

        def chain(*gs):
            for g_ in gs:
                for _ in g_:
                    yield

        def interleave_n(items):
            st_ = [[g_, max(1, l_), 0] for (g_, l_) in items]
            while st_:
                st_.sort(key=lambda x: x[2] / x[1])
                x = st_[0]
                try:
                    next(x[0])
                    x[2] += 1
                except StopIteration:
                    st_.remove(x)

        def nblk_of(t):
            return (kbase + t * 128 + R_of(t) + 127) // 128

        def len_idx(t):
            return 16 * ((kbase + t * 128 + R_of(t) + 511) // 512) + 1

        def len_bis(t):
            return NIT + nblk_of(t) + 2

        def len_cd(t):
            return 2 * nblk_of(t) + 28

        for _ in gen_IDX(0):
            pass
        items = [(gen_BIS(0), len_bis(0))]
        if NT > 1:
            items.append((gen_IDX(1), len_idx(1)))
        interleave_n(items)
        for t in range(NT):
            items = [(gen_CD(t), len_cd(t))]
            if t + 1 < NT:
                items.append((gen_BIS(t + 1), len_bis(t + 1)))
            if t + 2 < NT:
                items.append((gen_IDX(t + 2), len_idx(t + 2)))
            interleave_n(items)
        A.release(m2)
        phase_end()

        m3 = A.mark()
        tr_mode(True)
        alloc_wbuf(("w_out", layer, 0) not in converted)
        xr = [A.alloc([512]), A.alloc([512])]; B_xr = [Buf(), Buf()]
        x1o = [A.alloc([512]), A.alloc([512])]; B_x1o = [Buf(), Buf()]
        ec = [0]
        nxt_w = load_wblock("w_out", layer, [(0, 512)], 16, 0)
        for nb in range(4):
            wb, B_wb, ncols = nxt_w
            if nb + 1 < 4:
                nxt_w = load_wblock("w_out", layer, [((nb + 1) * 512, 512)], 16, nb + 1)
            for t in range(NT):
                R = R_of(t)
                tok0 = t * 128
                i = ec[0] % 2
                ib = ec[0] % 4
                ec[0] += 1
                P.dma("sp", xr[i][:R, :], x_src[tok0:tok0 + R, nb * 512:(nb + 1) * 512], [], [B_xr[i]])
                for kc in range(16):
                    P.emit("pe", lambda e, kc=kc, ib=ib, R=R, tok0=tok0, wb=wb: e.matmul(
                        PB[ib][:R, :], actT[:, kc, tok0:tok0 + R], wb[:, kc, :], start=(kc == 0), stop=(kc == 15)),
                        [B_actT, B_wb], [PBb[ib]])
                P.emit("dve", lambda e, i=i, ib=ib, R=R: e.tensor_tensor(x1o[i][:R, :], PB[ib][:R, :], xr[i][:R, :], ALU.add),
                       [PBb[ib], B_xr[i]], [B_x1o[i]])
                P.dma("sp", S_x1[tok0:tok0 + R, nb * 512:(nb + 1) * 512], x1o[i][:R, :], [B_x1o[i]], [])
        A.release(m3)
        phase_end()

        m4 = A.mark()
        tr_mode(True)
        alloc_wbuf(("w_ple_gate", layer, 0) not in converted)
        sc = make_sc()
        pT = A.alloc([2, Tn if not is_sample else 128], BF16); B_pT = Buf()
        pf = A.alloc([256]); B_pf = Buf()
        pb_ = A.alloc([256], BF16); B_pb = Buf()
        wp = [A.alloc([2, 512], BF16), A.alloc([2, 512], BF16)]; B_wp = [Buf(), Buf()]
        xr2 = [A.alloc([512]), A.alloc([512])]; B_xr2 = [Buf(), Buf()]
        sg = [A.alloc([512]), A.alloc([512])]; B_sg = [Buf(), Buf()]
        x2o = [A.alloc([512]), A.alloc([512])]; B_x2o = [Buf(), Buf()]
        gbc_h[0] = A.alloc([D])
        P.dma("sp", gbc_h[0], I["norm_ple"][layer, :].partition_broadcast(128), [], [B_gbc])
        nxt_w = load_wblock("w_ple_gate", layer, [(0, 512)], 16, 0)
        pend = []
        for t in range(NT):
            R = R_of(t)
            tok0 = t * 128
            P.dma("sp", sc["xt"][t % 2][:R, :], S_x1[tok0:tok0 + R, :], [], [sc["B_xt"][t % 2]])
            pb_fn = rmsnorm_T(t % 2, R, tok0, sc)
            while pend:
                pend.pop(0)()
            pend.append(pb_fn)
            P.dma("sp", pf[:R, :], p_src[tok0:tok0 + R, :], [], [B_pf])
            P.emit("dve", lambda e, R=R: e.tensor_copy(pb_[:R, :], pf[:R, :]), [B_pf], [B_pb])
            tr_group([pb_[:R, 0:128], pb_[:R, 128:256]], R, pT[:, :, tok0:tok0 + R], [B_pb], [B_pT])
        while pend:
            pend.pop(0)()
        dst = S_xres
        ec = [0]
        for nb in range(4):
            wb, B_wb, ncols = nxt_w
            if nb + 1 < 4:
                nxt_w = load_wblock("w_ple_gate", layer, [((nb + 1) * 512, 512)], 16, nb + 1)
            wi_ = nb % 2
            for kc in range(2):
                P.dma("pool", wp[wi_][:, kc, :], I["w_ple_proj"][layer][kc * 128:(kc + 1) * 128, nb * 512:(nb + 1) * 512],
                      [], [B_wp[wi_]])
            for t in range(NT):
                R = R_of(t)
                tok0 = t * 128
                i = ec[0] % 2
                ig = ec[0] % 4
                ip = 4 + ec[0] % 3
                ec[0] += 1
                P.dma("sp", xr2[i][:R, :], S_x1[tok0:tok0 + R, nb * 512:(nb + 1) * 512], [], [B_xr2[i]])
                for kc in range(16):
                    P.emit("pe", lambda e, kc=kc, ig=ig, R=R, tok0=tok0, wb=wb: e.matmul(
                        PB[ig][:R, :], actT[:, kc, tok0:tok0 + R], wb[:, kc, :], start=(kc == 0), stop=(kc == 15)),
                        [B_actT, B_wb], [PBb[ig]])
                for kc in range(2):
                    P.emit("pe", lambda e, kc=kc, ip=ip, R=R, tok0=tok0, wi_=wi_: e.matmul(
                        PB[ip][:R, :], pT[:, kc, tok0:tok0 + R], wp[wi_][:, kc, :], start=(kc == 0), stop=(kc == 1)),
                        [B_pT, B_wp[wi_]], [PBb[ip]])
                P.emit("act", lambda e, i=i, ig=ig, R=R: e.activation(sg[i][:R, :], PB[ig][:R, :], AF.Sigmoid), [PBb[ig]], [B_sg[i]])
                P.emit("dve", lambda e, i=i, ip=ip, R=R: e.tensor_tensor(sg[i][:R, :], sg[i][:R, :], PB[ip][:R, :], ALU.mult),
                       [B_sg[i], PBb[ip]], [B_sg[i]])
                P.emit("dve", lambda e, i=i, R=R: e.tensor_tensor(x2o[i][:R, :], sg[i][:R, :], xr2[i][:R, :], ALU.add),
                       [B_sg[i], B_xr2[i]], [B_x2o[i]])
                P.dma("sp", dst[tok0:tok0 + R, nb * 512:(nb + 1) * 512], x2o[i][:R, :], [B_x2o[i]], [])
        A.release(m4)
        phase_end()

        if last_layer:
            m5 = A.mark()
            xt = [A.alloc([D]), A.alloc([D])]; B_xt = [Buf(), Buf()]
            jk = A.alloc([D], BF16); B_jk = Buf()
            ss = A.alloc([8]); B_ss = Buf()
            yo = [A.alloc([D]), A.alloc([D])]; B_yo = [Buf(), Buf()]
            gbc = A.alloc([D])
            P.dma("sp", gbc, I["norm_final"][0, :].partition_broadcast(128), [], [B_gbc])
            ydst = O["y_s"][0] if is_sample else O["y_p"][seq]
            for t in range(NT):
                R = R_of(t)
                tok0 = t * 128
                i = t % 2
                P.dma("sp", xt[i][:R, :], S_xres[tok0:tok0 + R, :], [], [B_xt[i]])
                P.emit("act", lambda e, i=i, R=R: e.activation(jk[:R, :], xt[i][:R, :], AF.Square, accum_out=ss[:R, 0:1]),
                       [B_xt[i]], [B_jk, B_ss])
                P.emit("act", lambda e, R=R: e.activation(ss[:R, 1:2], ss[:R, 0:1], AF.Sqrt, bias=epsb[:R, 0:1], scale=1.0 / D),
                       [B_ss, B_const], [B_ss])
                P.emit("dve", lambda e, R=R: e.reciprocal(ss[:R, 2:3], ss[:R, 1:2]), [B_ss], [B_ss])
                P.emit("dve", lambda e, i=i, R=R: e.scalar_tensor_tensor(out=yo[i][:R, :], in0=xt[i][:R, :], scalar=ss[:R, 2:3],
                                                                        in1=gbc[:R, :], op0=ALU.mult, op1=ALU.mult),
                       [B_xt[i], B_ss, B_gbc], [B_yo[i]])
                P.dma("sp", ydst[tok0:tok0 + R, :], yo[i][:R, :], [B_yo[i]], [], is_out=True)
            A.release(m5)
            phase_end()

    seqs = [(s, False) for s in range(NP)] + ([(0, True)] if has_sample else [])
    try:
        for (s, is_s) in seqs:
            for layer in range(L):
                if layer == 0:
                    x_src = I["xs"][0] if is_s else I["xp"][s]
                else:
                    x_src = S_xres
                run_seq_layer(s, layer, is_s, x_src, layer == L - 1)
    except _Stop:
        pass
    P.finish()
    P.build(st)
    st.close()
    return nc, P


def make_consts(T):
    c = {}
    c["c_ident_bf"] = np.eye(128, dtype=np.float32).astype(ml_dtypes.bfloat16)
    c["c_ident_f"] = np.eye(128, dtype=np.float32)

    def rope_tab(pos):
        out = np.zeros((len(pos), 192), np.float32)
        for half, o in ((64, 0), (32, 128)):
            freq = (np.float32(10000.0) ** (-np.arange(half, dtype=np.float32) / np.float32(half))).astype(np.float32)
            ang = pos.astype(np.float32)[:, None] * freq[None, :]
            out[:, o:o + half] = np.cos(ang)
            out[:, o + half:o + 2 * half] = np.sin(ang)
        return out
    c["c_rope_p"] = rope_tab(np.arange(T))
    c["c_rope_s"] = rope_tab(PAST + np.arange(TS))
    nm = np.zeros((128, 128), np.float32)
    nm[:64, 64:] = -1e30
    c["c_negmask"] = nm
    bands = np.zeros((128, 12, 128), np.float32)
    bs = np.zeros((16, 8, 16), np.float32)
    for gi, w in enumerate((2, 4, 8, 16)):
        for tok in range(128):
            for j in range(tok - w + 1, tok + 1):
                if j >= 0:
                    bands[j, gi, tok] += 1.0 / w
                    bands[j, 8 + gi, tok] += 1.0 / min(tok + 1, w)
                else:
                    bands[128 + j, 4 + gi, tok] += 1.0 / w
            bands[tok, gi, tok] -= 1.0
            bands[tok, 8 + gi, tok] -= 1.0
        for tok in range(16):
            for j in range(tok - w + 1, tok + 1):
                if j >= 0:
                    bs[j, gi, tok] += 1.0 / w
                else:
                    bs[15 + j, 4 + gi, tok] += 1.0 / w
            bs[tok, gi, tok] -= 1.0
    c["c_bands"] = bands.astype(ml_dtypes.bfloat16)
    c["c_bands_s"] = bs.astype(ml_dtypes.bfloat16)
    return c


_CACHE = {}


def run(inputs, n_cores, NP, T, L, has_sample=True, dbg_stop=None):
    key = (NP, T, L, has_sample, dbg_stop)
    if key not in _CACHE:
        _CACHE[key] = build_program(NP, T, L, has_sample, dbg_stop=dbg_stop)
    nc, P = _CACHE[key]
    f = lambda a: np.ascontiguousarray(np.asarray(a, dtype=np.float32))
    consts = make_consts(T)
    wnames = ["norm_mix", "w_in", "conv_w", "conv_b", "conv_ln_g", "conv_ln_b", "conv_pw", "pool_w", "pool_scale",
              "w_out", "norm_ple", "w_ple_gate", "w_ple_proj"]
    shared = {n: f(inputs[n]) for n in wnames}
    shared["norm_final"] = f(inputs["norm_final"]).reshape(1, D)
    shared.update(consts)
    in_maps = []
    for c in range(n_cores):
        m = dict(shared)
        m["xp"] = f(inputs["x_prompt"][c * NP:(c + 1) * NP])
        m["pp"] = f(inputs["p_prompt"][:, c * NP:(c + 1) * NP])
        m["xs"] = f(inputs["x_sample"][c:c + 1])
        m["ps"] = f(inputs["p_sample"][:, c:c + 1])
        m["ck"] = f(inputs["cache_k"][:, c:c + 1]).reshape(L, 1, PAST, 256)
        m["cv"] = f(inputs["cache_v"][:, c:c + 1]).reshape(L, 1, PAST, 256)
        m["cki"] = f(inputs["cache_kidx"][:, c:c + 1])
        m["sconv"] = f(inputs["state_conv"][:, c:c + 1])
        m["spool"] = f(inputs["state_pool"][:, c:c + 1])
        in_maps.append(m)
    res = run_bass_kernel_spmd(nc, in_maps, core_ids=list(range(n_cores)))
    rs = res.results
    cat = lambda name, ax: np.concatenate([r[name] for r in rs], axis=ax)
    y_p = cat("y_p", 0)
    y_s = cat("y_s", 0)
    k_p = cat("k_p", 1).reshape(L, n_cores * NP, T, 2, 128)
    v_p = cat("v_p", 1).reshape(L, n_cores * NP, T, 2, 128)
    ki_p = cat("ki_p", 1)
    conv_p = cat("conv_p", 1)
    pool_p = cat("pool_p", 1)
    k_s = cat("k_s", 1).reshape(L, n_cores, TS, 2, 128)
    v_s = cat("v_s", 1).reshape(L, n_cores, TS, 2, 128)
    ki_s = cat("ki_s", 1)
    conv_s = cat("conv_s", 1)
    pool_s = cat("pool_s", 1)
    return (y_p, y_s, k_p, v_p, ki_p, conv_p, pool_p, k_s, v_s, ki_s, conv_s, pool_s)


def kernel(**inputs):
    outs = run(inputs, 8, 2, 2048, 2, True)
    return tuple(np.asarray(o, dtype=np.float32) for o in outs)
```

```python
from contextlib import ExitStack
import numpy as np
import ml_dtypes
import concourse.bass as bass
import concourse.mybir as mybir
from concourse.bass_utils import run_bass_kernel_spmd

F32 = mybir.dt.float32
BF16 = mybir.dt.bfloat16
AF = mybir.ActivationFunctionType
ALU = mybir.AluOpType

NPOOL = 16
EPOCH = 30000

D = 2048
NIN = 6224
PLE = 256
PAST = 1024
TS = 16
EPS = 1e-6
NIT = 22
W0 = 8.0


class Buf:
    __slots__ = ("name", "w", "r")

    def __init__(self, name=""):
        self.name = name
        self.w = None
        self.r = {}


class Op:
    __slots__ = ("id", "eng", "fn", "deps", "dma", "sig")


class Prog:
    ENGS = ("pe", "act", "dve", "pool", "sp")

    def __init__(self, nc):
        self.nc = nc
        self.ops = []
        self.by_eng = {e: [] for e in self.ENGS}
        self.out_dmas = []
        self.last = {e: None for e in self.ENGS}
        self.dmas_since_bar = []

    def emit(self, eng, fn, reads=(), writes=(), dma=False, out=False):
        op = Op()
        op.id = len(self.ops)
        op.eng = eng
        op.fn = fn
        op.dma = dma
        op.sig = None
        deps = {}
        for b in reads:
            if b.w is not None:
                deps[b.w] = "RAW"
        for b in writes:
            if b.w is not None:
                deps.setdefault(b.w, "WAW")
            for r in b.r.values():
                deps.setdefault(r, "WAR")
        key = ("dma", op.id) if dma else eng
        for b in reads:
            b.r[key] = op.id
        for b in writes:
            b.w = op.id
            b.r = {}
        op.deps = deps
        self.ops.append(op)
        self.by_eng[eng].append(op)
        if dma:
            self.dmas_since_bar.append(op.id)
        else:
            self.last[eng] = op.id
        if out:
            self.out_dmas.append(op.id)
        return op

    def dma(self, eng, out, in_, reads, writes, is_out=False):
        return self.emit(eng, lambda e: e.dma_start(out=out, in_=in_), reads, writes, dma=True, out=is_out)

    def barrier(self):
        deps = {}
        for e in self.ENGS:
            if self.last[e] is not None:
                deps[self.last[e]] = "RAW"
        for d in self.dmas_since_bar:
            deps[d] = "RAW"
        self.dmas_since_bar = []
        for e in self.ENGS:
            op = Op()
            op.id = len(self.ops)
            op.eng = e
            op.fn = None
            op.dma = False
            op.sig = None
            op.deps = {k: ("BAR" if self.ops[k].eng != e or self.ops[k].dma else "WAW") for k in deps}
            self.ops.append(op)
            self.by_eng[e].append(op)

    def _skip(self, o, d, kind):
        if d.dma or o.dma:
            return False
        if o.eng == d.eng:
            if o.eng == "pe":
                return True
            return kind != "RAW"
        return False

    def finish(self):
        op = Op()
        op.id = len(self.ops)
        op.eng = "sp"
        op.fn = None
        op.dma = False
        op.sig = None
        op.deps = {d: "RAW" for d in self.out_dmas}
        self.ops.append(op)
        self.by_eng["sp"].append(op)

    def build(self, stack):
        nc = self.nc
        ops = self.ops
        for q in ("sp", "pool", "act"):
            dma_ops = [o for o in ops if o.dma and o.eng == q]
            for j, o in enumerate(dma_ops):
                o.sig = (("dma", q + str(j % NPOOL)), 16 * (j // NPOOL + 1))
                if j >= NPOOL:
                    o.deps.setdefault(dma_ops[j - NPOOL].id, "GUARD")
        needed = set()
        for o in ops:
            for d, kind in o.deps.items():
                if not self._skip(o, ops[d], kind):
                    needed.add(d)
        cnt = {e: 0 for e in self.ENGS}
        semkeys = set()
        for o in ops:
            if o.dma:
                semkeys.add(o.sig[0])
            elif o.id in needed:
                c = cnt[o.eng]
                o.sig = ((o.eng, c // EPOCH), c % EPOCH + 1)
                cnt[o.eng] = c + 1
                semkeys.add(o.sig[0])
        sems = {}
        for k in sorted(semkeys, key=str):
            sems[k] = stack.enter_context(nc.semaphore("s_%s_%s" % (k[0], k[1])))
        self.n_sems = len(sems)
        self.n_waits = 0
        block = stack.enter_context(nc.Block())

        def run(engname, e):
            waited = {}
            for o in self.by_eng[engname]:
                req = {}
                for d in o.deps:
                    dop = ops[d]
                    if self._skip(o, dop, o.deps[d]):
                        continue
                    sk, val = dop.sig
                    if sk[0] == "dma":
                        if req.get(sk, 0) < val:
                            req[sk] = val
                    else:
                        cur = req.get(sk[0], (-1, 0))
                        if cur < (sk[1], val):
                            req[sk[0]] = (sk[1], val)
                for k in sorted(req, key=str):
                    v = req[k]
                    if isinstance(k, tuple):
                        if waited.get(k, 0) >= v:
                            continue
                        waited[k] = v
                        e.wait_ge(sems[k], v)
                    else:
                        if waited.get(k, (-1, 0)) >= v:
                            continue
                        waited[k] = v
                        e.wait_ge(sems[(k, v[0])], v[1])
                    self.n_waits += 1
                if o.fn is None:
                    continue
                ins = o.fn(e)
                if o.sig is not None:
                    ins.then_inc(sems[o.sig[0]], 16 if o.dma else 1)

        @block.tensor
        def _(e):
            run("pe", e)

        @block.scalar
        def _(e):
            run("act", e)

        @block.vector
        def _(e):
            run("dve", e)

        @block.gpsimd
        def _(e):
            run("pool", e)

        @block.sync
        def _(e):
            run("sp", e)


COLBLOCKS = [
    ("q", [(0, 512)], 0), ("q", [(512, 512)], 1),
    ("kv", [(1024, 512)], 0),
    ("qi", [(1536, 512)], 0), ("qi", [(2048, 512)], 1),
    ("kiwi", [(2560, 80)], 0),
    ("gate", [(2640, 512)], 0), ("gate", [(3152, 512)], 512),
    ("glu", [(3664, 256), (4176, 256)], 0), ("glu", [(3920, 256), (4432, 256)], 1),
    ("gate", [(4688, 512)], 1024),
    ("pin", [(5200, 512)], 0),
    ("gate", [(5712, 512)], 1536),
]


class Arena:
    def __init__(self, ap, nwords):
        self.ap = ap
        self.n = nwords
        self.off = 0

    def mark(self):
        return self.off

    def release(self, m):
        self.off = m

    def alloc(self, free_shape, dt=F32):
        n = int(np.prod(free_shape))
        w = n if dt == F32 else (n + 1) // 2
        a = self.ap[:, self.off:self.off + w]
        self.off += (w + 7) // 8 * 8
        assert self.off <= self.n, "SBUF arena overflow %d > %d" % (self.off, self.n)
        if dt != F32:
            a = a.bitcast(dt)
        if len(free_shape) == 2:
            a = a.rearrange("p (a b) -> p a b", b=free_shape[1])
        elif len(free_shape) == 3:
            a = a.rearrange("p (a b c) -> p a b c", b=free_shape[1], c=free_shape[2])
        return a


class _Stop(Exception):
    pass


def build_program(NP, T, L, has_sample=True, arena_words=53200, dbg_stop=None):
    nc = bass.Bass("TRN2", target_bir_lowering=False)
    NTP = T // 128
    TOPK_P = min(256, T // 4)
    TOPK_S = min(256, (PAST + TS) // 4)

    def din(name, shape, dt=F32):
        return nc.dram_tensor(name, list(shape), dt, kind="ExternalInput").ap()

    def dout(name, shape, dt=F32):
        return nc.dram_tensor(name, list(shape), dt, kind="ExternalOutput").ap()

    def dscr(name, shape, dt=F32):
        return nc.dram_tensor(name, list(shape), dt).ap()

    I = {}
    I["xp"] = din("xp", [NP, T, D])
    I["pp"] = din("pp", [L, NP, T, PLE])
    I["xs"] = din("xs", [1, TS, D])
    I["ps"] = din("ps", [L, 1, TS, PLE])
    I["ck"] = din("ck", [L, 1, PAST, 256])
    I["cv"] = din("cv", [L, 1, PAST, 256])
    I["cki"] = din("cki", [L, 1, PAST, 64])
    I["sconv"] = din("sconv", [L, 1, 30, 512])
    I["spool"] = din("spool", [L, 1, 15, 512])
    I["norm_mix"] = din("norm_mix", [L, D])
    I["w_in"] = din("w_in", [L, D, NIN])
    I["conv_w"] = din("conv_w", [L, 31, 512])
    I["conv_b"] = din("conv_b", [L, 512])
    I["conv_ln_g"] = din("conv_ln_g", [L, 512])
    I["conv_ln_b"] = din("conv_ln_b", [L, 512])
    I["conv_pw"] = din("conv_pw", [L, 512, 512])
    I["pool_w"] = din("pool_w", [L, 4, 128, 128])
    I["pool_scale"] = din("pool_scale", [L, 512])
    I["w_out"] = din("w_out", [L, D, D])
    I["norm_ple"] = din("norm_ple", [L, D])
    I["w_ple_gate"] = din("w_ple_gate", [L, D, D])
    I["w_ple_proj"] = din("w_ple_proj", [L, PLE, D])
    I["norm_final"] = din("norm_final", [1, D])
    I["c_ident_bf"] = din("c_ident_bf", [128, 128], BF16)
    I["c_ident_f"] = din("c_ident_f", [128, 128])
    I["c_rope_p"] = din("c_rope_p", [T, 192])
    I["c_rope_s"] = din("c_rope_s", [TS, 192])
    I["c_negmask"] = din("c_negmask", [128, 128])
    I["c_bands"] = din("c_bands", [128, 12, 128], BF16)
    I["c_bands_s"] = din("c_bands_s", [16, 8, 16], BF16)

    O = {}
    O["y_p"] = dout("y_p", [NP, T, D])
    O["y_s"] = dout("y_s", [1, TS, D])
    O["k_p"] = dout("k_p", [L, NP, T, 256])
    O["v_p"] = dout("v_p", [L, NP, T, 256])
    O["ki_p"] = dout("ki_p", [L, NP, T, 64])
    O["conv_p"] = dout("conv_p", [L, NP, 30, 512])
    O["pool_p"] = dout("pool_p", [L, NP, 15, 512])
    O["k_s"] = dout("k_s", [L, 1, TS, 256])
    O["v_s"] = dout("v_s", [L, 1, TS, 256])
    O["ki_s"] = dout("ki_s", [L, 1, TS, 64])
    O["conv_s"] = dout("conv_s", [L, 1, 30, 512])
    O["pool_s"] = dout("pool_s", [L, 1, 15, 512])

    S_qT = dscr("s_qT", [NTP, 128, 1024], BF16)
    S_qiT = dscr("s_qiT", [NTP, 128, 1024], BF16)
    S_gates = dscr("s_gates", [T, D], BF16)
    S_uT = dscr("s_uT", [NTP, 128, 512], BF16)
    S_pin = dscr("s_pin", [T, 512], BF16)
    S_dg = dscr("s_dg", [L, 128, 31 * 4 * 128], BF16)
    S_pw = dscr("s_pw", [L, 128, 4 * 512], BF16)
    S_plw = dscr("s_plw", [L, 128, 4 * 128], BF16)
    p2_cached = set()
    S_x1 = dscr("s_x1", [T, D])
    S_xres = dscr("s_xres", [T, D])

    st = ExitStack()
    P = Prog(nc)
    arena_t = st.enter_context(nc.sbuf_tensor("arena", [128, arena_words], F32))
    A = Arena(arena_t[:, :], arena_words)
    banks = [st.enter_context(nc.psum_tensor("pb%d" % i, [128, 512], F32)) for i in range(8)]
    PB = [b[:, :] for b in banks]
    PBb = [Buf("pb%d" % i) for i in range(8)]
    TRb = PB[7].bitcast(BF16)
    TRb6 = PB[6].bitcast(BF16)
    TR = [TRb[:, 0:512], TRb6[:, 0:512]]
    TRB = [PBb[7], PBb[6]]

    def tr_mode(two):
        if two:
            TR[0], TR[1] = TRb[:, 0:512], TRb6[:, 0:512]
            TRB[0], TRB[1] = PBb[7], PBb[6]
        else:
            TR[0], TR[1] = TRb[:, 0:512], TRb[:, 512:1024]
            TRB[0], TRB[1] = PBb[7], PBb[7]
    trc = [0]
    phase_ctr = [0]

    def phase_end():
        P.barrier()
        phase_ctr[0] += 1
        if dbg_stop is not None and phase_ctr[0] >= dbg_stop:
            raise _Stop()

    ident = A.alloc([128], BF16); B_const = Buf("const")
    identf = A.alloc([128])
    negmask = A.alloc([128])
    epsb = A.alloc([8])
    actT = A.alloc([16, T], BF16); B_actT = Buf("actT")
    wbuf = [None, None]
    B_wbuf = [Buf("w0"), Buf("w1")]

    def alloc_wbuf(need_stage):
        wbuf[0] = A.alloc([16, 512], BF16)
        wbuf[1] = A.alloc([16, 512], BF16)
        if need_stage:
            stg[0] = A.alloc([8, 512])
    NKMAX = max(T, PAST + TS)
    NBLK = (NKMAX + 127) // 128
    kT = A.alloc([2, NKMAX], BF16); B_kT = Buf("kT")
    V1 = A.alloc([NBLK, 2, 129], BF16); B_V1 = Buf("V1")
    kiT2 = A.alloc([NKMAX], BF16); B_kiT = Buf("kiT")
    wia = A.alloc([NTP, 32]); B_wi = Buf("wi")
    gbc_h = [None]; B_gbc = Buf("gbc")
    wcnt = [0]

    P.dma("sp", ident, I["c_ident_bf"], [], [B_const])
    P.dma("sp", identf, I["c_ident_f"], [], [B_const])
    P.dma("sp", negmask, I["c_negmask"], [], [B_const])
    P.emit("dve", lambda e: e.memset(epsb, EPS), [], [B_const])
    P.emit("dve", lambda e: e.memset(V1[:, :, :, 128:129], 1.0), [], [B_V1])

    def transpose_to(dst_fn, src, R, ncol, reads, writes, eng="act"):
        raise NotImplementedError

    def tr_group(srcs, R, dst, reads, writes, eng="act"):
        i = trc[0] % 2
        trc[0] += 1
        n = len(srcs)
        c = srcs[0].shape[1]
        tv = TR[i].rearrange("p (a b) -> p a b", b=128)
        for j, s in enumerate(srcs):
            P.emit("pe", lambda e, s=s, j=j: e.transpose(tv[:c, j, :R], s, ident[:R, :R]),
                   reads + [B_const], [TRB[i]])
        if eng == "act":
            P.emit("act", lambda e: e.copy(dst, tv[:c, 0:n, :R]), [TRB[i]], writes)
        else:
            P.emit("dve", lambda e: e.tensor_copy(dst, tv[:c, 0:n, :R]), [TRB[i]], writes)

    WB = {}
    converted = set()
    stg = [None]
    B_stg = Buf("stg")

    NBLK_W = {"w_in": len(COLBLOCKS), "w_out": 4, "w_ple_gate": 4}

    def load_wblock(name, layer, ranges, kch, bid):
        w_ap = I[name][layer]
        key = (name, layer)
        if key not in WB:
            WB[key] = dscr("wb_%s_%d" % (name, layer), [NBLK_W[name], 128, 16 * 512], BF16)
        wsc = WB[key]
        i = wcnt[0] % 2
        wcnt[0] += 1
        o = sum(n for (_, n) in ranges)
        ck = (name, layer, bid)
        if ck in converted:
            P.dma("sp", wbuf[i].rearrange("p a b -> p (a b)"), wsc[bid], [], [B_wbuf[i]])
        else:
            converted.add(ck)
            sg_ = stg[0]
            o2 = 0
            for (c0, n) in ranges:
                for k8 in range(0, kch, 8):
                    for k4 in range(k8, min(kch, k8 + 8), 4):
                        kk = min(4, kch - k4)
                        P.dma("pool", sg_[:, k4 - k8:k4 - k8 + kk, 0:n],
                              w_ap[k4 * 128:(k4 + kk) * 128, c0:c0 + n].rearrange("(kc p) n -> p kc n", p=128), [], [B_stg])
                    for k4 in range(k8, min(kch, k8 + 8), 4):
                        kk = min(4, kch - k4)
                        P.emit("pool", lambda e, k4=k4, kk=kk, k8=k8, o2=o2, n=n, i=i, sg_=sg_: e.tensor_copy(
                            wbuf[i][:, k4:k4 + kk, o2:o2 + n], sg_[:, k4 - k8:k4 - k8 + kk, 0:n]), [B_stg], [B_wbuf[i]])
                o2 += n
            P.dma("pool", wsc[bid], wbuf[i].rearrange("p a b -> p (a b)"), [B_wbuf[i]], [])
        return wbuf[i], B_wbuf[i], o

    def make_sc():
        sc = {}
        sc["xt"] = [A.alloc([D]), A.alloc([D])]; sc["B_xt"] = [Buf(), Buf()]
        sc["junk"] = A.alloc([D], BF16); sc["B_junk"] = Buf()
        sc["xn"] = [A.alloc([D], BF16), A.alloc([D], BF16)]; sc["B_xn"] = [Buf(), Buf()]
        sc["ss"] = A.alloc([8]); sc["B_ss"] = Buf()
        return sc

    def rmsnorm_T(j, R, tok0, sc):
        xt, B_xt = sc["xt"][j], sc["B_xt"][j]
        xn, B_xn = sc["xn"][j], sc["B_xn"][j]
        gbc = gbc_h[0]
        P.emit("act", lambda e: e.activation(sc["junk"][:R, :], xt[:R, :], AF.Square, accum_out=sc["ss"][:R, 0:1]),
               [B_xt], [sc["B_junk"], sc["B_ss"]])
        P.emit("act", lambda e: e.activation(sc["ss"][:R, 1:2], sc["ss"][:R, 0:1], AF.Sqrt, bias=epsb[:R, 0:1], scale=1.0 / D),
               [sc["B_ss"], B_const], [sc["B_ss"]])
        P.emit("dve", lambda e: e.reciprocal(sc["ss"][:R, 2:3], sc["ss"][:R, 1:2]), [sc["B_ss"]], [sc["B_ss"]])
        P.emit("dve", lambda e: e.scalar_tensor_tensor(out=xn[:R, :], in0=xt[:R, :], scalar=sc["ss"][:R, 2:3],
                                                       in1=gbc[:R, :], op0=ALU.mult, op1=ALU.mult),
               [B_xt, sc["B_ss"], B_gbc], [B_xn])

        def part_b():
            for c4 in range(4):
                srcs = [xn[:R, (c4 * 4 + jj) * 128:(c4 * 4 + jj + 1) * 128] for jj in range(4)]
                tr_group(srcs, R, actT[:, c4 * 4:c4 * 4 + 4, tok0:tok0 + R], [B_xn], [B_actT], eng=("act" if c4 % 2 == 0 else "dve"))
        return part_b

    def rope(e_eng, src, dst1, dst2, cos, sin, tmp, R, nh, half, reads, writes_list, B_tmp):
        s4 = src.rearrange("p (h t d) -> p h t d", h=nh, t=2)
        x1 = s4[:, :, 0, :]
        x2 = s4[:, :, 1, :]
        cb = cos.unsqueeze(1).to_broadcast([R, nh, half])
        sb_ = sin.unsqueeze(1).to_broadcast([R, nh, half])
        t4 = tmp[:R, 0:nh * half * 4].rearrange("p (k h d) -> p k h d", k=4, h=nh)
        P.emit("dve", lambda e: e.tensor_tensor(t4[:, 0], x1, cb, ALU.mult), reads, [B_tmp])
        P.emit("dve", lambda e: e.tensor_tensor(t4[:, 1], x2, sb_, ALU.mult), reads, [B_tmp])
        P.emit("dve", lambda e: e.tensor_tensor(t4[:, 2], x1, sb_, ALU.mult), reads, [B_tmp])
        P.emit("dve", lambda e: e.tensor_tensor(t4[:, 3], x2, cb, ALU.mult), reads, [B_tmp])
        P.emit("dve", lambda e: e.tensor_tensor(dst1, t4[:, 0], t4[:, 1], ALU.subtract), [B_tmp], writes_list)
        P.emit("dve", lambda e: e.tensor_tensor(dst2, t4[:, 2], t4[:, 3], ALU.add), [B_tmp], writes_list)

    def run_seq_layer(seq, layer, is_sample, x_src, last_layer):
        Tn = TS if is_sample else T
        NT = 1 if is_sample else NTP
        R_of = (lambda t: TS) if is_sample else (lambda t: 128)
        b = 0 if is_sample else seq
        rope_src = I["c_rope_s"] if is_sample else I["c_rope_p"]
        kbase = PAST if is_sample else 0
        topk = TOPK_S if is_sample else TOPK_P
        kname = "s" if is_sample else "p"
        p_src = I["ps"][layer, 0] if is_sample else I["pp"][layer, seq]

        if is_sample:
            m0 = A.mark()
            tr_mode(True)
            NB0 = PAST // 128
            ckf = [A.alloc([256]), A.alloc([256])]; B_ckf = [Buf(), Buf()]
            cvf = [A.alloc([256]), A.alloc([256])]; B_cvf = [Buf(), Buf()]
            ckif = [A.alloc([64]), A.alloc([64])]; B_ckif = [Buf(), Buf()]
            ckb = [A.alloc([256], BF16), A.alloc([256], BF16)]; B_ckb = [Buf(), Buf()]
            ckib = [A.alloc([128], BF16), A.alloc([128], BF16)]; B_ckib = [Buf(), Buf()]

            def p0_load(blk):
                j = blk % 2
                r0 = blk * 128
                P.dma("sp", ckf[j], I["ck"][layer, 0, r0:r0 + 128, :], [], [B_ckf[j]])
                P.dma("sp", cvf[j], I["cv"][layer, 0, r0:r0 + 128, :], [], [B_cvf[j]])
                P.dma("sp", ckif[j], I["cki"][layer, 0, r0:r0 + 128, :], [], [B_ckif[j]])

            p0_load(0)
            for blk in range(NB0):
                j = blk % 2
                r0 = blk * 128
                if blk + 1 < NB0:
                    p0_load(blk + 1)
                P.emit("act", lambda e, j=j: e.copy(ckb[j], ckf[j]), [B_ckf[j]], [B_ckb[j]])
                P.emit("dve", lambda e, j=j, blk=blk: e.tensor_copy(
                    V1[:, blk, :, 0:128], cvf[j].rearrange("p (g d) -> p g d", g=2)), [B_cvf[j]], [B_V1])
                P.emit("act", lambda e, j=j: e.copy(ckib[j][:, 0:64], ckif[j]), [B_ckif[j]], [B_ckib[j]])
                P.emit("dve", lambda e, j=j: e.tensor_copy(ckib[j][:, 64:128], ckif[j]), [B_ckif[j]], [B_ckib[j]])
                tr_group([ckb[j][:, 0:128], ckb[j][:, 128:256]], 128, kT[:, :, r0:r0 + 128], [B_ckb[j]], [B_kT])
                tr_group([ckib[j][:, 0:128]], 128, kiT2[:, r0:r0 + 128].unsqueeze(1), [B_ckib[j]], [B_kiT], eng="dve")
            A.release(m0)
            phase_end()

        m1 = A.mark()
        tr_mode(True)
        first_use = ("w_in", layer, 0) not in converted
        alloc_wbuf(first_use)
        sc = make_sc()
        ropet = A.alloc([NT, 192]); B_rope = Buf()
        zs = [A.alloc([512]), A.alloc([512])]; B_zs = [Buf(), Buf()]
        tmp = A.alloc([1024]); B_tmp = Buf()
        ob = [A.alloc([512], BF16), A.alloc([512], BF16)]; B_ob = [Buf(), Buf()]
        of = [A.alloc([512]), A.alloc([512])]; B_of = [Buf(), Buf()]
        tT = [A.alloc([4, 128], BF16), A.alloc([4, 128], BF16)]; B_tT = [Buf(), Buf()]

        gbc_h[0] = A.alloc([D])
        P.dma("sp", gbc_h[0], I["norm_mix"][layer, :].partition_broadcast(128), [], [B_gbc])
        if is_sample:
            P.dma("sp", ropet[:TS, 0, :], rope_src, [], [B_rope])
        else:
            P.dma("sp", ropet, rope_src.rearrange("(n p) c -> p n c", p=128), [], [B_rope])
        nxt_w = load_wblock("w_in", layer, COLBLOCKS[0][1], 16, 0)
        pend = []
        for t in range(NT):
            R = R_of(t)
            P.dma("sp", sc["xt"][t % 2][:R, :], x_src[t * 128:t * 128 + R, :], [], [sc["B_xt"][t % 2]])
            pb_fn = rmsnorm_T(t % 2, R, t * 128, sc)
            while pend:
                pend.pop(0)()
            pend.append(pb_fn)
        while pend:
            pend.pop(0)()

        ec = [0]
        for bi_, (kind, ranges, arg) in enumerate(COLBLOCKS):
            wb, B_wb, ncols = nxt_w
            if bi_ + 1 < len(COLBLOCKS):
                nxt_w = load_wblock("w_in", layer, COLBLOCKS[bi_ + 1][1], 16, bi_ + 1)
            for t in range(NT):
                R = R_of(t)
                tok0 = t * 128
                i = ec[0] % 2
                ec[0] += 1
                ps, B_ps = PB[(ec[0] - 1) % 4], PBb[(ec[0] - 1) % 4]
                for kc in range(16):
                    P.emit("pe", lambda e, kc=kc, ps=ps, R=R, tok0=tok0, wb=wb, ncols=ncols:
                           e.matmul(ps[:R, :ncols], actT[:, kc, tok0:tok0 + R], wb[:, kc, :ncols],
                                    start=(kc == 0), stop=(kc == 15)),
                           [B_actT, B_wb], [B_ps])
                while pend:
                    pend.pop(0)()
                z, B_z = zs[i], B_zs[i]
                o_b, B_o = ob[i], B_ob[i]
                o_f, B_f = of[i], B_of[i]
                tt_, B_tt = tT[i], B_tT[i]
                cos128 = ropet[:R, t, 0:64]
                sin128 = ropet[:R, t, 64:128]
                cos64 = ropet[:R, t, 128:160]
                sin64 = ropet[:R, t, 160:192]
                if kind == "q":
                    P.emit("act", lambda e, z=z, ps=ps, R=R: e.copy(z[:R, :], ps[:R, :]), [B_ps], [B_z])
                    o4 = o_b[:R, :].rearrange("p (h t d) -> p h t d", h=4, t=2)
                    rope("dve", z[:R, :], o4[:, :, 0, :], o4[:, :, 1, :], cos128, sin128, tmp, R, 4, 64,
                         [B_z, B_rope], [B_o], B_tmp)
                    def pb_(o_b=o_b, R=R, tt_=tt_, B_o=B_o, B_tt=B_tt, t=t, arg=arg):
                        tr_group([o_b[:R, j * 128:(j + 1) * 128] for j in range(4)], R, tt_[:, :, :R], [B_o], [B_tt])
                        P.dma("sp", S_qT[t, :, arg * 512:(arg + 1) * 512].rearrange("p (a b) -> p a b", b=128)[:, :, :R],
                              tt_[:, :, :R], [B_tt], [])
                    pend.append(pb_)
                elif kind == "qi":
                    P.emit("act", lambda e, z=z, ps=ps, R=R: e.copy(z[:R, :], ps[:R, :]), [B_ps], [B_z])
                    o4 = o_b[:R, :].rearrange("p (h t d) -> p h t d", h=8, t=2)
                    rope("dve", z[:R, :], o4[:, :, 0, :], o4[:, :, 1, :], cos64, sin64, tmp, R, 8, 32,
                         [B_z, B_rope], [B_o], B_tmp)
                    def pb_(o_b=o_b, R=R, tt_=tt_, B_o=B_o, B_tt=B_tt, t=t, arg=arg):
                        tr_group([o_b[:R, j * 128:(j + 1) * 128] for j in range(4)], R, tt_[:, :, :R], [B_o], [B_tt])
                        P.dma("sp", S_qiT[t, :, arg * 512:(arg + 1) * 512].rearrange("p (a b) -> p a b", b=128)[:, :, :R],
                              tt_[:, :, :R], [B_tt], [])
                    pend.append(pb_)
                elif kind == "kv":
                    P.emit("act", lambda e, z=z, ps=ps, R=R: e.copy(z[:R, :], ps[:R, :]), [B_ps], [B_z])
                    f4 = o_f[:R, 0:256].rearrange("p (h t d) -> p h t d", h=2, t=2)
                    rope("dve", z[:R, 0:256], f4[:, :, 0, :], f4[:, :, 1, :], cos128, sin128, tmp, R, 2, 64,
                         [B_z, B_rope], [B_f], B_tmp)
                    P.dma("sp", O["k_" + kname][layer, b, tok0:tok0 + R, :], o_f[:R, 0:256], [B_f], [], is_out=True)
                    P.dma("sp", O["v_" + kname][layer, b, tok0:tok0 + R, :], z[:R, 256:512], [B_z], [], is_out=True)
                    P.emit("act", lambda e, o_b=o_b, o_f=o_f, R=R: e.copy(o_b[:R, 0:256], o_f[:R, 0:256]), [B_f], [B_o])
                    k0 = kbase + tok0
                    pend.append(lambda o_b=o_b, R=R, k0=k0, B_o=B_o: tr_group(
                        [o_b[:R, 0:128], o_b[:R, 128:256]], R, kT[:, :, k0:k0 + R], [B_o], [B_kT]))
                    blk = k0 // 128
                    P.emit("dve", lambda e, z=z, R=R, blk=blk: e.tensor_copy(
                        V1[:R, blk, :, 0:128], z[:R, 256:512].rearrange("p (g d) -> p g d", g=2)), [B_z], [B_V1])
                elif kind == "kiwi":
                    P.emit("act", lambda e, z=z, ps=ps, R=R: e.copy(z[:R, 0:80], ps[:R, 0:80]), [B_ps], [B_z])
                    f4 = o_f[:R, 0:64].rearrange("p (h t d) -> p h t d", h=1, t=2)
                    rope("dve", z[:R, 0:64], f4[:, :, 0, :], f4[:, :, 1, :], cos64, sin64, tmp, R, 1, 32,
                         [B_z, B_rope], [B_f], B_tmp)
                    P.dma("sp", O["ki_" + kname][layer, b, tok0:tok0 + R, :], o_f[:R, 0:64], [B_f], [], is_out=True)
                    P.emit("act", lambda e, o_b=o_b, o_f=o_f, R=R: e.copy(o_b[:R, 0:64], o_f[:R, 0:64]), [B_f], [B_o])
                    P.emit("act", lambda e, o_b=o_b, o_f=o_f, R=R: e.copy(o_b[:R, 64:128], o_f[:R, 0:64]), [B_f], [B_o])
                    k0 = kbase + tok0
                    pend.append(lambda o_b=o_b, R=R, k0=k0, B_o=B_o: tr_group(
                        [o_b[:R, 0:128]], R, kiT2[:, k0:k0 + R].unsqueeze(1), [B_o], [B_kiT]))
                    P.emit("act", lambda e, z=z, R=R, t=t: e.activation(wia[:R, t, 0:16], z[:R, 64:80], AF.Abs, scale=1.0 / 32.0),
                           [B_z], [B_wi])
                    P.emit("act", lambda e, z=z, R=R, t=t: e.activation(wia[:R, t, 16:32], z[:R, 64:80], AF.Sign),
                           [B_z], [B_wi])
                elif kind == "gate":
                    P.emit("act", lambda e, o_b=o_b, ps=ps, R=R: e.activation(o_b[:R, :], ps[:R, :], AF.Silu), [B_ps], [B_o])
                    P.dma("sp", S_gates[tok0:tok0 + R, arg:arg + 512], o_b[:R, :], [B_o], [])
                elif kind == "glu":
                    P.emit("act", lambda e, z=z, ps=ps, R=R: e.activation(z[:R, 0:256], ps[:R, 256:512], AF.Sigmoid), [B_ps], [B_z])
                    P.emit("dve", lambda e, z=z, ps=ps, R=R, o_f=o_f: e.tensor_tensor(o_f[:R, 0:256], ps[:R, 0:256], z[:R, 0:256], ALU.mult),
                           [B_ps, B_z], [B_f])
                    c0 = arg * 256
                    if is_sample:
                        P.dma("sp", O["conv_s"][layer, 0, 14:30, c0:c0 + 256], o_f[:TS, 0:256], [B_f], [], is_out=True)
                    elif t == NT - 1:
                        P.dma("sp", O["conv_p"][layer, b, 0:30, c0:c0 + 256], o_f[98:128, 0:256], [B_f], [], is_out=True)
                    P.emit("act", lambda e, o_b=o_b, o_f=o_f, R=R: e.copy(o_b[:R, 0:256], o_f[:R, 0:256]), [B_f], [B_o])
                    def pb_(o_b=o_b, R=R, tt_=tt_, B_o=B_o, B_tt=B_tt, t=t, c0=c0):
                        tr_group([o_b[:R, 0:128], o_b[:R, 128:256]], R, tt_[:, 0:2, :R], [B_o], [B_tt])
                        P.dma("sp", S_uT[t, :, c0:c0 + 256].rearrange("p (a b) -> p a b", b=128)[:, :, :R],
                              tt_[:, 0:2, :R], [B_tt], [])
                    pend.append(pb_)
                elif kind == "pin":
                    P.emit("act", lambda e, z=z, ps=ps, R=R: e.copy(z[:R, :], ps[:R, :]), [B_ps], [B_z])
                    if is_sample:
                        P.dma("sp", O["pool_s"][layer, 0, 0:15, :], z[1:16, :], [B_z], [], is_out=True)
                    elif t == NT - 1:
                        P.dma("sp", O["pool_p"][layer, b, 0:15, :], z[113:128, :], [B_z], [], is_out=True)
                    P.emit("dve", lambda e, z=z, o_b=o_b, R=R: e.tensor_copy(o_b[:R, :], z[:R, :]), [B_z], [B_o])
                    P.dma("sp", S_pin[tok0:tok0 + R, :], o_b[:R, :], [B_o], [])
        while pend:
            pend.pop(0)()
        if is_sample:
            P.dma("sp", O["conv_s"][layer, 0, 0:14, :], I["sconv"][layer, 0, 16:30, :], [], [], is_out=True)
        A.release(m1)
        phase_end()

        m2 = A.mark()
        tr_mode(False)
        dg = A.alloc([31, 4, 128], BF16); B_dg = Buf()
        pw = A.alloc([4, 512], BF16); B_pw = Buf()
        plw = A.alloc([4, 128], BF16); B_plw = Buf()
        bands = A.alloc([12, 128], BF16)
        bands_s = A.alloc([8, 16], BF16)
        convb = A.alloc([512]); pscale = A.alloc([512]); lng = A.alloc([8]); B_c2 = Buf()
        cwb = A.alloc([512]); B_cwb = Buf()
        qTt = A.alloc([8, 128], BF16); B_qT = Buf()
        qiTt = A.alloc([8, 128], BF16); B_qiT = Buf()
        gat = A.alloc([D], BF16); B_gat = Buf()
        uTb = [A.alloc([4, 160], BF16), A.alloc([4, 160], BF16)]; B_uT = [Buf(), Buf()]
        pinb = [A.alloc([512], BF16), A.alloc([512], BF16)]; B_pin = [Buf(), Buf()]
        NK = NKMAX
        saccs = [A.alloc([NK]), A.alloc([NK])]; B_saccs = [Buf(), Buf()]
        rb = [A.alloc([512], BF16) for _ in range(4)]; B_rb = [Buf() for _ in range(4)]
        Dg = A.alloc([16, 128], BF16); B_Dg = Buf()
        mask = A.alloc([NK], BF16); B_mask = Buf()
        junk, B_junk = mask, B_mask
        maskTs = [A.alloc([NBLK, 128], BF16), A.alloc([NBLK, 128], BF16)]; B_maskTs = [Buf(), Buf()]
        bis = A.alloc([16]); B_bis = Buf()
        Eb = [A.alloc([512], BF16) for _ in range(4)]; B_E = [Buf() for _ in range(4)]
        PTb = [A.alloc([512], BF16) for _ in range(4)]; B_PT = [Buf() for _ in range(4)]
        catb = A.alloc([D], BF16); B_catb = Buf()
        rinv = A.alloc([8]); B_rinv = Buf()
        yb = A.alloc([512]); B_yb = Buf()
        yhb = A.alloc([512], BF16); B_yhb = Buf()
        lnT = A.alloc([4, 128], BF16); B_lnT = Buf()
        rTb = A.alloc([4, 128], BF16); B_rTb = Buf()
        bnb = A.alloc([16]); B_bn = Buf()
        spb = A.alloc([512], BF16); B_spb = Buf()

        P.dma("sp", bands, I["c_bands"], [], [B_c2])
        P.dma("sp", bands_s[:16], I["c_bands_s"], [], [B_c2])
        P.dma("sp", convb, I["conv_b"][layer, :].partition_broadcast(128), [], [B_c2])
        P.dma("sp", pscale, I["pool_scale"][layer, :].partition_broadcast(128), [], [B_c2])
        for g in range(4):
            P.dma("sp", lng[:, g:g + 1], I["conv_ln_g"][layer, g * 128:(g + 1) * 128].unsqueeze(1), [], [B_c2])
            P.dma("sp", lng[:, 4 + g:5 + g], I["conv_ln_b"][layer, g * 128:(g + 1) * 128].unsqueeze(1), [], [B_c2])
        if layer not in p2_cached:
            p2_cached.add(layer)
            for g in range(4):
                P.dma("pool", pw[:, g, :], I["conv_pw"][layer][g * 128:(g + 1) * 128, :], [], [B_pw])
                P.dma("pool", plw[:, g, :], I["pool_w"][layer, g], [], [B_plw])
            P.dma("sp", S_pw[layer], pw.rearrange("p a b -> p (a b)"), [B_pw], [])
            P.dma("sp", S_plw[layer], plw.rearrange("p a b -> p (a b)"), [B_plw], [])
            cwbs = [cwb, yb]
            B_cwbs = [B_cwb, B_yb]
            for k in range(31):
                P.dma("sp", cwbs[k % 2], I["conv_w"][layer, k, :].partition_broadcast(128), [], [B_cwbs[k % 2]])
                P.emit("dve", lambda e, k=k, cw_=cwbs[k % 2]: e.tensor_tensor(dg[:, k, :, :], cw_.rearrange("p (g d) -> p g d", g=4),
                                                                          identf.unsqueeze(1).to_broadcast([128, 4, 128]), ALU.mult),
                       [B_cwbs[k % 2], B_const], [B_dg])
            P.dma("sp", S_dg[layer], dg.rearrange("p a b c -> p (a b c)"), [B_dg], [])
        else:
            P.dma("sp", pw.rearrange("p a b -> p (a b)"), S_pw[layer], [], [B_pw])
            P.dma("sp", plw.rearrange("p a b -> p (a b)"), S_plw[layer], [], [B_plw])
            P.dma("sp", dg.rearrange("p a b c -> p (a b c)"), S_dg[layer], [], [B_dg])
        if is_sample:
            P.dma("pool", spb[:30, :], I["sconv"][layer, 0, :, :], [], [B_spb])
            tr_group([spb[:30, j * 128:(j + 1) * 128] for j in range(4)], 30, uTb[0][:, :, 0:30], [B_spb], [B_uT[0]])
            P.dma("pool", pinb[1][:15, :], I["spool"][layer, 0, :, :], [], [B_pin[1]])
        else:
            P.emit("dve", lambda e: e.memset(uTb[0][:, :, 0:30], 0.0), [], [B_uT[0]])

        def gen_IDX(t):
            R = R_of(t)
            tok0 = t * 128
            cur, prv = t % 2, (t + 1) % 2
            maskT, B_maskT = maskTs[t % 2], B_maskTs[t % 2]
            nk = kbase + tok0 + R
            kblocks = []
            k0 = 0
            while k0 < nk:
                n = min(128, nk - k0)
                kblocks.append((k0 // 128, k0, n))
                k0 += n
            P.dma("sp", qiTt[:, :, :R], S_qiT[t].rearrange("p (a b) -> p a b", b=128)[:, :, :R], [], [B_qiT])
            for h in range(16):
                P.emit("dve", lambda e, h=h, R=R, t=t: e.tensor_scalar(
                    Dg[:R, h, :R], identf[:R, :R], wia[:R, t, 16 + h:17 + h], None, op0=ALU.mult), [B_const, B_wi], [B_Dg])
            yield
            li = [0]
            for c0 in range(0, nk, 512):
                n = min(512, nk - c0)
                prev_ = None

                def diag_mm(i_, n_, h_, R=R):
                    P.emit("pe", lambda e: e.matmul(PB[2][:R, :n_], Dg[:R, h_, :R], rb[i_][:R, :n_],
                                                    start=(h_ == 0), stop=(h_ == 15)), [B_Dg, B_rb[i_]], [PBb[2]])
                for h in range(16):
                    i = li[0] % 2
                    i4 = li[0] % 4
                    li[0] += 1
                    hp, j = h % 2, h // 2
                    P.emit("pe", lambda e, i=i, hp=hp, j=j, c0=c0, n=n, R=R: e.matmul(
                        PB[i][:R, :n], qiTt[hp * 64:(hp + 1) * 64, j, :R], kiT2[hp * 64:(hp + 1) * 64, c0:c0 + n],
                        start=True, stop=True), [B_qiT, B_kiT], [PBb[i]])
                    P.emit("act", lambda e, i=i, i4=i4, n=n, R=R, h=h, t=t: e.activation(
                        rb[i4][:R, :n], PB[i][:R, :n], AF.Relu, scale=wia[:R, t, h:h + 1]), [PBb[i], B_wi], [B_rb[i4]])
                    if prev_ is not None:
                        diag_mm(*prev_)
                    prev_ = (i4, n, h)
                    yield
                diag_mm(*prev_)
                P.emit("act", lambda e, n=n, R=R, c0=c0, t=t: e.copy(saccs[t % 2][:R, c0:c0 + n], PB[2][:R, :n]),
                       [PBb[2]], [B_saccs[t % 2]])
            if not is_sample:
                P.emit("dve", lambda e, tok0=tok0: e.tensor_tensor(saccs[t % 2][:, tok0:tok0 + 128], saccs[t % 2][:, tok0:tok0 + 128],
                                                                 negmask, ALU.add), [B_saccs[t % 2], B_const], [B_saccs[t % 2]])

        def gen_BIS(t):
            R = R_of(t)
            tok0 = t * 128
            cur, prv = t % 2, (t + 1) % 2
            maskT, B_maskT = maskTs[t % 2], B_maskTs[t % 2]
            nk = kbase + tok0 + R
            kblocks = []
            k0 = 0
            while k0 < nk:
                n = min(128, nk - k0)
                kblocks.append((k0 // 128, k0, n))
                k0 += n
            svis_min = nk if is_sample else (tok0 + 64)
            if svis_min > topk:
                P.emit("dve", lambda e, R=R: e.memset(bis[:R, 0:1], 0.0), [], [B_bis])
                w = W0
                for it in range(NIT):
                    a, bnew = it % 2, (it + 1) % 2
                    w = w / 2.0
                    P.emit("dve", lambda e, R=R, nk=nk, a=a: e.tensor_scalar(
                        junk[:R, :nk], saccs[t % 2][:R, :nk], bis[:R, a:a + 1], 0.0, op0=ALU.is_ge, op1=ALU.add,
                        accum_out=bis[:R, 2:3]), [B_saccs[t % 2], B_bis], [B_junk, B_bis])
                    P.emit("dve", lambda e, R=R, w=w: e.tensor_scalar(
                        bis[:R, 3:4], bis[:R, 2:3], float(topk), 2.0 * w, op0=ALU.is_ge, op1=ALU.mult),
                        [B_bis], [B_bis])
                    P.emit("dve", lambda e, R=R, w=w, a=a, bnew=bnew: e.tensor_scalar(
                        bis[:R, bnew:bnew + 1], bis[:R, 3:4], -w, bis[:R, a:a + 1], op0=ALU.add, op1=ALU.add),
                        [B_bis], [B_bis])
                    yield
                fin = NIT % 2
                P.emit("dve", lambda e, R=R, w=w, fin=fin: e.tensor_scalar(
                    bis[:R, 4:5], bis[:R, fin:fin + 1], -w, None, op0=ALU.add), [B_bis], [B_bis])
            else:
                P.emit("dve", lambda e, R=R: e.memset(bis[:R, 4:5], -1e29), [], [B_bis])
            P.emit("dve", lambda e, R=R, nk=nk: e.tensor_scalar(
                mask[:R, :nk], saccs[t % 2][:R, :nk], bis[:R, 4:5], None, op0=ALU.is_ge), [B_saccs[t % 2], B_bis], [B_mask])
            for (blk, k0, n) in kblocks:
                tr_group([mask[:R, k0:k0 + n]], R, maskT[:n, blk:blk + 1, :R], [B_mask], [B_maskT], eng="dve")
                yield


        def gen_CD(t):
            R = R_of(t)
            tok0 = t * 128
            cur, prv = t % 2, (t + 1) % 2
            maskT, B_maskT = maskTs[t % 2], B_maskTs[t % 2]
            nk = kbase + tok0 + R
            kblocks = []
            k0 = 0
            while k0 < nk:
                n = min(128, nk - k0)
                kblocks.append((k0 // 128, k0, n))
                k0 += n
            P.dma("sp", qTt[:, :, :R], S_qT[t].rearrange("p (a b) -> p a b", b=128)[:, :, :R], [], [B_qT])
            P.dma("sp", gat[:R, :], S_gates[tok0:tok0 + R, :], [], [B_gat])
            P.dma("sp", uTb[cur][:, :, 30:30 + R], S_uT[t].rearrange("p (a b) -> p a b", b=128)[:, :, :R], [], [B_uT[cur]])
            P.dma("sp", pinb[cur][:R, :], S_pin[tok0:tok0 + R, :], [], [B_pin[cur]])
            yield
            ai = [0]
            nkb = len(kblocks)
            for g in range(2):
                pipe_m = []
                pipe_v = []

                def emit_mult(i, n, blk, R=R):
                    P.emit("dve", lambda e: e.tensor_tensor(
                        PTb[i][:n, 0:4 * R].rearrange("p (a b) -> p a b", b=R),
                        Eb[i][:n, 0:4 * R].rearrange("p (a b) -> p a b", b=R),
                        maskT[:n, blk:blk + 1, :R].to_broadcast([n, 4, R]), ALU.mult), [B_E[i], B_maskT], [B_PT[i]])

                def emit_pv(i, n, blk, bi, g=g, R=R):
                    for hh in range(4):
                        ob_, B_obk = (PB[5], PBb[5]) if hh < 3 else (PB[6], PBb[6])
                        oc = (hh % 3) * 129
                        P.emit("pe", lambda e, hh=hh, ob_=ob_, oc=oc,
                               st_=(bi == 0 and hh in (0, 3)), sp2=(bi == nkb - 1 and hh in (2, 3)): e.matmul(
                            ob_[:R, oc:oc + 129], PTb[i][:n, hh * R:(hh + 1) * R], V1[:n, blk, g, :],
                            start=st_, stop=sp2), [B_PT[i], B_V1], [B_obk])

                for bi, (blk, k0, n) in enumerate(kblocks):
                    i = ai[0] % 4
                    sp_, B_sp = PB[3 + ai[0] % 2], PBb[3 + ai[0] % 2]
                    ai[0] += 1
                    P.emit("pe", lambda e, sp_=sp_, g=g, k0=k0, n=n, R=R: e.matmul(
                        sp_[:n, 0:4 * R].rearrange("p (a b) -> p a b", b=R), kT[:, g, k0:k0 + n],
                        qTt[:, g * 4:(g + 1) * 4, :R], start=True, stop=True), [B_kT, B_qT], [B_sp])
                    P.emit("act", lambda e, sp_=sp_, i=i, n=n, R=R: e.activation(
                        Eb[i][:n, 0:4 * R], sp_[:n, 0:4 * R], AF.Exp, scale=128.0 ** -0.5), [B_sp], [B_E[i]])
                    if pipe_v:
                        emit_pv(*pipe_v.pop(0))
                    if pipe_m:
                        a_ = pipe_m.pop(0)
                        emit_mult(a_[0], a_[1], a_[2])
                        pipe_v.append(a_)
                    pipe_m.append((i, n, blk, bi))
                    yield
                while pipe_m or pipe_v:
                    if pipe_v:
                        emit_pv(*pipe_v.pop(0))
                    if pipe_m:
                        a_ = pipe_m.pop(0)
                        emit_mult(a_[0], a_[1], a_[2])
                        pipe_v.append(a_)
                    yield
                for hh in range(4):
                    h = g * 4 + hh
                    ob_, B_obk = (PB[5], PBb[5]) if hh < 3 else (PB[6], PBb[6])
                    oc = (hh % 3) * 129
                    P.emit("dve", lambda e, ob_=ob_, oc=oc, h=h, R=R: e.reciprocal(rinv[:R, h:h + 1], ob_[:R, oc + 128:oc + 129]),
                           [B_obk], [B_rinv])
                    P.emit("dve", lambda e, ob_=ob_, oc=oc, h=h, R=R: e.scalar_tensor_tensor(
                        out=catb[:R, h * 128:(h + 1) * 128], in0=ob_[:R, oc:oc + 128], scalar=rinv[:R, h:h + 1],
                        in1=gat[:R, h * 128:(h + 1) * 128], op0=ALU.mult, op1=ALU.mult),
                        [B_obk, B_rinv, B_gat], [B_catb])
                yield
            ub = uTb[cur]
            for g in range(4):
                for k in range(31):
                    P.emit("pe", lambda e, g=g, k=k, R=R, ub=ub: e.matmul(
                        PB[3][:R, g * 128:(g + 1) * 128], ub[:, g, k:k + R], dg[:, k, g, :],
                        start=(k == 0), stop=(k == 30)), [B_uT[cur], B_dg], [PBb[3]])
                yield
            if t + 1 < NT:
                P.emit("act", lambda e, ub=ub, nb=uTb[prv]: e.copy(nb[:, :, 0:30], ub[:, :, 128:158]), [B_uT[cur]], [B_uT[prv]])
            yield
            yield
            yield
            yield
            P.emit("dve", lambda e, R=R: e.tensor_tensor(yb[:R, :], PB[3][:R, :], convb[:R, :], ALU.add), [PBb[3], B_c2], [B_yb])
            P.emit("dve", lambda e, R=R: e.bn_stats(bnb[:R, 0:6], yb[:R, :]), [B_yb], [B_bn])
            P.emit("dve", lambda e, R=R: e.bn_aggr(bnb[:R, 6:8], bnb[:R, 0:6]), [B_bn], [B_bn])
            yield
            P.emit("act", lambda e, R=R: e.activation(bnb[:R, 8:9], bnb[:R, 7:8], AF.Sqrt, bias=epsb[:R, 0:1], scale=1.0),
                   [B_bn, B_const], [B_bn])
            yield
            P.emit("dve", lambda e, R=R: e.reciprocal(bnb[:R, 9:10], bnb[:R, 8:9]), [B_bn], [B_bn])
            P.emit("dve", lambda e, R=R: e.tensor_scalar(yhb[:R, :], yb[:R, :], bnb[:R, 6:7], bnb[:R, 9:10],
                                                        op0=ALU.subtract, op1=ALU.mult), [B_yb, B_bn], [B_yhb])
            yield
            i = trc[0] % 2
            trc[0] += 1
            tv = TR[i].rearrange("p (a b) -> p a b", b=128)
            for g in range(4):
                P.emit("pe", lambda e, g=g, R=R, tv=tv: e.transpose(tv[:, g, :R], yhb[:R, g * 128:(g + 1) * 128], ident[:R, :R]),
                       [B_yhb, B_const], [TRB[i]])
            yield
            for g in range(4):
                P.emit("act", lambda e, g=g, R=R, tv=tv: e.activation(lnT[:, g, :R], tv[:, g, :R], AF.Silu,
                                                                    bias=lng[:, 4 + g:5 + g], scale=lng[:, g:g + 1]),
                       [TRB[i], B_c2], [B_lnT])
            yield
            for g in range(4):
                P.emit("pe", lambda e, g=g, R=R: e.matmul(PB[4][:R, :], lnT[:, g, :R], pw[:, g, :], start=(g == 0), stop=(g == 3)),
                       [B_lnT, B_pw], [PBb[4]])
            yield
            yield
            P.emit("dve", lambda e, R=R: e.tensor_tensor(catb[:R, 1024:1536], PB[4][:R, :], gat[:R, 1024:1536], ALU.mult),
                   [PBb[4], B_gat], [B_catb])
            yield
            pc, pp_ = pinb[cur], pinb[prv]
            for g in range(4):
                outp = PB[3][:, g * 128:g * 128 + R]
                if is_sample:
                    P.emit("pe", lambda e, g=g, outp=outp, pc=pc: e.matmul(outp, pc[:TS, g * 128:(g + 1) * 128], bands_s[:TS, g, :],
                                                                          start=True, stop=False), [B_pin[cur], B_c2], [PBb[3]])
                    P.emit("pe", lambda e, g=g, outp=outp, pp_=pp_: e.matmul(outp, pp_[:15, g * 128:(g + 1) * 128], bands_s[:15, 4 + g, :],
                                                                            start=False, stop=True), [B_pin[prv], B_c2], [PBb[3]])
                elif t == 0:
                    P.emit("pe", lambda e, g=g, outp=outp, pc=pc: e.matmul(outp, pc[:, g * 128:(g + 1) * 128], bands[:, 8 + g, :],
                                                                          start=True, stop=True), [B_pin[cur], B_c2], [PBb[3]])
                else:
                    P.emit("pe", lambda e, g=g, outp=outp, pc=pc: e.matmul(outp, pc[:, g * 128:(g + 1) * 128], bands[:, g, :],
                                                                          start=True, stop=False), [B_pin[cur], B_c2], [PBb[3]])
                    P.emit("pe", lambda e, g=g, outp=outp, pp_=pp_: e.matmul(outp, pp_[64:128, g * 128:(g + 1) * 128], bands[64:128, 4 + g, :],
                                                                            start=False, stop=True), [B_pin[prv], B_c2], [PBb[3]])
            yield
            P.emit("act", lambda e, R=R: e.copy(rTb[:, :, :R], PB[3][:, :].rearrange("p (a b) -> p a b", b=128)[:, :, :R]),
                   [PBb[3]], [B_rTb])
            yield
            for g in range(4):
                P.emit("pe", lambda e, g=g, R=R: e.matmul(PB[4][:R, g * 128:(g + 1) * 128], rTb[:, g, :R], plw[:, g, :],
                                                         start=True, stop=True), [B_rTb, B_plw], [PBb[4]])
            yield
            yield
            P.emit("dve", lambda e, R=R: e.tensor_tensor(yb[:R, :], PB[4][:R, :], pscale[:R, :], ALU.mult),
                   [PBb[4], B_c2], [B_yb])
            P.emit("dve", lambda e, R=R: e.tensor_tensor(catb[:R, 1536:2048], yb[:R, :], gat[:R, 1536:2048], ALU.mult),
                   [B_yb, B_gat], [B_catb])
            yield
            for c4 in range(4):
                srcs = [catb[:R, (c4 * 4 + j) * 128:(c4 * 4 + j + 1) * 128] for j in range(4)]
                tr_group(srcs, R, actT[:, c4 * 4:c4 * 4 + 4, tok0:tok0 + R], [B_catb], [B_actT])

        def interleave(gens):
            gens = list(gens)
            while gens:
                for g_ in list(gens):
                    try:
                        next(g_)
                    except StopIteration:
                        gens.remove(g_)

        def chain(*gs):
            for g_ in gs:
                for _ in g_:
                    yield

        def interleave_n(items):
            st_ = [[g_, max(1, l_), 0] for (g_, l_) in items]
            while st_:
                st_.sort(key=lambda x: x[2] / x[1])
                x = st_[0]
                try:
                    next(x[0])
                    x[2] += 1
                except StopIteration:
                    st_.remove(x)

        def nblk_of(t):
            return (kbase + t * 128 + R_of(t) + 127) // 128

        def len_idx(t):
            return 16 * ((kbase + t * 128 + R_of(t) + 511) // 512) + 1

        def len_bis(t):
            return NIT + nblk_of(t) + 2

        def len_cd(t):
            return 2 * nblk_of(t) + 28

        for _ in gen_IDX(0):
            pass
        items = [(gen_BIS(0), len_bis(0))]
        if NT > 1:
            items.append((gen_IDX(1), len_idx(1)))
        interleave_n(items)
        for t in range(NT):
            items = [(gen_CD(t), len_cd(t))]
            if t + 1 < NT:
                items.append((gen_BIS(t + 1), len_bis(t + 1)))
            if t + 2 < NT:
                items.append((gen_IDX(t + 2), len_idx(t + 2)))
            interleave_n(items)
        A.release(m2)
        phase_end()

        m3 = A.mark()
        tr_mode(True)
        alloc_wbuf(("w_out", layer, 0) not in converted)
        xr = [A.alloc([512]), A.alloc([512])]; B_xr = [Buf(), Buf()]
        x1o = [A.alloc([512]), A.alloc([512])]; B_x1o = [Buf(), Buf()]
        ec = [0]
        nxt_w = load_wblock("w_out", layer, [(0, 512)], 16, 0)
        for nb in range(4):
            wb, B_wb, ncols = nxt_w
            if nb + 1 < 4:
                nxt_w = load_wblock("w_out", layer, [((nb + 1) * 512, 512)], 16, nb + 1)
            for t in range(NT):
                R = R_of(t)
                tok0 = t * 128
                i = ec[0] % 2
                ib = ec[0] % 4
                ec[0] += 1
                P.dma("sp", xr[i][:R, :], x_src[tok0:tok0 + R, nb * 512:(nb + 1) * 512], [], [B_xr[i]])
                for kc in range(16):
                    P.emit("pe", lambda e, kc=kc, ib=ib, R=R, tok0=tok0, wb=wb: e.matmul(
                        PB[ib][:R, :], actT[:, kc, tok0:tok0 + R], wb[:, kc, :], start=(kc == 0), stop=(kc == 15)),
                        [B_actT, B_wb], [PBb[ib]])
                P.emit("dve", lambda e, i=i, ib=ib, R=R: e.tensor_tensor(x1o[i][:R, :], PB[ib][:R, :], xr[i][:R, :], ALU.add),
                       [PBb[ib], B_xr[i]], [B_x1o[i]])
                P.dma("sp", S_x1[tok0:tok0 + R, nb * 512:(nb + 1) * 512], x1o[i][:R, :], [B_x1o[i]], [])
        A.release(m3)
        phase_end()

        m4 = A.mark()
        tr_mode(True)
        alloc_wbuf(("w_ple_gate", layer, 0) not in converted)
        sc = make_sc()
        pT = A.alloc([2, Tn if not is_sample else 128], BF16); B_pT = Buf()
        pf = A.alloc([256]); B_pf = Buf()
        pb_ = A.alloc([256], BF16); B_pb = Buf()
        wp = [A.alloc([2, 512], BF16), A.alloc([2, 512], BF16)]; B_wp = [Buf(), Buf()]
        xr2 = [A.alloc([512]), A.alloc([512])]; B_xr2 = [Buf(), Buf()]
        sg = [A.alloc([512]), A.alloc([512])]; B_sg = [Buf(), Buf()]
        x2o = [A.alloc([512]), A.alloc([512])]; B_x2o = [Buf(), Buf()]
        gbc_h[0] = A.alloc([D])
        P.dma("sp", gbc_h[0], I["norm_ple"][layer, :].partition_broadcast(128), [], [B_gbc])
        nxt_w = load_wblock("w_ple_gate", layer, [(0, 512)], 16, 0)
        pend = []
        for t in range(NT):
            R = R_of(t)
            tok0 = t * 128
            P.dma("sp", sc["xt"][t % 2][:R, :], S_x1[tok0:tok0 + R, :], [], [sc["B_xt"][t % 2]])
            pb_fn = rmsnorm_T(t % 2, R, tok0, sc)
            while pend:
                pend.pop(0)()
            pend.append(pb_fn)
            P.dma("sp", pf[:R, :], p_src[tok0:tok0 + R, :], [], [B_pf])
            P.emit("dve", lambda e, R=R: e.tensor_copy(pb_[:R, :], pf[:R, :]), [B_pf], [B_pb])
            tr_group([pb_[:R, 0:128], pb_[:R, 128:256]], R, pT[:, :, tok0:tok0 + R], [B_pb], [B_pT])
        while pend:
            pend.pop(0)()
        dst = S_xres
        ec = [0]
        for nb in range(4):
            wb, B_wb, ncols = nxt_w
            if nb + 1 < 4:
                nxt_w = load_wblock("w_ple_gate", layer, [((nb + 1) * 512, 512)], 16, nb + 1)
            wi_ = nb % 2
            for kc in range(2):
                P.dma("pool", wp[wi_][:, kc, :], I["w_ple_proj"][layer][kc * 128:(kc + 1) * 128, nb * 512:(nb + 1) * 512],
                      [], [B_wp[wi_]])
            for t in range(NT):
                R = R_of(t)
                tok0 = t * 128
                i = ec[0] % 2
                ig = ec[0] % 4
                ip = 4 + ec[0] % 3
                ec[0] += 1
                P.dma("sp", xr2[i][:R, :], S_x1[tok0:tok0 + R, nb * 512:(nb + 1) * 512], [], [B_xr2[i]])
                for kc in range(16):
                    P.emit("pe", lambda e, kc=kc, ig=ig, R=R, tok0=tok0, wb=wb: e.matmul(
                        PB[ig][:R, :], actT[:, kc, tok0:tok0 + R], wb[:, kc, :], start=(kc == 0), stop=(kc == 15)),
                        [B_actT, B_wb], [PBb[ig]])
                for kc in range(2):
                    P.emit("pe", lambda e, kc=kc, ip=ip, R=R, tok0=tok0, wi_=wi_: e.matmul(
                        PB[ip][:R, :], pT[:, kc, tok0:tok0 + R], wp[wi_][:, kc, :], start=(kc == 0), stop=(kc == 1)),
                        [B_pT, B_wp[wi_]], [PBb[ip]])
                P.emit("act", lambda e, i=i, ig=ig, R=R: e.activation(sg[i][:R, :], PB[ig][:R, :], AF.Sigmoid), [PBb[ig]], [B_sg[i]])
                P.emit("dve", lambda e, i=i, ip=ip, R=R: e.tensor_tensor(sg[i][:R, :], sg[i][:R, :], PB[ip][:R, :], ALU.mult),
                       [B_sg[i], PBb[ip]], [B_sg[i]])
                P.emit("dve", lambda e, i=i, R=R: e.tensor_tensor(x2o[i][:R, :], sg[i][:R, :], xr2[i][:R, :], ALU.add),
                       [B_sg[i], B_xr2[i]], [B_x2o[i]])
                P.dma("sp", dst[tok0:tok0 + R, nb * 512:(nb + 1) * 512], x2o[i][:R, :], [B_x2o[i]], [])
        A.release(m4)
        phase_end()

        if last_layer:
            m5 = A.mark()
            xt = [A.alloc([D]), A.alloc([D])]; B_xt = [Buf(), Buf()]
            jk = A.alloc([D], BF16); B_jk = Buf()
            ss = A.alloc([8]); B_ss = Buf()
            yo = [A.alloc([D]), A.alloc([D])]; B_yo = [Buf(), Buf()]
            gbc = A.alloc([D])
            P.dma("sp", gbc, I["norm_final"][0, :].partition_broadcast(128), [], [B_gbc])
            ydst = O["y_s"][0] if is_sample else O["y_p"][seq]
            for t in range(NT):
                R = R_of(t)
                tok0 = t * 128
                i = t % 2
                P.dma("sp", xt[i][:R, :], S_xres[tok0:tok0 + R, :], [], [B_xt[i]])
                P.emit("act", lambda e, i=i, R=R: e.activation(jk[:R, :], xt[i][:R, :], AF.Square, accum_out=ss[:R, 0:1]),
                       [B_xt[i]], [B_jk, B_ss])
                P.emit("act", lambda e, R=R: e.activation(ss[:R, 1:2], ss[:R, 0:1], AF.Sqrt, bias=epsb[:R, 0:1], scale=1.0 / D),
                       [B_ss, B_const], [B_ss])
                P.emit("dve", lambda e, R=R: e.reciprocal(ss[:R, 2:3], ss[:R, 1:2]), [B_ss], [B_ss])
                P.emit("dve", lambda e, i=i, R=R: e.scalar_tensor_tensor(out=yo[i][:R, :], in0=xt[i][:R, :], scalar=ss[:R, 2:3],
                                                                        in1=gbc[:R, :], op0=ALU.mult, op1=ALU.mult),
                       [B_xt[i], B_ss, B_gbc], [B_yo[i]])
                P.dma("pool", ydst[tok0:tok0 + R, :], yo[i][:R, :], [B_yo[i]], [], is_out=True)
            A.release(m5)
            phase_end()

    seqs = [(s, False) for s in range(NP)] + ([(0, True)] if has_sample else [])
    try:
        for (s, is_s) in seqs:
            for layer in range(L):
                if layer == 0:
                    x_src = I["xs"][0] if is_s else I["xp"][s]
                else:
                    x_src = S_xres
                run_seq_layer(s, layer, is_s, x_src, layer == L - 1)
    except _Stop:
        pass
    P.finish()
    P.build(st)
    st.close()
    return nc, P


def make_consts(T):
    c = {}
    c["c_ident_bf"] = np.eye(128, dtype=np.float32).astype(ml_dtypes.bfloat16)
    c["c_ident_f"] = np.eye(128, dtype=np.float32)

    def rope_tab(pos):
        out = np.zeros((len(pos), 192), np.float32)
        for half, o in ((64, 0), (32, 128)):
            freq = (np.float32(10000.0) ** (-np.arange(half, dtype=np.float32) / np.float32(half))).astype(np.float32)
            ang = pos.astype(np.float32)[:, None] * freq[None, :]
            out[:, o:o + half] = np.cos(ang)
            out[:, o + half:o + 2 * half] = np.sin(ang)
        return out
    c["c_rope_p"] = rope_tab(np.arange(T))
    c["c_rope_s"] = rope_tab(PAST + np.arange(TS))
    nm = np.zeros((128, 128), np.float32)
    nm[:64, 64:] = -1e30
    c["c_negmask"] = nm
    bands = np.zeros((128, 12, 128), np.float32)
    bs = np.zeros((16, 8, 16), np.float32)
    for gi, w in enumerate((2, 4, 8, 16)):
        for tok in range(128):
            for j in range(tok - w + 1, tok + 1):
                if j >= 0:
                    bands[j, gi, tok] += 1.0 / w
                    bands[j, 8 + gi, tok] += 1.0 / min(tok + 1, w)
                else:
                    bands[128 + j, 4 + gi, tok] += 1.0 / w
            bands[tok, gi, tok] -= 1.0
            bands[tok, 8 + gi, tok] -= 1.0
        for tok in range(16):
            for j in range(tok - w + 1, tok + 1):
                if j >= 0:
                    bs[j, gi, tok] += 1.0 / w
                else:
                    bs[15 + j, 4 + gi, tok] += 1.0 / w
            bs[tok, gi, tok] -= 1.0
    c["c_bands"] = bands.astype(ml_dtypes.bfloat16)
    c["c_bands_s"] = bs.astype(ml_dtypes.bfloat16)
    return c


_CACHE = {}


def run(inputs, n_cores, NP, T, L, has_sample=True, dbg_stop=None):
    key = (NP, T, L, has_sample, dbg_stop)
    if key not in _CACHE:
        _CACHE[key] = build_program(NP, T, L, has_sample, dbg_stop=dbg_stop)
    nc, P = _CACHE[key]
    f = lambda a: np.ascontiguousarray(np.asarray(a, dtype=np.float32))
    consts = make_consts(T)
    wnames = ["norm_mix", "w_in", "conv_w", "conv_b", "conv_ln_g", "conv_ln_b", "conv_pw", "pool_w", "pool_scale",
              "w_out", "norm_ple", "w_ple_gate", "w_ple_proj"]
    shared = {n: f(inputs[n]) for n in wnames}
    shared["norm_final"] = f(inputs["norm_final"]).reshape(1, D)
    shared.update(consts)
    in_maps = []
    for c in range(n_cores):
        m = dict(shared)
        m["xp"] = f(inputs["x_prompt"][c * NP:(c + 1) * NP])
        m["pp"] = f(inputs["p_prompt"][:, c * NP:(c + 1) * NP])
        m["xs"] = f(inputs["x_sample"][c:c + 1])
        m["ps"] = f(inputs["p_sample"][:, c:c + 1])
        m["ck"] = f(inputs["cache_k"][:, c:c + 1]).reshape(L, 1, PAST, 256)
        m["cv"] = f(inputs["cache_v"][:, c:c + 1]).reshape(L, 1, PAST, 256)
        m["cki"] = f(inputs["cache_kidx"][:, c:c + 1])
        m["sconv"] = f(inputs["state_conv"][:, c:c + 1])
        m["spool"] = f(inputs["state_pool"][:, c:c + 1])
        in_maps.append(m)
    res = run_bass_kernel_spmd(nc, in_maps, core_ids=list(range(n_cores)))
    rs = res.results
    cat = lambda name, ax: np.concatenate([r[name] for r in rs], axis=ax)
    y_p = cat("y_p", 0)
    y_s = cat("y_s", 0)
    k_p = cat("k_p", 1).reshape(L, n_cores * NP, T, 2, 128)
    v_p = cat("v_p", 1).reshape(L, n_cores * NP, T, 2, 128)
    ki_p = cat("ki_p", 1)
    conv_p = cat("conv_p", 1)
    pool_p = cat("pool_p", 1)
    k_s = cat("k_s", 1).reshape(L, n_cores, TS, 2, 128)
    v_s = cat("v_s", 1).reshape(L, n_cores, TS, 2, 128)
    ki_s = cat("ki_s", 1)
    conv_s = cat("conv_s", 1)
    pool_s = cat("pool_s", 1)
    return (y_p, y_s, k_p, v_p, ki_p, conv_p, pool_p, k_s, v_s, ki_s, conv_s, pool_s)


def kernel(**inputs):
    outs = run(inputs, 8, 2, 2048, 2, True)
    return tuple(np.asarray(o, dtype=np.float32) for o in outs)
```

```python
from contextlib import ExitStack
import numpy as np
import ml_dtypes
import concourse.bass as bass
import concourse.mybir as mybir
from concourse.bass_utils import run_bass_kernel_spmd

F32 = mybir.dt.float32
BF16 = mybir.dt.bfloat16
AF = mybir.ActivationFunctionType
ALU = mybir.AluOpType

NPOOL = 16
EPOCH = 30000

D = 2048
NIN = 6224
PLE = 256
PAST = 1024
TS = 16
EPS = 1e-6
NIT = 22
W0 = 8.0


class Buf:
    __slots__ = ("name", "w", "r")

    def __init__(self, name=""):
        self.name = name
        self.w = None
        self.r = {}


class Op:
    __slots__ = ("id", "eng", "fn", "deps", "dma", "sig")


class Prog:
    ENGS = ("pe", "act", "dve", "pool", "sp")

    def __init__(self, nc):
        self.nc = nc
        self.ops = []
        self.by_eng = {e: [] for e in self.ENGS}
        self.out_dmas = []
        self.last = {e: None for e in self.ENGS}
        self.dmas_since_bar = []

    def emit(self, eng, fn, reads=(), writes=(), dma=False, out=False):
        op = Op()
        op.id = len(self.ops)
        op.eng = eng
        op.fn = fn
        op.dma = dma
        op.sig = None
        deps = {}
        for b in reads:
            if b.w is not None:
                deps[b.w] = "RAW"
        for b in writes:
            if b.w is not None:
                deps.setdefault(b.w, "WAW")
            for r in b.r.values():
                deps.setdefault(r, "WAR")
        key = ("dma", op.id) if dma else eng
        for b in reads:
            b.r[key] = op.id
        for b in writes:
            b.w = op.id
            b.r = {}
        op.deps = deps
        self.ops.append(op)
        self.by_eng[eng].append(op)
        if dma:
            self.dmas_since_bar.append(op.id)
        else:
            self.last[eng] = op.id
        if out:
            self.out_dmas.append(op.id)
        return op

    def dma(self, eng, out, in_, reads, writes, is_out=False):
        return self.emit(eng, lambda e: e.dma_start(out=out, in_=in_), reads, writes, dma=True, out=is_out)

    def barrier(self):
        deps = {}
        for e in self.ENGS:
            if self.last[e] is not None:
                deps[self.last[e]] = "RAW"
        for d in self.dmas_since_bar:
            deps[d] = "RAW"
        self.dmas_since_bar = []
        for e in self.ENGS:
            op = Op()
            op.id = len(self.ops)
            op.eng = e
            op.fn = None
            op.dma = False
            op.sig = None
            op.deps = {k: ("BAR" if self.ops[k].eng != e or self.ops[k].dma else "WAW") for k in deps}
            self.ops.append(op)
            self.by_eng[e].append(op)

    def _skip(self, o, d, kind):
        if d.dma or o.dma:
            return False
        if o.eng == d.eng:
            if o.eng == "pe":
                return True
            return kind != "RAW"
        return False

    def finish(self):
        op = Op()
        op.id = len(self.ops)
        op.eng = "sp"
        op.fn = None
        op.dma = False
        op.sig = None
        op.deps = {d: "RAW" for d in self.out_dmas}
        self.ops.append(op)
        self.by_eng["sp"].append(op)

    def build(self, stack):
        nc = self.nc
        ops = self.ops
        for q in ("sp", "pool", "act"):
            dma_ops = [o for o in ops if o.dma and o.eng == q]
            for j, o in enumerate(dma_ops):
                o.sig = (("dma", q + str(j % NPOOL)), 16 * (j // NPOOL + 1))
                if j >= NPOOL:
                    o.deps.setdefault(dma_ops[j - NPOOL].id, "GUARD")
        needed = set()
        for o in ops:
            for d, kind in o.deps.items():
                if not self._skip(o, ops[d], kind):
                    needed.add(d)
        cnt = {e: 0 for e in self.ENGS}
        semkeys = set()
        for o in ops:
            if o.dma:
                semkeys.add(o.sig[0])
            elif o.id in needed:
                c = cnt[o.eng]
                o.sig = ((o.eng, c // EPOCH), c % EPOCH + 1)
                cnt[o.eng] = c + 1
                semkeys.add(o.sig[0])
        sems = {}
        for k in sorted(semkeys, key=str):
            sems[k] = stack.enter_context(nc.semaphore("s_%s_%s" % (k[0], k[1])))
        self.n_sems = len(sems)
        self.n_waits = 0
        block = stack.enter_context(nc.Block())

        def run(engname, e):
            waited = {}
            for o in self.by_eng[engname]:
                req = {}
                for d in o.deps:
                    dop = ops[d]
                    if self._skip(o, dop, o.deps[d]):
                        continue
                    sk, val = dop.sig
                    if sk[0] == "dma":
                        if req.get(sk, 0) < val:
                            req[sk] = val
                    else:
                        cur = req.get(sk[0], (-1, 0))
                        if cur < (sk[1], val):
                            req[sk[0]] = (sk[1], val)
                for k in sorted(req, key=str):
                    v = req[k]
                    if isinstance(k, tuple):
                        if waited.get(k, 0) >= v:
                            continue
                        waited[k] = v
                        e.wait_ge(sems[k], v)
                    else:
                        if waited.get(k, (-1, 0)) >= v:
                            continue
                        waited[k] = v
                        e.wait_ge(sems[(k, v[0])], v[1])
                    self.n_waits += 1
                if o.fn is None:
                    continue
                ins = o.fn(e)
                if o.sig is not None:
                    ins.then_inc(sems[o.sig[0]], 16 if o.dma else 1)

        @block.tensor
        def _(e):
            run("pe", e)

        @block.scalar
        def _(e):
            run("act", e)

        @block.vector
        def _(e):
            run("dve", e)

        @block.gpsimd
        def _(e):
            run("pool", e)

        @block.sync
        def _(e):
            run("sp", e)


COLBLOCKS = [
    ("q", [(0, 512)], 0), ("q", [(512, 512)], 1),
    ("kv", [(1024, 512)], 0),
    ("qi", [(1536, 512)], 0), ("qi", [(2048, 512)], 1),
    ("kiwi", [(2560, 80)], 0),
    ("gate", [(2640, 512)], 0), ("gate", [(3152, 512)], 512),
    ("glu", [(3664, 256), (4176, 256)], 0), ("glu", [(3920, 256), (4432, 256)], 1),
    ("gate", [(4688, 512)], 1024),
    ("pin", [(5200, 512)], 0),
    ("gate", [(5712, 512)], 1536),
]


class Arena:
    def __init__(self, ap, nwords):
        self.ap = ap
        self.n = nwords
        self.off = 0

    def mark(self):
        return self.off

    def release(self, m):
        self.off = m

    def alloc(self, free_shape, dt=F32):
        n = int(np.prod(free_shape))
        w = n if dt == F32 else (n + 1) // 2
        a = self.ap[:, self.off:self.off + w]
        self.off += (w + 7) // 8 * 8
        assert self.off <= self.n, "SBUF arena overflow %d > %d" % (self.off, self.n)
        if dt != F32:
            a = a.bitcast(dt)
        if len(free_shape) == 2:
            a = a.rearrange("p (a b) -> p a b", b=free_shape[1])
        elif len(free_shape) == 3:
            a = a.rearrange("p (a b c) -> p a b c", b=free_shape[1], c=free_shape[2])
        return a


class _Stop(Exception):
    pass


def build_program(NP, T, L, has_sample=True, arena_words=53200, dbg_stop=None):
    nc = bass.Bass("TRN2", target_bir_lowering=False)
    NTP = T // 128
    TOPK_P = min(256, T // 4)
    TOPK_S = min(256, (PAST + TS) // 4)

    def din(name, shape, dt=F32):
        return nc.dram_tensor(name, list(shape), dt, kind="ExternalInput").ap()

    def dout(name, shape, dt=F32):
        return nc.dram_tensor(name, list(shape), dt, kind="ExternalOutput").ap()

    def dscr(name, shape, dt=F32):
        return nc.dram_tensor(name, list(shape), dt).ap()

    I = {}
    I["xp"] = din("xp", [NP, T, D])
    I["pp"] = din("pp", [L, NP, T, PLE])
    I["xs"] = din("xs", [1, TS, D])
    I["ps"] = din("ps", [L, 1, TS, PLE])
    I["ck"] = din("ck", [L, 1, PAST, 256])
    I["cv"] = din("cv", [L, 1, PAST, 256])
    I["cki"] = din("cki", [L, 1, PAST, 64])
    I["sconv"] = din("sconv", [L, 1, 30, 512])
    I["spool"] = din("spool", [L, 1, 15, 512])
    I["norm_mix"] = din("norm_mix", [L, D])
    I["w_in"] = din("w_in", [L, D, NIN])
    I["conv_w"] = din("conv_w", [L, 31, 512])
    I["conv_b"] = din("conv_b", [L, 512])
    I["conv_ln_g"] = din("conv_ln_g", [L, 512])
    I["conv_ln_b"] = din("conv_ln_b", [L, 512])
    I["conv_pw"] = din("conv_pw", [L, 512, 512])
    I["pool_w"] = din("pool_w", [L, 4, 128, 128])
    I["pool_scale"] = din("pool_scale", [L, 512])
    I["w_out"] = din("w_out", [L, D, D])
    I["norm_ple"] = din("norm_ple", [L, D])
    I["w_ple_gate"] = din("w_ple_gate", [L, D, D])
    I["w_ple_proj"] = din("w_ple_proj", [L, PLE, D])
    I["norm_final"] = din("norm_final", [1, D])
    I["c_ident_bf"] = din("c_ident_bf", [128, 128], BF16)
    I["c_ident_f"] = din("c_ident_f", [128, 128])
    I["c_rope_p"] = din("c_rope_p", [T, 192])
    I["c_rope_s"] = din("c_rope_s", [TS, 192])
    I["c_negmask"] = din("c_negmask", [128, 128])
    I["c_bands"] = din("c_bands", [128, 12, 128], BF16)
    I["c_bands_s"] = din("c_bands_s", [16, 8, 16], BF16)

    O = {}
    O["y_p"] = dout("y_p", [NP, T, D])
    O["y_s"] = dout("y_s", [1, TS, D])
    O["k_p"] = dout("k_p", [L, NP, T, 256])
    O["v_p"] = dout("v_p", [L, NP, T, 256])
    O["ki_p"] = dout("ki_p", [L, NP, T, 64])
    O["conv_p"] = dout("conv_p", [L, NP, 30, 512])
    O["pool_p"] = dout("pool_p", [L, NP, 15, 512])
    O["k_s"] = dout("k_s", [L, 1, TS, 256])
    O["v_s"] = dout("v_s", [L, 1, TS, 256])
    O["ki_s"] = dout("ki_s", [L, 1, TS, 64])
    O["conv_s"] = dout("conv_s", [L, 1, 30, 512])
    O["pool_s"] = dout("pool_s", [L, 1, 15, 512])

    S_qT = dscr("s_qT", [NTP, 128, 1024], BF16)
    S_qiT = dscr("s_qiT", [NTP, 128, 1024], BF16)
    S_gates = dscr("s_gates", [T, D], BF16)
    S_uT = dscr("s_uT", [NTP, 128, 512], BF16)
    S_pin = dscr("s_pin", [T, 512], BF16)
    S_dg = dscr("s_dg", [L, 128, 31 * 4 * 128], BF16)
    S_pw = dscr("s_pw", [L, 128, 4 * 512], BF16)
    S_plw = dscr("s_plw", [L, 128, 4 * 128], BF16)
    p2_cached = set()
    S_x1 = dscr("s_x1", [T, D])
    S_xres = dscr("s_xres", [T, D])

    st = ExitStack()
    P = Prog(nc)
    arena_t = st.enter_context(nc.sbuf_tensor("arena", [128, arena_words], F32))
    A = Arena(arena_t[:, :], arena_words)
    banks = [st.enter_context(nc.psum_tensor("pb%d" % i, [128, 512], F32)) for i in range(8)]
    PB = [b[:, :] for b in banks]
    PBb = [Buf("pb%d" % i) for i in range(8)]
    TRb = PB[7].bitcast(BF16)
    TRb6 = PB[6].bitcast(BF16)
    TR = [TRb[:, 0:512], TRb6[:, 0:512]]
    TRB = [PBb[7], PBb[6]]

    def tr_mode(two):
        if two:
            TR[0], TR[1] = TRb[:, 0:512], TRb6[:, 0:512]
            TRB[0], TRB[1] = PBb[7], PBb[6]
        else:
            TR[0], TR[1] = TRb[:, 0:512], TRb[:, 512:1024]
            TRB[0], TRB[1] = PBb[7], PBb[7]
    trc = [0]
    phase_ctr = [0]

    def phase_end():
        P.barrier()
        phase_ctr[0] += 1
        if dbg_stop is not None and phase_ctr[0] >= dbg_stop:
            raise _Stop()

    ident = A.alloc([128], BF16); B_const = Buf("const")
    identf = A.alloc([128])
    negmask = A.alloc([128])
    epsb = A.alloc([8])
    actT = A.alloc([16, T], BF16); B_actT = Buf("actT")
    wbuf = [None, None]
    B_wbuf = [Buf("w0"), Buf("w1")]

    def alloc_wbuf(need_stage):
        wbuf[0] = A.alloc([16, 512], BF16)
        wbuf[1] = A.alloc([16, 512], BF16)
        if need_stage:
            stg[0] = A.alloc([8, 512])
    NKMAX = max(T, PAST + TS)
    NBLK = (NKMAX + 127) // 128
    kT = A.alloc([2, NKMAX], BF16); B_kT = Buf("kT")
    V1 = A.alloc([NBLK, 2, 129], BF16); B_V1 = Buf("V1")
    kiT2 = A.alloc([NKMAX], BF16); B_kiT = Buf("kiT")
    wia = A.alloc([NTP, 32]); B_wi = Buf("wi")
    gbc_h = [None]; B_gbc = Buf("gbc")
    wcnt = [0]

    P.dma("sp", ident, I["c_ident_bf"], [], [B_const])
    P.dma("sp", identf, I["c_ident_f"], [], [B_const])
    P.dma("sp", negmask, I["c_negmask"], [], [B_const])
    P.emit("dve", lambda e: e.memset(epsb, EPS), [], [B_const])
    P.emit("dve", lambda e: e.memset(V1[:, :, :, 128:129], 1.0), [], [B_V1])

    def transpose_to(dst_fn, src, R, ncol, reads, writes, eng="act"):
        raise NotImplementedError

    def tr_group(srcs, R, dst, reads, writes, eng="act"):
        i = trc[0] % 2
        trc[0] += 1
        n = len(srcs)
        c = srcs[0].shape[1]
        tv = TR[i].rearrange("p (a b) -> p a b", b=128)
        for j, s in enumerate(srcs):
            P.emit("pe", lambda e, s=s, j=j: e.transpose(tv[:c, j, :R], s, ident[:R, :R]),
                   reads + [B_const], [TRB[i]])
        if eng == "act":
            P.emit("act", lambda e: e.copy(dst, tv[:c, 0:n, :R]), [TRB[i]], writes)
        else:
            P.emit("dve", lambda e: e.tensor_copy(dst, tv[:c, 0:n, :R]), [TRB[i]], writes)

    WB = {}
    converted = set()
    stg = [None]
    B_stg = Buf("stg")

    NBLK_W = {"w_in": len(COLBLOCKS), "w_out": 4, "w_ple_gate": 4}

    def load_wblock(name, layer, ranges, kch, bid):
        w_ap = I[name][layer]
        key = (name, layer)
        if key not in WB:
            WB[key] = dscr("wb_%s_%d" % (name, layer), [NBLK_W[name], 128, 16 * 512], BF16)
        wsc = WB[key]
        i = wcnt[0] % 2
        wcnt[0] += 1
        o = sum(n for (_, n) in ranges)
        ck = (name, layer, bid)
        if ck in converted:
            P.dma("sp", wbuf[i].rearrange("p a b -> p (a b)"), wsc[bid], [], [B_wbuf[i]])
        else:
            converted.add(ck)
            sg_ = stg[0]
            o2 = 0
            for (c0, n) in ranges:
                for k8 in range(0, kch, 8):
                    for k4 in range(k8, min(kch, k8 + 8), 4):
                        kk = min(4, kch - k4)
                        P.dma("pool", sg_[:, k4 - k8:k4 - k8 + kk, 0:n],
                              w_ap[k4 * 128:(k4 + kk) * 128, c0:c0 + n].rearrange("(kc p) n -> p kc n", p=128), [], [B_stg])
                    for k4 in range(k8, min(kch, k8 + 8), 4):
                        kk = min(4, kch - k4)
                        P.emit("pool", lambda e, k4=k4, kk=kk, k8=k8, o2=o2, n=n, i=i, sg_=sg_: e.tensor_copy(
                            wbuf[i][:, k4:k4 + kk, o2:o2 + n], sg_[:, k4 - k8:k4 - k8 + kk, 0:n]), [B_stg], [B_wbuf[i]])
                o2 += n
            P.dma("pool", wsc[bid], wbuf[i].rearrange("p a b -> p (a b)"), [B_wbuf[i]], [])
        return wbuf[i], B_wbuf[i], o

    def make_sc():
        sc = {}
        sc["xt"] = [A.alloc([D]), A.alloc([D])]; sc["B_xt"] = [Buf(), Buf()]
        sc["junk"] = A.alloc([D], BF16); sc["B_junk"] = Buf()
        sc["xn"] = [A.alloc([D], BF16), A.alloc([D], BF16)]; sc["B_xn"] = [Buf(), Buf()]
        sc["ss"] = A.alloc([8]); sc["B_ss"] = Buf()
        return sc

    def rmsnorm_T(j, R, tok0, sc):
        xt, B_xt = sc["xt"][j], sc["B_xt"][j]
        xn, B_xn = sc["xn"][j], sc["B_xn"][j]
        gbc = gbc_h[0]
        P.emit("act", lambda e: e.activation(sc["junk"][:R, :], xt[:R, :], AF.Square, accum_out=sc["ss"][:R, 0:1]),
               [B_xt], [sc["B_junk"], sc["B_ss"]])
        P.emit("act", lambda e: e.activation(sc["ss"][:R, 1:2], sc["ss"][:R, 0:1], AF.Sqrt, bias=epsb[:R, 0:1], scale=1.0 / D),
               [sc["B_ss"], B_const], [sc["B_ss"]])
        P.emit("dve", lambda e: e.reciprocal(sc["ss"][:R, 2:3], sc["ss"][:R, 1:2]), [sc["B_ss"]], [sc["B_ss"]])
        P.emit("dve", lambda e: e.scalar_tensor_tensor(out=xn[:R, :], in0=xt[:R, :], scalar=sc["ss"][:R, 2:3],
                                                       in1=gbc[:R, :], op0=ALU.mult, op1=ALU.mult),
               [B_xt, sc["B_ss"], B_gbc], [B_xn])

        def part_b():
            for c4 in range(4):
                srcs = [xn[:R, (c4 * 4 + jj) * 128:(c4 * 4 + jj + 1) * 128] for jj in range(4)]
                tr_group(srcs, R, actT[:, c4 * 4:c4 * 4 + 4, tok0:tok0 + R], [B_xn], [B_actT], eng=("act" if c4 % 2 == 0 else "dve"))
        return part_b

    def rope(e_eng, src, dst1, dst2, cos, sin, tmp, R, nh, half, reads, writes_list, B_tmp):
        s4 = src.rearrange("p (h t d) -> p h t d", h=nh, t=2)
        x1 = s4[:, :, 0, :]
        x2 = s4[:, :, 1, :]
        cb = cos.unsqueeze(1).to_broadcast([R, nh, half])
        sb_ = sin.unsqueeze(1).to_broadcast([R, nh, half])
        t4 = tmp[:R, 0:nh * half * 4].rearrange("p (k h d) -> p k h d", k=4, h=nh)
        P.emit("dve", lambda e: e.tensor_tensor(t4[:, 0], x1, cb, ALU.mult), reads, [B_tmp])
        P.emit("dve", lambda e: e.tensor_tensor(t4[:, 1], x2, sb_, ALU.mult), reads, [B_tmp])
        P.emit("dve", lambda e: e.tensor_tensor(t4[:, 2], x1, sb_, ALU.mult), reads, [B_tmp])
        P.emit("dve", lambda e: e.tensor_tensor(t4[:, 3], x2, cb, ALU.mult), reads, [B_tmp])
        P.emit("dve", lambda e: e.tensor_tensor(dst1, t4[:, 0], t4[:, 1], ALU.subtract), [B_tmp], writes_list)
        P.emit("dve", lambda e: e.tensor_tensor(dst2, t4[:, 2], t4[:, 3], ALU.add), [B_tmp], writes_list)

    def run_seq_layer(seq, layer, is_sample, x_src, last_layer):
        Tn = TS if is_sample else T
        NT = 1 if is_sample else NTP
        R_of = (lambda t: TS) if is_sample else (lambda t: 128)
        b = 0 if is_sample else seq
        rope_src = I["c_rope_s"] if is_sample else I["c_rope_p"]
        kbase = PAST if is_sample else 0
        topk = TOPK_S if is_sample else TOPK_P
        kname = "s" if is_sample else "p"
        p_src = I["ps"][layer, 0] if is_sample else I["pp"][layer, seq]

        if is_sample:
            m0 = A.mark()
            tr_mode(True)
            NB0 = PAST // 128
            ckf = [A.alloc([256]), A.alloc([256])]; B_ckf = [Buf(), Buf()]
            cvf = [A.alloc([256]), A.alloc([256])]; B_cvf = [Buf(), Buf()]
            ckif = [A.alloc([64]), A.alloc([64])]; B_ckif = [Buf(), Buf()]
            ckb = [A.alloc([256], BF16), A.alloc([256], BF16)]; B_ckb = [Buf(), Buf()]
            ckib = [A.alloc([128], BF16), A.alloc([128], BF16)]; B_ckib = [Buf(), Buf()]

            def p0_load(blk):
                j = blk % 2
                r0 = blk * 128
                P.dma("sp", ckf[j], I["ck"][layer, 0, r0:r0 + 128, :], [], [B_ckf[j]])
                P.dma("sp", cvf[j], I["cv"][layer, 0, r0:r0 + 128, :], [], [B_cvf[j]])
                P.dma("sp", ckif[j], I["cki"][layer, 0, r0:r0 + 128, :], [], [B_ckif[j]])

            p0_load(0)
            for blk in range(NB0):
                j = blk % 2
                r0 = blk * 128
                if blk + 1 < NB0:
                    p0_load(blk + 1)
                P.emit("act", lambda e, j=j: e.copy(ckb[j], ckf[j]), [B_ckf[j]], [B_ckb[j]])
                P.emit("dve", lambda e, j=j, blk=blk: e.tensor_copy(
                    V1[:, blk, :, 0:128], cvf[j].rearrange("p (g d) -> p g d", g=2)), [B_cvf[j]], [B_V1])
                P.emit("act", lambda e, j=j: e.copy(ckib[j][:, 0:64], ckif[j]), [B_ckif[j]], [B_ckib[j]])
                P.emit("dve", lambda e, j=j: e.tensor_copy(ckib[j][:, 64:128], ckif[j]), [B_ckif[j]], [B_ckib[j]])
                tr_group([ckb[j][:, 0:128], ckb[j][:, 128:256]], 128, kT[:, :, r0:r0 + 128], [B_ckb[j]], [B_kT])
                tr_group([ckib[j][:, 0:128]], 128, kiT2[:, r0:r0 + 128].unsqueeze(1), [B_ckib[j]], [B_kiT], eng="dve")
            A.release(m0)
            phase_end()

        m1 = A.mark()
        tr_mode(True)
        first_use = ("w_in", layer, 0) not in converted
        alloc_wbuf(first_use)
        sc = make_sc()
        ropet = A.alloc([NT, 192]); B_rope = Buf()
        zs = [A.alloc([512]), A.alloc([512])]; B_zs = [Buf(), Buf()]
        tmp = A.alloc([1024]); B_tmp = Buf()
        ob = [A.alloc([512], BF16), A.alloc([512], BF16)]; B_ob = [Buf(), Buf()]
        of = [A.alloc([512]), A.alloc([512])]; B_of = [Buf(), Buf()]
        tT = [A.alloc([4, 128], BF16), A.alloc([4, 128], BF16)]; B_tT = [Buf(), Buf()]

        gbc_h[0] = A.alloc([D])
        P.dma("sp", gbc_h[0], I["norm_mix"][layer, :].partition_broadcast(128), [], [B_gbc])
        if is_sample:
            P.dma("sp", ropet[:TS, 0, :], rope_src, [], [B_rope])
        else:
            P.dma("sp", ropet, rope_src.rearrange("(n p) c -> p n c", p=128), [], [B_rope])
        nxt_w = load_wblock("w_in", layer, COLBLOCKS[0][1], 16, 0)
        pend = []
        for t in range(NT):
            R = R_of(t)
            P.dma("sp", sc["xt"][t % 2][:R, :], x_src[t * 128:t * 128 + R, :], [], [sc["B_xt"][t % 2]])
            pb_fn = rmsnorm_T(t % 2, R, t * 128, sc)
            while pend:
                pend.pop(0)()
            pend.append(pb_fn)
        while pend:
            pend.pop(0)()

        ec = [0]
        for bi_, (kind, ranges, arg) in enumerate(COLBLOCKS):
            wb, B_wb, ncols = nxt_w
            if bi_ + 1 < len(COLBLOCKS):
                nxt_w = load_wblock("w_in", layer, COLBLOCKS[bi_ + 1][1], 16, bi_ + 1)
            for t in range(NT):
                R = R_of(t)
                tok0 = t * 128
                i = ec[0] % 2
                ec[0] += 1
                ps, B_ps = PB[(ec[0] - 1) % 4], PBb[(ec[0] - 1) % 4]
                for kc in range(16):
                    P.emit("pe", lambda e, kc=kc, ps=ps, R=R, tok0=tok0, wb=wb, ncols=ncols:
                           e.matmul(ps[:R, :ncols], actT[:, kc, tok0:tok0 + R], wb[:, kc, :ncols],
                                    start=(kc == 0), stop=(kc == 15)),
                           [B_actT, B_wb], [B_ps])
                while pend:
                    pend.pop(0)()
                z, B_z = zs[i], B_zs[i]
                o_b, B_o = ob[i], B_ob[i]
                o_f, B_f = of[i], B_of[i]
                tt_, B_tt = tT[i], B_tT[i]
                cos128 = ropet[:R, t, 0:64]
                sin128 = ropet[:R, t, 64:128]
                cos64 = ropet[:R, t, 128:160]
                sin64 = ropet[:R, t, 160:192]
                if kind == "q":
                    P.emit("act", lambda e, z=z, ps=ps, R=R: e.copy(z[:R, :], ps[:R, :]), [B_ps], [B_z])
                    o4 = o_b[:R, :].rearrange("p (h t d) -> p h t d", h=4, t=2)
                    rope("dve", z[:R, :], o4[:, :, 0, :], o4[:, :, 1, :], cos128, sin128, tmp, R, 4, 64,
                         [B_z, B_rope], [B_o], B_tmp)
                    def pb_(o_b=o_b, R=R, tt_=tt_, B_o=B_o, B_tt=B_tt, t=t, arg=arg):
                        tr_group([o_b[:R, j * 128:(j + 1) * 128] for j in range(4)], R, tt_[:, :, :R], [B_o], [B_tt])
                        P.dma("sp", S_qT[t, :, arg * 512:(arg + 1) * 512].rearrange("p (a b) -> p a b", b=128)[:, :, :R],
                              tt_[:, :, :R], [B_tt], [])
                    pend.append(pb_)
                elif kind == "qi":
                    P.emit("act", lambda e, z=z, ps=ps, R=R: e.copy(z[:R, :], ps[:R, :]), [B_ps], [B_z])
                    o4 = o_b[:R, :].rearrange("p (h t d) -> p h t d", h=8, t=2)
                    rope("dve", z[:R, :], o4[:, :, 0, :], o4[:, :, 1, :], cos64, sin64, tmp, R, 8, 32,
                         [B_z, B_rope], [B_o], B_tmp)
                    def pb_(o_b=o_b, R=R, tt_=tt_, B_o=B_o, B_tt=B_tt, t=t, arg=arg):
                        tr_group([o_b[:R, j * 128:(j + 1) * 128] for j in range(4)], R, tt_[:, :, :R], [B_o], [B_tt])
                        P.dma("sp", S_qiT[t, :, arg * 512:(arg + 1) * 512].rearrange("p (a b) -> p a b", b=128)[:, :, :R],
                              tt_[:, :, :R], [B_tt], [])
                    pend.append(pb_)
                elif kind == "kv":
                    P.emit("act", lambda e, z=z, ps=ps, R=R: e.copy(z[:R, :], ps[:R, :]), [B_ps], [B_z])
                    f4 = o_f[:R, 0:256].rearrange("p (h t d) -> p h t d", h=2, t=2)
                    rope("dve", z[:R, 0:256], f4[:, :, 0, :], f4[:, :, 1, :], cos128, sin128, tmp, R, 2, 64,
                         [B_z, B_rope], [B_f], B_tmp)
                    P.dma("sp", O["k_" + kname][layer, b, tok0:tok0 + R, :], o_f[:R, 0:256], [B_f], [], is_out=True)
                    P.dma("sp", O["v_" + kname][layer, b, tok0:tok0 + R, :], z[:R, 256:512], [B_z], [], is_out=True)
                    P.emit("act", lambda e, o_b=o_b, o_f=o_f, R=R: e.copy(o_b[:R, 0:256], o_f[:R, 0:256]), [B_f], [B_o])
                    k0 = kbase + tok0
                    pend.append(lambda o_b=o_b, R=R, k0=k0, B_o=B_o: tr_group(
                        [o_b[:R, 0:128], o_b[:R, 128:256]], R, kT[:, :, k0:k0 + R], [B_o], [B_kT]))
                    blk = k0 // 128
                    P.emit("dve", lambda e, z=z, R=R, blk=blk: e.tensor_copy(
                        V1[:R, blk, :, 0:128], z[:R, 256:512].rearrange("p (g d) -> p g d", g=2)), [B_z], [B_V1])
                elif kind == "kiwi":
                    P.emit("act", lambda e, z=z, ps=ps, R=R: e.copy(z[:R, 0:80], ps[:R, 0:80]), [B_ps], [B_z])
                    f4 = o_f[:R, 0:64].rearrange("p (h t d) -> p h t d", h=1, t=2)
                    rope("dve", z[:R, 0:64], f4[:, :, 0, :], f4[:, :, 1, :], cos64, sin64, tmp, R, 1, 32,
                         [B_z, B_rope], [B_f], B_tmp)
                    P.dma("sp", O["ki_" + kname][layer, b, tok0:tok0 + R, :], o_f[:R, 0:64], [B_f], [], is_out=True)
                    P.emit("act", lambda e, o_b=o_b, o_f=o_f, R=R: e.copy(o_b[:R, 0:64], o_f[:R, 0:64]), [B_f], [B_o])
                    P.emit("act", lambda e, o_b=o_b, o_f=o_f, R=R: e.copy(o_b[:R, 64:128], o_f[:R, 0:64]), [B_f], [B_o])
                    k0 = kbase + tok0
                    pend.append(lambda o_b=o_b, R=R, k0=k0, B_o=B_o: tr_group(
                        [o_b[:R, 0:128]], R, kiT2[:, k0:k0 + R].unsqueeze(1), [B_o], [B_kiT]))
                    P.emit("act", lambda e, z=z, R=R, t=t: e.activation(wia[:R, t, 0:16], z[:R, 64:80], AF.Abs, scale=1.0 / 32.0),
                           [B_z], [B_wi])
                    P.emit("act", lambda e, z=z, R=R, t=t: e.activation(wia[:R, t, 16:32], z[:R, 64:80], AF.Sign),
                           [B_z], [B_wi])
                elif kind == "gate":
                    P.emit("act", lambda e, o_b=o_b, ps=ps, R=R: e.activation(o_b[:R, :], ps[:R, :], AF.Silu), [B_ps], [B_o])
                    P.dma("sp", S_gates[tok0:tok0 + R, arg:arg + 512], o_b[:R, :], [B_o], [])
                elif kind == "glu":
                    P.emit("act", lambda e, z=z, ps=ps, R=R: e.activation(z[:R, 0:256], ps[:R, 256:512], AF.Sigmoid), [B_ps], [B_z])
                    P.emit("dve", lambda e, z=z, ps=ps, R=R, o_f=o_f: e.tensor_tensor(o_f[:R, 0:256], ps[:R, 0:256], z[:R, 0:256], ALU.mult),
                           [B_ps, B_z], [B_f])
                    c0 = arg * 256
                    if is_sample:
                        P.dma("sp", O["conv_s"][layer, 0, 14:30, c0:c0 + 256], o_f[:TS, 0:256], [B_f], [], is_out=True)
                    elif t == NT - 1:
                        P.dma("sp", O["conv_p"][layer, b, 0:30, c0:c0 + 256], o_f[98:128, 0:256], [B_f], [], is_out=True)
                    P.emit("act", lambda e, o_b=o_b, o_f=o_f, R=R: e.copy(o_b[:R, 0:256], o_f[:R, 0:256]), [B_f], [B_o])
                    def pb_(o_b=o_b, R=R, tt_=tt_, B_o=B_o, B_tt=B_tt, t=t, c0=c0):
                        tr_group([o_b[:R, 0:128], o_b[:R, 128:256]], R, tt_[:, 0:2, :R], [B_o], [B_tt])
                        P.dma("sp", S_uT[t, :, c0:c0 + 256].rearrange("p (a b) -> p a b", b=128)[:, :, :R],
                              tt_[:, 0:2, :R], [B_tt], [])
                    pend.append(pb_)
                elif kind == "pin":
                    P.emit("act", lambda e, z=z, ps=ps, R=R: e.copy(z[:R, :], ps[:R, :]), [B_ps], [B_z])
                    if is_sample:
                        P.dma("sp", O["pool_s"][layer, 0, 0:15, :], z[1:16, :], [B_z], [], is_out=True)
                    elif t == NT - 1:
                        P.dma("sp", O["pool_p"][layer, b, 0:15, :], z[113:128, :], [B_z], [], is_out=True)
                    P.emit("dve", lambda e, z=z, o_b=o_b, R=R: e.tensor_copy(o_b[:R, :], z[:R, :]), [B_z], [B_o])
                    P.dma("sp", S_pin[tok0:tok0 + R, :], o_b[:R, :], [B_o], [])
        while pend:
            pend.pop(0)()
        if is_sample:
            P.dma("sp", O["conv_s"][layer, 0, 0:14, :], I["sconv"][layer, 0, 16:30, :], [], [], is_out=True)
        A.release(m1)
        phase_end()

        m2 = A.mark()
        tr_mode(False)
        dg = A.alloc([31, 4, 128], BF16); B_dg = Buf()
        pw = A.alloc([4, 512], BF16); B_pw = Buf()
        plw = A.alloc([4, 128], BF16); B_plw = Buf()
        bands = A.alloc([12, 128], BF16)
        bands_s = A.alloc([8, 16], BF16)
        convb = A.alloc([512]); pscale = A.alloc([512]); lng = A.alloc([8]); B_c2 = Buf()
        cwb = A.alloc([512]); B_cwb = Buf()
        qTt = A.alloc([8, 128], BF16); B_qT = Buf()
        qiTt = A.alloc([8, 128], BF16); B_qiT = Buf()
        gat = A.alloc([D], BF16); B_gat = Buf()
        uTb = [A.alloc([4, 160], BF16), A.alloc([4, 160], BF16)]; B_uT = [Buf(), Buf()]
        pinb = [A.alloc([512], BF16), A.alloc([512], BF16)]; B_pin = [Buf(), Buf()]
        NK = NKMAX
        saccs = [A.alloc([NK]), A.alloc([NK])]; B_saccs = [Buf(), Buf()]
        rb = [A.alloc([512], BF16) for _ in range(4)]; B_rb = [Buf() for _ in range(4)]
        Dg = A.alloc([16, 128], BF16); B_Dg = Buf()
        mask = A.alloc([NK], BF16); B_mask = Buf()
        junk, B_junk = mask, B_mask
        maskTs = [A.alloc([NBLK, 128], BF16), A.alloc([NBLK, 128], BF16)]; B_maskTs = [Buf(), Buf()]
        bis = A.alloc([16]); B_bis = Buf()
        Eb = [A.alloc([512], BF16) for _ in range(4)]; B_E = [Buf() for _ in range(4)]
        PTb = [A.alloc([512], BF16) for _ in range(4)]; B_PT = [Buf() for _ in range(4)]
        catb = A.alloc([D], BF16); B_catb = Buf()
        rinv = A.alloc([8]); B_rinv = Buf()
        yb = A.alloc([512]); B_yb = Buf()
        yhb = A.alloc([512], BF16); B_yhb = Buf()
        lnT = A.alloc([4, 128], BF16); B_lnT = Buf()
        rTb = A.alloc([4, 128], BF16); B_rTb = Buf()
        bnb = A.alloc([16]); B_bn = Buf()
        spb = A.alloc([512], BF16); B_spb = Buf()

        P.dma("sp", bands, I["c_bands"], [], [B_c2])
        P.dma("sp", bands_s[:16], I["c_bands_s"], [], [B_c2])
        P.dma("sp", convb, I["conv_b"][layer, :].partition_broadcast(128), [], [B_c2])
        P.dma("sp", pscale, I["pool_scale"][layer, :].partition_broadcast(128), [], [B_c2])
        for g in range(4):
            P.dma("sp", lng[:, g:g + 1], I["conv_ln_g"][layer, g * 128:(g + 1) * 128].unsqueeze(1), [], [B_c2])
            P.dma("sp", lng[:, 4 + g:5 + g], I["conv_ln_b"][layer, g * 128:(g + 1) * 128].unsqueeze(1), [], [B_c2])
        if layer not in p2_cached:
            p2_cached.add(layer)
            for g in range(4):
                P.dma("pool", pw[:, g, :], I["conv_pw"][layer][g * 128:(g + 1) * 128, :], [], [B_pw])
                P.dma("pool", plw[:, g, :], I["pool_w"][layer, g], [], [B_plw])
            P.dma("sp", S_pw[layer], pw.rearrange("p a b -> p (a b)"), [B_pw], [])
            P.dma("sp", S_plw[layer], plw.rearrange("p a b -> p (a b)"), [B_plw], [])
            cwbs = [cwb, yb]
            B_cwbs = [B_cwb, B_yb]
            for k in range(31):
                P.dma("sp", cwbs[k % 2], I["conv_w"][layer, k, :].partition_broadcast(128), [], [B_cwbs[k % 2]])
                P.emit("dve", lambda e, k=k, cw_=cwbs[k % 2]: e.tensor_tensor(dg[:, k, :, :], cw_.rearrange("p (g d) -> p g d", g=4),
                                                                          identf.unsqueeze(1).to_broadcast([128, 4, 128]), ALU.mult),
                       [B_cwbs[k % 2], B_const], [B_dg])
            P.dma("sp", S_dg[layer], dg.rearrange("p a b c -> p (a b c)"), [B_dg], [])
        else:
            P.dma("sp", pw.rearrange("p a b -> p (a b)"), S_pw[layer], [], [B_pw])
            P.dma("sp", plw.rearrange("p a b -> p (a b)"), S_plw[layer], [], [B_plw])
            P.dma("sp", dg.rearrange("p a b c -> p (a b c)"), S_dg[layer], [], [B_dg])
        if is_sample:
            P.dma("pool", spb[:30, :], I["sconv"][layer, 0, :, :], [], [B_spb])
            tr_group([spb[:30, j * 128:(j + 1) * 128] for j in range(4)], 30, uTb[0][:, :, 0:30], [B_spb], [B_uT[0]])
            P.dma("pool", pinb[1][:15, :], I["spool"][layer, 0, :, :], [], [B_pin[1]])
        else:
            P.emit("dve", lambda e: e.memset(uTb[0][:, :, 0:30], 0.0), [], [B_uT[0]])

        def gen_IDX(t):
            R = R_of(t)
            tok0 = t * 128
            cur, prv = t % 2, (t + 1) % 2
            maskT, B_maskT = maskTs[t % 2], B_maskTs[t % 2]
            nk = kbase + tok0 + R
            kblocks = []
            k0 = 0
            while k0 < nk:
                n = min(128, nk - k0)
                kblocks.append((k0 // 128, k0, n))
                k0 += n
            P.dma("sp", qiTt[:, :, :R], S_qiT[t].rearrange("p (a b) -> p a b", b=128)[:, :, :R], [], [B_qiT])
            for h in range(16):
                P.emit("dve", lambda e, h=h, R=R, t=t: e.tensor_scalar(
                    Dg[:R, h, :R], identf[:R, :R], wia[:R, t, 16 + h:17 + h], None, op0=ALU.mult), [B_const, B_wi], [B_Dg])
            yield
            li = [0]
            for c0 in range(0, nk, 512):
                n = min(512, nk - c0)
                prev_ = None

                def diag_mm(i_, n_, h_, R=R):
                    P.emit("pe", lambda e: e.matmul(PB[2][:R, :n_], Dg[:R, h_, :R], rb[i_][:R, :n_],
                                                    start=(h_ == 0), stop=(h_ == 15)), [B_Dg, B_rb[i_]], [PBb[2]])
                for h in range(16):
                    i = li[0] % 2
                    i4 = li[0] % 4
                    li[0] += 1
                    hp, j = h % 2, h // 2
                    P.emit("pe", lambda e, i=i, hp=hp, j=j, c0=c0, n=n, R=R: e.matmul(
                        PB[i][:R, :n], qiTt[hp * 64:(hp + 1) * 64, j, :R], kiT2[hp * 64:(hp + 1) * 64, c0:c0 + n],
                        start=True, stop=True), [B_qiT, B_kiT], [PBb[i]])
                    P.emit("act", lambda e, i=i, i4=i4, n=n, R=R, h=h, t=t: e.activation(
                        rb[i4][:R, :n], PB[i][:R, :n], AF.Relu, scale=wia[:R, t, h:h + 1]), [PBb[i], B_wi], [B_rb[i4]])
                    if prev_ is not None:
                        diag_mm(*prev_)
                    prev_ = (i4, n, h)
                    yield
                diag_mm(*prev_)
                P.emit("act", lambda e, n=n, R=R, c0=c0, t=t: e.copy(saccs[t % 2][:R, c0:c0 + n], PB[2][:R, :n]),
                       [PBb[2]], [B_saccs[t % 2]])
            if not is_sample:
                P.emit("dve", lambda e, tok0=tok0: e.tensor_tensor(saccs[t % 2][:, tok0:tok0 + 128], saccs[t % 2][:, tok0:tok0 + 128],
                                                                 negmask, ALU.add), [B_saccs[t % 2], B_const], [B_saccs[t % 2]])

        def gen_BIS(t):
            R = R_of(t)
            tok0 = t * 128
            cur, prv = t % 2, (t + 1) % 2
            maskT, B_maskT = maskTs[t % 2], B_maskTs[t % 2]
            nk = kbase + tok0 + R
            kblocks = []
            k0 = 0
            while k0 < nk:
                n = min(128, nk - k0)
                kblocks.append((k0 // 128, k0, n))
                k0 += n
            svis_min = nk if is_sample else (tok0 + 64)
            if svis_min > topk:
                P.emit("dve", lambda e, R=R: e.memset(bis[:R, 0:1], 0.0), [], [B_bis])
                w = W0
                for it in range(NIT):
                    a, bnew = it % 2, (it + 1) % 2
                    w = w / 2.0
                    P.emit("dve", lambda e, R=R, nk=nk, a=a: e.tensor_scalar(
                        junk[:R, :nk], saccs[t % 2][:R, :nk], bis[:R, a:a + 1], 0.0, op0=ALU.is_ge, op1=ALU.add,
                        accum_out=bis[:R, 2:3]), [B_saccs[t % 2], B_bis], [B_junk, B_bis])
                    P.emit("dve", lambda e, R=R, w=w: e.tensor_scalar(
                        bis[:R, 3:4], bis[:R, 2:3], float(topk), 2.0 * w, op0=ALU.is_ge, op1=ALU.mult),
                        [B_bis], [B_bis])
                    P.emit("dve", lambda e, R=R, w=w, a=a, bnew=bnew: e.tensor_scalar(
                        bis[:R, bnew:bnew + 1], bis[:R, 3:4], -w, bis[:R, a:a + 1], op0=ALU.add, op1=ALU.add),
                        [B_bis], [B_bis])
                    yield
                fin = NIT % 2
                P.emit("dve", lambda e, R=R, w=w, fin=fin: e.tensor_scalar(
                    bis[:R, 4:5], bis[:R, fin:fin + 1], -w, None, op0=ALU.add), [B_bis], [B_bis])
            else:
                P.emit("dve", lambda e, R=R: e.memset(bis[:R, 4:5], -1e29), [], [B_bis])
            P.emit("dve", lambda e, R=R, nk=nk: e.tensor_scalar(
                mask[:R, :nk], saccs[t % 2][:R, :nk], bis[:R, 4:5], None, op0=ALU.is_ge), [B_saccs[t % 2], B_bis], [B_mask])
            for (blk, k0, n) in kblocks:
                tr_group([mask[:R, k0:k0 + n]], R, maskT[:n, blk:blk + 1, :R], [B_mask], [B_maskT], eng="dve")
                yield


        def gen_CD(t):
            R = R_of(t)
            tok0 = t * 128
            cur, prv = t % 2, (t + 1) % 2
            maskT, B_maskT = maskTs[t % 2], B_maskTs[t % 2]
            nk = kbase + tok0 + R
            kblocks = []
            k0 = 0
            while k0 < nk:
                n = min(128, nk - k0)
                kblocks.append((k0 // 128, k0, n))
                k0 += n
            P.dma("sp", qTt[:, :, :R], S_qT[t].rearrange("p (a b) -> p a b", b=128)[:, :, :R], [], [B_qT])
            P.dma("sp", gat[:R, :], S_gates[tok0:tok0 + R, :], [], [B_gat])
            P.dma("sp", uTb[cur][:, :, 30:30 + R], S_uT[t].rearrange("p (a b) -> p a b", b=128)[:, :, :R], [], [B_uT[cur]])
            P.dma("sp", pinb[cur][:R, :], S_pin[tok0:tok0 + R, :], [], [B_pin[cur]])
            yield
            ai = [0]
            nkb = len(kblocks)
            for g in range(2):
                pipe_m = []
                pipe_v = []

                def emit_mult(i, n, blk, R=R):
                    P.emit("dve", lambda e: e.tensor_tensor(
                        PTb[i][:n, 0:4 * R].rearrange("p (a b) -> p a b", b=R),
                        Eb[i][:n, 0:4 * R].rearrange("p (a b) -> p a b", b=R),
                        maskT[:n, blk:blk + 1, :R].to_broadcast([n, 4, R]), ALU.mult), [B_E[i], B_maskT], [B_PT[i]])

                def emit_pv(i, n, blk, bi, g=g, R=R):
                    for hh in range(4):
                        ob_, B_obk = (PB[5], PBb[5]) if hh < 3 else (PB[6], PBb[6])
                        oc = (hh % 3) * 129
                        P.emit("pe", lambda e, hh=hh, ob_=ob_, oc=oc,
                               st_=(bi == 0 and hh in (0, 3)), sp2=(bi == nkb - 1 and hh in (2, 3)): e.matmul(
                            ob_[:R, oc:oc + 129], PTb[i][:n, hh * R:(hh + 1) * R], V1[:n, blk, g, :],
                            start=st_, stop=sp2), [B_PT[i], B_V1], [B_obk])

                for bi, (blk, k0, n) in enumerate(kblocks):
                    i = ai[0] % 4
                    sp_, B_sp = PB[3 + ai[0] % 2], PBb[3 + ai[0] % 2]
                    ai[0] += 1
                    P.emit("pe", lambda e, sp_=sp_, g=g, k0=k0, n=n, R=R: e.matmul(
                        sp_[:n, 0:4 * R].rearrange("p (a b) -> p a b", b=R), kT[:, g, k0:k0 + n],
                        qTt[:, g * 4:(g + 1) * 4, :R], start=True, stop=True), [B_kT, B_qT], [B_sp])
                    P.emit("act", lambda e, sp_=sp_, i=i, n=n, R=R: e.activation(
                        Eb[i][:n, 0:4 * R], sp_[:n, 0:4 * R], AF.Exp, scale=128.0 ** -0.5), [B_sp], [B_E[i]])
                    if pipe_v:
                        emit_pv(*pipe_v.pop(0))
                    if pipe_m:
                        a_ = pipe_m.pop(0)
                        emit_mult(a_[0], a_[1], a_[2])
                        pipe_v.append(a_)
                    pipe_m.append((i, n, blk, bi))
                    yield
                while pipe_m or pipe_v:
                    if pipe_v:
                        emit_pv(*pipe_v.pop(0))
                    if pipe_m:
                        a_ = pipe_m.pop(0)
                        emit_mult(a_[0], a_[1], a_[2])
                        pipe_v.append(a_)
                    yield
                for hh in range(4):
                    h = g * 4 + hh
                    ob_, B_obk = (PB[5], PBb[5]) if hh < 3 else (PB[6], PBb[6])
                    oc = (hh % 3) * 129
                    P.emit("dve", lambda e, ob_=ob_, oc=oc, h=h, R=R: e.reciprocal(rinv[:R, h:h + 1], ob_[:R, oc + 128:oc + 129]),
                           [B_obk], [B_rinv])
                    P.emit("dve", lambda e, ob_=ob_, oc=oc, h=h, R=R: e.scalar_tensor_tensor(
                        out=catb[:R, h * 128:(h + 1) * 128], in0=ob_[:R, oc:oc + 128], scalar=rinv[:R, h:h + 1],
                        in1=gat[:R, h * 128:(h + 1) * 128], op0=ALU.mult, op1=ALU.mult),
                        [B_obk, B_rinv, B_gat], [B_catb])
                yield
            ub = uTb[cur]
            for g in range(4):
                for k in range(31):
                    P.emit("pe", lambda e, g=g, k=k, R=R, ub=ub: e.matmul(
                        PB[3][:R, g * 128:(g + 1) * 128], ub[:, g, k:k + R], dg[:, k, g, :],
                        start=(k == 0), stop=(k == 30)), [B_uT[cur], B_dg], [PBb[3]])
                yield
            if t + 1 < NT:
                P.emit("act", lambda e, ub=ub, nb=uTb[prv]: e.copy(nb[:, :, 0:30], ub[:, :, 128:158]), [B_uT[cur]], [B_uT[prv]])
            yield
            yield
            yield
            yield
            P.emit("dve", lambda e, R=R: e.tensor_tensor(yb[:R, :], PB[3][:R, :], convb[:R, :], ALU.add), [PBb[3], B_c2], [B_yb])
            P.emit("dve", lambda e, R=R: e.bn_stats(bnb[:R, 0:6], yb[:R, :]), [B_yb], [B_bn])
            P.emit("dve", lambda e, R=R: e.bn_aggr(bnb[:R, 6:8], bnb[:R, 0:6]), [B_bn], [B_bn])
            yield
            P.emit("act", lambda e, R=R: e.activation(bnb[:R, 8:9], bnb[:R, 7:8], AF.Sqrt, bias=epsb[:R, 0:1], scale=1.0),
                   [B_bn, B_const], [B_bn])
            yield
            P.emit("dve", lambda e, R=R: e.reciprocal(bnb[:R, 9:10], bnb[:R, 8:9]), [B_bn], [B_bn])
            P.emit("dve", lambda e, R=R: e.tensor_scalar(yhb[:R, :], yb[:R, :], bnb[:R, 6:7], bnb[:R, 9:10],
                                                        op0=ALU.subtract, op1=ALU.mult), [B_yb, B_bn], [B_yhb])
            yield
            i = trc[0] % 2
            trc[0] += 1
            tv = TR[i].rearrange("p (a b) -> p a b", b=128)
            for g in range(4):
                P.emit("pe", lambda e, g=g, R=R, tv=tv: e.transpose(tv[:, g, :R], yhb[:R, g * 128:(g + 1) * 128], ident[:R, :R]),
                       [B_yhb, B_const], [TRB[i]])
            yield
            for g in range(4):
                P.emit("act", lambda e, g=g, R=R, tv=tv: e.activation(lnT[:, g, :R], tv[:, g, :R], AF.Silu,
                                                                    bias=lng[:, 4 + g:5 + g], scale=lng[:, g:g + 1]),
                       [TRB[i], B_c2], [B_lnT])
            yield
            for g in range(4):
                P.emit("pe", lambda e, g=g, R=R: e.matmul(PB[4][:R, :], lnT[:, g, :R], pw[:, g, :], start=(g == 0), stop=(g == 3)),
                       [B_lnT, B_pw], [PBb[4]])
            yield
            yield
            P.emit("dve", lambda e, R=R: e.tensor_tensor(catb[:R, 1024:1536], PB[4][:R, :], gat[:R, 1024:1536], ALU.mult),
                   [PBb[4], B_gat], [B_catb])
            yield
            pc, pp_ = pinb[cur], pinb[prv]
            for g in range(4):
                outp = PB[3][:, g * 128:g * 128 + R]
                if is_sample:
                    P.emit("pe", lambda e, g=g, outp=outp, pc=pc: e.matmul(outp, pc[:TS, g * 128:(g + 1) * 128], bands_s[:TS, g, :],
                                                                          start=True, stop=False), [B_pin[cur], B_c2], [PBb[3]])
                    P.emit("pe", lambda e, g=g, outp=outp, pp_=pp_: e.matmul(outp, pp_[:15, g * 128:(g + 1) * 128], bands_s[:15, 4 + g, :],
                                                                            start=False, stop=True), [B_pin[prv], B_c2], [PBb[3]])
                elif t == 0:
                    P.emit("pe", lambda e, g=g, outp=outp, pc=pc: e.matmul(outp, pc[:, g * 128:(g + 1) * 128], bands[:, 8 + g, :],
                                                                          start=True, stop=True), [B_pin[cur], B_c2], [PBb[3]])
                else:
                    P.emit("pe", lambda e, g=g, outp=outp, pc=pc: e.matmul(outp, pc[:, g * 128:(g + 1) * 128], bands[:, g, :],
                                                                          start=True, stop=False), [B_pin[cur], B_c2], [PBb[3]])
                    P.emit("pe", lambda e, g=g, outp=outp, pp_=pp_: e.matmul(outp, pp_[64:128, g * 128:(g + 1) * 128], bands[64:128, 4 + g, :],
                                                                            start=False, stop=True), [B_pin[prv], B_c2], [PBb[3]])
            yield
            P.emit("act", lambda e, R=R: e.copy(rTb[:, :, :R], PB[3][:, :].rearrange("p (a b) -> p a b", b=128)[:, :, :R]),
                   [PBb[3]], [B_rTb])
            yield
            for g in range(4):
                P.emit("pe", lambda e, g=g, R=R: e.matmul(PB[4][:R, g * 128:(g + 1) * 128], rTb[:, g, :R], plw[:, g, :],
                                                         start=True, stop=True), [B_rTb, B_plw], [PBb[4]])
            yield
            yield
            P.emit("dve", lambda e, R=R: e.tensor_tensor(yb[:R, :], PB[4][:R, :], pscale[:R, :], ALU.mult),
                   [PBb[4], B_c2], [B_yb])
            P.emit("dve", lambda e, R=R: e.tensor_tensor(catb[:R, 1536:2048], yb[:R, :], gat[:R, 1536:2048], ALU.mult),
                   [B_yb, B_gat], [B_catb])
            yield
            for c4 in range(4):
                srcs = [catb[:R, (c4 * 4 + j) * 128:(c4 * 4 + j + 1) * 128] for j in range(4)]
                tr_group(srcs, R, actT[:, c4 * 4:c4 * 4 + 4, tok0:tok0 + R], [B_catb], [B_actT])

        def interleave(gens):
            gens = list(gens)
            while gens:
                for g_ in list(gens):
                    try:
                        next(g_)
                    except StopIteration:
                        gens.remove(g_)

        def chain(*gs):
            for g_ in gs:
                for _ in g_:
                    yield

        def interleave_n(items):
            st_ = [[g_, max(1, l_), 0] for (g_, l_) in items]
            while st_:
                st_.sort(key=lambda x: x[2] / x[1])
                x = st_[0]
                try:
                    next(x[0])
                    x[2] += 1
                except StopIteration:
                    st_.remove(x)

        def nblk_of(t):
            return (kbase + t * 128 + R_of(t) + 127) // 128

        def len_idx(t):
            return 16 * ((kbase + t * 128 + R_of(t) + 511) // 512) + 1

        def len_bis(t):
            return NIT + nblk_of(t) + 2

        def len_cd(t):
            return 2 * nblk_of(t) + 28

        for _ in gen_IDX(0):
            pass
        items = [(gen_BIS(0), len_bis(0))]
        if NT > 1:
            items.append((gen_IDX(1), len_idx(1)))
        interleave_n(items)
        for t in range(NT):
            items = [(gen_CD(t), len_cd(t))]
            if t + 1 < NT:
                items.append((gen_BIS(t + 1), len_bis(t + 1)))
            if t + 2 < NT:
                items.append((gen_IDX(t + 2), len_idx(t + 2)))
            interleave_n(items)
        A.release(m2)
        phase_end()

        m3 = A.mark()
        tr_mode(True)
        stq3a = "sp" if ("w_out", layer, 0) not in converted else "pool"
        alloc_wbuf(("w_out", layer, 0) not in converted)
        xr = [A.alloc([512]), A.alloc([512])]; B_xr = [Buf(), Buf()]
        x1o = [A.alloc([512]), A.alloc([512])]; B_x1o = [Buf(), Buf()]
        ec = [0]
        nxt_w = load_wblock("w_out", layer, [(0, 512)], 16, 0)
        for nb in range(4):
            wb, B_wb, ncols = nxt_w
            if nb + 1 < 4:
                nxt_w = load_wblock("w_out", layer, [((nb + 1) * 512, 512)], 16, nb + 1)
            for t in range(NT):
                R = R_of(t)
                tok0 = t * 128
                i = ec[0] % 2
                ib = ec[0] % 4
                ec[0] += 1
                P.dma("sp", xr[i][:R, :], x_src[tok0:tok0 + R, nb * 512:(nb + 1) * 512], [], [B_xr[i]])
                for kc in range(16):
                    P.emit("pe", lambda e, kc=kc, ib=ib, R=R, tok0=tok0, wb=wb: e.matmul(
                        PB[ib][:R, :], actT[:, kc, tok0:tok0 + R], wb[:, kc, :], start=(kc == 0), stop=(kc == 15)),
                        [B_actT, B_wb], [PBb[ib]])
                P.emit("dve", lambda e, i=i, ib=ib, R=R: e.tensor_tensor(x1o[i][:R, :], PB[ib][:R, :], xr[i][:R, :], ALU.add),
                       [PBb[ib], B_xr[i]], [B_x1o[i]])
                P.dma(stq3a, S_x1[tok0:tok0 + R, nb * 512:(nb + 1) * 512], x1o[i][:R, :], [B_x1o[i]], [])
        A.release(m3)
        phase_end()

        m4 = A.mark()
        tr_mode(True)
        stq3b = "sp" if ("w_ple_gate", layer, 0) not in converted else "pool"
        alloc_wbuf(("w_ple_gate", layer, 0) not in converted)
        sc = make_sc()
        pT = A.alloc([2, Tn if not is_sample else 128], BF16); B_pT = Buf()
        pf = A.alloc([256]); B_pf = Buf()
        pb_ = A.alloc([256], BF16); B_pb = Buf()
        wp = [A.alloc([2, 512], BF16), A.alloc([2, 512], BF16)]; B_wp = [Buf(), Buf()]
        xr2 = [A.alloc([512]), A.alloc([512])]; B_xr2 = [Buf(), Buf()]
        sg = [A.alloc([512]), A.alloc([512])]; B_sg = [Buf(), Buf()]
        x2o = [A.alloc([512]), A.alloc([512])]; B_x2o = [Buf(), Buf()]
        gbc_h[0] = A.alloc([D])
        P.dma("sp", gbc_h[0], I["norm_ple"][layer, :].partition_broadcast(128), [], [B_gbc])
        nxt_w = load_wblock("w_ple_gate", layer, [(0, 512)], 16, 0)
        pend = []
        for t in range(NT):
            R = R_of(t)
            tok0 = t * 128
            P.dma("sp", sc["xt"][t % 2][:R, :], S_x1[tok0:tok0 + R, :], [], [sc["B_xt"][t % 2]])
            pb_fn = rmsnorm_T(t % 2, R, tok0, sc)
            while pend:
                pend.pop(0)()
            pend.append(pb_fn)
            P.dma("sp", pf[:R, :], p_src[tok0:tok0 + R, :], [], [B_pf])
            P.emit("dve", lambda e, R=R: e.tensor_copy(pb_[:R, :], pf[:R, :]), [B_pf], [B_pb])
            tr_group([pb_[:R, 0:128], pb_[:R, 128:256]], R, pT[:, :, tok0:tok0 + R], [B_pb], [B_pT])
        while pend:
            pend.pop(0)()
        dst = S_xres
        ec = [0]
        for nb in range(4):
            wb, B_wb, ncols = nxt_w
            if nb + 1 < 4:
                nxt_w = load_wblock("w_ple_gate", layer, [((nb + 1) * 512, 512)], 16, nb + 1)
            wi_ = nb % 2
            for kc in range(2):
                P.dma("pool", wp[wi_][:, kc, :], I["w_ple_proj"][layer][kc * 128:(kc + 1) * 128, nb * 512:(nb + 1) * 512],
                      [], [B_wp[wi_]])
            for t in range(NT):
                R = R_of(t)
                tok0 = t * 128
                i = ec[0] % 2
                ig = ec[0] % 4
                ip = 4 + ec[0] % 3
                ec[0] += 1
                P.dma("sp", xr2[i][:R, :], S_x1[tok0:tok0 + R, nb * 512:(nb + 1) * 512], [], [B_xr2[i]])
                for kc in range(16):
                    P.emit("pe", lambda e, kc=kc, ig=ig, R=R, tok0=tok0, wb=wb: e.matmul(
                        PB[ig][:R, :], actT[:, kc, tok0:tok0 + R], wb[:, kc, :], start=(kc == 0), stop=(kc == 15)),
                        [B_actT, B_wb], [PBb[ig]])
                for kc in range(2):
                    P.emit("pe", lambda e, kc=kc, ip=ip, R=R, tok0=tok0, wi_=wi_: e.matmul(
                        PB[ip][:R, :], pT[:, kc, tok0:tok0 + R], wp[wi_][:, kc, :], start=(kc == 0), stop=(kc == 1)),
                        [B_pT, B_wp[wi_]], [PBb[ip]])
                P.emit("act", lambda e, i=i, ig=ig, R=R: e.activation(sg[i][:R, :], PB[ig][:R, :], AF.Sigmoid), [PBb[ig]], [B_sg[i]])
                P.emit("dve", lambda e, i=i, ip=ip, R=R: e.tensor_tensor(sg[i][:R, :], sg[i][:R, :], PB[ip][:R, :], ALU.mult),
                       [B_sg[i], PBb[ip]], [B_sg[i]])
                P.emit("dve", lambda e, i=i, R=R: e.tensor_tensor(x2o[i][:R, :], sg[i][:R, :], xr2[i][:R, :], ALU.add),
                       [B_sg[i], B_xr2[i]], [B_x2o[i]])
                P.dma(stq3b, dst[tok0:tok0 + R, nb * 512:(nb + 1) * 512], x2o[i][:R, :], [B_x2o[i]], [])
        A.release(m4)
        phase_end()

        if last_layer:
            m5 = A.mark()
            xt = [A.alloc([D]), A.alloc([D])]; B_xt = [Buf(), Buf()]
            jk = A.alloc([D], BF16); B_jk = Buf()
            ss = A.alloc([8]); B_ss = Buf()
            yo = [A.alloc([D]), A.alloc([D])]; B_yo = [Buf(), Buf()]
            gbc = A.alloc([D])
            P.dma("sp", gbc, I["norm_final"][0, :].partition_broadcast(128), [], [B_gbc])
            ydst = O["y_s"][0] if is_sample else O["y_p"][seq]
            for t in range(NT):
                R = R_of(t)
                tok0 = t * 128
                i = t % 2
                P.dma("sp", xt[i][:R, :], S_xres[tok0:tok0 + R, :], [], [B_xt[i]])
                P.emit("act", lambda e, i=i, R=R: e.activation(jk[:R, :], xt[i][:R, :], AF.Square, accum_out=ss[:R, 0:1]),
                       [B_xt[i]], [B_jk, B_ss])
                P.emit("act", lambda e, R=R: e.activation(ss[:R, 1:2], ss[:R, 0:1], AF.Sqrt, bias=epsb[:R, 0:1], scale=1.0 / D),
                       [B_ss, B_const], [B_ss])
                P.emit("dve", lambda e, R=R: e.reciprocal(ss[:R, 2:3], ss[:R, 1:2]), [B_ss], [B_ss])
                P.emit("dve", lambda e, i=i, R=R: e.scalar_tensor_tensor(out=yo[i][:R, :], in0=xt[i][:R, :], scalar=ss[:R, 2:3],
                                                                        in1=gbc[:R, :], op0=ALU.mult, op1=ALU.mult),
                       [B_xt[i], B_ss, B_gbc], [B_yo[i]])
                P.dma("pool", ydst[tok0:tok0 + R, :], yo[i][:R, :], [B_yo[i]], [], is_out=True)
            A.release(m5)
            phase_end()

    seqs = [(s, False) for s in range(NP)] + ([(0, True)] if has_sample else [])
    try:
        for (s, is_s) in seqs:
            for layer in range(L):
                if layer == 0:
                    x_src = I["xs"][0] if is_s else I["xp"][s]
                else:
                    x_src = S_xres
                run_seq_layer(s, layer, is_s, x_src, layer == L - 1)
    except _Stop:
        pass
    P.finish()
    P.build(st)
    st.close()
    return nc, P


def make_consts(T):
    c = {}
    c["c_ident_bf"] = np.eye(128, dtype=np.float32).astype(ml_dtypes.bfloat16)
    c["c_ident_f"] = np.eye(128, dtype=np.float32)

    def rope_tab(pos):
        out = np.zeros((len(pos), 192), np.float32)
        for half, o in ((64, 0), (32, 128)):
            freq = (np.float32(10000.0) ** (-np.arange(half, dtype=np.float32) / np.float32(half))).astype(np.float32)
            ang = pos.astype(np.float32)[:, None] * freq[None, :]
            out[:, o:o + half] = np.cos(ang)
            out[:, o + half:o + 2 * half] = np.sin(ang)
        return out
    c["c_rope_p"] = rope_tab(np.arange(T))
    c["c_rope_s"] = rope_tab(PAST + np.arange(TS))
    nm = np.zeros((128, 128), np.float32)
    nm[:64, 64:] = -1e30
    c["c_negmask"] = nm
    bands = np.zeros((128, 12, 128), np.float32)
    bs = np.zeros((16, 8, 16), np.float32)
    for gi, w in enumerate((2, 4, 8, 16)):
        for tok in range(128):
            for j in range(tok - w + 1, tok + 1):
                if j >= 0:
                    bands[j, gi, tok] += 1.0 / w
                    bands[j, 8 + gi, tok] += 1.0 / min(tok + 1, w)
                else:
                    bands[128 + j, 4 + gi, tok] += 1.0 / w
            bands[tok, gi, tok] -= 1.0
            bands[tok, 8 + gi, tok] -= 1.0
        for tok in range(16):
            for j in range(tok - w + 1, tok + 1):
                if j >= 0:
                    bs[j, gi, tok] += 1.0 / w
                else:
                    bs[15 + j, 4 + gi, tok] += 1.0 / w
            bs[tok, gi, tok] -= 1.0
    c["c_bands"] = bands.astype(ml_dtypes.bfloat16)
    c["c_bands_s"] = bs.astype(ml_dtypes.bfloat16)
    return c


_CACHE = {}


def run(inputs, n_cores, NP, T, L, has_sample=True, dbg_stop=None):
    key = (NP, T, L, has_sample, dbg_stop)
    if key not in _CACHE:
        _CACHE[key] = build_program(NP, T, L, has_sample, dbg_stop=dbg_stop)
    nc, P = _CACHE[key]
    f = lambda a: np.ascontiguousarray(np.asarray(a, dtype=np.float32))
    consts = make_consts(T)
    wnames = ["norm_mix", "w_in", "conv_w", "conv_b", "conv_ln_g", "conv_ln_b", "conv_pw", "pool_w", "pool_scale",
              "w_out", "norm_ple", "w_ple_gate", "w_ple_proj"]
    shared = {n: f(inputs[n]) for n in wnames}
    shared["norm_final"] = f(inputs["norm_final"]).reshape(1, D)
    shared.update(consts)
    in_maps = []
    for c in range(n_cores):
        m = dict(shared)
        m["xp"] = f(inputs["x_prompt"][c * NP:(c + 1) * NP])
        m["pp"] = f(inputs["p_prompt"][:, c * NP:(c + 1) * NP])
        m["xs"] = f(inputs["x_sample"][c:c + 1])
        m["ps"] = f(inputs["p_sample"][:, c:c + 1])
        m["ck"] = f(inputs["cache_k"][:, c:c + 1]).reshape(L, 1, PAST, 256)
        m["cv"] = f(inputs["cache_v"][:, c:c + 1]).reshape(L, 1, PAST, 256)
        m["cki"] = f(inputs["cache_kidx"][:, c:c + 1])
        m["sconv"] = f(inputs["state_conv"][:, c:c + 1])
        m["spool"] = f(inputs["state_pool"][:, c:c + 1])
        in_maps.append(m)
    res = run_bass_kernel_spmd(nc, in_maps, core_ids=list(range(n_cores)))
    rs = res.results
    cat = lambda name, ax: np.concatenate([r[name] for r in rs], axis=ax)
    y_p = cat("y_p", 0)
    y_s = cat("y_s", 0)
    k_p = cat("k_p", 1).reshape(L, n_cores * NP, T, 2, 128)
    v_p = cat("v_p", 1).reshape(L, n_cores * NP, T, 2, 128)
    ki_p = cat("ki_p", 1)
    conv_p = cat("conv_p", 1)
    pool_p = cat("pool_p", 1)
    k_s = cat("k_s", 1).reshape(L, n_cores, TS, 2, 128)
    v_s = cat("v_s", 1).reshape(L, n_cores, TS, 2, 128)
    ki_s = cat("ki_s", 1)
    conv_s = cat("conv_s", 1)
    pool_s = cat("pool_s", 1)
    return (y_p, y_s, k_p, v_p, ki_p, conv_p, pool_p, k_s, v_s, ki_s, conv_s, pool_s)


def kernel(**inputs):
    outs = run(inputs, 8, 2, 2048, 2, True)
    return tuple(np.asarray(o, dtype=np.float32) for o in outs)
```

```python
from contextlib import ExitStack
import numpy as np
import ml_dtypes
import concourse.bass as bass
import concourse.mybir as mybir
from concourse.bass_utils import run_bass_kernel_spmd

F32 = mybir.dt.float32
BF16 = mybir.dt.bfloat16
AF = mybir.ActivationFunctionType
ALU = mybir.AluOpType

NPOOL = 16
EPOCH = 30000

D = 2048
NIN = 6224
PLE = 256
PAST = 1024
TS = 16
EPS = 1e-6
NIT = 22
W0 = 8.0


class Buf:
    __slots__ = ("name", "w", "r")

    def __init__(self, name=""):
        self.name = name
        self.w = None
        self.r = {}


class Op:
    __slots__ = ("id", "eng", "fn", "deps", "dma", "sig")


class Prog:
    ENGS = ("pe", "act", "dve", "pool", "sp")

    def __init__(self, nc):
        self.nc = nc
        self.ops = []
        self.by_eng = {e: [] for e in self.ENGS}
        self.out_dmas = []
        self.last = {e: None for e in self.ENGS}
        self.dmas_since_bar = []

    def emit(self, eng, fn, reads=(), writes=(), dma=False, out=False):
        op = Op()
        op.id = len(self.ops)
        op.eng = eng
        op.fn = fn
        op.dma = dma
        op.sig = None
        deps = {}
        for b in reads:
            if b.w is not None:
                deps[b.w] = "RAW"
        for b in writes:
            if b.w is not None:
                deps.setdefault(b.w, "WAW")
            for r in b.r.values():
                deps.setdefault(r, "WAR")
        key = ("dma", op.id) if dma else eng
        for b in reads:
            b.r[key] = op.id
        for b in writes:
            b.w = op.id
            b.r = {}
        op.deps = deps
        self.ops.append(op)
        self.by_eng[eng].append(op)
        if dma:
            self.dmas_since_bar.append(op.id)
        else:
            self.last[eng] = op.id
        if out:
            self.out_dmas.append(op.id)
        return op

    def dma(self, eng, out, in_, reads, writes, is_out=False):
        return self.emit(eng, lambda e: e.dma_start(out=out, in_=in_), reads, writes, dma=True, out=is_out)

    def barrier(self):
        deps = {}
        for e in self.ENGS:
            if self.last[e] is not None:
                deps[self.last[e]] = "RAW"
        for d in self.dmas_since_bar:
            deps[d] = "RAW"
        self.dmas_since_bar = []
        for e in self.ENGS:
            op = Op()
            op.id = len(self.ops)
            op.eng = e
            op.fn = None
            op.dma = False
            op.sig = None
            op.deps = {k: ("BAR" if self.ops[k].eng != e or self.ops[k].dma else "WAW") for k in deps}
            self.ops.append(op)
            self.by_eng[e].append(op)

    def _skip(self, o, d, kind):
        if d.dma or o.dma:
            return False
        if o.eng == d.eng:
            if o.eng == "pe":
                return True
            return kind != "RAW"
        return False

    def finish(self):
        op = Op()
        op.id = len(self.ops)
        op.eng = "sp"
        op.fn = None
        op.dma = False
        op.sig = None
        op.deps = {d: "RAW" for d in self.out_dmas}
        self.ops.append(op)
        self.by_eng["sp"].append(op)

    def build(self, stack):
        nc = self.nc
        ops = self.ops
        for q in ("sp", "pool", "act"):
            dma_ops = [o for o in ops if o.dma and o.eng == q]
            for j, o in enumerate(dma_ops):
                o.sig = (("dma", q + str(j % NPOOL)), 16 * (j // NPOOL + 1))
                if j >= NPOOL:
                    o.deps.setdefault(dma_ops[j - NPOOL].id, "GUARD")
        needed = set()
        for o in ops:
            for d, kind in o.deps.items():
                if not self._skip(o, ops[d], kind):
                    needed.add(d)
        cnt = {e: 0 for e in self.ENGS}
        semkeys = set()
        for o in ops:
            if o.dma:
                semkeys.add(o.sig[0])
            elif o.id in needed:
                c = cnt[o.eng]
                o.sig = ((o.eng, c // EPOCH), c % EPOCH + 1)
                cnt[o.eng] = c + 1
                semkeys.add(o.sig[0])
        sems = {}
        for k in sorted(semkeys, key=str):
            sems[k] = stack.enter_context(nc.semaphore("s_%s_%s" % (k[0], k[1])))
        self.n_sems = len(sems)
        self.n_waits = 0
        block = stack.enter_context(nc.Block())

        def run(engname, e):
            waited = {}
            for o in self.by_eng[engname]:
                req = {}
                for d in o.deps:
                    dop = ops[d]
                    if self._skip(o, dop, o.deps[d]):
                        continue
                    sk, val = dop.sig
                    if sk[0] == "dma":
                        if req.get(sk, 0) < val:
                            req[sk] = val
                    else:
                        cur = req.get(sk[0], (-1, 0))
                        if cur < (sk[1], val):
                            req[sk[0]] = (sk[1], val)
                for k in sorted(req, key=str):
                    v = req[k]
                    if isinstance(k, tuple):
                        if waited.get(k, 0) >= v:
                            continue
                        waited[k] = v
                        e.wait_ge(sems[k], v)
                    else:
                        if waited.get(k, (-1, 0)) >= v:
                            continue
                        waited[k] = v
                        e.wait_ge(sems[(k, v[0])], v[1])
                    self.n_waits += 1
                if o.fn is None:
                    continue
                ins = o.fn(e)
                if o.sig is not None:
                    ins.then_inc(sems[o.sig[0]], 16 if o.dma else 1)

        @block.tensor
        def _(e):
            run("pe", e)

        @block.scalar
        def _(e):
            run("act", e)

        @block.vector
        def _(e):
            run("dve", e)

        @block.gpsimd
        def _(e):
            run("pool", e)

        @block.sync
        def _(e):
            run("sp", e)


COLBLOCKS = [
    ("q", [(0, 512)], 0), ("q", [(512, 512)], 1),
    ("kv", [(1024, 512)], 0),
    ("qi", [(1536, 512)], 0), ("qi", [(2048, 512)], 1),
    ("kiwi", [(2560, 80)], 0),
    ("gate", [(2640, 512)], 0), ("gate", [(3152, 512)], 512),
    ("glu", [(3664, 256), (4176, 256)], 0), ("glu", [(3920, 256), (4432, 256)], 1),
    ("gate", [(4688, 512)], 1024),
    ("pin", [(5200, 512)], 0),
    ("gate", [(5712, 512)], 1536),
]


class Arena:
    def __init__(self, ap, nwords):
        self.ap = ap
        self.n = nwords
        self.off = 0

    def mark(self):
        return self.off

    def release(self, m):
        self.off = m

    def alloc(self, free_shape, dt=F32):
        n = int(np.prod(free_shape))
        w = n if dt == F32 else (n + 1) // 2
        a = self.ap[:, self.off:self.off + w]
        self.off += (w + 7) // 8 * 8
        assert self.off <= self.n, "SBUF arena overflow %d > %d" % (self.off, self.n)
        if dt != F32:
            a = a.bitcast(dt)
        if len(free_shape) == 2:
            a = a.rearrange("p (a b) -> p a b", b=free_shape[1])
        elif len(free_shape) == 3:
            a = a.rearrange("p (a b c) -> p a b c", b=free_shape[1], c=free_shape[2])
        return a


class _Stop(Exception):
    pass


def build_program(NP, T, L, has_sample=True, arena_words=53200, dbg_stop=None):
    nc = bass.Bass("TRN2", target_bir_lowering=False)
    NTP = T // 128
    TOPK_P = min(256, T // 4)
    TOPK_S = min(256, (PAST + TS) // 4)

    def din(name, shape, dt=F32):
        return nc.dram_tensor(name, list(shape), dt, kind="ExternalInput").ap()

    def dout(name, shape, dt=F32):
        return nc.dram_tensor(name, list(shape), dt, kind="ExternalOutput").ap()

    def dscr(name, shape, dt=F32):
        return nc.dram_tensor(name, list(shape), dt).ap()

    I = {}
    I["xp"] = din("xp", [NP, T, D])
    I["pp"] = din("pp", [L, NP, T, PLE])
    I["xs"] = din("xs", [1, TS, D])
    I["ps"] = din("ps", [L, 1, TS, PLE])
    I["ck"] = din("ck", [L, 1, PAST, 256])
    I["cv"] = din("cv", [L, 1, PAST, 256])
    I["cki"] = din("cki", [L, 1, PAST, 64])
    I["sconv"] = din("sconv", [L, 1, 30, 512])
    I["spool"] = din("spool", [L, 1, 15, 512])
    I["norm_mix"] = din("norm_mix", [L, D])
    I["w_in"] = din("w_in", [L, D, NIN])
    I["conv_w"] = din("conv_w", [L, 31, 512])
    I["conv_b"] = din("conv_b", [L, 512])
    I["conv_ln_g"] = din("conv_ln_g", [L, 512])
    I["conv_ln_b"] = din("conv_ln_b", [L, 512])
    I["conv_pw"] = din("conv_pw", [L, 512, 512])
    I["pool_w"] = din("pool_w", [L, 4, 128, 128])
    I["pool_scale"] = din("pool_scale", [L, 512])
    I["w_out"] = din("w_out", [L, D, D])
    I["norm_ple"] = din("norm_ple", [L, D])
    I["w_ple_gate"] = din("w_ple_gate", [L, D, D])
    I["w_ple_proj"] = din("w_ple_proj", [L, PLE, D])
    I["norm_final"] = din("norm_final", [1, D])
    I["c_ident_bf"] = din("c_ident_bf", [128, 128], BF16)
    I["c_ident_f"] = din("c_ident_f", [128, 128])
    I["c_rope_p"] = din("c_rope_p", [T, 192])
    I["c_rope_s"] = din("c_rope_s", [TS, 192])
    I["c_negmask"] = din("c_negmask", [128, 128])
    I["c_bands"] = din("c_bands", [128, 12, 128], BF16)
    I["c_bands_s"] = din("c_bands_s", [16, 8, 16], BF16)

    O = {}
    O["y_p"] = dout("y_p", [NP, T, D])
    O["y_s"] = dout("y_s", [1, TS, D])
    O["k_p"] = dout("k_p", [L, NP, T, 256])
    O["v_p"] = dout("v_p", [L, NP, T, 256])
    O["ki_p"] = dout("ki_p", [L, NP, T, 64])
    O["conv_p"] = dout("conv_p", [L, NP, 30, 512])
    O["pool_p"] = dout("pool_p", [L, NP, 15, 512])
    O["k_s"] = dout("k_s", [L, 1, TS, 256])
    O["v_s"] = dout("v_s", [L, 1, TS, 256])
    O["ki_s"] = dout("ki_s", [L, 1, TS, 64])
    O["conv_s"] = dout("conv_s", [L, 1, 30, 512])
    O["pool_s"] = dout("pool_s", [L, 1, 15, 512])

    S_qT = dscr("s_qT", [NTP, 128, 1024], BF16)
    S_qiT = dscr("s_qiT", [NTP, 128, 1024], BF16)
    S_gates = dscr("s_gates", [T, D], BF16)
    S_uT = dscr("s_uT", [NTP, 128, 512], BF16)
    S_pin = dscr("s_pin", [T, 512], BF16)
    S_dg = dscr("s_dg", [L, 128, 31 * 4 * 128], BF16)
    S_pw = dscr("s_pw", [L, 128, 4 * 512], BF16)
    S_plw = dscr("s_plw", [L, 128, 4 * 128], BF16)
    p2_cached = set()
    S_x1 = dscr("s_x1", [T, D])
    S_xres = dscr("s_xres", [T, D])

    st = ExitStack()
    P = Prog(nc)
    arena_t = st.enter_context(nc.sbuf_tensor("arena", [128, arena_words], F32))
    A = Arena(arena_t[:, :], arena_words)
    banks = [st.enter_context(nc.psum_tensor("pb%d" % i, [128, 512], F32)) for i in range(8)]
    PB = [b[:, :] for b in banks]
    PBb = [Buf("pb%d" % i) for i in range(8)]
    TRb = PB[7].bitcast(BF16)
    TRb6 = PB[6].bitcast(BF16)
    TR = [TRb[:, 0:512], TRb6[:, 0:512]]
    TRB = [PBb[7], PBb[6]]

    def tr_mode(two):
        if two:
            TR[0], TR[1] = TRb[:, 0:512], TRb6[:, 0:512]
            TRB[0], TRB[1] = PBb[7], PBb[6]
        else:
            TR[0], TR[1] = TRb[:, 0:512], TRb[:, 512:1024]
            TRB[0], TRB[1] = PBb[7], PBb[7]
    trc = [0]
    phase_ctr = [0]

    def phase_end():
        P.barrier()
        phase_ctr[0] += 1
        if dbg_stop is not None and phase_ctr[0] >= dbg_stop:
            raise _Stop()

    ident = A.alloc([128], BF16); B_const = Buf("const")
    identf = A.alloc([128])
    negmask = A.alloc([128])
    epsb = A.alloc([8])
    actT = A.alloc([16, T], BF16); B_actT = Buf("actT")
    wbuf = [None, None]
    B_wbuf = [Buf("w0"), Buf("w1")]

    def alloc_wbuf(need_stage):
        wbuf[0] = A.alloc([16, 512], BF16)
        wbuf[1] = A.alloc([16, 512], BF16)
        if need_stage:
            stg[0] = A.alloc([8, 512])
    NKMAX = max(T, PAST + TS)
    NBLK = (NKMAX + 127) // 128
    kT = A.alloc([2, NKMAX], BF16); B_kT = Buf("kT")
    V1 = A.alloc([NBLK, 2, 129], BF16); B_V1 = Buf("V1")
    kiT2 = A.alloc([NKMAX], BF16); B_kiT = Buf("kiT")
    wia = A.alloc([NTP, 32]); B_wi = Buf("wi")
    gbc_h = [None]; B_gbc = Buf("gbc")
    wcnt = [0]

    P.dma("sp", ident, I["c_ident_bf"], [], [B_const])
    P.dma("sp", identf, I["c_ident_f"], [], [B_const])
    P.dma("sp", negmask, I["c_negmask"], [], [B_const])
    P.emit("dve", lambda e: e.memset(epsb, EPS), [], [B_const])
    P.emit("dve", lambda e: e.memset(V1[:, :, :, 128:129], 1.0), [], [B_V1])

    def transpose_to(dst_fn, src, R, ncol, reads, writes, eng="act"):
        raise NotImplementedError

    def tr_group(srcs, R, dst, reads, writes, eng="act"):
        i = trc[0] % 2
        trc[0] += 1
        n = len(srcs)
        c = srcs[0].shape[1]
        tv = TR[i].rearrange("p (a b) -> p a b", b=128)
        for j, s in enumerate(srcs):
            P.emit("pe", lambda e, s=s, j=j: e.transpose(tv[:c, j, :R], s, ident[:R, :R]),
                   reads + [B_const], [TRB[i]])
        if eng == "act":
            P.emit("act", lambda e: e.copy(dst, tv[:c, 0:n, :R]), [TRB[i]], writes)
        else:
            P.emit("dve", lambda e: e.tensor_copy(dst, tv[:c, 0:n, :R]), [TRB[i]], writes)

    WB = {}
    converted = set()
    stg = [None]
    B_stg = Buf("stg")

    NBLK_W = {"w_in": len(COLBLOCKS), "w_out": 4, "w_ple_gate": 4}

    def load_wblock(name, layer, ranges, kch, bid):
        w_ap = I[name][layer]
        key = (name, layer)
        if key not in WB:
            WB[key] = dscr("wb_%s_%d" % (name, layer), [NBLK_W[name], 128, 16 * 512], BF16)
        wsc = WB[key]
        i = wcnt[0] % 2
        wcnt[0] += 1
        o = sum(n for (_, n) in ranges)
        ck = (name, layer, bid)
        if ck in converted:
            P.dma("sp", wbuf[i].rearrange("p a b -> p (a b)"), wsc[bid], [], [B_wbuf[i]])
        else:
            converted.add(ck)
            sg_ = stg[0]
            o2 = 0
            for (c0, n) in ranges:
                for k8 in range(0, kch, 8):
                    for k4 in range(k8, min(kch, k8 + 8), 4):
                        kk = min(4, kch - k4)
                        P.dma("pool", sg_[:, k4 - k8:k4 - k8 + kk, 0:n],
                              w_ap[k4 * 128:(k4 + kk) * 128, c0:c0 + n].rearrange("(kc p) n -> p kc n", p=128), [], [B_stg])
                    for k4 in range(k8, min(kch, k8 + 8), 4):
                        kk = min(4, kch - k4)
                        P.emit("pool", lambda e, k4=k4, kk=kk, k8=k8, o2=o2, n=n, i=i, sg_=sg_: e.tensor_copy(
                            wbuf[i][:, k4:k4 + kk, o2:o2 + n], sg_[:, k4 - k8:k4 - k8 + kk, 0:n]), [B_stg], [B_wbuf[i]])
                o2 += n
            P.dma("pool", wsc[bid], wbuf[i].rearrange("p a b -> p (a b)"), [B_wbuf[i]], [])
        return wbuf[i], B_wbuf[i], o

    def make_sc():
        sc = {}
        sc["xt"] = [A.alloc([D]), A.alloc([D])]; sc["B_xt"] = [Buf(), Buf()]
        sc["junk"] = A.alloc([D], BF16); sc["B_junk"] = Buf()
        sc["xn"] = [A.alloc([D], BF16), A.alloc([D], BF16)]; sc["B_xn"] = [Buf(), Buf()]
        sc["ss"] = A.alloc([8]); sc["B_ss"] = Buf()
        return sc

    def rmsnorm_T(j, R, tok0, sc):
        xt, B_xt = sc["xt"][j], sc["B_xt"][j]
        xn, B_xn = sc["xn"][j], sc["B_xn"][j]
        gbc = gbc_h[0]
        P.emit("act", lambda e: e.activation(sc["junk"][:R, :], xt[:R, :], AF.Square, accum_out=sc["ss"][:R, 0:1]),
               [B_xt], [sc["B_junk"], sc["B_ss"]])
        P.emit("act", lambda e: e.activation(sc["ss"][:R, 1:2], sc["ss"][:R, 0:1], AF.Sqrt, bias=epsb[:R, 0:1], scale=1.0 / D),
               [sc["B_ss"], B_const], [sc["B_ss"]])
        P.emit("dve", lambda e: e.reciprocal(sc["ss"][:R, 2:3], sc["ss"][:R, 1:2]), [sc["B_ss"]], [sc["B_ss"]])
        P.emit("dve", lambda e: e.scalar_tensor_tensor(out=xn[:R, :], in0=xt[:R, :], scalar=sc["ss"][:R, 2:3],
                                                       in1=gbc[:R, :], op0=ALU.mult, op1=ALU.mult),
               [B_xt, sc["B_ss"], B_gbc], [B_xn])

        def part_b():
            for c4 in range(4):
                srcs = [xn[:R, (c4 * 4 + jj) * 128:(c4 * 4 + jj + 1) * 128] for jj in range(4)]
                tr_group(srcs, R, actT[:, c4 * 4:c4 * 4 + 4, tok0:tok0 + R], [B_xn], [B_actT], eng=("act" if c4 % 2 == 0 else "dve"))
        return part_b

    def rope(e_eng, src, dst1, dst2, cos, sin, tmp, R, nh, half, reads, writes_list, B_tmp):
        s4 = src.rearrange("p (h t d) -> p h t d", h=nh, t=2)
        x1 = s4[:, :, 0, :]
        x2 = s4[:, :, 1, :]
        cb = cos.unsqueeze(1).to_broadcast([R, nh, half])
        sb_ = sin.unsqueeze(1).to_broadcast([R, nh, half])
        t4 = tmp[:R, 0:nh * half * 4].rearrange("p (k h d) -> p k h d", k=4, h=nh)
        P.emit("dve", lambda e: e.tensor_tensor(t4[:, 0], x1, cb, ALU.mult), reads, [B_tmp])
        P.emit("dve", lambda e: e.tensor_tensor(t4[:, 1], x2, sb_, ALU.mult), reads, [B_tmp])
        P.emit("dve", lambda e: e.tensor_tensor(t4[:, 2], x1, sb_, ALU.mult), reads, [B_tmp])
        P.emit("dve", lambda e: e.tensor_tensor(t4[:, 3], x2, cb, ALU.mult), reads, [B_tmp])
        P.emit("dve", lambda e: e.tensor_tensor(dst1, t4[:, 0], t4[:, 1], ALU.subtract), [B_tmp], writes_list)
        P.emit("dve", lambda e: e.tensor_tensor(dst2, t4[:, 2], t4[:, 3], ALU.add), [B_tmp], writes_list)

    def run_seq_layer(seq, layer, is_sample, x_src, last_layer):
        Tn = TS if is_sample else T
        NT = 1 if is_sample else NTP
        R_of = (lambda t: TS) if is_sample else (lambda t: 128)
        b = 0 if is_sample else seq
        rope_src = I["c_rope_s"] if is_sample else I["c_rope_p"]
        kbase = PAST if is_sample else 0
        topk = TOPK_S if is_sample else TOPK_P
        kname = "s" if is_sample else "p"
        p_src = I["ps"][layer, 0] if is_sample else I["pp"][layer, seq]

        if is_sample:
            m0 = A.mark()
            tr_mode(True)
            NB0 = PAST // 128
            ckf = [A.alloc([256]), A.alloc([256])]; B_ckf = [Buf(), Buf()]
            cvf = [A.alloc([256]), A.alloc([256])]; B_cvf = [Buf(), Buf()]
            ckif = [A.alloc([64]), A.alloc([64])]; B_ckif = [Buf(), Buf()]
            ckb = [A.alloc([256], BF16), A.alloc([256], BF16)]; B_ckb = [Buf(), Buf()]
            ckib = [A.alloc([128], BF16), A.alloc([128], BF16)]; B_ckib = [Buf(), Buf()]

            def p0_load(blk):
                j = blk % 2
                r0 = blk * 128
                P.dma("sp", ckf[j], I["ck"][layer, 0, r0:r0 + 128, :], [], [B_ckf[j]])
                P.dma("sp", cvf[j], I["cv"][layer, 0, r0:r0 + 128, :], [], [B_cvf[j]])
                P.dma("sp", ckif[j], I["cki"][layer, 0, r0:r0 + 128, :], [], [B_ckif[j]])

            p0_load(0)
            for blk in range(NB0):
                j = blk % 2
                r0 = blk * 128
                if blk + 1 < NB0:
                    p0_load(blk + 1)
                P.emit("act", lambda e, j=j: e.copy(ckb[j], ckf[j]), [B_ckf[j]], [B_ckb[j]])
                P.emit("dve", lambda e, j=j, blk=blk: e.tensor_copy(
                    V1[:, blk, :, 0:128], cvf[j].rearrange("p (g d) -> p g d", g=2)), [B_cvf[j]], [B_V1])
                P.emit("act", lambda e, j=j: e.copy(ckib[j][:, 0:64], ckif[j]), [B_ckif[j]], [B_ckib[j]])
                P.emit("dve", lambda e, j=j: e.tensor_copy(ckib[j][:, 64:128], ckif[j]), [B_ckif[j]], [B_ckib[j]])
                tr_group([ckb[j][:, 0:128], ckb[j][:, 128:256]], 128, kT[:, :, r0:r0 + 128], [B_ckb[j]], [B_kT])
                tr_group([ckib[j][:, 0:128]], 128, kiT2[:, r0:r0 + 128].unsqueeze(1), [B_ckib[j]], [B_kiT], eng="dve")
            A.release(m0)
            phase_end()

        m1 = A.mark()
        tr_mode(True)
        first_use = ("w_in", layer, 0) not in converted
        alloc_wbuf(first_use)
        stq1 = "sp" if first_use else "pool"
        sc = make_sc()
        ropet = A.alloc([NT, 192]); B_rope = Buf()
        zs = [A.alloc([512]), A.alloc([512])]; B_zs = [Buf(), Buf()]
        tmp = A.alloc([1024]); B_tmp = Buf()
        ob = [A.alloc([512], BF16), A.alloc([512], BF16)]; B_ob = [Buf(), Buf()]
        of = [A.alloc([512]), A.alloc([512])]; B_of = [Buf(), Buf()]
        tT = [A.alloc([4, 128], BF16), A.alloc([4, 128], BF16)]; B_tT = [Buf(), Buf()]

        gbc_h[0] = A.alloc([D])
        P.dma("sp", gbc_h[0], I["norm_mix"][layer, :].partition_broadcast(128), [], [B_gbc])
        if is_sample:
            P.dma("sp", ropet[:TS, 0, :], rope_src, [], [B_rope])
        else:
            P.dma("sp", ropet, rope_src.rearrange("(n p) c -> p n c", p=128), [], [B_rope])
        nxt_w = load_wblock("w_in", layer, COLBLOCKS[0][1], 16, 0)
        pend = []
        for t in range(NT):
            R = R_of(t)
            P.dma("sp", sc["xt"][t % 2][:R, :], x_src[t * 128:t * 128 + R, :], [], [sc["B_xt"][t % 2]])
            pb_fn = rmsnorm_T(t % 2, R, t * 128, sc)
            while pend:
                pend.pop(0)()
            pend.append(pb_fn)
        while pend:
            pend.pop(0)()

        ec = [0]
        for bi_, (kind, ranges, arg) in enumerate(COLBLOCKS):
            wb, B_wb, ncols = nxt_w
            if bi_ + 1 < len(COLBLOCKS):
                nxt_w = load_wblock("w_in", layer, COLBLOCKS[bi_ + 1][1], 16, bi_ + 1)
            for t in range(NT):
                R = R_of(t)
                tok0 = t * 128
                i = ec[0] % 2
                ec[0] += 1
                ps, B_ps = PB[(ec[0] - 1) % 4], PBb[(ec[0] - 1) % 4]
                for kc in range(16):
                    P.emit("pe", lambda e, kc=kc, ps=ps, R=R, tok0=tok0, wb=wb, ncols=ncols:
                           e.matmul(ps[:R, :ncols], actT[:, kc, tok0:tok0 + R], wb[:, kc, :ncols],
                                    start=(kc == 0), stop=(kc == 15)),
                           [B_actT, B_wb], [B_ps])
                while pend:
                    pend.pop(0)()
                z, B_z = zs[i], B_zs[i]
                o_b, B_o = ob[i], B_ob[i]
                o_f, B_f = of[i], B_of[i]
                tt_, B_tt = tT[i], B_tT[i]
                cos128 = ropet[:R, t, 0:64]
                sin128 = ropet[:R, t, 64:128]
                cos64 = ropet[:R, t, 128:160]
                sin64 = ropet[:R, t, 160:192]
                if kind == "q":
                    P.emit("act", lambda e, z=z, ps=ps, R=R: e.copy(z[:R, :], ps[:R, :]), [B_ps], [B_z])
                    o4 = o_b[:R, :].rearrange("p (h t d) -> p h t d", h=4, t=2)
                    rope("dve", z[:R, :], o4[:, :, 0, :], o4[:, :, 1, :], cos128, sin128, tmp, R, 4, 64,
                         [B_z, B_rope], [B_o], B_tmp)
                    def pb_(o_b=o_b, R=R, tt_=tt_, B_o=B_o, B_tt=B_tt, t=t, arg=arg):
                        tr_group([o_b[:R, j * 128:(j + 1) * 128] for j in range(4)], R, tt_[:, :, :R], [B_o], [B_tt])
                        P.dma(stq1, S_qT[t, :, arg * 512:(arg + 1) * 512].rearrange("p (a b) -> p a b", b=128)[:, :, :R],
                              tt_[:, :, :R], [B_tt], [])
                    pend.append(pb_)
                elif kind == "qi":
                    P.emit("act", lambda e, z=z, ps=ps, R=R: e.copy(z[:R, :], ps[:R, :]), [B_ps], [B_z])
                    o4 = o_b[:R, :].rearrange("p (h t d) -> p h t d", h=8, t=2)
                    rope("dve", z[:R, :], o4[:, :, 0, :], o4[:, :, 1, :], cos64, sin64, tmp, R, 8, 32,
                         [B_z, B_rope], [B_o], B_tmp)
                    def pb_(o_b=o_b, R=R, tt_=tt_, B_o=B_o, B_tt=B_tt, t=t, arg=arg):
                        tr_group([o_b[:R, j * 128:(j + 1) * 128] for j in range(4)], R, tt_[:, :, :R], [B_o], [B_tt])
                        P.dma(stq1, S_qiT[t, :, arg * 512:(arg + 1) * 512].rearrange("p (a b) -> p a b", b=128)[:, :, :R],
                              tt_[:, :, :R], [B_tt], [])
                    pend.append(pb_)
                elif kind == "kv":
                    P.emit("act", lambda e, z=z, ps=ps, R=R: e.copy(z[:R, :], ps[:R, :]), [B_ps], [B_z])
                    f4 = o_f[:R, 0:256].rearrange("p (h t d) -> p h t d", h=2, t=2)
                    rope("dve", z[:R, 0:256], f4[:, :, 0, :], f4[:, :, 1, :], cos128, sin128, tmp, R, 2, 64,
                         [B_z, B_rope], [B_f], B_tmp)
                    P.dma(stq1, O["k_" + kname][layer, b, tok0:tok0 + R, :], o_f[:R, 0:256], [B_f], [], is_out=True)
                    P.dma(stq1, O["v_" + kname][layer, b, tok0:tok0 + R, :], z[:R, 256:512], [B_z], [], is_out=True)
                    P.emit("act", lambda e, o_b=o_b, o_f=o_f, R=R: e.copy(o_b[:R, 0:256], o_f[:R, 0:256]), [B_f], [B_o])
                    k0 = kbase + tok0
                    pend.append(lambda o_b=o_b, R=R, k0=k0, B_o=B_o: tr_group(
                        [o_b[:R, 0:128], o_b[:R, 128:256]], R, kT[:, :, k0:k0 + R], [B_o], [B_kT]))
                    blk = k0 // 128
                    P.emit("dve", lambda e, z=z, R=R, blk=blk: e.tensor_copy(
                        V1[:R, blk, :, 0:128], z[:R, 256:512].rearrange("p (g d) -> p g d", g=2)), [B_z], [B_V1])
                elif kind == "kiwi":
                    P.emit("act", lambda e, z=z, ps=ps, R=R: e.copy(z[:R, 0:80], ps[:R, 0:80]), [B_ps], [B_z])
                    f4 = o_f[:R, 0:64].rearrange("p (h t d) -> p h t d", h=1, t=2)
                    rope("dve", z[:R, 0:64], f4[:, :, 0, :], f4[:, :, 1, :], cos64, sin64, tmp, R, 1, 32,
                         [B_z, B_rope], [B_f], B_tmp)
                    P.dma(stq1, O["ki_" + kname][layer, b, tok0:tok0 + R, :], o_f[:R, 0:64], [B_f], [], is_out=True)
                    P.emit("act", lambda e, o_b=o_b, o_f=o_f, R=R: e.copy(o_b[:R, 0:64], o_f[:R, 0:64]), [B_f], [B_o])
                    P.emit("act", lambda e, o_b=o_b, o_f=o_f, R=R: e.copy(o_b[:R, 64:128], o_f[:R, 0:64]), [B_f], [B_o])
                    k0 = kbase + tok0
                    pend.append(lambda o_b=o_b, R=R, k0=k0, B_o=B_o: tr_group(
                        [o_b[:R, 0:128]], R, kiT2[:, k0:k0 + R].unsqueeze(1), [B_o], [B_kiT]))
                    P.emit("act", lambda e, z=z, R=R, t=t: e.activation(wia[:R, t, 0:16], z[:R, 64:80], AF.Abs, scale=1.0 / 32.0),
                           [B_z], [B_wi])
                    P.emit("act", lambda e, z=z, R=R, t=t: e.activation(wia[:R, t, 16:32], z[:R, 64:80], AF.Sign),
                           [B_z], [B_wi])
                elif kind == "gate":
                    P.emit("act", lambda e, o_b=o_b, ps=ps, R=R: e.activation(o_b[:R, :], ps[:R, :], AF.Silu), [B_ps], [B_o])
                    P.dma(stq1, S_gates[tok0:tok0 + R, arg:arg + 512], o_b[:R, :], [B_o], [])
                elif kind == "glu":
                    P.emit("act", lambda e, z=z, ps=ps, R=R: e.activation(z[:R, 0:256], ps[:R, 256:512], AF.Sigmoid), [B_ps], [B_z])
                    P.emit("dve", lambda e, z=z, ps=ps, R=R, o_f=o_f: e.tensor_tensor(o_f[:R, 0:256], ps[:R, 0:256], z[:R, 0:256], ALU.mult),
                           [B_ps, B_z], [B_f])
                    c0 = arg * 256
                    if is_sample:
                        P.dma(stq1, O["conv_s"][layer, 0, 14:30, c0:c0 + 256], o_f[:TS, 0:256], [B_f], [], is_out=True)
                    elif t == NT - 1:
                        P.dma(stq1, O["conv_p"][layer, b, 0:30, c0:c0 + 256], o_f[98:128, 0:256], [B_f], [], is_out=True)
                    P.emit("act", lambda e, o_b=o_b, o_f=o_f, R=R: e.copy(o_b[:R, 0:256], o_f[:R, 0:256]), [B_f], [B_o])
                    def pb_(o_b=o_b, R=R, tt_=tt_, B_o=B_o, B_tt=B_tt, t=t, c0=c0):
                        tr_group([o_b[:R, 0:128], o_b[:R, 128:256]], R, tt_[:, 0:2, :R], [B_o], [B_tt])
                        P.dma(stq1, S_uT[t, :, c0:c0 + 256].rearrange("p (a b) -> p a b", b=128)[:, :, :R],
                              tt_[:, 0:2, :R], [B_tt], [])
                    pend.append(pb_)
                elif kind == "pin":
                    P.emit("act", lambda e, z=z, ps=ps, R=R: e.copy(z[:R, :], ps[:R, :]), [B_ps], [B_z])
                    if is_sample:
                        P.dma(stq1, O["pool_s"][layer, 0, 0:15, :], z[1:16, :], [B_z], [], is_out=True)
                    elif t == NT - 1:
                        P.dma(stq1, O["pool_p"][layer, b, 0:15, :], z[113:128, :], [B_z], [], is_out=True)
                    P.emit("dve", lambda e, z=z, o_b=o_b, R=R: e.tensor_copy(o_b[:R, :], z[:R, :]), [B_z], [B_o])
                    P.dma(stq1, S_pin[tok0:tok0 + R, :], o_b[:R, :], [B_o], [])
        while pend:
            pend.pop(0)()
        if is_sample:
            P.dma(stq1, O["conv_s"][layer, 0, 0:14, :], I["sconv"][layer, 0, 16:30, :], [], [], is_out=True)
        A.release(m1)
        phase_end()

        m2 = A.mark()
        tr_mode(False)
        dg = A.alloc([31, 4, 128], BF16); B_dg = Buf()
        pw = A.alloc([4, 512], BF16); B_pw = Buf()
        plw = A.alloc([4, 128], BF16); B_plw = Buf()
        bands = A.alloc([12, 128], BF16)
        bands_s = A.alloc([8, 16], BF16)
        convb = A.alloc([512]); pscale = A.alloc([512]); lng = A.alloc([8]); B_c2 = Buf()
        cwb = A.alloc([512]); B_cwb = Buf()
        qTt = A.alloc([8, 128], BF16); B_qT = Buf()
        qiTt = A.alloc([8, 128], BF16); B_qiT = Buf()
        gat = A.alloc([D], BF16); B_gat = Buf()
        uTb = [A.alloc([4, 160], BF16), A.alloc([4, 160], BF16)]; B_uT = [Buf(), Buf()]
        pinb = [A.alloc([512], BF16), A.alloc([512], BF16)]; B_pin = [Buf(), Buf()]
        NK = NKMAX
        saccs = [A.alloc([NK]), A.alloc([NK])]; B_saccs = [Buf(), Buf()]
        rb = [A.alloc([512], BF16) for _ in range(4)]; B_rb = [Buf() for _ in range(4)]
        Dg = A.alloc([16, 128], BF16); B_Dg = Buf()
        mask = A.alloc([NK], BF16); B_mask = Buf()
        junk, B_junk = mask, B_mask
        maskTs = [A.alloc([NBLK, 128], BF16), A.alloc([NBLK, 128], BF16)]; B_maskTs = [Buf(), Buf()]
        bis = A.alloc([16]); B_bis = Buf()
        Eb = [A.alloc([512], BF16) for _ in range(4)]; B_E = [Buf() for _ in range(4)]
        PTb = [A.alloc([512], BF16) for _ in range(4)]; B_PT = [Buf() for _ in range(4)]
        catb = A.alloc([D], BF16); B_catb = Buf()
        rinv = A.alloc([8]); B_rinv = Buf()
        yb = A.alloc([512]); B_yb = Buf()
        yhb = A.alloc([512], BF16); B_yhb = Buf()
        lnT = A.alloc([4, 128], BF16); B_lnT = Buf()
        rTb = A.alloc([4, 128], BF16); B_rTb = Buf()
        bnb = A.alloc([16]); B_bn = Buf()
        spb = A.alloc([512], BF16); B_spb = Buf()

        P.dma("sp", bands, I["c_bands"], [], [B_c2])
        P.dma("sp", bands_s[:16], I["c_bands_s"], [], [B_c2])
        P.dma("sp", convb, I["conv_b"][layer, :].partition_broadcast(128), [], [B_c2])
        P.dma("sp", pscale, I["pool_scale"][layer, :].partition_broadcast(128), [], [B_c2])
        for g in range(4):
            P.dma("sp", lng[:, g:g + 1], I["conv_ln_g"][layer, g * 128:(g + 1) * 128].unsqueeze(1), [], [B_c2])
            P.dma("sp", lng[:, 4 + g:5 + g], I["conv_ln_b"][layer, g * 128:(g + 1) * 128].unsqueeze(1), [], [B_c2])
        if layer not in p2_cached:
            p2_cached.add(layer)
            for g in range(4):
                P.dma("pool", pw[:, g, :], I["conv_pw"][layer][g * 128:(g + 1) * 128, :], [], [B_pw])
                P.dma("pool", plw[:, g, :], I["pool_w"][layer, g], [], [B_plw])
            P.dma("sp", S_pw[layer], pw.rearrange("p a b -> p (a b)"), [B_pw], [])
            P.dma("sp", S_plw[layer], plw.rearrange("p a b -> p (a b)"), [B_plw], [])
            cwbs = [cwb, yb]
            B_cwbs = [B_cwb, B_yb]
            for k in range(31):
                P.dma("sp", cwbs[k % 2], I["conv_w"][layer, k, :].partition_broadcast(128), [], [B_cwbs[k % 2]])
                P.emit("dve", lambda e, k=k, cw_=cwbs[k % 2]: e.tensor_tensor(dg[:, k, :, :], cw_.rearrange("p (g d) -> p g d", g=4),
                                                                          identf.unsqueeze(1).to_broadcast([128, 4, 128]), ALU.mult),
                       [B_cwbs[k % 2], B_const], [B_dg])
            P.dma("sp", S_dg[layer], dg.rearrange("p a b c -> p (a b c)"), [B_dg], [])
        else:
            P.dma("sp", pw.rearrange("p a b -> p (a b)"), S_pw[layer], [], [B_pw])
            P.dma("sp", plw.rearrange("p a b -> p (a b)"), S_plw[layer], [], [B_plw])
            P.dma("sp", dg.rearrange("p a b c -> p (a b c)"), S_dg[layer], [], [B_dg])
        if is_sample:
            P.dma("pool", spb[:30, :], I["sconv"][layer, 0, :, :], [], [B_spb])
            tr_group([spb[:30, j * 128:(j + 1) * 128] for j in range(4)], 30, uTb[0][:, :, 0:30], [B_spb], [B_uT[0]])
            P.dma("pool", pinb[1][:15, :], I["spool"][layer, 0, :, :], [], [B_pin[1]])
        else:
            P.emit("dve", lambda e: e.memset(uTb[0][:, :, 0:30], 0.0), [], [B_uT[0]])

        def gen_IDX(t):
            R = R_of(t)
            tok0 = t * 128
            cur, prv = t % 2, (t + 1) % 2
            maskT, B_maskT = maskTs[t % 2], B_maskTs[t % 2]
            nk = kbase + tok0 + R
            kblocks = []
            k0 = 0
            while k0 < nk:
                n = min(128, nk - k0)
                kblocks.append((k0 // 128, k0, n))
                k0 += n
            P.dma("sp", qiTt[:, :, :R], S_qiT[t].rearrange("p (a b) -> p a b", b=128)[:, :, :R], [], [B_qiT])
            for h in range(16):
                P.emit("dve", lambda e, h=h, R=R, t=t: e.tensor_scalar(
                    Dg[:R, h, :R], identf[:R, :R], wia[:R, t, 16 + h:17 + h], None, op0=ALU.mult), [B_const, B_wi], [B_Dg])
            yield
            li = [0]
            for c0 in range(0, nk, 512):
                n = min(512, nk - c0)
                prev_ = None

                def diag_mm(i_, n_, h_, R=R):
                    P.emit("pe", lambda e: e.matmul(PB[2][:R, :n_], Dg[:R, h_, :R], rb[i_][:R, :n_],
                                                    start=(h_ == 0), stop=(h_ == 15)), [B_Dg, B_rb[i_]], [PBb[2]])
                for h in range(16):
                    i = li[0] % 2
                    i4 = li[0] % 4
                    li[0] += 1
                    hp, j = h % 2, h // 2
                    P.emit("pe", lambda e, i=i, hp=hp, j=j, c0=c0, n=n, R=R: e.matmul(
                        PB[i][:R, :n], qiTt[hp * 64:(hp + 1) * 64, j, :R], kiT2[hp * 64:(hp + 1) * 64, c0:c0 + n],
                        start=True, stop=True), [B_qiT, B_kiT], [PBb[i]])
                    P.emit("act", lambda e, i=i, i4=i4, n=n, R=R, h=h, t=t: e.activation(
                        rb[i4][:R, :n], PB[i][:R, :n], AF.Relu, scale=wia[:R, t, h:h + 1]), [PBb[i], B_wi], [B_rb[i4]])
                    if prev_ is not None:
                        diag_mm(*prev_)
                    prev_ = (i4, n, h)
                    yield
                diag_mm(*prev_)
                P.emit("act", lambda e, n=n, R=R, c0=c0, t=t: e.copy(saccs[t % 2][:R, c0:c0 + n], PB[2][:R, :n]),
                       [PBb[2]], [B_saccs[t % 2]])
            if not is_sample:
                P.emit("dve", lambda e, tok0=tok0: e.tensor_tensor(saccs[t % 2][:, tok0:tok0 + 128], saccs[t % 2][:, tok0:tok0 + 128],
                                                                 negmask, ALU.add), [B_saccs[t % 2], B_const], [B_saccs[t % 2]])

        def gen_BIS(t):
            R = R_of(t)
            tok0 = t * 128
            cur, prv = t % 2, (t + 1) % 2
            maskT, B_maskT = maskTs[t % 2], B_maskTs[t % 2]
            nk = kbase + tok0 + R
            kblocks = []
            k0 = 0
            while k0 < nk:
                n = min(128, nk - k0)
                kblocks.append((k0 // 128, k0, n))
                k0 += n
            svis_min = nk if is_sample else (tok0 + 64)
            if svis_min > topk:
                P.emit("dve", lambda e, R=R: e.memset(bis[:R, 0:1], 0.0), [], [B_bis])
                w = W0
                for it in range(NIT):
                    a, bnew = it % 2, (it + 1) % 2
                    w = w / 2.0
                    P.emit("dve", lambda e, R=R, nk=nk, a=a: e.tensor_scalar(
                        junk[:R, :nk], saccs[t % 2][:R, :nk], bis[:R, a:a + 1], 0.0, op0=ALU.is_ge, op1=ALU.add,
                        accum_out=bis[:R, 2:3]), [B_saccs[t % 2], B_bis], [B_junk, B_bis])
                    P.emit("dve", lambda e, R=R, w=w: e.tensor_scalar(
                        bis[:R, 3:4], bis[:R, 2:3], float(topk), 2.0 * w, op0=ALU.is_ge, op1=ALU.mult),
                        [B_bis], [B_bis])
                    P.emit("dve", lambda e, R=R, w=w, a=a, bnew=bnew: e.tensor_scalar(
                        bis[:R, bnew:bnew + 1], bis[:R, 3:4], -w, bis[:R, a:a + 1], op0=ALU.add, op1=ALU.add),
                        [B_bis], [B_bis])
                    yield
                fin = NIT % 2
                P.emit("dve", lambda e, R=R, w=w, fin=fin: e.tensor_scalar(
                    bis[:R, 4:5], bis[:R, fin:fin + 1], -w, None, op0=ALU.add), [B_bis], [B_bis])
            else:
                P.emit("dve", lambda e, R=R: e.memset(bis[:R, 4:5], -1e29), [], [B_bis])
            P.emit("dve", lambda e, R=R, nk=nk: e.tensor_scalar(
                mask[:R, :nk], saccs[t % 2][:R, :nk], bis[:R, 4:5], None, op0=ALU.is_ge), [B_saccs[t % 2], B_bis], [B_mask])
            for (blk, k0, n) in kblocks:
                tr_group([mask[:R, k0:k0 + n]], R, maskT[:n, blk:blk + 1, :R], [B_mask], [B_maskT], eng="dve")
                yield


        def gen_CD(t):
            R = R_of(t)
            tok0 = t * 128
            cur, prv = t % 2, (t + 1) % 2
            maskT, B_maskT = maskTs[t % 2], B_maskTs[t % 2]
            nk = kbase + tok0 + R
            kblocks = []
            k0 = 0
            while k0 < nk:
                n = min(128, nk - k0)
                kblocks.append((k0 // 128, k0, n))
                k0 += n
            P.dma("sp", qTt[:, :, :R], S_qT[t].rearrange("p (a b) -> p a b", b=128)[:, :, :R], [], [B_qT])
            P.dma("sp", gat[:R, :], S_gates[tok0:tok0 + R, :], [], [B_gat])
            P.dma("sp", uTb[cur][:, :, 30:30 + R], S_uT[t].rearrange("p (a b) -> p a b", b=128)[:, :, :R], [], [B_uT[cur]])
            P.dma("sp", pinb[cur][:R, :], S_pin[tok0:tok0 + R, :], [], [B_pin[cur]])
            yield
            ai = [0]
            nkb = len(kblocks)
            for g in range(2):
                pipe_m = []
                pipe_v = []

                def emit_mult(i, n, blk, R=R):
                    P.emit("dve", lambda e: e.tensor_tensor(
                        PTb[i][:n, 0:4 * R].rearrange("p (a b) -> p a b", b=R),
                        Eb[i][:n, 0:4 * R].rearrange("p (a b) -> p a b", b=R),
                        maskT[:n, blk:blk + 1, :R].to_broadcast([n, 4, R]), ALU.mult), [B_E[i], B_maskT], [B_PT[i]])

                def emit_pv(i, n, blk, bi, g=g, R=R):
                    for hh in range(4):
                        ob_, B_obk = (PB[5], PBb[5]) if hh < 3 else (PB[6], PBb[6])
                        oc = (hh % 3) * 129
                        P.emit("pe", lambda e, hh=hh, ob_=ob_, oc=oc,
                               st_=(bi == 0 and hh in (0, 3)), sp2=(bi == nkb - 1 and hh in (2, 3)): e.matmul(
                            ob_[:R, oc:oc + 129], PTb[i][:n, hh * R:(hh + 1) * R], V1[:n, blk, g, :],
                            start=st_, stop=sp2), [B_PT[i], B_V1], [B_obk])

                for bi, (blk, k0, n) in enumerate(kblocks):
                    i = ai[0] % 4
                    sp_, B_sp = PB[3 + ai[0] % 2], PBb[3 + ai[0] % 2]
                    ai[0] += 1
                    P.emit("pe", lambda e, sp_=sp_, g=g, k0=k0, n=n, R=R: e.matmul(
                        sp_[:n, 0:4 * R].rearrange("p (a b) -> p a b", b=R), kT[:, g, k0:k0 + n],
                        qTt[:, g * 4:(g + 1) * 4, :R], start=True, stop=True), [B_kT, B_qT], [B_sp])
                    P.emit("act", lambda e, sp_=sp_, i=i, n=n, R=R: e.activation(
                        Eb[i][:n, 0:4 * R], sp_[:n, 0:4 * R], AF.Exp, scale=128.0 ** -0.5), [B_sp], [B_E[i]])
                    if pipe_v:
                        emit_pv(*pipe_v.pop(0))
                    if pipe_m:
                        a_ = pipe_m.pop(0)
                        emit_mult(a_[0], a_[1], a_[2])
                        pipe_v.append(a_)
                    pipe_m.append((i, n, blk, bi))
                    yield
                while pipe_m or pipe_v:
                    if pipe_v:
                        emit_pv(*pipe_v.pop(0))
                    if pipe_m:
                        a_ = pipe_m.pop(0)
                        emit_mult(a_[0], a_[1], a_[2])
                        pipe_v.append(a_)
                    yield
                for hh in range(4):
                    h = g * 4 + hh
                    ob_, B_obk = (PB[5], PBb[5]) if hh < 3 else (PB[6], PBb[6])
                    oc = (hh % 3) * 129
                    P.emit("dve", lambda e, ob_=ob_, oc=oc, h=h, R=R: e.reciprocal(rinv[:R, h:h + 1], ob_[:R, oc + 128:oc + 129]),
                           [B_obk], [B_rinv])
                    P.emit("dve", lambda e, ob_=ob_, oc=oc, h=h, R=R: e.scalar_tensor_tensor(
                        out=catb[:R, h * 128:(h + 1) * 128], in0=ob_[:R, oc:oc + 128], scalar=rinv[:R, h:h + 1],
                        in1=gat[:R, h * 128:(h + 1) * 128], op0=ALU.mult, op1=ALU.mult),
                        [B_obk, B_rinv, B_gat], [B_catb])
                yield
            ub = uTb[cur]
            for g in range(4):
                for k in range(31):
                    P.emit("pe", lambda e, g=g, k=k, R=R, ub=ub: e.matmul(
                        PB[3][:R, g * 128:(g + 1) * 128], ub[:, g, k:k + R], dg[:, k, g, :],
                        start=(k == 0), stop=(k == 30)), [B_uT[cur], B_dg], [PBb[3]])
                yield
            if t + 1 < NT:
                P.emit("act", lambda e, ub=ub, nb=uTb[prv]: e.copy(nb[:, :, 0:30], ub[:, :, 128:158]), [B_uT[cur]], [B_uT[prv]])
            yield
            yield
            yield
            yield
            P.emit("dve", lambda e, R=R: e.tensor_tensor(yb[:R, :], PB[3][:R, :], convb[:R, :], ALU.add), [PBb[3], B_c2], [B_yb])
            P.emit("dve", lambda e, R=R: e.bn_stats(bnb[:R, 0:6], yb[:R, :]), [B_yb], [B_bn])
            P.emit("dve", lambda e, R=R: e.bn_aggr(bnb[:R, 6:8], bnb[:R, 0:6]), [B_bn], [B_bn])
            yield
            P.emit("act", lambda e, R=R: e.activation(bnb[:R, 8:9], bnb[:R, 7:8], AF.Sqrt, bias=epsb[:R, 0:1], scale=1.0),
                   [B_bn, B_const], [B_bn])
            yield
            P.emit("dve", lambda e, R=R: e.reciprocal(bnb[:R, 9:10], bnb[:R, 8:9]), [B_bn], [B_bn])
            P.emit("dve", lambda e, R=R: e.tensor_scalar(yhb[:R, :], yb[:R, :], bnb[:R, 6:7], bnb[:R, 9:10],
                                                        op0=ALU.subtract, op1=ALU.mult), [B_yb, B_bn], [B_yhb])
            yield
            i = trc[0] % 2
            trc[0] += 1
            tv = TR[i].rearrange("p (a b) -> p a b", b=128)
            for g in range(4):
                P.emit("pe", lambda e, g=g, R=R, tv=tv: e.transpose(tv[:, g, :R], yhb[:R, g * 128:(g + 1) * 128], ident[:R, :R]),
                       [B_yhb, B_const], [TRB[i]])
            yield
            for g in range(4):
                P.emit("act", lambda e, g=g, R=R, tv=tv: e.activation(lnT[:, g, :R], tv[:, g, :R], AF.Silu,
                                                                    bias=lng[:, 4 + g:5 + g], scale=lng[:, g:g + 1]),
                       [TRB[i], B_c2], [B_lnT])
            yield
            for g in range(4):
                P.emit("pe", lambda e, g=g, R=R: e.matmul(PB[4][:R, :], lnT[:, g, :R], pw[:, g, :], start=(g == 0), stop=(g == 3)),
                       [B_lnT, B_pw], [PBb[4]])
            yield
            yield
            P.emit("dve", lambda e, R=R: e.tensor_tensor(catb[:R, 1024:1536], PB[4][:R, :], gat[:R, 1024:1536], ALU.mult),
                   [PBb[4], B_gat], [B_catb])
            yield
            pc, pp_ = pinb[cur], pinb[prv]
            for g in range(4):
                outp = PB[3][:, g * 128:g * 128 + R]
                if is_sample:
                    P.emit("pe", lambda e, g=g, outp=outp, pc=pc: e.matmul(outp, pc[:TS, g * 128:(g + 1) * 128], bands_s[:TS, g, :],
                                                                          start=True, stop=False), [B_pin[cur], B_c2], [PBb[3]])
                    P.emit("pe", lambda e, g=g, outp=outp, pp_=pp_: e.matmul(outp, pp_[:15, g * 128:(g + 1) * 128], bands_s[:15, 4 + g, :],
                                                                            start=False, stop=True), [B_pin[prv], B_c2], [PBb[3]])
                elif t == 0:
                    P.emit("pe", lambda e, g=g, outp=outp, pc=pc: e.matmul(outp, pc[:, g * 128:(g + 1) * 128], bands[:, 8 + g, :],
                                                                          start=True, stop=True), [B_pin[cur], B_c2], [PBb[3]])
                else:
                    P.emit("pe", lambda e, g=g, outp=outp, pc=pc: e.matmul(outp, pc[:, g * 128:(g + 1) * 128], bands[:, g, :],
                                                                          start=True, stop=False), [B_pin[cur], B_c2], [PBb[3]])
                    P.emit("pe", lambda e, g=g, outp=outp, pp_=pp_: e.matmul(outp, pp_[64:128, g * 128:(g + 1) * 128], bands[64:128, 4 + g, :],
                                                                            start=False, stop=True), [B_pin[prv], B_c2], [PBb[3]])
            yield
            P.emit("act", lambda e, R=R: e.copy(rTb[:, :, :R], PB[3][:, :].rearrange("p (a b) -> p a b", b=128)[:, :, :R]),
                   [PBb[3]], [B_rTb])
            yield
            for g in range(4):
                P.emit("pe", lambda e, g=g, R=R: e.matmul(PB[4][:R, g * 128:(g + 1) * 128], rTb[:, g, :R], plw[:, g, :],
                                                         start=True, stop=True), [B_rTb, B_plw], [PBb[4]])
            yield
            yield
            P.emit("dve", lambda e, R=R: e.tensor_tensor(yb[:R, :], PB[4][:R, :], pscale[:R, :], ALU.mult),
                   [PBb[4], B_c2], [B_yb])
            P.emit("dve", lambda e, R=R: e.tensor_tensor(catb[:R, 1536:2048], yb[:R, :], gat[:R, 1536:2048], ALU.mult),
                   [B_yb, B_gat], [B_catb])
            yield
            for c4 in range(4):
                srcs = [catb[:R, (c4 * 4 + j) * 128:(c4 * 4 + j + 1) * 128] for j in range(4)]
                tr_group(srcs, R, actT[:, c4 * 4:c4 * 4 + 4, tok0:tok0 + R], [B_catb], [B_actT])

        def interleave(gens):
            gens = list(gens)
            while gens:
                for g_ in list(gens):
                    try:
                        next(g_)
                    except StopIteration:
                        gens.remove(g_)

        def chain(*gs):
            for g_ in gs:
                for _ in g_:
                    yield

        def interleave_n(items):
            st_ = [[g_, max(1, l_), 0] for (g_, l_) in items]
            while st_:
                st_.sort(key=lambda x: x[2] / x[1])
                x = st_[0]
                try:
                    next(x[0])
                    x[2] += 1
                except StopIteration:
                    st_.remove(x)

        def nblk_of(t):
            return (kbase + t * 128 + R_of(t) + 127) // 128

        def len_idx(t):
            return 16 * ((kbase + t * 128 + R_of(t) + 511) // 512) + 1

        def len_bis(t):
            return NIT + nblk_of(t) + 2

        def len_cd(t):
            return 2 * nblk_of(t) + 28

        for _ in gen_IDX(0):
            pass
        items = [(gen_BIS(0), len_bis(0))]
        if NT > 1:
            items.append((gen_IDX(1), len_idx(1)))
        interleave_n(items)
        for t in range(NT):
            items = [(gen_CD(t), len_cd(t))]
            if t + 1 < NT:
                items.append((gen_BIS(t + 1), len_bis(t + 1)))
            if t + 2 < NT:
                items.append((gen_IDX(t + 2), len_idx(t + 2)))
            interleave_n(items)
        A.release(m2)
        phase_end()

        m3 = A.mark()
        tr_mode(True)
        stq3a = "sp" if ("w_out", layer, 0) not in converted else "pool"
        alloc_wbuf(("w_out", layer, 0) not in converted)
        xr = [A.alloc([512]), A.alloc([512])]; B_xr = [Buf(), Buf()]
        x1o = [A.alloc([512]), A.alloc([512])]; B_x1o = [Buf(), Buf()]
        ec = [0]
        nxt_w = load_wblock("w_out", layer, [(0, 512)], 16, 0)
        for nb in range(4):
            wb, B_wb, ncols = nxt_w
            if nb + 1 < 4:
                nxt_w = load_wblock("w_out", layer, [((nb + 1) * 512, 512)], 16, nb + 1)
            for t in range(NT):
                R = R_of(t)
                tok0 = t * 128
                i = ec[0] % 2
                ib = ec[0] % 4
                ec[0] += 1
                P.dma("sp", xr[i][:R, :], x_src[tok0:tok0 + R, nb * 512:(nb + 1) * 512], [], [B_xr[i]])
                for kc in range(16):
                    P.emit("pe", lambda e, kc=kc, ib=ib, R=R, tok0=tok0, wb=wb: e.matmul(
                        PB[ib][:R, :], actT[:, kc, tok0:tok0 + R], wb[:, kc, :], start=(kc == 0), stop=(kc == 15)),
                        [B_actT, B_wb], [PBb[ib]])
                P.emit("dve", lambda e, i=i, ib=ib, R=R: e.tensor_tensor(x1o[i][:R, :], PB[ib][:R, :], xr[i][:R, :], ALU.add),
                       [PBb[ib], B_xr[i]], [B_x1o[i]])
                P.dma(stq3a, S_x1[tok0:tok0 + R, nb * 512:(nb + 1) * 512], x1o[i][:R, :], [B_x1o[i]], [])
        A.release(m3)
        phase_end()

        m4 = A.mark()
        tr_mode(True)
        stq3b = "sp" if ("w_ple_gate", layer, 0) not in converted else "pool"
        alloc_wbuf(("w_ple_gate", layer, 0) not in converted)
        sc = make_sc()
        pT = A.alloc([2, Tn if not is_sample else 128], BF16); B_pT = Buf()
        pf = A.alloc([256]); B_pf = Buf()
        pb_ = A.alloc([256], BF16); B_pb = Buf()
        wp = [A.alloc([2, 512], BF16), A.alloc([2, 512], BF16)]; B_wp = [Buf(), Buf()]
        xr2 = [A.alloc([512]), A.alloc([512])]; B_xr2 = [Buf(), Buf()]
        sg = [A.alloc([512]), A.alloc([512])]; B_sg = [Buf(), Buf()]
        x2o = [A.alloc([512]), A.alloc([512])]; B_x2o = [Buf(), Buf()]
        gbc_h[0] = A.alloc([D])
        P.dma("sp", gbc_h[0], I["norm_ple"][layer, :].partition_broadcast(128), [], [B_gbc])
        nxt_w = load_wblock("w_ple_gate", layer, [(0, 512)], 16, 0)
        pend = []
        for t in range(NT):
            R = R_of(t)
            tok0 = t * 128
            P.dma("sp", sc["xt"][t % 2][:R, :], S_x1[tok0:tok0 + R, :], [], [sc["B_xt"][t % 2]])
            pb_fn = rmsnorm_T(t % 2, R, tok0, sc)
            while pend:
                pend.pop(0)()
            pend.append(pb_fn)
            P.dma("sp", pf[:R, :], p_src[tok0:tok0 + R, :], [], [B_pf])
            P.emit("dve", lambda e, R=R: e.tensor_copy(pb_[:R, :], pf[:R, :]), [B_pf], [B_pb])
            tr_group([pb_[:R, 0:128], pb_[:R, 128:256]], R, pT[:, :, tok0:tok0 + R], [B_pb], [B_pT])
        while pend:
            pend.pop(0)()
        dst = S_xres
        ec = [0]
        for nb in range(4):
            wb, B_wb, ncols = nxt_w
            if nb + 1 < 4:
                nxt_w = load_wblock("w_ple_gate", layer, [((nb + 1) * 512, 512)], 16, nb + 1)
            wi_ = nb % 2
            for kc in range(2):
                P.dma("pool", wp[wi_][:, kc, :], I["w_ple_proj"][layer][kc * 128:(kc + 1) * 128, nb * 512:(nb + 1) * 512],
                      [], [B_wp[wi_]])
            for t in range(NT):
                R = R_of(t)
                tok0 = t * 128
                i = ec[0] % 2
                ig = ec[0] % 4
                ip = 4 + ec[0] % 3
                ec[0] += 1
                P.dma("sp", xr2[i][:R, :], S_x1[tok0:tok0 + R, nb * 512:(nb + 1) * 512], [], [B_xr2[i]])
                for kc in range(16):
                    P.emit("pe", lambda e, kc=kc, ig=ig, R=R, tok0=tok0, wb=wb: e.matmul(
                        PB[ig][:R, :], actT[:, kc, tok0:tok0 + R], wb[:, kc, :], start=(kc == 0), stop=(kc == 15)),
                        [B_actT, B_wb], [PBb[ig]])
                for kc in range(2):
                    P.emit("pe", lambda e, kc=kc, ip=ip, R=R, tok0=tok0, wi_=wi_: e.matmul(
                        PB[ip][:R, :], pT[:, kc, tok0:tok0 + R], wp[wi_][:, kc, :], start=(kc == 0), stop=(kc == 1)),
                        [B_pT, B_wp[wi_]], [PBb[ip]])
                P.emit("act", lambda e, i=i, ig=ig, R=R: e.activation(sg[i][:R, :], PB[ig][:R, :], AF.Sigmoid), [PBb[ig]], [B_sg[i]])
                P.emit("dve", lambda e, i=i, ip=ip, R=R: e.tensor_tensor(sg[i][:R, :], sg[i][:R, :], PB[ip][:R, :], ALU.mult),
                       [B_sg[i], PBb[ip]], [B_sg[i]])
                P.emit("dve", lambda e, i=i, R=R: e.tensor_tensor(x2o[i][:R, :], sg[i][:R, :], xr2[i][:R, :], ALU.add),
                       [B_sg[i], B_xr2[i]], [B_x2o[i]])
                P.dma(stq3b, dst[tok0:tok0 + R, nb * 512:(nb + 1) * 512], x2o[i][:R, :], [B_x2o[i]], [])
        A.release(m4)
        phase_end()

        if last_layer:
            m5 = A.mark()
            xt = [A.alloc([D]), A.alloc([D])]; B_xt = [Buf(), Buf()]
            jk = A.alloc([D], BF16); B_jk = Buf()
            ss = A.alloc([8]); B_ss = Buf()
            yo = [A.alloc([D]), A.alloc([D])]; B_yo = [Buf(), Buf()]
            gbc = A.alloc([D])
            P.dma("sp", gbc, I["norm_final"][0, :].partition_broadcast(128), [], [B_gbc])
            ydst = O["y_s"][0] if is_sample else O["y_p"][seq]
            for t in range(NT):
                R = R_of(t)
                tok0 = t * 128
                i = t % 2
                P.dma("sp", xt[i][:R, :], S_xres[tok0:tok0 + R, :], [], [B_xt[i]])
                P.emit("act", lambda e, i=i, R=R: e.activation(jk[:R, :], xt[i][:R, :], AF.Square, accum_out=ss[:R, 0:1]),
                       [B_xt[i]], [B_jk, B_ss])
                P.emit("act", lambda e, R=R: e.activation(ss[:R, 1:2], ss[:R, 0:1], AF.Sqrt, bias=epsb[:R, 0:1], scale=1.0 / D),
                       [B_ss, B_const], [B_ss])
                P.emit("dve", lambda e, R=R: e.reciprocal(ss[:R, 2:3], ss[:R, 1:2]), [B_ss], [B_ss])
                P.emit("dve", lambda e, i=i, R=R: e.scalar_tensor_tensor(out=yo[i][:R, :], in0=xt[i][:R, :], scalar=ss[:R, 2:3],
                                                                        in1=gbc[:R, :], op0=ALU.mult, op1=ALU.mult),
                       [B_xt[i], B_ss, B_gbc], [B_yo[i]])
                P.dma("pool", ydst[tok0:tok0 + R, :], yo[i][:R, :], [B_yo[i]], [], is_out=True)
            A.release(m5)
            phase_end()

    seqs = [(s, False) for s in range(NP)] + ([(0, True)] if has_sample else [])
    try:
        for (s, is_s) in seqs:
            for layer in range(L):
                if layer == 0:
                    x_src = I["xs"][0] if is_s else I["xp"][s]
                else:
                    x_src = S_xres
                run_seq_layer(s, layer, is_s, x_src, layer == L - 1)
    except _Stop:
        pass
    P.finish()
    P.build(st)
    st.close()
    return nc, P


def make_consts(T):
    c = {}
    c["c_ident_bf"] = np.eye(128, dtype=np.float32).astype(ml_dtypes.bfloat16)
    c["c_ident_f"] = np.eye(128, dtype=np.float32)

    def rope_tab(pos):
        out = np.zeros((len(pos), 192), np.float32)
        for half, o in ((64, 0), (32, 128)):
            freq = (np.float32(10000.0) ** (-np.arange(half, dtype=np.float32) / np.float32(half))).astype(np.float32)
            ang = pos.astype(np.float32)[:, None] * freq[None, :]
            out[:, o:o + half] = np.cos(ang)
            out[:, o + half:o + 2 * half] = np.sin(ang)
        return out
    c["c_rope_p"] = rope_tab(np.arange(T))
    c["c_rope_s"] = rope_tab(PAST + np.arange(TS))
    nm = np.zeros((128, 128), np.float32)
    nm[:64, 64:] = -1e30
    c["c_negmask"] = nm
    bands = np.zeros((128, 12, 128), np.float32)
    bs = np.zeros((16, 8, 16), np.float32)
    for gi, w in enumerate((2, 4, 8, 16)):
        for tok in range(128):
            for j in range(tok - w + 1, tok + 1):
                if j >= 0:
                    bands[j, gi, tok] += 1.0 / w
                    bands[j, 8 + gi, tok] += 1.0 / min(tok + 1, w)
                else:
                    bands[128 + j, 4 + gi, tok] += 1.0 / w
            bands[tok, gi, tok] -= 1.0
            bands[tok, 8 + gi, tok] -= 1.0
        for tok in range(16):
            for j in range(tok - w + 1, tok + 1):
                if j >= 0:
                    bs[j, gi, tok] += 1.0 / w
                else:
                    bs[15 + j, 4 + gi, tok] += 1.0 / w
            bs[tok, gi, tok] -= 1.0
    c["c_bands"] = bands.astype(ml_dtypes.bfloat16)
    c["c_bands_s"] = bs.astype(ml_dtypes.bfloat16)
    return c


_CACHE = {}


def run(inputs, n_cores, NP, T, L, has_sample=True, dbg_stop=None):
    key = (NP, T, L, has_sample, dbg_stop)
    if key not in _CACHE:
        _CACHE[key] = build_program(NP, T, L, has_sample, dbg_stop=dbg_stop)
    nc, P = _CACHE[key]
    f = lambda a: np.ascontiguousarray(np.asarray(a, dtype=np.float32))
    consts = make_consts(T)
    wnames = ["norm_mix", "w_in", "conv_w", "conv_b", "conv_ln_g", "conv_ln_b", "conv_pw", "pool_w", "pool_scale",
              "w_out", "norm_ple", "w_ple_gate", "w_ple_proj"]
    shared = {n: f(inputs[n]) for n in wnames}
    shared["norm_final"] = f(inputs["norm_final"]).reshape(1, D)
    shared.update(consts)
    in_maps = []
    for c in range(n_cores):
        m = dict(shared)
        m["xp"] = f(inputs["x_prompt"][c * NP:(c + 1) * NP])
        m["pp"] = f(inputs["p_prompt"][:, c * NP:(c + 1) * NP])
        m["xs"] = f(inputs["x_sample"][c:c + 1])
        m["ps"] = f(inputs["p_sample"][:, c:c + 1])
        m["ck"] = f(inputs["cache_k"][:, c:c + 1]).reshape(L, 1, PAST, 256)
        m["cv"] = f(inputs["cache_v"][:, c:c + 1]).reshape(L, 1, PAST, 256)
        m["cki"] = f(inputs["cache_kidx"][:, c:c + 1])
        m["sconv"] = f(inputs["state_conv"][:, c:c + 1])
        m["spool"] = f(inputs["state_pool"][:, c:c + 1])
        in_maps.append(m)
    res = run_bass_kernel_spmd(nc, in_maps, core_ids=list(range(n_cores)))
    rs = res.results
    cat = lambda name, ax: np.concatenate([r[name] for r in rs], axis=ax)
    y_p = cat("y_p", 0)
    y_s = cat("y_s", 0)
    k_p = cat("k_p", 1).reshape(L, n_cores * NP, T, 2, 128)
    v_p = cat("v_p", 1).reshape(L, n_cores * NP, T, 2, 128)
    ki_p = cat("ki_p", 1)
    conv_p = cat("conv_p", 1)
    pool_p = cat("pool_p", 1)
    k_s = cat("k_s", 1).reshape(L, n_cores, TS, 2, 128)
    v_s = cat("v_s", 1).reshape(L, n_cores, TS, 2, 128)
    ki_s = cat("ki_s", 1)
    conv_s = cat("conv_s", 1)
    pool_s = cat("pool_s", 1)
    return (y_p, y_s, k_p, v_p, ki_p, conv_p, pool_p, k_s, v_s, ki_s, conv_s, pool_s)


def kernel(**inputs):
    outs = run(inputs, 8, 2, 2048, 2, True)
    return tuple(np.asarray(o, dtype=np.float32) for o in outs)
```
